# Optimizing a Trainium2 kernel written in Bass

```python
import math
import jax, jax.numpy as jnp
from jax import lax
import numpy as np

D_MODEL = 2048
BATCH = 32
SEQ = 256
DEPTH = 2
DEC_BATCH = 8
DEC_SEQ = 4096
PAST_LEN = 512

GRID_W = 64
N_BRANCH = 4
D_MIX = D_MODEL // 4
V_DIM = 128
H_A = D_MIX // V_DIM
DH = V_DIM // 2
ROPE_PAIRS = DH // 4
ROPE_BASE = 10000.0
Q_BLOCK = 128
CONV_B = 31
CONV_C = 3
CONV_R = 4
BS_R = 64
NB_R = D_MIX // BS_R
LRU_C = 8.0
D_FF = ((8 * D_MODEL // 3 + 127) // 128) * 128
CONV_FF = 3
EPS = 1e-6
IN_SIZES = (D_MIX, D_MIX, D_MIX, 2 * D_MIX, 3 * D_MIX, 2 * D_MIX, N_BRANCH * D_MODEL)
IN_SPLITS = tuple(int(s) for s in np.cumsum(IN_SIZES)[:-1])
N_IN = int(sum(IN_SIZES))

kernel_name = 'hybrid_diff_flow_trunk_step'


def rms_norm(x, g):
    xf = x.astype(jnp.float32)
    y = xf * lax.rsqrt(jnp.mean(xf * xf, axis=-1, keepdims=True) + EPS)
    return (y * g.astype(jnp.float32)).astype(x.dtype)


def layer_norm(x, g, b):
    xf = x.astype(jnp.float32)
    mu = jnp.mean(xf, axis=-1, keepdims=True)
    xc = xf - mu
    y = xc * lax.rsqrt(jnp.mean(xc * xc, axis=-1, keepdims=True) + EPS)
    return (y * g.astype(jnp.float32) + b.astype(jnp.float32)).astype(x.dtype)


def dwconv(x, w):
    k = w.shape[0]
    return lax.conv_general_dilated(
        x, w[:, None, :].astype(x.dtype), window_strides=(1,),
        padding=[((k - 1) // 2, k // 2)],
        dimension_numbers=('NWC', 'WIO', 'NWC'),
        feature_group_count=x.shape[-1])


def grid_angles(n):
    rows = n // GRID_W
    t_row = jnp.repeat(jnp.arange(rows), GRID_W).astype(jnp.float32)
    t_col = jnp.tile(jnp.arange(GRID_W), rows).astype(jnp.float32)
    inv = jnp.power(ROPE_BASE, -jnp.arange(ROPE_PAIRS, dtype=jnp.float32) / ROPE_PAIRS)
    return t_row[:, None] * inv, t_col[:, None] * inv


def _rot(xh, ang):
    cos = jnp.cos(ang)[:, None, None, :]
    sin = jnp.sin(ang)[:, None, None, :]
    x1, x2 = jnp.split(xh, 2, axis=-1)
    return jnp.concatenate([x1 * cos - x2 * sin, x1 * sin + x2 * cos], axis=-1)


def apply_rope_2d(x, angles):
    ang_row, ang_col = angles
    xf = x.astype(jnp.float32)
    xr, xc = jnp.split(xf, 2, axis=-1)
    return jnp.concatenate([_rot(xr, ang_row), _rot(xc, ang_col)], axis=-1).astype(x.dtype)


def diff_attention(q, k, v, lam):
    b, n = q.shape[0], q.shape[1]
    nb = n // Q_BLOCK
    qb = q.reshape(b, nb, Q_BLOCK, H_A, 2, DH).swapaxes(0, 1)
    scale = DH ** -0.5

    def block(qi):
        s = jnp.einsum('bqhjd,bkhjd->bhjqk', qi, k).astype(jnp.float32) * scale
        p = jax.nn.softmax(s, axis=-1)
        w = p[:, :, 0] - lam * p[:, :, 1]
        return jnp.einsum('bhqk,bkhe->bqhe', w.astype(v.dtype), v)

    o = lax.map(block, qb)
    return o.swapaxes(0, 1).reshape(b, n, H_A, V_DIM)


def _lin_combine(e1, e2):
    a1, b1 = e1
    a2, b2 = e2
    return a1 * a2, a2 * b1 + b2


def lru_scan(a, u, h0, reverse):
    if h0 is not None:
        idx = -1 if reverse else 0
        u = u.at[:, idx].add(a[:, idx] * h0)
    _, h = lax.associative_scan(_lin_combine, (a, u), axis=1, reverse=reverse)
    return h


def token_mixers(h, l, P, angles, ctx_k, ctx_v, h0):
    b, n, _ = h.shape
    f32 = jnp.float32
    proj = h @ P['w_in'][l]
    q, k, v, zb, zc, zr, zg = jnp.split(proj, IN_SPLITS, axis=-1)

    q = q.reshape(b, n, H_A, 2, DH)
    k = k.reshape(b, n, H_A, 2, DH)
    v = v.reshape(b, n, H_A, V_DIM)
    if angles is not None:
        q = apply_rope_2d(q, angles)
        k = apply_rope_2d(k, angles)
    if ctx_k is None:
        k_all, v_all = k, v
    else:
        k_all = jnp.concatenate([ctx_k.astype(k.dtype), k], axis=1)
        v_all = jnp.concatenate([ctx_v.astype(v.dtype), v], axis=1)
    lam_init = 0.8 - 0.6 * math.exp(-0.3 * l)
    lam = (jnp.exp(jnp.sum(P['lam_q1'][l].astype(f32) * P['lam_k1'][l].astype(f32)))
           - jnp.exp(jnp.sum(P['lam_q2'][l].astype(f32) * P['lam_k2'][l].astype(f32))) + lam_init)
    o = diff_attention(q, k_all, v_all, lam)
    y_a = (rms_norm(o, P['g_subln'][l]) * (1.0 - lam_init)).reshape(b, n, D_MIX)

    za, zgl = jnp.split(zb, 2, axis=-1)
    ub = za * jax.nn.sigmoid(zgl)
    ub = dwconv(ub, P['w_dw31'][l]) + P['b_dw31'][l]
    y_b = jax.nn.silu(layer_norm(ub, P['g_ln_conv'][l], P['b_ln_conv'][l]))

    gb, gc, xc = jnp.split(zc, 3, axis=-1)
    y_c = gb * dwconv(gc * xc, P['w_dw3'][l])

    xr_in, yr = jnp.split(zr, 2, axis=-1)
    xr = dwconv(xr_in, P['w_conv4'][l]) + P['b_conv4'][l]
    xrf = xr.astype(f32)
    xb = xr.reshape(b, n, NB_R, BS_R)

    def rglru(d, reverse):
        r = jax.nn.sigmoid((jnp.einsum('bnhi,hij->bnhj', xb, P['w_rg_a'][l, d]).reshape(b, n, D_MIX)
                            + P['b_rg_a'][l, d]).astype(f32))
        i = jax.nn.sigmoid((jnp.einsum('bnhi,hij->bnhj', xb, P['w_rg_x'][l, d]).reshape(b, n, D_MIX)
                            + P['b_rg_x'][l, d]).astype(f32))
        log_a = -LRU_C * r * jax.nn.softplus(-P['lru_lambda'][l, d].astype(f32))
        a = jnp.exp(log_a)
        u = jnp.sqrt(jnp.maximum(1.0 - a * a, 0.0)) * (i * xrf)
        h_init = None if h0 is None else h0[:, d].astype(f32)
        return lru_scan(a, u, h_init, reverse)

    h_f = rglru(0, False)
    h_b = rglru(1, True)
    y_d = ((h_f + h_b) * jax.nn.gelu(yr.astype(f32))).astype(h.dtype)
    h_last = jnp.stack([h_f[:, -1], h_b[:, 0]], axis=1).astype(h.dtype)

    merged = None
    for j, y_j in enumerate((y_a, y_b, y_c, y_d)):
        g_j = jax.nn.sigmoid(zg[..., j * D_MODEL:(j + 1) * D_MODEL])
        term = g_j * (y_j @ P['w_branch'][l, j])
        merged = term if merged is None else merged + term
    return merged @ P['w_out'][l], k, v, h_last


def conv_ffn(h, l, P):
    a, u = jnp.split(h @ P['w_ffn_up'][l], 2, axis=-1)
    a = dwconv(a, P['w_ffn_conv'][l]) + P['b_ffn_conv'][l]
    return (jax.nn.silu(a) * u) @ P['w_ffn_down'][l]


def trunk_layer(x, cvec, l, P, angles, ctx_k, ctx_v, h0):
    mod = jax.nn.silu(cvec) @ P['w_mod'][l] + P['b_mod'][l]
    sh1, sc1, gt1, sh2, sc2, gt2 = jnp.split(mod[:, None, :], 6, axis=-1)
    h = rms_norm(x, P['g_norm1'][l]) * (1.0 + sc1) + sh1
    mix, k, v, h_last = token_mixers(h, l, P, angles, ctx_k, ctx_v, h0)
    x = x + gt1 * mix
    h = rms_norm(x, P['g_norm2'][l]) * (1.0 + sc2) + sh2
    x = x + gt2 * conv_ffn(h, l, P)
    return x, k, v, h_last


def setup_inputs(seed: int = 0) -> dict:
    key = jax.random.key(seed)
    k = jax.random.split(key, 40)
    f32 = jnp.float32

    def nrm(kk, shape, scale):
        return jax.random.normal(kk, shape, f32) * scale

    def gain(kk, shape):
        return 1.0 + 0.02 * jax.random.normal(kk, shape, f32)

    u = jax.random.uniform(k[29], (DEPTH, 2, D_MIX), f32, 0.9, 0.999)
    a_base = u ** (1.0 / LRU_C)
    lru_lambda = jnp.log(a_base) - jnp.log1p(-a_base)
    return {
        'x_prompt': nrm(k[0], (BATCH, SEQ, D_MODEL), 1.0),
        'x_sample': nrm(k[1], (DEC_BATCH, DEC_SEQ, D_MODEL), 1.0),
        'cache_k': nrm(k[2], (DEC_BATCH, DEPTH, PAST_LEN, H_A, 2, DH), 1.0),
        'cache_v': nrm(k[3], (DEC_BATCH, DEPTH, PAST_LEN, H_A, V_DIM), 1.0),
        'state_lru': nrm(k[4], (DEC_BATCH, DEPTH, 2, D_MIX), 0.5),
        'c': nrm(k[5], (DEC_BATCH, D_MODEL), 1.0),
        'c_ctx': nrm(k[6], (D_MODEL,), 1.0),
        'w_mod': nrm(k[7], (DEPTH, D_MODEL, 6 * D_MODEL), 0.5 * D_MODEL ** -0.5),
        'b_mod': nrm(k[8], (DEPTH, 6 * D_MODEL), 0.02),
        'g_norm1': gain(k[9], (DEPTH, D_MODEL)),
        'g_norm2': gain(k[10], (DEPTH, D_MODEL)),
        'g_final': gain(k[11], (D_MODEL,)),
        'w_in': nrm(k[12], (DEPTH, D_MODEL, N_IN), D_MODEL ** -0.5),
        'lam_q1': nrm(k[13], (DEPTH, DH), 0.1),
        'lam_k1': nrm(k[14], (DEPTH, DH), 0.1),
        'lam_q2': nrm(k[15], (DEPTH, DH), 0.1),
        'lam_k2': nrm(k[16], (DEPTH, DH), 0.1),
        'g_subln': gain(k[17], (DEPTH, V_DIM)),
        'w_dw31': nrm(k[18], (DEPTH, CONV_B, D_MIX), CONV_B ** -0.5),
        'b_dw31': nrm(k[19], (DEPTH, D_MIX), 0.02),
        'g_ln_conv': gain(k[20], (DEPTH, D_MIX)),
        'b_ln_conv': nrm(k[21], (DEPTH, D_MIX), 0.02),
        'w_dw3': nrm(k[22], (DEPTH, CONV_C, D_MIX), CONV_C ** -0.5),
        'w_conv4': nrm(k[23], (DEPTH, CONV_R, D_MIX), CONV_R ** -0.5),
        'b_conv4': nrm(k[24], (DEPTH, D_MIX), 0.02),
        'w_rg_a': nrm(k[25], (DEPTH, 2, NB_R, BS_R, BS_R), BS_R ** -0.5),
        'b_rg_a': nrm(k[26], (DEPTH, 2, D_MIX), 0.02),
        'w_rg_x': nrm(k[27], (DEPTH, 2, NB_R, BS_R, BS_R), BS_R ** -0.5),
        'b_rg_x': nrm(k[28], (DEPTH, 2, D_MIX), 0.02),
        'lru_lambda': lru_lambda,
        'w_branch': nrm(k[30], (DEPTH, N_BRANCH, D_MIX, D_MODEL), D_MIX ** -0.5),
        'w_out': nrm(k[31], (DEPTH, D_MODEL, D_MODEL), D_MODEL ** -0.5),
        'w_ffn_up': nrm(k[32], (DEPTH, D_MODEL, 2 * D_FF), D_MODEL ** -0.5),
        'w_ffn_conv': nrm(k[33], (DEPTH, CONV_FF, D_FF), CONV_FF ** -0.5),
        'b_ffn_conv': nrm(k[34], (DEPTH, D_FF), 0.02),
        'w_ffn_down': nrm(k[35], (DEPTH, D_FF, D_MODEL), D_FF ** -0.5),
    }


def reference(x_prompt, x_sample, cache_k, cache_v, state_lru, c, c_ctx, w_mod, b_mod,
              g_norm1, g_norm2, g_final, w_in, lam_q1, lam_k1, lam_q2, lam_k2, g_subln,
              w_dw31, b_dw31, g_ln_conv, b_ln_conv, w_dw3, w_conv4, b_conv4,
              w_rg_a, b_rg_a, w_rg_x, b_rg_x, lru_lambda, w_branch, w_out,
              w_ffn_up, w_ffn_conv, b_ffn_conv, w_ffn_down):
    P = dict(w_mod=w_mod, b_mod=b_mod, g_norm1=g_norm1, g_norm2=g_norm2, w_in=w_in,
             lam_q1=lam_q1, lam_k1=lam_k1, lam_q2=lam_q2, lam_k2=lam_k2, g_subln=g_subln,
             w_dw31=w_dw31, b_dw31=b_dw31, g_ln_conv=g_ln_conv, b_ln_conv=b_ln_conv,
             w_dw3=w_dw3, w_conv4=w_conv4, b_conv4=b_conv4, w_rg_a=w_rg_a, b_rg_a=b_rg_a,
             w_rg_x=w_rg_x, b_rg_x=b_rg_x, lru_lambda=lru_lambda, w_branch=w_branch,
             w_out=w_out, w_ffn_up=w_ffn_up, w_ffn_conv=w_ffn_conv, b_ffn_conv=b_ffn_conv,
             w_ffn_down=w_ffn_down)

    xp = x_prompt
    c_ctx_b = c_ctx[None, :]
    ks, vs, ss = [], [], []
    for l in range(DEPTH):
        xp, k_l, v_l, s_l = trunk_layer(xp, c_ctx_b, l, P, None, None, None, None)
        ks.append(k_l)
        vs.append(v_l)
        ss.append(s_l)
    y_prompt = rms_norm(xp, g_final)
    new_cache_k = jnp.stack(ks, axis=1)
    new_cache_v = jnp.stack(vs, axis=1)
    new_state_lru = jnp.stack(ss, axis=1)

    angles = grid_angles(x_sample.shape[1])
    xs = x_sample
    for l in range(DEPTH):
        xs, _, _, _ = trunk_layer(xs, c, l, P, angles, cache_k[:, l], cache_v[:, l], state_lru[:, l])
    y_sample = rms_norm(xs, g_final)
    return (y_prompt, y_sample, new_cache_k, new_cache_v, new_state_lru)
```

```python
import contextlib
import math
import numpy as np
import concourse.bass as bass
import concourse.mybir as mybir
from concourse.bass_utils import run_bass_kernel_spmd

F32 = mybir.dt.float32
BF16 = mybir.dt.bfloat16
I32 = mybir.dt.int32
AF = mybir.ActivationFunctionType
ALU = mybir.AluOpType

COMPUTE = ("tensor", "vector", "scalar", "gpsimd")
QUEUES = ("sync", "scalar", "gpsimd")
ALL = ("sync", "tensor", "vector", "scalar", "gpsimd")
EPS = 1e-6


class Tok:
    __slots__ = ("w", "r")

    def __init__(self):
        self.w = None
        self.r = {}


class Prog:
    def __init__(self, nc, n_dma_sems=6):
        self.nc = nc
        self.es = contextlib.ExitStack()
        self.lists = {e: [] for e in ALL}
        self.sems = {}
        self.cnt = {}
        for e in COMPUTE:
            self.sems["c_" + e] = self.es.enter_context(nc.semaphore("c_" + e))
            self.cnt["c_" + e] = 0
        self.dpool = {}
        self.dnext = {}
        for q in QUEUES:
            keys = []
            for i in range(n_dma_sems if q != "scalar" else 3):
                k = f"d_{q}{i}"
                self.sems[k] = self.es.enter_context(nc.semaphore(k))
                self.cnt[k] = 0
                keys.append(k)
            self.dpool[q] = keys
            self.dnext[q] = 0
        self.waited = {e: {} for e in ALL}

    def sbuf(self, name, shape, dtype):
        return self.es.enter_context(self.nc.sbuf_tensor(name, list(shape), dtype))

    def psum(self, name, shape, dtype=F32):
        return self.es.enter_context(self.nc.psum_tensor(name, list(shape), dtype))

    def _collect(self, eng, reads, writes, extra=()):
        need = {}

        def add(ev):
            if ev is None:
                return
            k, v = ev
            if need.get(k, 0) < v:
                need[k] = v

        for t in reads:
            add(t.w)
        for t in writes:
            add(t.w)
            for k, v in t.r.items():
                add((k, v))
        for ev in extra:
            add(ev)
        out = []
        wd = self.waited[eng]
        own = "c_" + eng
        for k, v in need.items():
            if k == own and v > self.cnt.get(own, 0):
                continue
            if wd.get(k, 0) < v:
                wd[k] = v
                out.append((self.sems[k], v))
        return out

    def _commit(self, ev, reads, writes):
        k, v = ev
        for t in reads:
            if t.r.get(k, 0) < v:
                t.r[k] = v
        for t in writes:
            t.w = ev
            t.r = {}

    def op(self, eng, fn, reads=(), writes=(), inc=True):
        waits = self._collect(eng, reads, writes)
        key = "c_" + eng
        sem = self.sems[key]
        if inc:
            self.cnt[key] += 1
            ev = (key, self.cnt[key])
        else:
            ev = None

        def emit(e, waits=waits, fn=fn, inc=inc, sem=sem):
            for s, v in waits:
                e.wait_ge(s, v)
            ins = fn(e)
            if inc:
                ins.then_inc(sem, 1)

        self.lists[eng].append(emit)
        if inc:
            self._commit(ev, reads, writes)
        else:
            nxt = (key, self.cnt[key] + 1)
            self._commit(nxt, reads, ())
            for t in writes:
                t.w = nxt
                t.r = {}
        return ev

    def dma(self, q, fn, reads=(), writes=()):
        pool = self.dpool[q]
        k = pool[self.dnext[q] % len(pool)]
        self.dnext[q] += 1
        prev = (k, self.cnt[k]) if self.cnt[k] > 0 else None
        waits = self._collect(q, reads, writes, extra=(prev,) if prev else ())
        self.cnt[k] += 16
        ev = (k, self.cnt[k])
        sem = self.sems[k]

        def emit(e, waits=waits, fn=fn, sem=sem):
            for s, v in waits:
                e.wait_ge(s, v)
            fn(e).then_inc(sem, 16)

        self.lists[q].append(emit)
        self._commit(ev, reads, writes)
        return ev

    def barrier(self):
        allev = [(k, v) for k, v in self.cnt.items() if v > 0]
        for eng in ALL:
            waits = []
            wd = self.waited[eng]
            for k, v in allev:
                if wd.get(k, 0) < v:
                    wd[k] = v
                    waits.append((self.sems[k], v))
            if waits:
                def emit(e, waits=waits):
                    for s, v in waits:
                        e.wait_ge(s, v)
                self.lists[eng].append(emit)

    def finish(self):
        self.barrier()
        lists = self.lists
        with self.nc.Block() as block:
            @block.sync
            def _(e):
                for f in lists["sync"]:
                    f(e)

            @block.tensor
            def _(e):
                for f in lists["tensor"]:
                    f(e)

            @block.vector
            def _(e):
                for f in lists["vector"]:
                    f(e)

            @block.scalar
            def _(e):
                for f in lists["scalar"]:
                    f(e)

            @block.gpsimd
            def _(e):
                for f in lists["gpsimd"]:
                    f(e)
        self.es.close()


class Buf:
    __slots__ = ("ap", "t")

    def __init__(self, ap):
        self.ap = ap
        self.t = Tok()


class Arena:
    def __init__(self, ap):
        self.ap = ap
        self.W = ap.shape[1]
        self.off = 0

    def reset(self):
        self.off = 0

    def f32(self, n):
        n2 = (n + 1) // 2 * 2
        a = self.ap[:, self.off:self.off + n]
        self.off += n2
        assert self.off <= self.W, ("arena overflow", self.off, self.W)
        return a

    def bf16(self, n):
        w = (n + 3) // 4 * 2
        a = self.ap[:, self.off:self.off + w].bitcast(BF16)[:, 0:n]
        self.off += w
        assert self.off <= self.W, ("arena overflow", self.off, self.W)
        return a

    def i32(self, n):
        return self.f32(n).bitcast(I32)


class Cfg:
    def __init__(self, D=2048, NS=4096, SP=256, NPB=4, PAST=512, DEPTH=2, GW=64, ARENA=31500, NWB=3):
        self.D = D
        self.KC = D // 128
        self.DM = D // 4
        self.MC = self.DM // 128
        self.DFF = ((8 * D // 3 + 127) // 128) * 128
        self.FC = self.DFF // 128
        self.NS, self.SP, self.NPB, self.PAST, self.DEPTH, self.GW = NS, SP, NPB, PAST, DEPTH, GW
        self.PC = PAST // 128
        self.NTOK = NS + NPB * SP
        self.NIN = 10 * self.DM + 4 * D
        self.NINC = self.NIN // 128
        self.TS = min(512, NS)
        self.ARENA = ARENA
        self.NWB = NWB
        self.LRU_TT = 256
        self.UMAX = max(self.KC * 128 * 2, self.FC * 128, 4 * self.MC * 128)
        assert self.MC >= 1 and NS % self.TS == 0 and SP % 128 == 0 and SP <= 512 and PAST % 128 == 0


WSHAPES = lambda c: dict(
    w_mod=(c.DEPTH, c.D, 6 * c.D), b_mod=(c.DEPTH, 6 * c.D), g_norm1=(c.DEPTH, c.D), g_norm2=(c.DEPTH, c.D),
    g_final=(c.D,), w_in=(c.DEPTH, c.D, c.NIN), lam_q1=(c.DEPTH, 64), lam_k1=(c.DEPTH, 64), lam_q2=(c.DEPTH, 64),
    lam_k2=(c.DEPTH, 64), g_subln=(c.DEPTH, 128), w_dw31=(c.DEPTH, 31, c.DM), b_dw31=(c.DEPTH, c.DM),
    g_ln_conv=(c.DEPTH, c.DM), b_ln_conv=(c.DEPTH, c.DM), w_dw3=(c.DEPTH, 3, c.DM), w_conv4=(c.DEPTH, 4, c.DM),
    b_conv4=(c.DEPTH, c.DM), w_rg_a=(c.DEPTH, 2, c.DM // 64, 64, 64), b_rg_a=(c.DEPTH, 2, c.DM),
    w_rg_x=(c.DEPTH, 2, c.DM // 64, 64, 64), b_rg_x=(c.DEPTH, 2, c.DM), lru_lambda=(c.DEPTH, 2, c.DM),
    w_branch=(c.DEPTH, 4, c.DM, c.D), w_out=(c.DEPTH, c.D, c.D), w_ffn_up=(c.DEPTH, c.D, 2 * c.DFF),
    w_ffn_conv=(c.DEPTH, 3, c.DFF), b_ffn_conv=(c.DEPTH, c.DFF), w_ffn_down=(c.DEPTH, c.DFF, c.D))


def rows128(ap):
    nd = len(ap.shape)
    if nd == 1:
        return ap.rearrange("(r c) -> r c", c=128)
    if nd == 2:
        return ap.rearrange("a (r c) -> (a r) c", c=128)
    raise ValueError


class K:
    def __init__(self, cfg):
        c = self.c = cfg
        nc = self.nc = bass.Bass("TRN2", target_bir_lowering=False)
        self.p = Prog(nc)
        din = lambda n, s: nc.dram_tensor(n, list(s), F32, kind="ExternalInput").ap()
        dout = lambda n, s: nc.dram_tensor(n, list(s), F32, kind="ExternalOutput").ap()
        dscr = lambda n, s, dt: nc.dram_tensor(n, list(s), dt, kind="Internal").ap()
        self.I = dict(xs=din("xs", (c.NS, c.D)), xp=din("xp", (c.NPB * c.SP, c.D)),
                      ck=din("ck", (c.DEPTH, c.PAST, c.DM)), cv=din("cv", (c.DEPTH, c.PAST, c.DM)),
                      st=din("st", (c.DEPTH, 2 * c.DM)), cvec=din("cvec", (2 * c.D,)))
        for n, s in WSHAPES(c).items():
            self.I[n] = din(n, s)
        self.O = dict(ys=dout("ys", (c.NS, c.D)), yp=dout("yp", (c.NPB * c.SP, c.D)),
                      nk=dout("nk", (c.NPB, c.DEPTH, c.SP, c.DM)), nv=dout("nv", (c.NPB, c.DEPTH, c.SP, c.DM)),
                      nst=dout("nst", (c.NPB, c.DEPTH, 2 * c.DM)))
        S = self.S = {}
        S["XT"] = dscr("s_xt", (c.D, c.NTOK), F32)
        S["HT"] = dscr("s_ht", (c.D, c.NTOK), BF16)
        S["H2"] = dscr("s_h2", (c.D, c.NTOK), BF16)
        S["Q"] = dscr("s_q", (c.DM, c.NTOK), BF16)
        S["KT"] = dscr("s_k", (c.DM, c.NTOK), BF16)
        S["V"] = dscr("s_v", (c.NTOK, c.DM), BF16)
        for n in ("UB", "GCX", "GB", "XR", "GY"):
            S[n] = dscr("s_" + n, (c.DM, c.NTOK), F32)
        S["Y"] = dscr("s_y", (4 * c.DM, c.NTOK), BF16)
        S["RC"] = dscr("s_rc", (128, c.NS), F32)
        S["RS"] = dscr("s_rs", (128, c.NS), F32)
        for l in range(c.DEPTH):
            S[f"WIN{l}"] = dscr(f"w_in_b{l}", (c.NINC, 128, c.KC * 128), BF16)
            S[f"WBR{l}"] = dscr(f"w_br_b{l}", (c.KC, 128, 4 * c.MC * 128), BF16)
            S[f"WOUT{l}"] = dscr(f"w_out_b{l}", (c.KC, 128, c.KC * 128), BF16)
            S[f"WUP{l}"] = dscr(f"w_up_b{l}", (c.FC, 128, 2 * c.KC * 128), BF16)
            S[f"WDN{l}"] = dscr(f"w_dn_b{l}", (c.KC, 128, c.FC * 128), BF16)
        p = self.p
        self.ident = Buf(p.sbuf("ident", (128, 128), F32)[:])
        self.ones = Buf(p.sbuf("ones", (128, 128), F32)[:])
        self.onesb = Buf(p.sbuf("onesb", (128, 128), BF16)[:])
        self.pm = Buf(p.sbuf("pm", (128, 128), F32)[:])
        self.wb = [Buf(p.sbuf(f"wb{i}", (128, c.UMAX), BF16)[:]) for i in range(c.NWB)]
        self.wbi = 0
        self.PS = [Buf(p.psum(f"ps{i}", (128, 512))[:]) for i in range(8)]
        self.ar = Arena(p.sbuf("arena", (128, c.ARENA), F32)[:])
        self.tiles = [(i * c.TS, c.TS, 0, 0, i == 0, i == c.NS // c.TS - 1) for i in range(c.NS // c.TS)] + \
                     [(c.NS + b * c.SP, c.SP, 1, 1 + b, True, True) for b in range(c.NPB)]
        self.seqs = [(0, c.NS, True, 0)] + [(c.NS + b * c.SP, c.SP, False, b) for b in range(c.NPB)]

    def mm(self, out, lhsT, rhs, start, stop, reads, writes, inc=None):
        self.p.op("tensor", lambda e: e.matmul(out, lhsT=lhsT, rhs=rhs, start=start, stop=stop),
                  reads, writes, inc=True)

    def tr(self, out, in_, ident, reads, writes, inc=True):
        self.p.op("tensor", lambda e: e.transpose(out=out, in_=in_, identity=ident), reads, writes, inc=inc)

    def act(self, out, in_, func, reads, writes, scale=1.0, bias=0.0):
        self.p.op("scalar", lambda e: e.activation(out=out, in_=in_, func=func, bias=bias, scale=scale), reads, writes)

    def tt(self, out, in0, in1, op, reads, writes, eng="vector"):
        self.p.op(eng, lambda e: e.tensor_tensor(out=out, in0=in0, in1=in1, op=op), reads, writes)

    def ts(self, out, in0, s1, s2, op0, op1, reads, writes, eng="vector"):
        if op1 is None:
            self.p.op(eng, lambda e: e.tensor_scalar(out=out, in0=in0, scalar1=s1, scalar2=None, op0=op0), reads, writes)
        else:
            self.p.op(eng, lambda e: e.tensor_scalar(out=out, in0=in0, scalar1=s1, scalar2=s2, op0=op0, op1=op1), reads, writes)

    def stt(self, out, in0, scalar, in1, op0, op1, reads, writes):
        self.p.op("vector", lambda e: e.scalar_tensor_tensor(out=out, in0=in0, scalar=scalar, in1=in1, op0=op0, op1=op1),
                  reads, writes)

    def cp(self, out, in_, reads, writes, eng="vector"):
        if eng == "scalar":
            self.p.op("scalar", lambda e: e.copy(out=out, in_=in_), reads, writes)
        else:
            self.p.op(eng, lambda e: e.tensor_copy(out=out, in_=in_), reads, writes)

    def recip(self, out, in_, reads, writes):
        self.p.op("vector", lambda e: e.reciprocal(out=out, in_=in_), reads, writes)

    def memset(self, ap, val, writes, eng="gpsimd"):
        self.p.op(eng, lambda e: e.memset(ap, val), (), writes)

    def dma(self, q, out, in_, reads, writes):
        reads = [t for t in reads if t is not None]
        writes = [t for t in writes if t is not None]
        self.p.dma(q, lambda e: e.dma_start(out=out, in_=in_), reads, writes)

    def wload(self, src):
        b = self.wb[self.wbi % len(self.wb)]
        self.wbi += 1
        E = src.shape[1]
        self.dma("sync", b.ap[:, 0:E], src, [self.wtok], [b.t])
        return b

    def setup_consts(self):
        p = self.p
        self.memset(self.ident.ap, 0.0, [self.ident.t])
        p.op("gpsimd", lambda e: e.affine_select(out=self.ident.ap, in_=self.ident.ap, compare_op=ALU.not_equal, fill=1.0,
                                                 base=0, pattern=[[-1, 128]], channel_multiplier=1),
             [self.ident.t], [self.ident.t])
        self.memset(self.ones.ap, 1.0, [self.ones.t])
        self.memset(self.onesb.ap, 1.0, [self.onesb.t])

    def load_params(self):
        c, I = self.c, self.I
        ents = []

        def add(name, ap):
            r = rows128(ap)
            ents.append((name, r, r.shape[0]))

        add("gfinal", I["g_final"])
        add("cvec", I["cvec"])
        for l in range(c.DEPTH):
            add(f"g1_{l}", I["g_norm1"][l])
            add(f"g2_{l}", I["g_norm2"][l])
            add(f"bmod_{l}", I["b_mod"][l])
            add(f"dw31_{l}", I["w_dw31"][l])
            for n in ("b_dw31", "g_ln_conv", "b_ln_conv", "b_conv4", "g_subln"):
                add(f"{n}_{l}", I[n][l])
            for n in ("w_dw3", "w_conv4", "b_rg_a", "b_rg_x", "lru_lambda"):
                add(f"{n}_{l}", I[n][l])
            add(f"st_{l}", I["st"][l])
            for t in range(3):
                add(f"fcw{t}_{l}", I["w_ffn_conv"][l, t])
            add(f"fcb_{l}", I["b_ffn_conv"][l])
        tiles_ = [[]]
        used = 0
        for name, r, R in ents:
            assert R <= 128
            if used + R > 128:
                tiles_.append([])
                used = 0
            tiles_[-1].append((name, r, R, used))
            used += R
        NT = len(tiles_)
        self.PRM = self.p.sbuf("prm", (128, NT * 128), F32)[:]
        self.prm_t = Tok()
        self.prm = {}
        stg = [Buf(self.ar.f32(128)) for _ in range(2)]
        for s in stg:
            self.memset(s.ap, 0.0, [s.t])
        for ti, tl in enumerate(tiles_):
            s = stg[ti % 2]
            ps = self.PS[ti % 2]
            for name, r, R, r0 in tl:
                self.dma("gpsimd", s.ap[r0:r0 + R, :], r, [], [s.t])
                self.prm[name] = self.PRM[:, ti * 128 + r0: ti * 128 + r0 + R]
            self.tr(ps.ap[:, 0:128], s.ap, self.ident.ap, [s.t, self.ident.t], [ps.t])
            self.cp(self.PRM[:, ti * 128:(ti + 1) * 128], ps.ap[:, 0:128], [ps.t], [self.prm_t])

    def convert_weights(self):
        c, I, S = self.c, self.I, self.S
        self.ar.reset()
        NB = 3
        CB = 2048
        st32 = [Buf(self.ar.f32(CB)) for _ in range(NB)]
        st16 = [Buf(self.ar.bf16(CB)) for _ in range(NB)]
        step = [0]

        def conv(src, dst, u0, offf):
            Kr, N = src.shape
            for kc in range(Kr // 128):
                for n0 in range(0, N, CB):
                    nn = min(CB, N - n0)
                    nb = nn // 128
                    i = step[0] % NB
                    step[0] += 1
                    a, b = st32[i], st16[i]
                    self.dma("sync", a.ap[:, 0:nn], src[kc * 128:(kc + 1) * 128, n0:n0 + nn], [], [a.t])
                    self.cp(b.ap[:, 0:nn], a.ap[:, 0:nn], [a.t], [b.t], eng=("vector" if step[0] % 2 else "gpsimd"))
                    off = offf(kc)
                    j0 = u0 + n0 // 128
                    self.dma("scalar", dst[j0:j0 + nb, :, off:off + 128].rearrange("u p c -> p u c"),
                             b.ap[:, 0:nn].rearrange("p (u c) -> p u c", c=128), [b.t], [self.wtok])

        for l in range(c.DEPTH):
            conv(I["w_in"][l], S[f"WIN{l}"], 0, lambda kc: kc * 128)
            for j in range(4):
                conv(I["w_branch"][l, j], S[f"WBR{l}"], 0, lambda kc, j=j: (j * c.MC + kc) * 128)
            conv(I["w_out"][l], S[f"WOUT{l}"], 0, lambda kc: kc * 128)
            for s in range(2):
                conv(I["w_ffn_up"][l][:, s * c.DFF:(s + 1) * c.DFF], S[f"WUP{l}"], 0,
                     lambda kc, s=s: (s * c.KC + kc) * 128)
            conv(I["w_ffn_down"][l], S[f"WDN{l}"], 0, lambda kc: kc * 128)

    def compute_mod(self):
        c, I = self.c, self.I
        KC = c.KC
        self.ar.reset()
        NMC = 6 * KC
        self.MOD = self.p.sbuf("mod", (128, c.DEPTH * NMC * 2), F32)[:]
        self.AB = self.p.sbuf("ab", (128, c.DEPTH * 2 * 2 * KC), F32)[:]
        self.mod_t = Tok()
        scv = Buf(self.ar.f32(2 * KC))
        self.act(scv.ap, self.prm["cvec"], AF.Silu, [self.prm_t], [scv.t])
        CBK = 256
        wst = [Buf(self.ar.f32(KC * CBK)) for _ in range(2)]
        ps = self.PS[2]
        k = 0
        for l in range(c.DEPTH):
            for cb in range(6 * c.D // CBK):
                w = wst[k % 2]
                k += 1
                self.dma("sync", w.ap.rearrange("p (k n) -> p k n", n=CBK),
                         I["w_mod"][l][:, cb * CBK:(cb + 1) * CBK].rearrange("(k p) n -> p k n", p=128), [], [w.t])
                for nn in range(CBK // 128):
                    n = cb * (CBK // 128) + nn
                    for kc in range(KC):
                        self.mm(ps.ap[:, 2 * n:2 * n + 2], w.ap[:, kc * CBK + nn * 128: kc * CBK + nn * 128 + 128],
                                scv.ap[:, kc::KC], kc == 0, kc == KC - 1, [w.t, scv.t], [ps.t])
            for v in range(2):
                base = (l * 2 + v) * NMC
                self.tt(self.MOD[:, base:base + NMC], ps.ap[:, v:2 * NMC:2], self.prm[f"bmod_{l}"], ALU.add,
                        [ps.t, self.prm_t], [self.mod_t])
                for s, (gname, scoff) in enumerate((("g1", KC), ("g2", 4 * KC))):
                    o = ((l * 2 + v) * 2 + s) * KC
                    self.stt(self.AB[:, o:o + KC], self.MOD[:, base + scoff: base + scoff + KC], 1.0,
                             self.prm[f"{gname}_{l}"], ALU.add, ALU.mult, [self.mod_t, self.prm_t], [self.mod_t])

    def modv(self, l, v, which):
        KC = self.c.KC
        base = (l * 2 + v) * 6 * KC + which * KC
        return self.MOD[:, base:base + KC]

    def abv(self, l, v, s):
        KC = self.c.KC
        o = ((l * 2 + v) * 2 + s) * KC
        return self.AB[:, o:o + KC]

    def setup_misc(self):
        c, I = self.c, self.I
        MC = c.MC
        self.ar.reset()
        self.MISC = self.p.sbuf("misc", (128, c.DEPTH * (4 + 4 * MC)), F32)[:]
        self.misc_t = Tok()
        self.BD = self.p.sbuf("bd", (128, c.DEPTH * 4 * MC * 128), BF16)[:]
        self.bd_t = Tok()
        lamst = Buf(self.ar.f32(4 * 64))
        tmp = Buf(self.ar.f32(8))
        bst = Buf(self.ar.f32(4 * MC * 128))
        for l in range(c.DEPTH):
            mb = l * (4 + 4 * MC)
            for i, n in enumerate(("lam_q1", "lam_k1", "lam_q2", "lam_k2")):
                self.dma("gpsimd", lamst.ap[:, i * 64:(i + 1) * 64], I[n][l].partition_broadcast(128), [], [lamst.t])
            for i in range(2):
                self.tt(lamst.ap[:, i * 128:i * 128 + 64], lamst.ap[:, i * 128:i * 128 + 64],
                        lamst.ap[:, i * 128 + 64:i * 128 + 128], ALU.mult, [lamst.t], [lamst.t])
                self.p.op("vector", lambda e, i=i: e.reduce_sum(out=tmp.ap[:, i:i + 1], in_=lamst.ap[:, i * 128:i * 128 + 64],
                                                               axis=mybir.AxisListType.X), [lamst.t], [tmp.t])
            self.act(tmp.ap[:, 2:4], tmp.ap[:, 0:2], AF.Exp, [tmp.t], [tmp.t])
            lam_init = 0.8 - 0.6 * math.exp(-0.3 * l)
            self.stt(self.MISC[:, mb:mb + 1], tmp.ap[:, 3:4], -lam_init, tmp.ap[:, 2:3], ALU.add, ALU.subtract,
                     [tmp.t], [self.misc_t])
            self.ts(self.MISC[:, mb + 1:mb + 2], self.prm[f"g_subln_{l}"], 1.0 - lam_init, None, ALU.mult, None,
                    [self.prm_t], [self.misc_t])
            sp = Buf(self.ar.f32(2 * MC))
            self.act(sp.ap, self.prm[f"lru_lambda_{l}"], AF.Exp, [self.prm_t], [sp.t], scale=-1.0)
            self.act(sp.ap, sp.ap, AF.Ln, [sp.t], [sp.t], bias=1.0)
            self.ts(self.MISC[:, mb + 4:mb + 4 + 2 * MC], sp.ap, -8.0, None, ALU.mult, None, [sp.t], [self.misc_t])
            self.ts(self.MISC[:, mb + 4 + 2 * MC:mb + 4 + 4 * MC], sp.ap, -16.0, None, ALU.mult, None, [sp.t], [self.misc_t])
            self.memset(bst.ap, 0.0, [bst.t])
            for g, n in enumerate(("w_rg_a", "w_rg_x")):
                for d in range(2):
                    for m in range(MC):
                        o = ((g * 2 + d) * MC + m) * 128
                        for hb in range(2):
                            self.dma("gpsimd", bst.ap[hb * 64:(hb + 1) * 64, o + hb * 64:o + hb * 64 + 64],
                                     I[n][l, d, 2 * m + hb], [], [bst.t])
            self.cp(self.BD[:, l * 4 * MC * 128:(l + 1) * 4 * MC * 128], bst.ap, [bst.t], [self.bd_t])

    def misc(self, l, i):
        mb = l * (4 + 4 * self.c.MC)
        return self.MISC[:, mb + i:mb + i + 1]

    def setup_rope(self):
        c = self.c
        self.ar.reset()
        NS, GW = c.NS, c.GW
        pi_ = Buf(self.ar.i32(2))
        ti = Buf(self.ar.i32(8))
        tf = Buf(self.ar.f32(16))
        self.p.op("gpsimd", lambda e: e.iota(pi_.ap[:, 0:1], pattern=[[0, 1]], base=0, channel_multiplier=1), [], [pi_.t])
        sh = lambda o, s, m: self.p.op("vector", lambda e: e.tensor_scalar(out=ti.ap[:, o:o + 1], in0=pi_.ap[:, 0:1], scalar1=s,
                                                                          scalar2=m, op0=ALU.arith_shift_right,
                                                                          op1=ALU.bitwise_and), [pi_.t], [ti.t])
        sh(0, 0, 15)
        sh(1, 5, 1)
        sh(2, 4, 1)
        self.cp(tf.ap[:, 0:3], ti.ap[:, 0:3], [ti.t], [tf.t])
        self.act(tf.ap[:, 3:4], tf.ap[:, 0:1], AF.Exp, [tf.t], [tf.t], scale=-math.log(10000.0) / 16.0)
        self.tt(tf.ap[:, 5:6], tf.ap[:, 3:4], tf.ap[:, 1:2], ALU.mult, [tf.t], [tf.t])
        self.tt(tf.ap[:, 4:5], tf.ap[:, 3:4], tf.ap[:, 5:6], ALU.subtract, [tf.t], [tf.t])
        R = Buf(self.ar.f32(NS))
        Cc = Buf(self.ar.f32(NS))
        ang = Buf(self.ar.f32(NS))
        t2 = Buf(self.ar.f32(NS))
        rows = NS // GW
        self.p.op("gpsimd", lambda e: e.iota(R.ap.rearrange("p (r g) -> p r g", g=GW), pattern=[[1, rows], [0, GW]], base=0,
                                             channel_multiplier=0, allow_small_or_imprecise_dtypes=True), [], [R.t])
        self.p.op("gpsimd", lambda e: e.iota(Cc.ap.rearrange("p (r g) -> p r g", g=GW), pattern=[[0, rows], [1, GW]], base=0,
                                             channel_multiplier=0, allow_small_or_imprecise_dtypes=True), [], [Cc.t])
        self.ts(ang.ap, R.ap, tf.ap[:, 4:5], None, ALU.mult, None, [R.t, tf.t], [ang.t])
        self.stt(ang.ap, Cc.ap, tf.ap[:, 5:6], ang.ap, ALU.mult, ALU.add, [Cc.t, tf.t, ang.t], [ang.t])
        MAGIC = 12582912.0
        TWO_PI = 2.0 * math.pi
        for which, shift in ((0, math.pi / 2), (1, 0.0)):
            self.ts(t2.ap, ang.ap, shift, 1.0 / TWO_PI, ALU.add, ALU.mult, [ang.t], [t2.t])
            self.ts(t2.ap, t2.ap, MAGIC, MAGIC, ALU.add, ALU.subtract, [t2.t], [t2.t])
            self.stt(t2.ap, t2.ap, -TWO_PI, ang.ap, ALU.mult, ALU.add, [t2.t, ang.t], [t2.t])
            self.ts(t2.ap, t2.ap, shift, 3.14159, ALU.add, ALU.min, [t2.t], [t2.t])
            self.ts(t2.ap, t2.ap, -3.14159, None, ALU.max, None, [t2.t], [t2.t])
            self.act(t2.ap, t2.ap, AF.Sin, [t2.t], [t2.t])
            self.dma("gpsimd", self.S["RC" if which == 0 else "RS"], t2.ap, [t2.t], [self.wtok])
        A = Buf(self.ar.f32(128))
        B_ = Buf(self.ar.f32(128))
        mi = Buf(self.ar.i32(128))
        mf = Buf(self.ar.f32(128))
        self.memset(A.ap, 0.0, [A.t])
        self.memset(B_.ap, 0.0, [B_.t])
        self.p.op("gpsimd", lambda e: e.affine_select(out=A.ap, in_=A.ap, compare_op=ALU.not_equal, fill=-1.0, base=-16,
                                                      pattern=[[-1, 128]], channel_multiplier=1), [A.t], [A.t])
        self.p.op("gpsimd", lambda e: e.affine_select(out=B_.ap, in_=B_.ap, compare_op=ALU.not_equal, fill=1.0, base=16,
                                                      pattern=[[-1, 128]], channel_multiplier=1), [B_.t], [B_.t])
        self.p.op("gpsimd", lambda e: e.iota(mi.ap, pattern=[[1, 128]], base=0, channel_multiplier=0), [], [mi.t])
        self.p.op("vector", lambda e: e.tensor_scalar(out=mi.ap, in0=mi.ap, scalar1=4, scalar2=1, op0=ALU.arith_shift_right,
                                                      op1=ALU.bitwise_and), [mi.t], [mi.t])
        self.cp(mf.ap, mi.ap, [mi.t], [mf.t])
        self.tt(B_.ap, B_.ap, mf.ap, ALU.mult, [B_.t, mf.t], [B_.t])
        self.ts(mf.ap, mf.ap, -1.0, 1.0, ALU.mult, ALU.add, [mf.t], [mf.t])
        self.tt(A.ap, A.ap, mf.ap, ALU.mult, [A.t, mf.t], [A.t])
        self.tt(self.pm.ap, A.ap, B_.ap, ALU.add, [A.t, B_.t], [self.pm.t])

    def norm_mod(self, x, T, Acols, Bcols, out, ps, sq, sd, tmp, extra_reads=()):
        c = self.c
        KC = c.KC
        for k in range(KC):
            s = sq[k % len(sq)]
            self.act(s.ap[:, 0:T], x.ap[:, k * T:(k + 1) * T], AF.Square, [x.t], [s.t])
            self.mm(ps.ap[:, 0:T], self.ones.ap, s.ap[:, 0:T], k == 0, k == KC - 1, [s.t, self.ones.t], [ps.t])
        self.act(sd.ap[:, 0:T], ps.ap[:, 0:T], AF.Sqrt, [ps.t], [sd.t], scale=1.0 / c.D, bias=EPS)
        self.recip(sd.ap[:, 0:T], sd.ap[:, 0:T], [sd.t], [sd.t])

    def apply_mod(self, x, T, sd, Acols, Bcols, outfn, out_t, tmp, reads):
        KC = self.c.KC
        for k in range(KC):
            t = tmp[k % len(tmp)]
            self.tt(t.ap[:, 0:T], x.ap[:, k * T:(k + 1) * T], sd.ap[:, 0:T], ALU.mult, [x.t, sd.t], [t.t])
            if Bcols is None:
                self.act(outfn(k), t.ap[:, 0:T], AF.Copy, [t.t] + reads, [out_t], scale=Acols[:, k:k + 1])
            else:
                self.act(outfn(k), t.ap[:, 0:T], AF.Identity, [t.t] + reads, [out_t], scale=Acols[:, k:k + 1],
                         bias=Bcols[:, k:k + 1])

    def phase_A(self, l):
        c, S, I = self.c, self.S, self.I
        KC, MC, DM = c.KC, c.MC, c.DM
        ar = self.ar
        ar.reset()
        TM = 512
        xT = [Buf(ar.f32(KC * TM)) for _ in range(1)]
        hT = [Buf(ar.bf16(KC * TM)) for _ in range(1)]
        xin = [Buf(ar.f32(c.D)) for _ in range(2)] if l == 0 else []
        sq = [Buf(ar.f32(TM)) for _ in range(2)]
        tmp = [Buf(ar.f32(TM)) for _ in range(2)]
        sd = Buf(ar.f32(TM))
        rc = Buf(ar.f32(TM))
        rs = Buf(ar.f32(TM))
        ob = [Buf(ar.f32(TM)) for _ in range(4)]
        o16 = [Buf(ar.bf16(TM)) for _ in range(3)]
        xf = [Buf(ar.f32(TM)) for _ in range(2)]
        vst = Buf(ar.bf16(4 * DM))
        kvf = [Buf(ar.f32(4 * 128)) for _ in range(2)]
        obi = [0]
        o16i = [0]
        W = S[f"WIN{l}"]
        psr = [self.PS[i] for i in (0, 1, 2, 3, 4)]
        pi = [0]

        def nps():
            b = psr[pi[0] % len(psr)]
            pi[0] += 1
            return b

        def fm(unit, h, T):
            ps = nps()
            for k in range(KC):
                self.mm(ps.ap[:, 0:T], unit.ap[:, k * 128:(k + 1) * 128], h.ap[:, k * T:(k + 1) * T], k == 0, k == KC - 1,
                        [unit.t, h.t], [ps.t])
            return ps

        def nob():
            b = ob[obi[0] % len(ob)]
            obi[0] += 1
            return b

        def no16():
            b = o16[o16i[0] % len(o16)]
            o16i[0] += 1
            return b

        for tix, (tok0, T, v, seq, first, last) in enumerate(self.tiles):
            if getattr(self, "a_tiles", None) is not None and tix not in self.a_tiles:
                continue
            self.p.barrier()
            x, h = xT[0], hT[0]
            TB = T // 128
            src_in = I["xs"] if v == 0 else I["xp"]
            r0 = tok0 if v == 0 else tok0 - c.NS
            if l == 0:
                for tb in range(TB):
                    xi = xin[tb % 2]
                    self.dma("gpsimd", xi.ap, src_in[r0 + tb * 128: r0 + (tb + 1) * 128, :], [], [xi.t])
                    for k0 in range(0, KC, 4):
                        ps = self.PS[5 + (k0 // 4) % 2]
                        for kk in range(4):
                            k = k0 + kk
                            self.tr(ps.ap[:, kk * 128:(kk + 1) * 128], xi.ap[:, k * 128:(k + 1) * 128], self.ident.ap,
                                    [xi.t, self.ident.t], [ps.t])
                        dst = x.ap[:, 0:KC * T].rearrange("p (k t) -> p k t", t=T)[:, k0:k0 + 4, tb * 128:(tb + 1) * 128]
                        self.cp(dst, ps.ap.rearrange("p (k t) -> p k t", t=128), [ps.t], [x.t],
                                eng=("vector" if (k0 // 4) % 2 else "scalar"))
                self.dma("gpsimd", S["XT"][:, tok0:tok0 + T].rearrange("(k p) t -> p k t", p=128),
                         x.ap[:, 0:KC * T].rearrange("p (k t) -> p k t", t=T), [x.t], [self.wtok])
            else:
                self.dma("gpsimd", x.ap[:, 0:KC * T].rearrange("p (k t) -> p k t", t=T),
                         S["XT"][:, tok0:tok0 + T].rearrange("(k p) t -> p k t", p=128), [self.wtok], [x.t])
            if getattr(self, "a_stop", 0) == 1:
                return
            self.norm_mod(x, T, None, None, None, self.PS[7], sq, sd, tmp)
            if getattr(self, "a_stop", 0) == 2:
                return
            self.apply_mod(x, T, sd, self.abv(l, v, 0), self.modv(l, v, 0), lambda k: h.ap[:, k * T:(k + 1) * T], h.t, tmp,
                           [self.mod_t])
            if getattr(self, "a_stop", 0) == 3:
                return
            self.dma("gpsimd", S["HT"][:, tok0:tok0 + T].rearrange("(k p) t -> p k t", p=128),
                     h.ap[:, 0:KC * T].rearrange("p (k t) -> p k t", t=T), [h.t], [self.wtok])
            if v == 0:
                self.dma("gpsimd", rc.ap[:, 0:T], S["RC"][:, tok0:tok0 + T], [self.wtok], [rc.t])
                self.dma("gpsimd", rs.ap[:, 0:T], S["RS"][:, tok0:tok0 + T], [self.wtok], [rs.t])
            for kind, base, dst in (("q", 0, S["Q"]), ("k", MC, S["KT"])):
                for m in range(MC):
                    u = self.wload(W[base + m])
                    ps = fm(u, h, T)
                    o = no16()
                    if v == 0:
                        f = xf[m % 2]
                        self.cp(f.ap[:, 0:T], ps.ap[:, 0:T], [ps.t], [f.t], eng="scalar")
                        ps2 = nps()
                        self.mm(ps2.ap[:, 0:T], self.pm.ap, f.ap[:, 0:T], True, True, [self.pm.t, f.t], [ps2.t])
                        t1 = tmp[m % 2]
                        self.tt(t1.ap[:, 0:T], f.ap[:, 0:T], rc.ap[:, 0:T], ALU.mult, [f.t, rc.t], [t1.t])
                        self.tt(f.ap[:, 0:T], ps2.ap[:, 0:T], rs.ap[:, 0:T], ALU.mult, [ps2.t, rs.t], [f.t])
                        self.tt(o.ap[:, 0:T], t1.ap[:, 0:T], f.ap[:, 0:T], ALU.add, [t1.t, f.t], [o.t])
                    else:
                        self.cp(o.ap[:, 0:T], ps.ap[:, 0:T], [ps.t], [o.t], eng="scalar")
                    self.dma("gpsimd", dst[m * 128:(m + 1) * 128, tok0:tok0 + T], o.ap[:, 0:T], [o.t], [self.wtok])
                    if kind == "k" and v == 1:
                        for tb in range(TB):
                            ps3 = nps()
                            for k in range(KC):
                                self.mm(ps3.ap[:, 0:128], h.ap[:, k * T + tb * 128:k * T + (tb + 1) * 128],
                                        u.ap[:, k * 128:(k + 1) * 128], k == 0, k == KC - 1, [u.t, h.t], [ps3.t])
                            kf = kvf[tb % 2]
                            self.cp(kf.ap[:, 0:128], ps3.ap[:, 0:128], [ps3.t], [kf.t])
                            self.dma("gpsimd", self.O["nk"][seq - 1, l, tb * 128:(tb + 1) * 128, m * 128:(m + 1) * 128],
                                     kf.ap[:, 0:128], [kf.t], [self.wtok])
            if getattr(self, "a_stop", 0) == 4:
                return
            for m in range(MC):
                u = self.wload(W[2 * MC + m])
                ps = nps()
                for tb in range(TB):
                    for k in range(KC):
                        self.mm(ps.ap[:, tb * 128:(tb + 1) * 128], h.ap[:, k * T + tb * 128:k * T + (tb + 1) * 128],
                                u.ap[:, k * 128:(k + 1) * 128], k == 0, k == KC - 1, [u.t, h.t], [ps.t],
                                inc=(k == KC - 1 and tb == TB - 1))
                dstv = vst.ap[:, 0:TB * DM].rearrange("p (b e) -> p b e", e=DM)[:, :, m * 128:(m + 1) * 128]
                if v == 0:
                    self.cp(dstv, ps.ap[:, 0:TB * 128].rearrange("p (b e) -> p b e", e=128), [ps.t], [vst.t])
                if v == 1:
                    kf = kvf[m % 2]
                    self.cp(kf.ap[:, 0:TB * 128], ps.ap[:, 0:TB * 128], [ps.t], [kf.t])
                    self.cp(dstv, kf.ap[:, 0:TB * 128].rearrange("p (b e) -> p b e", e=128), [kf.t], [vst.t])
                    for tb in range(TB):
                        self.dma("gpsimd", self.O["nv"][seq - 1, l, tb * 128:(tb + 1) * 128, m * 128:(m + 1) * 128],
                                 kf.ap[:, tb * 128:(tb + 1) * 128], [kf.t], [self.wtok])
            self.dma("gpsimd", S["V"][tok0:tok0 + T, :].rearrange("(b p) e -> p b e", p=128),
                     vst.ap[:, 0:TB * DM].rearrange("p (b e) -> p b e", e=DM), [vst.t], [self.wtok])
            if getattr(self, "a_stop", 0) == 5:
                return
            for m in range(MC):
                ua = self.wload(W[3 * MC + m])
                pa = fm(ua, h, T)
                ug = self.wload(W[4 * MC + m])
                pg = fm(ug, h, T)
                sg = tmp[m % 2]
                self.act(sg.ap[:, 0:T], pg.ap[:, 0:T], AF.Sigmoid, [pg.t], [sg.t])
                o = nob()
                self.tt(o.ap[:, 0:T], pa.ap[:, 0:T], sg.ap[:, 0:T], ALU.mult, [pa.t, sg.t], [o.t])
                self.dma("gpsimd", S["UB"][m * 128:(m + 1) * 128, tok0:tok0 + T], o.ap[:, 0:T], [o.t], [self.wtok])
            if getattr(self, "a_stop", 0) == 6:
                return
            for m in range(MC):
                ub_ = self.wload(W[5 * MC + m])
                pb = fm(ub_, h, T)
                o = nob()
                self.cp(o.ap[:, 0:T], pb.ap[:, 0:T], [pb.t], [o.t], eng="scalar")
                self.dma("gpsimd", S["GB"][m * 128:(m + 1) * 128, tok0:tok0 + T], o.ap[:, 0:T], [o.t], [self.wtok])
                uc = self.wload(W[6 * MC + m])
                pc = fm(uc, h, T)
                ux = self.wload(W[7 * MC + m])
                px = fm(ux, h, T)
                g = tmp[m % 2]
                self.cp(g.ap[:, 0:T], pc.ap[:, 0:T], [pc.t], [g.t], eng="scalar")
                o = nob()
                self.tt(o.ap[:, 0:T], px.ap[:, 0:T], g.ap[:, 0:T], ALU.mult, [px.t, g.t], [o.t])
                self.dma("gpsimd", S["GCX"][m * 128:(m + 1) * 128, tok0:tok0 + T], o.ap[:, 0:T], [o.t], [self.wtok])
            if getattr(self, "a_stop", 0) == 7:
                return
            for m in range(MC):
                u1 = self.wload(W[8 * MC + m])
                p1 = fm(u1, h, T)
                o = nob()
                self.cp(o.ap[:, 0:T], p1.ap[:, 0:T], [p1.t], [o.t])
                self.dma("gpsimd", S["XR"][m * 128:(m + 1) * 128, tok0:tok0 + T], o.ap[:, 0:T], [o.t], [self.wtok])
                u2 = self.wload(W[9 * MC + m])
                p2 = fm(u2, h, T)
                o = nob()
                self.act(o.ap[:, 0:T], p2.ap[:, 0:T], AF.Gelu, [p2.t], [o.t])
                self.dma("gpsimd", S["GY"][m * 128:(m + 1) * 128, tok0:tok0 + T], o.ap[:, 0:T], [o.t], [self.wtok])
            if getattr(self, "a_stop", 0) == 8:
                return

    def phase_B(self, l):
        self.B_conformer(l)
        self.p.barrier()
        self.B_sconv_lru(l)
        self.p.barrier()
        self.B_attn(l)

    def B_conformer(self, l):
        c, S = self.c, self.S
        MC, DM = c.MC, c.DM
        ar = self.ar
        ar.reset()
        LM = c.NS
        cb = Buf(ar.f32(MC * LM))
        ubp = [Buf(ar.f32(LM + 30)) for _ in range(2)]
        sq = [Buf(ar.f32(512)) for _ in range(2)]
        mean = Buf(ar.f32(512))
        msq = Buf(ar.f32(512))
        rstd = Buf(ar.f32(512))
        t1 = [Buf(ar.f32(512)) for _ in range(2)]
        yo = [Buf(ar.bf16(512)) for _ in range(2)]
        w31 = self.prm[f"dw31_{l}"]
        for si, (tok0, L, is_s, b) in enumerate(self.seqs):
            self.p.barrier()
            for m in range(MC):
                u = ubp[m % 2]
                self.memset(u.ap[:, 0:15], 0.0, [u.t])
                self.memset(u.ap[:, 15 + L:30 + L], 0.0, [u.t])
                self.dma("gpsimd", u.ap[:, 15:15 + L], S["UB"][m * 128:(m + 1) * 128, tok0:tok0 + L], [self.wtok], [u.t])
                acc = cb.ap[:, m * LM:m * LM + L]
                self.ts(acc, u.ap[:, 0:L], w31[:, m:m + 1], self.prm[f"b_dw31_{l}"][:, m:m + 1], ALU.mult, ALU.add,
                        [u.t, self.prm_t], [cb.t])
                for j in range(1, 31):
                    self.stt(acc, u.ap[:, j:j + L], w31[:, j * MC + m:j * MC + m + 1], acc, ALU.mult, ALU.add,
                             [u.t, cb.t, self.prm_t], [cb.t])
            TT = min(512, L)
            for t0 in range(0, L, TT):
                p1, p2 = self.PS[0 + (t0 // TT) % 2 * 2], self.PS[1 + (t0 // TT) % 2 * 2]
                for m in range(MC):
                    x = cb.ap[:, m * LM + t0:m * LM + t0 + TT]
                    self.mm(p1.ap[:, 0:TT], self.ones.ap, x, m == 0, m == MC - 1, [cb.t, self.ones.t], [p1.t])
                    s = sq[m % 2]
                    self.act(s.ap[:, 0:TT], x, AF.Square, [cb.t], [s.t])
                    self.mm(p2.ap[:, 0:TT], self.ones.ap, s.ap[:, 0:TT], m == 0, m == MC - 1, [s.t, self.ones.t], [p2.t])
                self.ts(mean.ap[:, 0:TT], p1.ap[:, 0:TT], 1.0 / DM, None, ALU.mult, None, [p1.t], [mean.t])
                self.tt(msq.ap[:, 0:TT], mean.ap[:, 0:TT], mean.ap[:, 0:TT], ALU.mult, [mean.t], [msq.t])
                self.stt(msq.ap[:, 0:TT], p2.ap[:, 0:TT], 1.0 / DM, msq.ap[:, 0:TT], ALU.mult, ALU.subtract, [p2.t, msq.t], [msq.t])
                self.act(rstd.ap[:, 0:TT], msq.ap[:, 0:TT], AF.Sqrt, [msq.t], [rstd.t], bias=EPS)
                self.recip(rstd.ap[:, 0:TT], rstd.ap[:, 0:TT], [rstd.t], [rstd.t])
                for m in range(MC):
                    x = cb.ap[:, m * LM + t0:m * LM + t0 + TT]
                    t = t1[m % 2]
                    self.tt(t.ap[:, 0:TT], x, mean.ap[:, 0:TT], ALU.subtract, [cb.t, mean.t], [t.t])
                    self.tt(t.ap[:, 0:TT], t.ap[:, 0:TT], rstd.ap[:, 0:TT], ALU.mult, [t.t, rstd.t], [t.t])
                    y = yo[m % 2]
                    self.act(y.ap[:, 0:TT], t.ap[:, 0:TT], AF.Silu, [t.t, self.prm_t], [y.t],
                             scale=self.prm[f"g_ln_conv_{l}"][:, m:m + 1], bias=self.prm[f"b_ln_conv_{l}"][:, m:m + 1])
                    self.dma("gpsimd", S["Y"][DM + m * 128:DM + (m + 1) * 128, tok0 + t0:tok0 + t0 + TT], y.ap[:, 0:TT],
                             [y.t], [self.wtok])

    def B_sconv_lru(self, l):
        c, S = self.c, self.S
        MC, DM = c.MC, c.DM
        ar = self.ar
        ar.reset()
        LM = c.NS
        xp = Buf(ar.f32(LM + 4))
        gy = Buf(ar.f32(LM))
        gp = [xp, xp]
        gb = [gy, gy]
        yo = [Buf(ar.bf16(LM)) for _ in range(1)] * 2
        xr = Buf(ar.f32(LM))
        xrb = Buf(ar.bf16(LM))
        hs = [Buf(ar.f32(LM)) for _ in range(2)]
        tA = [Buf(ar.f32(512)) for _ in range(8)]
        hl = Buf(ar.f32(128))
        hlt = Buf(ar.f32(128))
        w3 = self.prm[f"w_dw3_{l}"]
        w4 = self.prm[f"w_conv4_{l}"]
        bd0 = l * 4 * MC * 128
        self.memset(hl.ap, 0.0, [hl.t])
        for si, (tok0, L, is_s, b) in enumerate(self.seqs):
            self.p.barrier()
            for m in range(MC):
                self.p.barrier()
                g = gp[m % 2]
                self.memset(g.ap[:, 0:1], 0.0, [g.t])
                self.memset(g.ap[:, L + 1:L + 2], 0.0, [g.t])
                self.dma("gpsimd", g.ap[:, 1:1 + L], S["GCX"][m * 128:(m + 1) * 128, tok0:tok0 + L], [self.wtok], [g.t])
                gg = gb[m % 2]
                self.dma("gpsimd", gg.ap[:, 0:L], S["GB"][m * 128:(m + 1) * 128, tok0:tok0 + L], [self.wtok], [gg.t])
                acc = xr
                self.ts(acc.ap[:, 0:L], g.ap[:, 0:L], w3[:, m:m + 1], None, ALU.mult, None, [g.t, self.prm_t], [acc.t])
                for j in (1, 2):
                    self.stt(acc.ap[:, 0:L], g.ap[:, j:j + L], w3[:, j * MC + m:j * MC + m + 1], acc.ap[:, 0:L], ALU.mult, ALU.add,
                             [g.t, acc.t, self.prm_t], [acc.t])
                y = yo[m % 2]
                self.tt(y.ap[:, 0:L], acc.ap[:, 0:L], gg.ap[:, 0:L], ALU.mult, [acc.t, gg.t], [y.t])
                self.dma("gpsimd", S["Y"][2 * DM + m * 128:2 * DM + (m + 1) * 128, tok0:tok0 + L], y.ap[:, 0:L], [y.t], [self.wtok])
            TT = min(self.c.LRU_TT, L)
            NT = L // TT
            for m in range(MC):
                self.p.barrier()
                self.memset(xp.ap[:, 0:1], 0.0, [xp.t])
                self.memset(xp.ap[:, L + 1:L + 3], 0.0, [xp.t])
                self.dma("gpsimd", xp.ap[:, 1:1 + L], S["XR"][m * 128:(m + 1) * 128, tok0:tok0 + L], [self.wtok], [xp.t])
                self.dma("gpsimd", gy.ap[:, 0:L], S["GY"][m * 128:(m + 1) * 128, tok0:tok0 + L], [self.wtok], [gy.t])
                self.ts(xr.ap[:, 0:L], xp.ap[:, 0:L], w4[:, m:m + 1], self.prm[f"b_conv4_{l}"][:, m:m + 1], ALU.mult, ALU.add,
                        [xp.t, self.prm_t], [xr.t])
                for j in (1, 2, 3):
                    self.stt(xr.ap[:, 0:L], xp.ap[:, j:j + L], w4[:, j * MC + m:j * MC + m + 1], xr.ap[:, 0:L], ALU.mult, ALU.add,
                             [xp.t, xr.t, self.prm_t], [xr.t])
                self.cp(xrb.ap[:, 0:L], xr.ap[:, 0:L], [xr.t], [xrb.t], eng="gpsimd")
                for d in range(2):
                    h = hs[d]
                    order = range(NT) if d == 0 else range(NT - 1, -1, -1)
                    s1 = self.MISC[:, l * (4 + 4 * MC) + 4 + d * MC + m: l * (4 + 4 * MC) + 4 + d * MC + m + 1]
                    s2 = self.MISC[:, l * (4 + 4 * MC) + 4 + 2 * MC + d * MC + m: l * (4 + 4 * MC) + 4 + 2 * MC + d * MC + m + 1]
                    for ti_, tI in enumerate(order):
                        t0 = tI * TT
                        pa, px = self.PS[(ti_ % 2) * 2], self.PS[(ti_ % 2) * 2 + 1]
                        wa = self.BD[:, bd0 + ((0 * 2 + d) * MC + m) * 128: bd0 + ((0 * 2 + d) * MC + m + 1) * 128]
                        wx = self.BD[:, bd0 + ((1 * 2 + d) * MC + m) * 128: bd0 + ((1 * 2 + d) * MC + m + 1) * 128]
                        self.mm(pa.ap[:, 0:TT], wa, xrb.ap[:, t0:t0 + TT], True, True, [self.bd_t, xrb.t], [pa.t])
                        self.mm(px.ap[:, 0:TT], wx, xrb.ap[:, t0:t0 + TT], True, True, [self.bd_t, xrb.t], [px.t])
                        r, ig, a, a2, u = tA[0 + 4 * (ti_ % 2)], tA[1 + 4 * (ti_ % 2)], tA[2 + 4 * (ti_ % 2)], tA[3 + 4 * (ti_ % 2)], None
                        self.act(r.ap[:, 0:TT], pa.ap[:, 0:TT], AF.Sigmoid, [pa.t, self.prm_t], [r.t],
                                 bias=self.prm[f"b_rg_a_{l}"][:, d * MC + m:d * MC + m + 1])
                        self.act(ig.ap[:, 0:TT], px.ap[:, 0:TT], AF.Sigmoid, [px.t, self.prm_t], [ig.t],
                                 bias=self.prm[f"b_rg_x_{l}"][:, d * MC + m:d * MC + m + 1])
                        self.act(a.ap[:, 0:TT], r.ap[:, 0:TT], AF.Exp, [r.t, self.misc_t], [a.t], scale=s1)
                        self.act(a2.ap[:, 0:TT], r.ap[:, 0:TT], AF.Exp, [r.t, self.misc_t], [a2.t], scale=s2)
                        self.act(a2.ap[:, 0:TT], a2.ap[:, 0:TT], AF.Sqrt, [a2.t], [a2.t], scale=-1.0, bias=1.0)
                        self.tt(ig.ap[:, 0:TT], ig.ap[:, 0:TT], xr.ap[:, t0:t0 + TT], ALU.mult, [ig.t, xr.t], [ig.t])
                        self.tt(ig.ap[:, 0:TT], ig.ap[:, 0:TT], a2.ap[:, 0:TT], ALU.mult, [ig.t, a2.t], [ig.t])
                        if ti_ == 0:
                            init = self.prm[f"st_{l}"][:, d * MC + m:d * MC + m + 1] if is_s else 0.0
                        else:
                            pt0 = order[ti_ - 1] * TT
                            init = h.ap[:, pt0 + TT - 1:pt0 + TT] if d == 0 else h.ap[:, pt0:pt0 + 1]
                        if d == 0:
                            self.p.op("vector", lambda e, h=h, a=a, ig=ig, init=init, t0=t0: e.tensor_tensor_scan(
                                out=h.ap[:, t0:t0 + TT], data0=a.ap[:, 0:TT], data1=ig.ap[:, 0:TT], initial=init,
                                op0=ALU.mult, op1=ALU.add), [a.t, ig.t, h.t, self.prm_t], [h.t])
                        else:
                            self.p.op("vector", lambda e, h=h, a=a, ig=ig, init=init, t0=t0: e.tensor_tensor_scan(
                                out=h.ap[:, t0:t0 + TT][:, ::-1], data0=a.ap[:, 0:TT][:, ::-1], data1=ig.ap[:, 0:TT][:, ::-1],
                                initial=init, op0=ALU.mult, op1=ALU.add), [a.t, ig.t, h.t, self.prm_t], [h.t])
                    if not is_s:
                        col = (b * 2 + d) * MC + m
                        src = h.ap[:, L - 1:L] if d == 0 else h.ap[:, 0:1]
                        self.cp(hl.ap[:, col:col + 1], src, [h.t], [hl.t], eng="gpsimd")
                y = yo[m % 2]
                self.tt(hs[0].ap[:, 0:L], hs[0].ap[:, 0:L], hs[1].ap[:, 0:L], ALU.add, [hs[0].t, hs[1].t], [hs[0].t])
                self.tt(y.ap[:, 0:L], hs[0].ap[:, 0:L], gy.ap[:, 0:L], ALU.mult, [hs[0].t, gy.t], [y.t])
                self.dma("gpsimd", S["Y"][3 * DM + m * 128:3 * DM + (m + 1) * 128, tok0:tok0 + L], y.ap[:, 0:L], [y.t], [self.wtok])
        ncol = c.NPB * 2 * MC
        ps = self.PS[4]
        self.tr(ps.ap[0:ncol, 0:128], hl.ap[:, 0:ncol], self.ident.ap, [hl.t, self.ident.t], [ps.t])
        self.cp(hlt.ap[0:ncol, :], ps.ap[0:ncol, 0:128], [ps.t], [hlt.t])
        for b in range(c.NPB):
            self.dma("gpsimd", self.O["nst"][b, l, :].rearrange("(r c) -> r c", c=128),
                     hlt.ap[b * 2 * MC:(b + 1) * 2 * MC, :], [hlt.t], [self.wtok])

    def B_attn(self, l):
        c, S, I = self.c, self.S, self.I
        MC, DM, PC, PAST = c.MC, c.DM, c.PC, c.PAST
        HA = MC
        ar = self.ar
        ar.reset()
        MMAX = PAST + c.NS
        kT = Buf(ar.bf16(HA * MMAX))
        Va = Buf(ar.bf16((MMAX // 128) * DM))
        cst = [Buf(ar.f32(PC * DM)) for _ in range(2)]
        qp = [Buf(ar.bf16(2 * 512)) for _ in range(2)]
        pt = [Buf(ar.bf16(512)) for _ in range(4)]
        o0 = Buf(ar.f32(512))
        o1 = Buf(ar.f32(512))
        rl = [Buf(ar.f32(512)) for _ in range(2)]
        sqb = Buf(ar.f32(512))
        yb = [Buf(ar.bf16(512)) for _ in range(2)]
        for q in qp:
            self.memset(q.ap, 0.0, [q.t])
        neglam = self.misc(l, 0)
        gsub = self.misc(l, 1)
        psS = [self.PS[0], self.PS[1]]
        psO = [self.PS[2], self.PS[3]]
        psL = [self.PS[4], self.PS[5]]
        psX = self.PS[6]
        qi = 0
        for si, (tok0, L, is_s, b) in enumerate(self.seqs):
            self.p.barrier()
            P0 = PAST if is_s else 0
            M = P0 + L
            NKC = M // 128
            kv = kT.ap[:, 0:HA * M].rearrange("p (h m) -> p h m", m=M)
            vv = Va.ap[:, 0:NKC * DM].rearrange("p (k e) -> p k e", e=DM)
            if is_s:
                ks, vs = cst
                self.dma("gpsimd", ks.ap.rearrange("p (j e) -> p j e", e=DM), I["ck"][l].rearrange("(j p) e -> p j e", p=128), [], [ks.t])
                self.dma("gpsimd", vs.ap.rearrange("p (j e) -> p j e", e=DM), I["cv"][l].rearrange("(j p) e -> p j e", p=128), [], [vs.t])
                self.cp(vv[:, 0:PC, :], vs.ap.rearrange("p (j e) -> p j e", e=DM), [vs.t], [Va.t])
                n = 0
                for j in range(PC):
                    for h in range(HA):
                        ps = self.PS[6 + n % 2]
                        n += 1
                        self.tr(ps.ap[:, 0:128], ks.ap[:, j * DM + h * 128:j * DM + (h + 1) * 128], self.ident.ap,
                                [ks.t, self.ident.t], [ps.t])
                        self.cp(kv[:, h, j * 128:(j + 1) * 128], ps.ap[:, 0:128], [ps.t], [kT.t], eng=("scalar" if n % 2 else "vector"))
            self.dma("gpsimd", kv[:, :, P0:P0 + L], S["KT"][:, tok0:tok0 + L].rearrange("(h p) t -> p h t", p=128), [self.wtok], [kT.t])
            self.dma("gpsimd", vv[:, P0 // 128:NKC, :], S["V"][tok0:tok0 + L, :].rearrange("(j p) e -> p j e", p=128), [self.wtok], [Va.t])
            QB = min(512, L)
            for h in range(HA):
                for qb in range(L // QB):
                    self.p.barrier()
                    q = qp[qi % 2]
                    qi += 1
                    qv = q.ap.rearrange("p (j t) -> p j t", t=512)
                    c0 = tok0 + qb * QB
                    self.dma("gpsimd", qv[0:64, 0, 0:QB], S["Q"][h * 128:h * 128 + 64, c0:c0 + QB], [self.wtok], [q.t])
                    self.dma("gpsimd", qv[64:128, 1, 0:QB], S["Q"][h * 128 + 64:h * 128 + 128, c0:c0 + QB], [self.wtok], [q.t])
                    steps = [(kc, j) for kc in range(NKC) for j in range(2)]

                    def emitS(i):
                        kc, j = steps[i]
                        ps = psS[i % 2]
                        self.mm(ps.ap[:, 0:QB], kv[:, h, kc * 128:(kc + 1) * 128], qv[:, j, 0:QB], True, True, [kT.t, q.t], [ps.t])

                    emitS(0)
                    for i, (kc, j) in enumerate(steps):
                        if i + 1 < len(steps):
                            emitS(i + 1)
                        ps = psS[i % 2]
                        pb = pt[i % 4]
                        self.act(pb.ap[:, 0:QB], ps.ap[:, 0:QB], AF.Exp, [ps.t], [pb.t], scale=0.125)
                        self.mm(psO[j].ap[:, 0:QB], vv[:, kc, h * 128:(h + 1) * 128], pb.ap[:, 0:QB], kc == 0, kc == NKC - 1,
                                [Va.t, pb.t], [psO[j].t])
                        self.mm(psL[j].ap[:, 0:QB], self.onesb.ap, pb.ap[:, 0:QB], kc == 0, kc == NKC - 1,
                                [self.onesb.t, pb.t], [psL[j].t])
                    for j in range(2):
                        self.recip(rl[j].ap[:, 0:QB], psL[j].ap[:, 0:QB], [psL[j].t], [rl[j].t])
                    self.tt(o0.ap[:, 0:QB], psO[0].ap[:, 0:QB], rl[0].ap[:, 0:QB], ALU.mult, [psO[0].t, rl[0].t], [o0.t])
                    self.tt(o1.ap[:, 0:QB], psO[1].ap[:, 0:QB], rl[1].ap[:, 0:QB], ALU.mult, [psO[1].t, rl[1].t], [o1.t])
                    self.stt(o0.ap[:, 0:QB], o1.ap[:, 0:QB], neglam, o0.ap[:, 0:QB], ALU.mult, ALU.add, [o0.t, o1.t, self.misc_t], [o0.t])
                    self.act(sqb.ap[:, 0:QB], o0.ap[:, 0:QB], AF.Square, [o0.t], [sqb.t])
                    self.mm(psX.ap[:, 0:QB], self.ones.ap, sqb.ap[:, 0:QB], True, True, [self.ones.t, sqb.t], [psX.t])
                    self.act(sqb.ap[:, 0:QB], psX.ap[:, 0:QB], AF.Sqrt, [psX.t], [sqb.t], scale=1.0 / 128, bias=EPS)
                    self.recip(sqb.ap[:, 0:QB], sqb.ap[:, 0:QB], [sqb.t], [sqb.t])
                    y = yb[qi % 2]
                    self.stt(y.ap[:, 0:QB], o0.ap[:, 0:QB], gsub, sqb.ap[:, 0:QB], ALU.mult, ALU.mult, [o0.t, sqb.t, self.misc_t], [y.t])
                    self.dma("gpsimd", S["Y"][h * 128:(h + 1) * 128, c0:c0 + QB], y.ap[:, 0:QB], [y.t], [self.wtok])

    def phase_C(self, l):
        c, S = self.c, self.S
        KC, MC, DM = c.KC, c.MC, c.DM
        ar = self.ar
        ar.reset()
        TM = 512
        x = Buf(ar.f32(KC * TM))
        h = Buf(ar.bf16(KC * TM))
        yt = Buf(ar.bf16(4 * MC * TM))
        mg = Buf(ar.bf16(KC * TM))
        h2 = Buf(ar.bf16(KC * TM))
        g = [Buf(ar.f32(TM)) for _ in range(3)]
        tt_ = [Buf(ar.f32(TM)) for _ in range(2)]
        acc = [Buf(ar.f32(TM)) for _ in range(2)]
        sq = [Buf(ar.f32(TM)) for _ in range(2)]
        sd = Buf(ar.f32(TM))
        wbr = Buf(ar.bf16(4 * MC * 128))
        W, WB, WO = S[f"WIN{l}"], S[f"WBR{l}"], S[f"WOUT{l}"]
        psG = [self.PS[i] for i in (0, 1, 2)]
        psT = [self.PS[i] for i in (3, 4)]
        psM = [self.PS[i] for i in (5, 6)]
        gi = 0
        for (tok0, T, v, seq, first, last) in self.tiles:
            self.p.barrier()
            kt = lambda a: a.rearrange("p (k t) -> p k t", t=T)
            self.dma("gpsimd", kt(x.ap[:, 0:KC * T]), S["XT"][:, tok0:tok0 + T].rearrange("(k p) t -> p k t", p=128), [self.wtok], [x.t])
            self.dma("gpsimd", kt(h.ap[:, 0:KC * T]), S["HT"][:, tok0:tok0 + T].rearrange("(k p) t -> p k t", p=128), [self.wtok], [h.t])
            self.dma("gpsimd", kt(yt.ap[:, 0:4 * MC * T]), S["Y"][:, tok0:tok0 + T].rearrange("(k p) t -> p k t", p=128), [self.wtok], [yt.t])
            for n in range(KC):
                ub = wbr
                self.dma("sync", ub.ap, WB[n], [], [ub.t])
                a = acc[n % 2]
                for j in range(4):
                    ug = self.wload(W[10 * MC + j * KC + n])
                    pg = psG[gi % 3]
                    gb = g[gi % 3]
                    gi += 1
                    for k in range(KC):
                        self.mm(pg.ap[:, 0:T], ug.ap[:, k * 128:(k + 1) * 128], h.ap[:, k * T:(k + 1) * T], k == 0, k == KC - 1,
                                [ug.t, h.t], [pg.t])
                    self.act(gb.ap[:, 0:T], pg.ap[:, 0:T], AF.Sigmoid, [pg.t], [gb.t])
                    pt_ = psT[j % 2]
                    for m in range(MC):
                        self.mm(pt_.ap[:, 0:T], ub.ap[:, (j * MC + m) * 128:(j * MC + m + 1) * 128],
                                yt.ap[:, (j * MC + m) * T:(j * MC + m + 1) * T], m == 0, m == MC - 1, [ub.t, yt.t], [pt_.t])
                    if j == 0:
                        self.tt(a.ap[:, 0:T], pt_.ap[:, 0:T], gb.ap[:, 0:T], ALU.mult, [pt_.t, gb.t], [a.t])
                    else:
                        t = tt_[j % 2]
                        self.tt(t.ap[:, 0:T], pt_.ap[:, 0:T], gb.ap[:, 0:T], ALU.mult, [pt_.t, gb.t], [t.t])
                        if j < 3:
                            self.tt(a.ap[:, 0:T], a.ap[:, 0:T], t.ap[:, 0:T], ALU.add, [a.t, t.t], [a.t], eng="gpsimd")
                        else:
                            self.tt(mg.ap[:, n * T:(n + 1) * T], a.ap[:, 0:T], t.ap[:, 0:T], ALU.add, [a.t, t.t], [mg.t], eng="gpsimd")
            for n in range(KC):
                uo = self.wload(WO[n])
                pm_ = psM[n % 2]
                for k in range(KC):
                    self.mm(pm_.ap[:, 0:T], uo.ap[:, k * 128:(k + 1) * 128], mg.ap[:, k * T:(k + 1) * T], k == 0, k == KC - 1,
                            [uo.t, mg.t], [pm_.t])
                self.stt(x.ap[:, n * T:(n + 1) * T], pm_.ap[:, 0:T], self.modv(l, v, 2)[:, n:n + 1], x.ap[:, n * T:(n + 1) * T],
                         ALU.mult, ALU.add, [pm_.t, x.t, self.mod_t], [x.t])
            self.dma("gpsimd", S["XT"][:, tok0:tok0 + T].rearrange("(k p) t -> p k t", p=128), kt(x.ap[:, 0:KC * T]), [x.t], [self.wtok])
            self.norm_mod(x, T, None, None, None, self.PS[7], sq, sd, tt_)
            self.apply_mod(x, T, sd, self.abv(l, v, 1), self.modv(l, v, 3), lambda k: h2.ap[:, k * T:(k + 1) * T], h2.t, tt_,
                           [self.mod_t])
            self.dma("gpsimd", S["H2"][:, tok0:tok0 + T].rearrange("(k p) t -> p k t", p=128), kt(h2.ap[:, 0:KC * T]), [h2.t], [self.wtok])

    def phase_D(self, l):
        c, S = self.c, self.S
        KC, FC = c.KC, c.FC
        ar = self.ar
        ar.reset()
        TM = 512
        TH = TM + 2
        x = Buf(ar.f32(KC * TM))
        h2 = Buf(ar.bf16(KC * TH))
        gbuf = Buf(ar.bf16(FC * TM))
        ab = [Buf(ar.f32(TH)) for _ in range(2)]
        ac = [Buf(ar.f32(TM)) for _ in range(2)]
        sl = [Buf(ar.f32(TM)) for _ in range(2)]
        sq = [Buf(ar.f32(TM)) for _ in range(2)]
        tmp = sq
        sd = Buf(ar.f32(TM))
        last_layer = (l == c.DEPTH - 1)
        if last_layer:
            yf = [Buf(ar.f32(128)) for _ in range(2)]
            ost = [Buf(ar.f32(c.D)) for _ in range(1)] * 2
        WU, WD = S[f"WUP{l}"], S[f"WDN{l}"]
        psA = [self.PS[0], self.PS[1]]
        psU = [self.PS[2], self.PS[3]]
        psH = self.PS[4]
        psD = [self.PS[5], self.PS[6]]
        fw = [self.prm[f"fcw{t}_{l}"] for t in range(3)]
        fb = self.prm[f"fcb_{l}"]
        oi = 0
        for (tok0, T, v, seq, first, last) in self.tiles:
            self.p.barrier()
            TT = T + 2
            hv = h2.ap[:, 0:KC * TT].rearrange("p (k t) -> p k t", t=TT)
            kt = lambda a: a.rearrange("p (k t) -> p k t", t=T)
            self.dma("gpsimd", kt(x.ap[:, 0:KC * T]), S["XT"][:, tok0:tok0 + T].rearrange("(k p) t -> p k t", p=128), [self.wtok], [x.t])
            lo = tok0 - (0 if first else 1)
            hi = tok0 + T + (0 if last else 1)
            if first:
                self.memset(hv[:, :, 0:1], 0.0, [h2.t])
            if last:
                self.memset(hv[:, :, TT - 1:TT], 0.0, [h2.t])
            self.dma("gpsimd", hv[:, :, (1 if first else 0):(TT - 1 if last else TT)],
                     S["H2"][:, lo:hi].rearrange("(k p) t -> p k t", p=128), [self.wtok], [h2.t])
            for i in range(FC):
                u = self.wload(WU[i])
                pa, pu = psA[i % 2], psU[i % 2]
                for k in range(KC):
                    self.mm(pa.ap[:, 0:T], u.ap[:, k * 128:(k + 1) * 128], hv[:, k, 1:1 + T], k == 0, k == KC - 1, [u.t, h2.t], [pa.t])
                for k in range(KC):
                    self.mm(psH.ap[:, 2 * i:2 * i + 2], u.ap[:, k * 128:(k + 1) * 128], hv[:, k, 0:TT:TT - 1], k == 0, k == KC - 1,
                            [u.t, h2.t], [psH.t])
                for k in range(KC):
                    self.mm(pu.ap[:, 0:T], u.ap[:, (KC + k) * 128:(KC + k + 1) * 128], hv[:, k, 1:1 + T], k == 0, k == KC - 1,
                            [u.t, h2.t], [pu.t])
                a = ab[i % 2]
                self.cp(a.ap[:, 1:1 + T], pa.ap[:, 0:T], [pa.t], [a.t], eng="scalar")
                self.cp(a.ap[:, 0:TT:TT - 1], psH.ap[:, 2 * i:2 * i + 2], [psH.t], [a.t], eng="vector")
                cc = ac[i % 2]
                self.ts(cc.ap[:, 0:T], a.ap[:, 0:T], fw[0][:, i:i + 1], fb[:, i:i + 1], ALU.mult, ALU.add, [a.t, self.prm_t], [cc.t])
                self.stt(cc.ap[:, 0:T], a.ap[:, 1:1 + T], fw[1][:, i:i + 1], cc.ap[:, 0:T], ALU.mult, ALU.add, [a.t, cc.t, self.prm_t], [cc.t])
                self.stt(cc.ap[:, 0:T], a.ap[:, 2:2 + T], fw[2][:, i:i + 1], cc.ap[:, 0:T], ALU.mult, ALU.add, [a.t, cc.t, self.prm_t], [cc.t])
                s = sl[i % 2]
                self.act(s.ap[:, 0:T], cc.ap[:, 0:T], AF.Silu, [cc.t], [s.t])
                self.tt(gbuf.ap[:, i * T:(i + 1) * T], pu.ap[:, 0:T], s.ap[:, 0:T], ALU.mult, [pu.t, s.t], [gbuf.t])
            for n in range(KC):
                u = self.wload(WD[n])
                pd = psD[n % 2]
                for f in range(FC):
                    self.mm(pd.ap[:, 0:T], u.ap[:, f * 128:(f + 1) * 128], gbuf.ap[:, f * T:(f + 1) * T], f == 0, f == FC - 1,
                            [u.t, gbuf.t], [pd.t])
                self.stt(x.ap[:, n * T:(n + 1) * T], pd.ap[:, 0:T], self.modv(l, v, 5)[:, n:n + 1], x.ap[:, n * T:(n + 1) * T],
                         ALU.mult, ALU.add, [pd.t, x.t, self.mod_t], [x.t])
            if not last_layer:
                self.dma("gpsimd", S["XT"][:, tok0:tok0 + T].rearrange("(k p) t -> p k t", p=128), kt(x.ap[:, 0:KC * T]), [x.t], [self.wtok])
            else:
                self.norm_mod(x, T, None, None, None, self.PS[7], sq, sd, tmp)
                dst = self.O["ys"] if v == 0 else self.O["yp"]
                r0 = tok0 if v == 0 else tok0 - c.NS
                gF = self.prm["gfinal"]
                for tb in range(T // 128):
                    o = ost[oi % 2]
                    oi += 1
                    for k0 in range(0, KC, 4):
                        ps = psA[(k0 // 4) % 2] if (k0 // 4) % 4 < 2 else psU[(k0 // 4) % 2]
                        for kk in range(4):
                            k = k0 + kk
                            y = yf[k % 2]
                            self.stt(y.ap[:, 0:128], x.ap[:, k * T + tb * 128:k * T + (tb + 1) * 128], gF[:, k:k + 1],
                                     sd.ap[:, tb * 128:(tb + 1) * 128], ALU.mult, ALU.mult, [x.t, sd.t, self.prm_t], [y.t])
                            self.tr(ps.ap[:, kk * 128:(kk + 1) * 128], y.ap[:, 0:128], self.ident.ap, [y.t, self.ident.t], [ps.t])
                        self.cp(o.ap[:, k0 * 128:(k0 + 4) * 128], ps.ap[:, 0:512], [ps.t], [o.t], eng=("scalar" if (k0 // 4) % 2 else "vector"))
                    self.dma("gpsimd", dst[r0 + tb * 128:r0 + (tb + 1) * 128, :], o.ap, [o.t], [self.wtok])

    def build(self, stop=None):
        c = self.c
        self.wtok = None
        for nm, fn in (("consts", self.setup_consts), ("params", self.load_params), ("convert", self.convert_weights),
                       ("mod", self.compute_mod), ("misc", self.setup_misc), ("rope", self.setup_rope)):
            fn()
            self.p.barrier()
            if stop == nm:
                self.p.finish()
                return self.nc
        for l in range(c.DEPTH):
            done = False
            for ph, fn in (("A", self.phase_A), ("B", self.phase_B), ("C", self.phase_C), ("D", self.phase_D)):
                fn(l)
                self.p.barrier()
                if stop == f"{ph}{l}":
                    done = True
                    break
            if done:
                break
        self.p.finish()
        return self.nc


def make_in_maps(cfg, inputs, n_cores):
    c = cfg
    maps = []
    wn = list(WSHAPES(c).keys())
    for i in range(n_cores):
        m = {}
        m["xs"] = np.ascontiguousarray(inputs["x_sample"][i])
        m["xp"] = np.ascontiguousarray(inputs["x_prompt"][i * c.NPB:(i + 1) * c.NPB]).reshape(c.NPB * c.SP, c.D)
        m["ck"] = np.ascontiguousarray(inputs["cache_k"][i]).reshape(c.DEPTH, c.PAST, c.DM)
        m["cv"] = np.ascontiguousarray(inputs["cache_v"][i]).reshape(c.DEPTH, c.PAST, c.DM)
        m["st"] = np.ascontiguousarray(inputs["state_lru"][i]).reshape(c.DEPTH, 2 * c.DM)
        m["cvec"] = np.concatenate([np.asarray(inputs["c"][i]).reshape(-1), np.asarray(inputs["c_ctx"]).reshape(-1)])
        for n in wn:
            m[n] = np.ascontiguousarray(inputs[n])
        maps.append(m)
    return maps


def gather_outputs(cfg, results, n_cores):
    c = cfg
    B = n_cores * c.NPB
    ys = np.stack([results[i]["ys"] for i in range(n_cores)], 0)
    yp = np.concatenate([results[i]["yp"].reshape(c.NPB, c.SP, c.D) for i in range(n_cores)], 0)
    nk = np.concatenate([results[i]["nk"] for i in range(n_cores)], 0).reshape(B, c.DEPTH, c.SP, c.DM // 128, 2, 64)
    nv = np.concatenate([results[i]["nv"] for i in range(n_cores)], 0).reshape(B, c.DEPTH, c.SP, c.DM // 128, 128)
    ns = np.concatenate([results[i]["nst"] for i in range(n_cores)], 0).reshape(B, c.DEPTH, 2, c.DM)
    return (yp.astype(np.float32), ys.astype(np.float32), nk.astype(np.float32), nv.astype(np.float32), ns.astype(np.float32))


def kernel(**inputs):
    inputs = {k: np.asarray(v, dtype=np.float32) for k, v in inputs.items()}
    cfg = Cfg()
    n = 8
    nc = K(cfg).build()
    maps = make_in_maps(cfg, inputs, n)
    res = run_bass_kernel_spmd(nc, maps, core_ids=list(range(n)))
    return gather_outputs(cfg, res.results, n)
```

```python
import contextlib
import math
import numpy as np
import concourse.bass as bass
import concourse.mybir as mybir
from concourse.bass_utils import run_bass_kernel_spmd

F32 = mybir.dt.float32
BF16 = mybir.dt.bfloat16
I32 = mybir.dt.int32
AF = mybir.ActivationFunctionType
ALU = mybir.AluOpType

COMPUTE = ("tensor", "vector", "scalar", "gpsimd")
QUEUES = ("sync", "scalar", "gpsimd")
ALL = ("sync", "tensor", "vector", "scalar", "gpsimd")
EPS = 1e-6


class Tok:
    __slots__ = ("w", "r")

    def __init__(self):
        self.w = None
        self.r = {}


class Prog:
    def __init__(self, nc, n_dma_sems=6):
        self.nc = nc
        self.es = contextlib.ExitStack()
        self.lists = {e: [] for e in ALL}
        self.sems = {}
        self.cnt = {}
        for e in COMPUTE:
            self.sems["c_" + e] = self.es.enter_context(nc.semaphore("c_" + e))
            self.cnt["c_" + e] = 0
        self.dpool = {}
        self.dnext = {}
        for q in QUEUES:
            keys = []
            for i in range(n_dma_sems if q != "scalar" else 3):
                k = f"d_{q}{i}"
                self.sems[k] = self.es.enter_context(nc.semaphore(k))
                self.cnt[k] = 0
                keys.append(k)
            self.dpool[q] = keys
            self.dnext[q] = 0
        self.waited = {e: {} for e in ALL}

    def sbuf(self, name, shape, dtype):
        return self.es.enter_context(self.nc.sbuf_tensor(name, list(shape), dtype))

    def psum(self, name, shape, dtype=F32):
        return self.es.enter_context(self.nc.psum_tensor(name, list(shape), dtype))

    def _collect(self, eng, reads, writes, extra=()):
        need = {}

        def add(ev):
            if ev is None:
                return
            k, v = ev
            if need.get(k, 0) < v:
                need[k] = v

        for t in reads:
            add(t.w)
        for t in writes:
            if t.r:
                for k, v in t.r.items():
                    add((k, v))
            else:
                add(t.w)
        for ev in extra:
            add(ev)
        out = []
        wd = self.waited[eng]
        own = "c_" + eng
        for k, v in need.items():
            if k == own and v > self.cnt.get(own, 0):
                continue
            if k == own and eng == "tensor":
                continue
            if wd.get(k, 0) < v:
                wd[k] = v
                out.append((self.sems[k], v))
        return out

    def _commit(self, ev, reads, writes):
        k, v = ev
        for t in reads:
            if t.r.get(k, 0) < v:
                t.r[k] = v
        for t in writes:
            t.w = ev
            t.r = {}

    def op(self, eng, fn, reads=(), writes=(), inc=True):
        waits = self._collect(eng, reads, writes)
        key = "c_" + eng
        sem = self.sems[key]
        if inc:
            self.cnt[key] += 1
            ev = (key, self.cnt[key])
        else:
            ev = None

        def emit(e, waits=waits, fn=fn, inc=inc, sem=sem):
            for s, v in waits:
                e.wait_ge(s, v)
            ins = fn(e)
            if inc:
                ins.then_inc(sem, 1)

        self.lists[eng].append(emit)
        if inc:
            self._commit(ev, reads, writes)
        else:
            nxt = (key, self.cnt[key] + 1)
            self._commit(nxt, reads, ())
            for t in writes:
                t.w = nxt
                t.r = {}
        return ev

    def dma(self, q, fn, reads=(), writes=()):
        pool = self.dpool[q]
        k = pool[self.dnext[q] % len(pool)]
        self.dnext[q] += 1
        prev = (k, self.cnt[k]) if self.cnt[k] > 0 else None
        waits = self._collect(q, reads, writes, extra=(prev,) if prev else ())
        self.cnt[k] += 16
        ev = (k, self.cnt[k])
        sem = self.sems[k]

        def emit(e, waits=waits, fn=fn, sem=sem):
            for s, v in waits:
                e.wait_ge(s, v)
            fn(e).then_inc(sem, 16)

        self.lists[q].append(emit)
        self._commit(ev, reads, writes)
        return ev

    def barrier(self):
        allev = [(k, v) for k, v in self.cnt.items() if v > 0]
        for eng in ALL:
            waits = []
            wd = self.waited[eng]
            for k, v in allev:
                if wd.get(k, 0) < v:
                    wd[k] = v
                    waits.append((self.sems[k], v))
            if waits:
                def emit(e, waits=waits):
                    for s, v in waits:
                        e.wait_ge(s, v)
                self.lists[eng].append(emit)

    def finish(self):
        self.barrier()
        lists = self.lists
        with self.nc.Block() as block:
            @block.sync
            def _(e):
                for f in lists["sync"]:
                    f(e)

            @block.tensor
            def _(e):
                for f in lists["tensor"]:
                    f(e)

            @block.vector
            def _(e):
                for f in lists["vector"]:
                    f(e)

            @block.scalar
            def _(e):
                for f in lists["scalar"]:
                    f(e)

            @block.gpsimd
            def _(e):
                for f in lists["gpsimd"]:
                    f(e)
        self.es.close()


class Buf:
    __slots__ = ("ap", "t")

    def __init__(self, ap):
        self.ap = ap
        self.t = Tok()


class Arena:
    def __init__(self, ap):
        self.ap = ap
        self.W = ap.shape[1]
        self.off = 0

    def reset(self):
        self.off = 0

    def f32(self, n):
        n2 = (n + 1) // 2 * 2
        a = self.ap[:, self.off:self.off + n]
        self.off += n2
        assert self.off <= self.W, ("arena overflow", self.off, self.W)
        return a

    def bf16(self, n):
        w = (n + 3) // 4 * 2
        a = self.ap[:, self.off:self.off + w].bitcast(BF16)[:, 0:n]
        self.off += w
        assert self.off <= self.W, ("arena overflow", self.off, self.W)
        return a

    def i32(self, n):
        return self.f32(n).bitcast(I32)


class Cfg:
    def __init__(self, D=2048, NS=4096, SP=256, NPB=4, PAST=512, DEPTH=2, GW=64, ARENA=31500, NWB=3):
        self.D = D
        self.KC = D // 128
        self.DM = D // 4
        self.MC = self.DM // 128
        self.DFF = ((8 * D // 3 + 127) // 128) * 128
        self.FC = self.DFF // 128
        self.NS, self.SP, self.NPB, self.PAST, self.DEPTH, self.GW = NS, SP, NPB, PAST, DEPTH, GW
        self.PC = PAST // 128
        self.NTOK = NS + NPB * SP
        self.NIN = 10 * self.DM + 4 * D
        self.NINC = self.NIN // 128
        self.TS = min(512, NS)
        self.ARENA = ARENA
        self.NWB = NWB
        self.LRU_TT = 256
        self.UMAX = max(self.KC * 128 * 2, self.FC * 128, 4 * self.MC * 128)
        assert self.MC >= 1 and NS % self.TS == 0 and SP % 128 == 0 and SP <= 512 and PAST % 128 == 0


WSHAPES = lambda c: dict(
    w_mod=(c.DEPTH, c.D, 6 * c.D), b_mod=(c.DEPTH, 6 * c.D), g_norm1=(c.DEPTH, c.D), g_norm2=(c.DEPTH, c.D),
    g_final=(c.D,), w_in=(c.DEPTH, c.D, c.NIN), lam_q1=(c.DEPTH, 64), lam_k1=(c.DEPTH, 64), lam_q2=(c.DEPTH, 64),
    lam_k2=(c.DEPTH, 64), g_subln=(c.DEPTH, 128), w_dw31=(c.DEPTH, 31, c.DM), b_dw31=(c.DEPTH, c.DM),
    g_ln_conv=(c.DEPTH, c.DM), b_ln_conv=(c.DEPTH, c.DM), w_dw3=(c.DEPTH, 3, c.DM), w_conv4=(c.DEPTH, 4, c.DM),
    b_conv4=(c.DEPTH, c.DM), w_rg_a=(c.DEPTH, 2, c.DM // 64, 64, 64), b_rg_a=(c.DEPTH, 2, c.DM),
    w_rg_x=(c.DEPTH, 2, c.DM // 64, 64, 64), b_rg_x=(c.DEPTH, 2, c.DM), lru_lambda=(c.DEPTH, 2, c.DM),
    w_branch=(c.DEPTH, 4, c.DM, c.D), w_out=(c.DEPTH, c.D, c.D), w_ffn_up=(c.DEPTH, c.D, 2 * c.DFF),
    w_ffn_conv=(c.DEPTH, 3, c.DFF), b_ffn_conv=(c.DEPTH, c.DFF), w_ffn_down=(c.DEPTH, c.DFF, c.D))


def rows128(ap):
    nd = len(ap.shape)
    if nd == 1:
        return ap.rearrange("(r c) -> r c", c=128)
    if nd == 2:
        return ap.rearrange("a (r c) -> (a r) c", c=128)
    raise ValueError


class K:
    def __init__(self, cfg):
        c = self.c = cfg
        nc = self.nc = bass.Bass("TRN2", target_bir_lowering=False)
        self.p = Prog(nc)
        din = lambda n, s: nc.dram_tensor(n, list(s), F32, kind="ExternalInput").ap()
        dout = lambda n, s: nc.dram_tensor(n, list(s), F32, kind="ExternalOutput").ap()
        dscr = lambda n, s, dt: nc.dram_tensor(n, list(s), dt, kind="Internal").ap()
        self.I = dict(xs=din("xs", (c.NS, c.D)), xp=din("xp", (c.NPB * c.SP, c.D)),
                      ck=din("ck", (c.DEPTH, c.PAST, c.DM)), cv=din("cv", (c.DEPTH, c.PAST, c.DM)),
                      st=din("st", (c.DEPTH, 2 * c.DM)), cvec=din("cvec", (2 * c.D,)))
        for n, s in WSHAPES(c).items():
            self.I[n] = din(n, s)
        self.O = dict(ys=dout("ys", (c.NS, c.D)), yp=dout("yp", (c.NPB * c.SP, c.D)),
                      nk=dout("nk", (c.NPB, c.DEPTH, c.SP, c.DM)), nv=dout("nv", (c.NPB, c.DEPTH, c.SP, c.DM)),
                      nst=dout("nst", (c.NPB, c.DEPTH, 2 * c.DM)))
        S = self.S = {}
        S["XT"] = dscr("s_xt", (c.D, c.NTOK), F32)
        S["HT"] = dscr("s_ht", (c.D, c.NTOK), BF16)
        S["H2"] = dscr("s_h2", (c.D, c.NTOK), BF16)
        S["Q"] = dscr("s_q", (c.DM, c.NTOK), BF16)
        S["KT"] = dscr("s_k", (c.DM, c.NTOK), BF16)
        S["V"] = dscr("s_v", (c.NTOK, c.DM), BF16)
        for n in ("UB", "GCX", "GB", "XR", "GY"):
            S[n] = dscr("s_" + n, (c.DM, c.NTOK), F32)
        S["Y"] = dscr("s_y", (4 * c.DM, c.NTOK), BF16)
        S["RC"] = dscr("s_rc", (128, c.NS), F32)
        S["RS"] = dscr("s_rs", (128, c.NS), F32)
        for l in range(c.DEPTH):
            S[f"WIN{l}"] = dscr(f"w_in_b{l}", (c.NINC, 128, c.KC * 128), BF16)
            S[f"WBR{l}"] = dscr(f"w_br_b{l}", (c.KC, 128, 4 * c.MC * 128), BF16)
            S[f"WOUT{l}"] = dscr(f"w_out_b{l}", (c.KC, 128, c.KC * 128), BF16)
            S[f"WUP{l}"] = dscr(f"w_up_b{l}", (c.FC, 128, 2 * c.KC * 128), BF16)
            S[f"WDN{l}"] = dscr(f"w_dn_b{l}", (c.KC, 128, c.FC * 128), BF16)
        p = self.p
        self.ident = Buf(p.sbuf("ident", (128, 128), F32)[:])
        self.ones = Buf(p.sbuf("ones", (128, 128), F32)[:])
        self.onesb = Buf(p.sbuf("onesb", (128, 128), BF16)[:])
        self.pm = Buf(p.sbuf("pm", (128, 128), F32)[:])
        self.wb = [Buf(p.sbuf(f"wb{i}", (128, c.UMAX), BF16)[:]) for i in range(c.NWB)]
        self.wbi = 0
        self.PS = [Buf(p.psum(f"ps{i}", (128, 512))[:]) for i in range(8)]
        self.ar = Arena(p.sbuf("arena", (128, c.ARENA), F32)[:])
        self.tiles = [(i * c.TS, c.TS, 0, 0, i == 0, i == c.NS // c.TS - 1) for i in range(c.NS // c.TS)] + \
                     [(c.NS + b * c.SP, c.SP, 1, 1 + b, True, True) for b in range(c.NPB)]
        self.seqs = [(0, c.NS, True, 0)] + [(c.NS + b * c.SP, c.SP, False, b) for b in range(c.NPB)]

    def mm(self, out, lhsT, rhs, start, stop, reads, writes, inc=None):
        self.p.op("tensor", lambda e: e.matmul(out, lhsT=lhsT, rhs=rhs, start=start, stop=stop),
                  reads, writes, inc=True)

    def tr(self, out, in_, ident, reads, writes, inc=True):
        self.p.op("tensor", lambda e: e.transpose(out=out, in_=in_, identity=ident), reads, writes, inc=inc)

    def act(self, out, in_, func, reads, writes, scale=1.0, bias=0.0):
        self.p.op("scalar", lambda e: e.activation(out=out, in_=in_, func=func, bias=bias, scale=scale), reads, writes)

    def tt(self, out, in0, in1, op, reads, writes, eng="vector"):
        self.p.op(eng, lambda e: e.tensor_tensor(out=out, in0=in0, in1=in1, op=op), reads, writes)

    def ts(self, out, in0, s1, s2, op0, op1, reads, writes, eng="vector"):
        if op1 is None:
            self.p.op(eng, lambda e: e.tensor_scalar(out=out, in0=in0, scalar1=s1, scalar2=None, op0=op0), reads, writes)
        else:
            self.p.op(eng, lambda e: e.tensor_scalar(out=out, in0=in0, scalar1=s1, scalar2=s2, op0=op0, op1=op1), reads, writes)

    def stt(self, out, in0, scalar, in1, op0, op1, reads, writes):
        self.p.op("vector", lambda e: e.scalar_tensor_tensor(out=out, in0=in0, scalar=scalar, in1=in1, op0=op0, op1=op1),
                  reads, writes)

    def cp(self, out, in_, reads, writes, eng="vector"):
        if eng == "scalar":
            self.p.op("scalar", lambda e: e.copy(out=out, in_=in_), reads, writes)
        else:
            self.p.op(eng, lambda e: e.tensor_copy(out=out, in_=in_), reads, writes)

    def recip(self, out, in_, reads, writes):
        self.p.op("vector", lambda e: e.reciprocal(out=out, in_=in_), reads, writes)

    def memset(self, ap, val, writes, eng="gpsimd"):
        self.p.op(eng, lambda e: e.memset(ap, val), (), writes)

    def dma(self, q, out, in_, reads, writes):
        reads = [t for t in reads if t is not None]
        writes = [t for t in writes if t is not None]
        self.p.dma(q, lambda e: e.dma_start(out=out, in_=in_), reads, writes)

    def wload(self, src):
        b = self.wb[self.wbi % len(self.wb)]
        self.wbi += 1
        E = src.shape[1]
        self.dma("sync", b.ap[:, 0:E], src, [self.wtok], [b.t])
        return b

    def setup_consts(self):
        p = self.p
        self.memset(self.ident.ap, 0.0, [self.ident.t])
        p.op("gpsimd", lambda e: e.affine_select(out=self.ident.ap, in_=self.ident.ap, compare_op=ALU.not_equal, fill=1.0,
                                                 base=0, pattern=[[-1, 128]], channel_multiplier=1),
             [self.ident.t], [self.ident.t])
        self.memset(self.ones.ap, 1.0, [self.ones.t])
        self.memset(self.onesb.ap, 1.0, [self.onesb.t])

    def load_params(self):
        c, I = self.c, self.I
        ents = []

        def add(name, ap):
            r = rows128(ap)
            ents.append((name, r, r.shape[0]))

        add("gfinal", I["g_final"])
        add("cvec", I["cvec"])
        for l in range(c.DEPTH):
            add(f"g1_{l}", I["g_norm1"][l])
            add(f"g2_{l}", I["g_norm2"][l])
            add(f"bmod_{l}", I["b_mod"][l])
            add(f"dw31_{l}", I["w_dw31"][l])
            for n in ("b_dw31", "g_ln_conv", "b_ln_conv", "b_conv4", "g_subln"):
                add(f"{n}_{l}", I[n][l])
            for n in ("w_dw3", "w_conv4", "b_rg_a", "b_rg_x", "lru_lambda"):
                add(f"{n}_{l}", I[n][l])
            add(f"st_{l}", I["st"][l])
            for t in range(3):
                add(f"fcw{t}_{l}", I["w_ffn_conv"][l, t])
            add(f"fcb_{l}", I["b_ffn_conv"][l])
        tiles_ = [[]]
        used = 0
        for name, r, R in ents:
            assert R <= 128
            if used + R > 128:
                tiles_.append([])
                used = 0
            tiles_[-1].append((name, r, R, used))
            used += R
        NT = len(tiles_)
        self.PRM = self.p.sbuf("prm", (128, NT * 128), F32)[:]
        self.prm_t = Tok()
        self.prm = {}
        stg = [Buf(self.ar.f32(128)) for _ in range(2)]
        for s in stg:
            self.memset(s.ap, 0.0, [s.t])
        for ti, tl in enumerate(tiles_):
            s = stg[ti % 2]
            ps = self.PS[ti % 2]
            for name, r, R, r0 in tl:
                self.dma("gpsimd", s.ap[r0:r0 + R, :], r, [], [s.t])
                self.prm[name] = self.PRM[:, ti * 128 + r0: ti * 128 + r0 + R]
            self.tr(ps.ap[:, 0:128], s.ap, self.ident.ap, [s.t, self.ident.t], [ps.t])
            self.cp(self.PRM[:, ti * 128:(ti + 1) * 128], ps.ap[:, 0:128], [ps.t], [self.prm_t])

    def convert_weights(self):
        c, I, S = self.c, self.I, self.S
        self.ar.reset()
        NB = 3
        CB = 2048
        st32 = [Buf(self.ar.f32(CB)) for _ in range(NB)]
        st16 = [Buf(self.ar.bf16(CB)) for _ in range(NB)]
        step = [0]

        def conv(src, dst, u0, offf):
            Kr, N = src.shape
            for kc in range(Kr // 128):
                for n0 in range(0, N, CB):
                    nn = min(CB, N - n0)
                    nb = nn // 128
                    i = step[0] % NB
                    step[0] += 1
                    a, b = st32[i], st16[i]
                    self.dma("sync", a.ap[:, 0:nn], src[kc * 128:(kc + 1) * 128, n0:n0 + nn], [], [a.t])
                    self.cp(b.ap[:, 0:nn], a.ap[:, 0:nn], [a.t], [b.t], eng=("vector" if step[0] % 2 else "gpsimd"))
                    off = offf(kc)
                    j0 = u0 + n0 // 128
                    self.dma("scalar", dst[j0:j0 + nb, :, off:off + 128].rearrange("u p c -> p u c"),
                             b.ap[:, 0:nn].rearrange("p (u c) -> p u c", c=128), [b.t], [self.wtok])

        for l in range(c.DEPTH):
            conv(I["w_in"][l], S[f"WIN{l}"], 0, lambda kc: kc * 128)
            for j in range(4):
                conv(I["w_branch"][l, j], S[f"WBR{l}"], 0, lambda kc, j=j: (j * c.MC + kc) * 128)
            conv(I["w_out"][l], S[f"WOUT{l}"], 0, lambda kc: kc * 128)
            for s in range(2):
                conv(I["w_ffn_up"][l][:, s * c.DFF:(s + 1) * c.DFF], S[f"WUP{l}"], 0,
                     lambda kc, s=s: (s * c.KC + kc) * 128)
            conv(I["w_ffn_down"][l], S[f"WDN{l}"], 0, lambda kc: kc * 128)

    def compute_mod(self):
        c, I = self.c, self.I
        KC = c.KC
        self.ar.reset()
        NMC = 6 * KC
        self.MOD = self.p.sbuf("mod", (128, c.DEPTH * NMC * 2), F32)[:]
        self.AB = self.p.sbuf("ab", (128, c.DEPTH * 2 * 2 * KC), F32)[:]
        self.mod_t = Tok()
        scv = Buf(self.ar.f32(2 * KC))
        self.act(scv.ap, self.prm["cvec"], AF.Silu, [self.prm_t], [scv.t])
        CBK = 256
        wst = [Buf(self.ar.f32(KC * CBK)) for _ in range(2)]
        ps = self.PS[2]
        k = 0
        for l in range(c.DEPTH):
            for cb in range(6 * c.D // CBK):
                w = wst[k % 2]
                k += 1
                self.dma("sync", w.ap.rearrange("p (k n) -> p k n", n=CBK),
                         I["w_mod"][l][:, cb * CBK:(cb + 1) * CBK].rearrange("(k p) n -> p k n", p=128), [], [w.t])
                for nn in range(CBK // 128):
                    n = cb * (CBK // 128) + nn
                    for kc in range(KC):
                        self.mm(ps.ap[:, 2 * n:2 * n + 2], w.ap[:, kc * CBK + nn * 128: kc * CBK + nn * 128 + 128],
                                scv.ap[:, kc::KC], kc == 0, kc == KC - 1, [w.t, scv.t], [ps.t])
            for v in range(2):
                base = (l * 2 + v) * NMC
                self.tt(self.MOD[:, base:base + NMC], ps.ap[:, v:2 * NMC:2], self.prm[f"bmod_{l}"], ALU.add,
                        [ps.t, self.prm_t], [self.mod_t])
                for s, (gname, scoff) in enumerate((("g1", KC), ("g2", 4 * KC))):
                    o = ((l * 2 + v) * 2 + s) * KC
                    self.stt(self.AB[:, o:o + KC], self.MOD[:, base + scoff: base + scoff + KC], 1.0,
                             self.prm[f"{gname}_{l}"], ALU.add, ALU.mult, [self.mod_t, self.prm_t], [self.mod_t])

    def modv(self, l, v, which):
        KC = self.c.KC
        base = (l * 2 + v) * 6 * KC + which * KC
        return self.MOD[:, base:base + KC]

    def abv(self, l, v, s):
        KC = self.c.KC
        o = ((l * 2 + v) * 2 + s) * KC
        return self.AB[:, o:o + KC]

    def setup_misc(self):
        c, I = self.c, self.I
        MC = c.MC
        self.ar.reset()
        self.MISC = self.p.sbuf("misc", (128, c.DEPTH * (4 + 4 * MC)), F32)[:]
        self.misc_t = Tok()
        self.BD = self.p.sbuf("bd", (128, c.DEPTH * 4 * MC * 128), BF16)[:]
        self.bd_t = Tok()
        lamst = Buf(self.ar.f32(4 * 64))
        tmp = Buf(self.ar.f32(8))
        bst = Buf(self.ar.f32(4 * MC * 128))
        for l in range(c.DEPTH):
            mb = l * (4 + 4 * MC)
            for i, n in enumerate(("lam_q1", "lam_k1", "lam_q2", "lam_k2")):
                self.dma("gpsimd", lamst.ap[:, i * 64:(i + 1) * 64], I[n][l].partition_broadcast(128), [], [lamst.t])
            for i in range(2):
                self.tt(lamst.ap[:, i * 128:i * 128 + 64], lamst.ap[:, i * 128:i * 128 + 64],
                        lamst.ap[:, i * 128 + 64:i * 128 + 128], ALU.mult, [lamst.t], [lamst.t])
                self.p.op("vector", lambda e, i=i: e.reduce_sum(out=tmp.ap[:, i:i + 1], in_=lamst.ap[:, i * 128:i * 128 + 64],
                                                               axis=mybir.AxisListType.X), [lamst.t], [tmp.t])
            self.act(tmp.ap[:, 2:4], tmp.ap[:, 0:2], AF.Exp, [tmp.t], [tmp.t])
            lam_init = 0.8 - 0.6 * math.exp(-0.3 * l)
            self.stt(self.MISC[:, mb:mb + 1], tmp.ap[:, 3:4], -lam_init, tmp.ap[:, 2:3], ALU.add, ALU.subtract,
                     [tmp.t], [self.misc_t])
            self.ts(self.MISC[:, mb + 1:mb + 2], self.prm[f"g_subln_{l}"], 1.0 - lam_init, None, ALU.mult, None,
                    [self.prm_t], [self.misc_t])
            sp = Buf(self.ar.f32(2 * MC))
            self.act(sp.ap, self.prm[f"lru_lambda_{l}"], AF.Exp, [self.prm_t], [sp.t], scale=-1.0)
            self.act(sp.ap, sp.ap, AF.Ln, [sp.t], [sp.t], bias=1.0)
            self.ts(self.MISC[:, mb + 4:mb + 4 + 2 * MC], sp.ap, -8.0, None, ALU.mult, None, [sp.t], [self.misc_t])
            self.ts(self.MISC[:, mb + 4 + 2 * MC:mb + 4 + 4 * MC], sp.ap, -16.0, None, ALU.mult, None, [sp.t], [self.misc_t])
            self.memset(bst.ap, 0.0, [bst.t])
            for g, n in enumerate(("w_rg_a", "w_rg_x")):
                for d in range(2):
                    for m in range(MC):
                        o = ((g * 2 + d) * MC + m) * 128
                        for hb in range(2):
                            self.dma("gpsimd", bst.ap[hb * 64:(hb + 1) * 64, o + hb * 64:o + hb * 64 + 64],
                                     I[n][l, d, 2 * m + hb], [], [bst.t])
            self.cp(self.BD[:, l * 4 * MC * 128:(l + 1) * 4 * MC * 128], bst.ap, [bst.t], [self.bd_t])

    def misc(self, l, i):
        mb = l * (4 + 4 * self.c.MC)
        return self.MISC[:, mb + i:mb + i + 1]

    def setup_rope(self):
        c = self.c
        self.ar.reset()
        NS, GW = c.NS, c.GW
        pi_ = Buf(self.ar.i32(2))
        ti = Buf(self.ar.i32(8))
        tf = Buf(self.ar.f32(16))
        self.p.op("gpsimd", lambda e: e.iota(pi_.ap[:, 0:1], pattern=[[0, 1]], base=0, channel_multiplier=1), [], [pi_.t])
        sh = lambda o, s, m: self.p.op("vector", lambda e: e.tensor_scalar(out=ti.ap[:, o:o + 1], in0=pi_.ap[:, 0:1], scalar1=s,
                                                                          scalar2=m, op0=ALU.arith_shift_right,
                                                                          op1=ALU.bitwise_and), [pi_.t], [ti.t])
        sh(0, 0, 15)
        sh(1, 5, 1)
        sh(2, 4, 1)
        self.cp(tf.ap[:, 0:3], ti.ap[:, 0:3], [ti.t], [tf.t])
        self.act(tf.ap[:, 3:4], tf.ap[:, 0:1], AF.Exp, [tf.t], [tf.t], scale=-math.log(10000.0) / 16.0)
        self.tt(tf.ap[:, 5:6], tf.ap[:, 3:4], tf.ap[:, 1:2], ALU.mult, [tf.t], [tf.t])
        self.tt(tf.ap[:, 4:5], tf.ap[:, 3:4], tf.ap[:, 5:6], ALU.subtract, [tf.t], [tf.t])
        R = Buf(self.ar.f32(NS))
        Cc = Buf(self.ar.f32(NS))
        ang = Buf(self.ar.f32(NS))
        t2 = Buf(self.ar.f32(NS))
        rows = NS // GW
        self.p.op("gpsimd", lambda e: e.iota(R.ap.rearrange("p (r g) -> p r g", g=GW), pattern=[[1, rows], [0, GW]], base=0,
                                             channel_multiplier=0, allow_small_or_imprecise_dtypes=True), [], [R.t])
        self.p.op("gpsimd", lambda e: e.iota(Cc.ap.rearrange("p (r g) -> p r g", g=GW), pattern=[[0, rows], [1, GW]], base=0,
                                             channel_multiplier=0, allow_small_or_imprecise_dtypes=True), [], [Cc.t])
        self.ts(ang.ap, R.ap, tf.ap[:, 4:5], None, ALU.mult, None, [R.t, tf.t], [ang.t])
        self.stt(ang.ap, Cc.ap, tf.ap[:, 5:6], ang.ap, ALU.mult, ALU.add, [Cc.t, tf.t, ang.t], [ang.t])
        MAGIC = 12582912.0
        TWO_PI = 2.0 * math.pi
        for which, shift in ((0, math.pi / 2), (1, 0.0)):
            self.ts(t2.ap, ang.ap, shift, 1.0 / TWO_PI, ALU.add, ALU.mult, [ang.t], [t2.t])
            self.ts(t2.ap, t2.ap, MAGIC, MAGIC, ALU.add, ALU.subtract, [t2.t], [t2.t])
            self.stt(t2.ap, t2.ap, -TWO_PI, ang.ap, ALU.mult, ALU.add, [t2.t, ang.t], [t2.t])
            self.ts(t2.ap, t2.ap, shift, 3.14159, ALU.add, ALU.min, [t2.t], [t2.t])
            self.ts(t2.ap, t2.ap, -3.14159, None, ALU.max, None, [t2.t], [t2.t])
            self.act(t2.ap, t2.ap, AF.Sin, [t2.t], [t2.t])
            self.dma("gpsimd", self.S["RC" if which == 0 else "RS"], t2.ap, [t2.t], [self.wtok])
        A = Buf(self.ar.f32(128))
        B_ = Buf(self.ar.f32(128))
        mi = Buf(self.ar.i32(128))
        mf = Buf(self.ar.f32(128))
        self.memset(A.ap, 0.0, [A.t])
        self.memset(B_.ap, 0.0, [B_.t])
        self.p.op("gpsimd", lambda e: e.affine_select(out=A.ap, in_=A.ap, compare_op=ALU.not_equal, fill=-1.0, base=-16,
                                                      pattern=[[-1, 128]], channel_multiplier=1), [A.t], [A.t])
        self.p.op("gpsimd", lambda e: e.affine_select(out=B_.ap, in_=B_.ap, compare_op=ALU.not_equal, fill=1.0, base=16,
                                                      pattern=[[-1, 128]], channel_multiplier=1), [B_.t], [B_.t])
        self.p.op("gpsimd", lambda e: e.iota(mi.ap, pattern=[[1, 128]], base=0, channel_multiplier=0), [], [mi.t])
        self.p.op("vector", lambda e: e.tensor_scalar(out=mi.ap, in0=mi.ap, scalar1=4, scalar2=1, op0=ALU.arith_shift_right,
                                                      op1=ALU.bitwise_and), [mi.t], [mi.t])
        self.cp(mf.ap, mi.ap, [mi.t], [mf.t])
        self.tt(B_.ap, B_.ap, mf.ap, ALU.mult, [B_.t, mf.t], [B_.t])
        self.ts(mf.ap, mf.ap, -1.0, 1.0, ALU.mult, ALU.add, [mf.t], [mf.t])
        self.tt(A.ap, A.ap, mf.ap, ALU.mult, [A.t, mf.t], [A.t])
        self.tt(self.pm.ap, A.ap, B_.ap, ALU.add, [A.t, B_.t], [self.pm.t])

    def norm_mod(self, x, T, Acols, Bcols, out, ps, sq, sd, tmp, extra_reads=()):
        c = self.c
        KC = c.KC
        for k in range(KC):
            s = sq[k % len(sq)]
            self.act(s.ap[:, 0:T], x.ap[:, k * T:(k + 1) * T], AF.Square, [x.t], [s.t])
            self.mm(ps.ap[:, 0:T], self.ones.ap, s.ap[:, 0:T], k == 0, k == KC - 1, [s.t, self.ones.t], [ps.t])
        self.act(sd.ap[:, 0:T], ps.ap[:, 0:T], AF.Sqrt, [ps.t], [sd.t], scale=1.0 / c.D, bias=EPS)
        self.recip(sd.ap[:, 0:T], sd.ap[:, 0:T], [sd.t], [sd.t])

    def apply_mod(self, x, T, sd, Acols, Bcols, outfn, out_t, tmp, reads):
        KC = self.c.KC
        for k in range(KC):
            t = tmp[k % len(tmp)]
            self.tt(t.ap[:, 0:T], x.ap[:, k * T:(k + 1) * T], sd.ap[:, 0:T], ALU.mult, [x.t, sd.t], [t.t])
            if Bcols is None:
                self.act(outfn(k), t.ap[:, 0:T], AF.Copy, [t.t] + reads, [out_t], scale=Acols[:, k:k + 1])
            else:
                self.act(outfn(k), t.ap[:, 0:T], AF.Identity, [t.t] + reads, [out_t], scale=Acols[:, k:k + 1],
                         bias=Bcols[:, k:k + 1])

    def phase_A(self, l):
        c, S, I = self.c, self.S, self.I
        KC, MC, DM = c.KC, c.MC, c.DM
        ar = self.ar
        ar.reset()
        TM = 512
        xT = [Buf(ar.f32(KC * TM)) for _ in range(1)]
        hT = [Buf(ar.bf16(KC * TM)) for _ in range(1)]
        xin = [Buf(ar.f32(c.D)) for _ in range(2)] if l == 0 else []
        sq = [Buf(ar.f32(TM)) for _ in range(2)]
        tmp = [Buf(ar.f32(TM)) for _ in range(2)]
        sd = Buf(ar.f32(TM))
        rc = Buf(ar.f32(TM))
        rs = Buf(ar.f32(TM))
        ob = [Buf(ar.f32(TM)) for _ in range(4)]
        o16 = [Buf(ar.bf16(TM)) for _ in range(3)]
        xf = [Buf(ar.f32(TM)) for _ in range(2)]
        vst = Buf(ar.bf16(4 * DM))
        kvf = [Buf(ar.f32(4 * 128)) for _ in range(2)]
        obi = [0]
        o16i = [0]
        W = S[f"WIN{l}"]
        psr = [self.PS[i] for i in (0, 1, 2, 3, 4)]
        pi = [0]

        def nps():
            b = psr[pi[0] % len(psr)]
            pi[0] += 1
            return b

        def fm(unit, h, T):
            ps = nps()
            for k in range(KC):
                self.mm(ps.ap[:, 0:T], unit.ap[:, k * 128:(k + 1) * 128], h.ap[:, k * T:(k + 1) * T], k == 0, k == KC - 1,
                        [unit.t, h.t], [ps.t])
            return ps

        def nob():
            b = ob[obi[0] % len(ob)]
            obi[0] += 1
            return b

        def no16():
            b = o16[o16i[0] % len(o16)]
            o16i[0] += 1
            return b

        for tix, (tok0, T, v, seq, first, last) in enumerate(self.tiles):
            if getattr(self, "a_tiles", None) is not None and tix not in self.a_tiles:
                continue
            self.p.barrier()
            x, h = xT[0], hT[0]
            TB = T // 128
            src_in = I["xs"] if v == 0 else I["xp"]
            r0 = tok0 if v == 0 else tok0 - c.NS
            if l == 0:
                for tb in range(TB):
                    xi = xin[tb % 2]
                    self.dma("gpsimd", xi.ap, src_in[r0 + tb * 128: r0 + (tb + 1) * 128, :], [], [xi.t])
                    for k0 in range(0, KC, 4):
                        ps = self.PS[5 + (k0 // 4) % 2]
                        for kk in range(4):
                            k = k0 + kk
                            self.tr(ps.ap[:, kk * 128:(kk + 1) * 128], xi.ap[:, k * 128:(k + 1) * 128], self.ident.ap,
                                    [xi.t, self.ident.t], [ps.t])
                        dst = x.ap[:, 0:KC * T].rearrange("p (k t) -> p k t", t=T)[:, k0:k0 + 4, tb * 128:(tb + 1) * 128]
                        self.cp(dst, ps.ap.rearrange("p (k t) -> p k t", t=128), [ps.t], [x.t],
                                eng=("vector" if (k0 // 4) % 2 else "scalar"))
                self.dma("gpsimd", S["XT"][:, tok0:tok0 + T].rearrange("(k p) t -> p k t", p=128),
                         x.ap[:, 0:KC * T].rearrange("p (k t) -> p k t", t=T), [x.t], [self.wtok])
            else:
                self.dma("gpsimd", x.ap[:, 0:KC * T].rearrange("p (k t) -> p k t", t=T),
                         S["XT"][:, tok0:tok0 + T].rearrange("(k p) t -> p k t", p=128), [self.wtok], [x.t])
            if getattr(self, "a_stop", 0) == 1:
                return
            self.norm_mod(x, T, None, None, None, self.PS[7], sq, sd, tmp)
            if getattr(self, "a_stop", 0) == 2:
                return
            self.apply_mod(x, T, sd, self.abv(l, v, 0), self.modv(l, v, 0), lambda k: h.ap[:, k * T:(k + 1) * T], h.t, tmp,
                           [self.mod_t])
            if getattr(self, "a_stop", 0) == 3:
                return
            self.dma("gpsimd", S["HT"][:, tok0:tok0 + T].rearrange("(k p) t -> p k t", p=128),
                     h.ap[:, 0:KC * T].rearrange("p (k t) -> p k t", t=T), [h.t], [self.wtok])
            if v == 0:
                self.dma("gpsimd", rc.ap[:, 0:T], S["RC"][:, tok0:tok0 + T], [self.wtok], [rc.t])
                self.dma("gpsimd", rs.ap[:, 0:T], S["RS"][:, tok0:tok0 + T], [self.wtok], [rs.t])
            for kind, base, dst in (("q", 0, S["Q"]), ("k", MC, S["KT"])):
                for m in range(MC):
                    u = self.wload(W[base + m])
                    ps = fm(u, h, T)
                    o = no16()
                    if v == 0:
                        f = xf[m % 2]
                        self.cp(f.ap[:, 0:T], ps.ap[:, 0:T], [ps.t], [f.t], eng="scalar")
                        ps2 = nps()
                        self.mm(ps2.ap[:, 0:T], self.pm.ap, f.ap[:, 0:T], True, True, [self.pm.t, f.t], [ps2.t])
                        t1 = tmp[m % 2]
                        self.tt(t1.ap[:, 0:T], f.ap[:, 0:T], rc.ap[:, 0:T], ALU.mult, [f.t, rc.t], [t1.t])
                        self.tt(f.ap[:, 0:T], ps2.ap[:, 0:T], rs.ap[:, 0:T], ALU.mult, [ps2.t, rs.t], [f.t])
                        self.tt(o.ap[:, 0:T], t1.ap[:, 0:T], f.ap[:, 0:T], ALU.add, [t1.t, f.t], [o.t])
                    else:
                        self.cp(o.ap[:, 0:T], ps.ap[:, 0:T], [ps.t], [o.t], eng="scalar")
                    self.dma("gpsimd", dst[m * 128:(m + 1) * 128, tok0:tok0 + T], o.ap[:, 0:T], [o.t], [self.wtok])
                    if kind == "k" and v == 1:
                        for tb in range(TB):
                            ps3 = nps()
                            for k in range(KC):
                                self.mm(ps3.ap[:, 0:128], h.ap[:, k * T + tb * 128:k * T + (tb + 1) * 128],
                                        u.ap[:, k * 128:(k + 1) * 128], k == 0, k == KC - 1, [u.t, h.t], [ps3.t])
                            kf = kvf[tb % 2]
                            self.cp(kf.ap[:, 0:128], ps3.ap[:, 0:128], [ps3.t], [kf.t])
                            self.dma("gpsimd", self.O["nk"][seq - 1, l, tb * 128:(tb + 1) * 128, m * 128:(m + 1) * 128],
                                     kf.ap[:, 0:128], [kf.t], [self.wtok])
            if getattr(self, "a_stop", 0) == 4:
                return
            for m in range(MC):
                u = self.wload(W[2 * MC + m])
                ps = nps()
                for tb in range(TB):
                    for k in range(KC):
                        self.mm(ps.ap[:, tb * 128:(tb + 1) * 128], h.ap[:, k * T + tb * 128:k * T + (tb + 1) * 128],
                                u.ap[:, k * 128:(k + 1) * 128], k == 0, k == KC - 1, [u.t, h.t], [ps.t],
                                inc=(k == KC - 1 and tb == TB - 1))
                dstv = vst.ap[:, 0:TB * DM].rearrange("p (b e) -> p b e", e=DM)[:, :, m * 128:(m + 1) * 128]
                if v == 0:
                    self.cp(dstv, ps.ap[:, 0:TB * 128].rearrange("p (b e) -> p b e", e=128), [ps.t], [vst.t])
                if v == 1:
                    kf = kvf[m % 2]
                    self.cp(kf.ap[:, 0:TB * 128], ps.ap[:, 0:TB * 128], [ps.t], [kf.t])
                    self.cp(dstv, kf.ap[:, 0:TB * 128].rearrange("p (b e) -> p b e", e=128), [kf.t], [vst.t])
                    for tb in range(TB):
                        self.dma("gpsimd", self.O["nv"][seq - 1, l, tb * 128:(tb + 1) * 128, m * 128:(m + 1) * 128],
                                 kf.ap[:, tb * 128:(tb + 1) * 128], [kf.t], [self.wtok])
            self.dma("gpsimd", S["V"][tok0:tok0 + T, :].rearrange("(b p) e -> p b e", p=128),
                     vst.ap[:, 0:TB * DM].rearrange("p (b e) -> p b e", e=DM), [vst.t], [self.wtok])
            if getattr(self, "a_stop", 0) == 5:
                return
            for m in range(MC):
                ua = self.wload(W[3 * MC + m])
                pa = fm(ua, h, T)
                ug = self.wload(W[4 * MC + m])
                pg = fm(ug, h, T)
                sg = tmp[m % 2]
                self.act(sg.ap[:, 0:T], pg.ap[:, 0:T], AF.Sigmoid, [pg.t], [sg.t])
                o = nob()
                self.tt(o.ap[:, 0:T], pa.ap[:, 0:T], sg.ap[:, 0:T], ALU.mult, [pa.t, sg.t], [o.t])
                self.dma("gpsimd", S["UB"][m * 128:(m + 1) * 128, tok0:tok0 + T], o.ap[:, 0:T], [o.t], [self.wtok])
            if getattr(self, "a_stop", 0) == 6:
                return
            for m in range(MC):
                ub_ = self.wload(W[5 * MC + m])
                pb = fm(ub_, h, T)
                o = nob()
                self.cp(o.ap[:, 0:T], pb.ap[:, 0:T], [pb.t], [o.t], eng="scalar")
                self.dma("gpsimd", S["GB"][m * 128:(m + 1) * 128, tok0:tok0 + T], o.ap[:, 0:T], [o.t], [self.wtok])
                uc = self.wload(W[6 * MC + m])
                pc = fm(uc, h, T)
                ux = self.wload(W[7 * MC + m])
                px = fm(ux, h, T)
                g = tmp[m % 2]
                self.cp(g.ap[:, 0:T], pc.ap[:, 0:T], [pc.t], [g.t], eng="scalar")
                o = nob()
                self.tt(o.ap[:, 0:T], px.ap[:, 0:T], g.ap[:, 0:T], ALU.mult, [px.t, g.t], [o.t])
                self.dma("gpsimd", S["GCX"][m * 128:(m + 1) * 128, tok0:tok0 + T], o.ap[:, 0:T], [o.t], [self.wtok])
            if getattr(self, "a_stop", 0) == 7:
                return
            for m in range(MC):
                u1 = self.wload(W[8 * MC + m])
                p1 = fm(u1, h, T)
                o = nob()
                self.cp(o.ap[:, 0:T], p1.ap[:, 0:T], [p1.t], [o.t])
                self.dma("gpsimd", S["XR"][m * 128:(m + 1) * 128, tok0:tok0 + T], o.ap[:, 0:T], [o.t], [self.wtok])
                u2 = self.wload(W[9 * MC + m])
                p2 = fm(u2, h, T)
                o = nob()
                self.act(o.ap[:, 0:T], p2.ap[:, 0:T], AF.Gelu, [p2.t], [o.t])
                self.dma("gpsimd", S["GY"][m * 128:(m + 1) * 128, tok0:tok0 + T], o.ap[:, 0:T], [o.t], [self.wtok])
            if getattr(self, "a_stop", 0) == 8:
                return

    def phase_B(self, l):
        self.B_conformer(l)
        self.p.barrier()
        self.B_sconv_lru(l)
        self.p.barrier()
        self.B_attn(l)

    def B_conformer(self, l):
        c, S = self.c, self.S
        MC, DM = c.MC, c.DM
        ar = self.ar
        ar.reset()
        LM = c.NS
        cb = Buf(ar.f32(MC * LM))
        ubp = [Buf(ar.f32(LM + 30)) for _ in range(2)]
        sq = [Buf(ar.f32(512)) for _ in range(2)]
        mean = Buf(ar.f32(512))
        msq = Buf(ar.f32(512))
        rstd = Buf(ar.f32(512))
        t1 = [Buf(ar.f32(512)) for _ in range(2)]
        yo = [Buf(ar.bf16(512)) for _ in range(2)]
        w31 = self.prm[f"dw31_{l}"]
        for si, (tok0, L, is_s, b) in enumerate(self.seqs):
            self.p.barrier()
            for m in range(MC):
                u = ubp[m % 2]
                self.memset(u.ap[:, 0:15], 0.0, [u.t])
                self.memset(u.ap[:, 15 + L:30 + L], 0.0, [u.t])
                self.dma("gpsimd", u.ap[:, 15:15 + L], S["UB"][m * 128:(m + 1) * 128, tok0:tok0 + L], [self.wtok], [u.t])
                acc = cb.ap[:, m * LM:m * LM + L]
                self.ts(acc, u.ap[:, 0:L], w31[:, m:m + 1], self.prm[f"b_dw31_{l}"][:, m:m + 1], ALU.mult, ALU.add,
                        [u.t, self.prm_t], [cb.t])
                for j in range(1, 31):
                    self.stt(acc, u.ap[:, j:j + L], w31[:, j * MC + m:j * MC + m + 1], acc, ALU.mult, ALU.add,
                             [u.t, cb.t, self.prm_t], [cb.t])
            TT = min(512, L)
            for t0 in range(0, L, TT):
                p1, p2 = self.PS[0 + (t0 // TT) % 2 * 2], self.PS[1 + (t0 // TT) % 2 * 2]
                for m in range(MC):
                    x = cb.ap[:, m * LM + t0:m * LM + t0 + TT]
                    self.mm(p1.ap[:, 0:TT], self.ones.ap, x, m == 0, m == MC - 1, [cb.t, self.ones.t], [p1.t])
                    s = sq[m % 2]
                    self.act(s.ap[:, 0:TT], x, AF.Square, [cb.t], [s.t])
                    self.mm(p2.ap[:, 0:TT], self.ones.ap, s.ap[:, 0:TT], m == 0, m == MC - 1, [s.t, self.ones.t], [p2.t])
                self.ts(mean.ap[:, 0:TT], p1.ap[:, 0:TT], 1.0 / DM, None, ALU.mult, None, [p1.t], [mean.t])
                self.tt(msq.ap[:, 0:TT], mean.ap[:, 0:TT], mean.ap[:, 0:TT], ALU.mult, [mean.t], [msq.t])
                self.stt(msq.ap[:, 0:TT], p2.ap[:, 0:TT], 1.0 / DM, msq.ap[:, 0:TT], ALU.mult, ALU.subtract, [p2.t, msq.t], [msq.t])
                self.act(rstd.ap[:, 0:TT], msq.ap[:, 0:TT], AF.Sqrt, [msq.t], [rstd.t], bias=EPS)
                self.recip(rstd.ap[:, 0:TT], rstd.ap[:, 0:TT], [rstd.t], [rstd.t])
                for m in range(MC):
                    x = cb.ap[:, m * LM + t0:m * LM + t0 + TT]
                    t = t1[m % 2]
                    self.tt(t.ap[:, 0:TT], x, mean.ap[:, 0:TT], ALU.subtract, [cb.t, mean.t], [t.t])
                    self.tt(t.ap[:, 0:TT], t.ap[:, 0:TT], rstd.ap[:, 0:TT], ALU.mult, [t.t, rstd.t], [t.t])
                    y = yo[m % 2]
                    self.act(y.ap[:, 0:TT], t.ap[:, 0:TT], AF.Silu, [t.t, self.prm_t], [y.t],
                             scale=self.prm[f"g_ln_conv_{l}"][:, m:m + 1], bias=self.prm[f"b_ln_conv_{l}"][:, m:m + 1])
                    self.dma("gpsimd", S["Y"][DM + m * 128:DM + (m + 1) * 128, tok0 + t0:tok0 + t0 + TT], y.ap[:, 0:TT],
                             [y.t], [self.wtok])

    def B_sconv_lru(self, l):
        c, S = self.c, self.S
        MC, DM = c.MC, c.DM
        ar = self.ar
        ar.reset()
        LM = c.NS
        xp = Buf(ar.f32(LM + 4))
        gy = Buf(ar.f32(LM))
        gp = [xp, xp]
        gb = [gy, gy]
        yo = [Buf(ar.bf16(LM)) for _ in range(1)] * 2
        xr = Buf(ar.f32(LM))
        xrb = Buf(ar.bf16(LM))
        hs = [Buf(ar.f32(LM)) for _ in range(2)]
        tA = [Buf(ar.f32(512)) for _ in range(8)]
        hl = Buf(ar.f32(128))
        hlt = Buf(ar.f32(128))
        w3 = self.prm[f"w_dw3_{l}"]
        w4 = self.prm[f"w_conv4_{l}"]
        bd0 = l * 4 * MC * 128
        self.memset(hl.ap, 0.0, [hl.t])
        for si, (tok0, L, is_s, b) in enumerate(self.seqs):
            self.p.barrier()
            for m in range(MC):
                self.p.barrier()
                g = gp[m % 2]
                self.memset(g.ap[:, 0:1], 0.0, [g.t])
                self.memset(g.ap[:, L + 1:L + 2], 0.0, [g.t])
                self.dma("gpsimd", g.ap[:, 1:1 + L], S["GCX"][m * 128:(m + 1) * 128, tok0:tok0 + L], [self.wtok], [g.t])
                gg = gb[m % 2]
                self.dma("gpsimd", gg.ap[:, 0:L], S["GB"][m * 128:(m + 1) * 128, tok0:tok0 + L], [self.wtok], [gg.t])
                acc = xr
                self.ts(acc.ap[:, 0:L], g.ap[:, 0:L], w3[:, m:m + 1], None, ALU.mult, None, [g.t, self.prm_t], [acc.t])
                for j in (1, 2):
                    self.stt(acc.ap[:, 0:L], g.ap[:, j:j + L], w3[:, j * MC + m:j * MC + m + 1], acc.ap[:, 0:L], ALU.mult, ALU.add,
                             [g.t, acc.t, self.prm_t], [acc.t])
                y = yo[m % 2]
                self.tt(y.ap[:, 0:L], acc.ap[:, 0:L], gg.ap[:, 0:L], ALU.mult, [acc.t, gg.t], [y.t])
                self.dma("gpsimd", S["Y"][2 * DM + m * 128:2 * DM + (m + 1) * 128, tok0:tok0 + L], y.ap[:, 0:L], [y.t], [self.wtok])
            TT = min(self.c.LRU_TT, L)
            NT = L // TT
            for m in range(MC):
                self.p.barrier()
                self.memset(xp.ap[:, 0:1], 0.0, [xp.t])
                self.memset(xp.ap[:, L + 1:L + 3], 0.0, [xp.t])
                self.dma("gpsimd", xp.ap[:, 1:1 + L], S["XR"][m * 128:(m + 1) * 128, tok0:tok0 + L], [self.wtok], [xp.t])
                self.dma("gpsimd", gy.ap[:, 0:L], S["GY"][m * 128:(m + 1) * 128, tok0:tok0 + L], [self.wtok], [gy.t])
                self.ts(xr.ap[:, 0:L], xp.ap[:, 0:L], w4[:, m:m + 1], self.prm[f"b_conv4_{l}"][:, m:m + 1], ALU.mult, ALU.add,
                        [xp.t, self.prm_t], [xr.t])
                for j in (1, 2, 3):
                    self.stt(xr.ap[:, 0:L], xp.ap[:, j:j + L], w4[:, j * MC + m:j * MC + m + 1], xr.ap[:, 0:L], ALU.mult, ALU.add,
                             [xp.t, xr.t, self.prm_t], [xr.t])
                self.cp(xrb.ap[:, 0:L], xr.ap[:, 0:L], [xr.t], [xrb.t], eng="gpsimd")
                for d in range(2):
                    h = hs[d]
                    order = range(NT) if d == 0 else range(NT - 1, -1, -1)
                    s1 = self.MISC[:, l * (4 + 4 * MC) + 4 + d * MC + m: l * (4 + 4 * MC) + 4 + d * MC + m + 1]
                    s2 = self.MISC[:, l * (4 + 4 * MC) + 4 + 2 * MC + d * MC + m: l * (4 + 4 * MC) + 4 + 2 * MC + d * MC + m + 1]
                    for ti_, tI in enumerate(order):
                        t0 = tI * TT
                        pa, px = self.PS[(ti_ % 2) * 2], self.PS[(ti_ % 2) * 2 + 1]
                        wa = self.BD[:, bd0 + ((0 * 2 + d) * MC + m) * 128: bd0 + ((0 * 2 + d) * MC + m + 1) * 128]
                        wx = self.BD[:, bd0 + ((1 * 2 + d) * MC + m) * 128: bd0 + ((1 * 2 + d) * MC + m + 1) * 128]
                        self.mm(pa.ap[:, 0:TT], wa, xrb.ap[:, t0:t0 + TT], True, True, [self.bd_t, xrb.t], [pa.t])
                        self.mm(px.ap[:, 0:TT], wx, xrb.ap[:, t0:t0 + TT], True, True, [self.bd_t, xrb.t], [px.t])
                        r, ig, a, a2, u = tA[0 + 4 * (ti_ % 2)], tA[1 + 4 * (ti_ % 2)], tA[2 + 4 * (ti_ % 2)], tA[3 + 4 * (ti_ % 2)], None
                        self.act(r.ap[:, 0:TT], pa.ap[:, 0:TT], AF.Sigmoid, [pa.t, self.prm_t], [r.t],
                                 bias=self.prm[f"b_rg_a_{l}"][:, d * MC + m:d * MC + m + 1])
                        self.act(ig.ap[:, 0:TT], px.ap[:, 0:TT], AF.Sigmoid, [px.t, self.prm_t], [ig.t],
                                 bias=self.prm[f"b_rg_x_{l}"][:, d * MC + m:d * MC + m + 1])
                        self.act(a.ap[:, 0:TT], r.ap[:, 0:TT], AF.Exp, [r.t, self.misc_t], [a.t], scale=s1)
                        self.act(a2.ap[:, 0:TT], r.ap[:, 0:TT], AF.Exp, [r.t, self.misc_t], [a2.t], scale=s2)
                        self.act(a2.ap[:, 0:TT], a2.ap[:, 0:TT], AF.Sqrt, [a2.t], [a2.t], scale=-1.0, bias=1.0)
                        self.tt(ig.ap[:, 0:TT], ig.ap[:, 0:TT], xr.ap[:, t0:t0 + TT], ALU.mult, [ig.t, xr.t], [ig.t])
                        self.tt(ig.ap[:, 0:TT], ig.ap[:, 0:TT], a2.ap[:, 0:TT], ALU.mult, [ig.t, a2.t], [ig.t])
                        if ti_ == 0:
                            init = self.prm[f"st_{l}"][:, d * MC + m:d * MC + m + 1] if is_s else 0.0
                        else:
                            pt0 = order[ti_ - 1] * TT
                            init = h.ap[:, pt0 + TT - 1:pt0 + TT] if d == 0 else h.ap[:, pt0:pt0 + 1]
                        if d == 0:
                            self.p.op("vector", lambda e, h=h, a=a, ig=ig, init=init, t0=t0: e.tensor_tensor_scan(
                                out=h.ap[:, t0:t0 + TT], data0=a.ap[:, 0:TT], data1=ig.ap[:, 0:TT], initial=init,
                                op0=ALU.mult, op1=ALU.add), [a.t, ig.t, h.t, self.prm_t], [h.t])
                        else:
                            self.p.op("vector", lambda e, h=h, a=a, ig=ig, init=init, t0=t0: e.tensor_tensor_scan(
                                out=h.ap[:, t0:t0 + TT][:, ::-1], data0=a.ap[:, 0:TT][:, ::-1], data1=ig.ap[:, 0:TT][:, ::-1],
                                initial=init, op0=ALU.mult, op1=ALU.add), [a.t, ig.t, h.t, self.prm_t], [h.t])
                    if not is_s:
                        col = (b * 2 + d) * MC + m
                        src = h.ap[:, L - 1:L] if d == 0 else h.ap[:, 0:1]
                        self.cp(hl.ap[:, col:col + 1], src, [h.t], [hl.t], eng="gpsimd")
                y = yo[m % 2]
                self.tt(hs[0].ap[:, 0:L], hs[0].ap[:, 0:L], hs[1].ap[:, 0:L], ALU.add, [hs[0].t, hs[1].t], [hs[0].t])
                self.tt(y.ap[:, 0:L], hs[0].ap[:, 0:L], gy.ap[:, 0:L], ALU.mult, [hs[0].t, gy.t], [y.t])
                self.dma("gpsimd", S["Y"][3 * DM + m * 128:3 * DM + (m + 1) * 128, tok0:tok0 + L], y.ap[:, 0:L], [y.t], [self.wtok])
        ncol = c.NPB * 2 * MC
        ps = self.PS[4]
        self.tr(ps.ap[0:ncol, 0:128], hl.ap[:, 0:ncol], self.ident.ap, [hl.t, self.ident.t], [ps.t])
        self.cp(hlt.ap[0:ncol, :], ps.ap[0:ncol, 0:128], [ps.t], [hlt.t])
        for b in range(c.NPB):
            self.dma("gpsimd", self.O["nst"][b, l, :].rearrange("(r c) -> r c", c=128),
                     hlt.ap[b * 2 * MC:(b + 1) * 2 * MC, :], [hlt.t], [self.wtok])

    def B_attn(self, l):
        c, S, I = self.c, self.S, self.I
        MC, DM, PC, PAST = c.MC, c.DM, c.PC, c.PAST
        HA = MC
        ar = self.ar
        ar.reset()
        MMAX = PAST + c.NS
        kT = Buf(ar.bf16(HA * MMAX))
        Va = Buf(ar.bf16((MMAX // 128) * DM))
        cst = [Buf(ar.f32(PC * DM)) for _ in range(2)]
        qp = [Buf(ar.bf16(2 * 512)) for _ in range(2)]
        pt = [Buf(ar.bf16(512)) for _ in range(4)]
        o0 = Buf(ar.f32(512))
        o1 = Buf(ar.f32(512))
        rl = [Buf(ar.f32(512)) for _ in range(2)]
        sqb = Buf(ar.f32(512))
        yb = [Buf(ar.bf16(512)) for _ in range(2)]
        for q in qp:
            self.memset(q.ap, 0.0, [q.t])
        neglam = self.misc(l, 0)
        gsub = self.misc(l, 1)
        psS = [self.PS[0], self.PS[1]]
        psO = [self.PS[2], self.PS[3]]
        psL = [self.PS[4], self.PS[5]]
        psX = self.PS[6]
        qi = 0
        for si, (tok0, L, is_s, b) in enumerate(self.seqs):
            self.p.barrier()
            P0 = PAST if is_s else 0
            M = P0 + L
            NKC = M // 128
            kv = kT.ap[:, 0:HA * M].rearrange("p (h m) -> p h m", m=M)
            vv = Va.ap[:, 0:NKC * DM].rearrange("p (k e) -> p k e", e=DM)
            if is_s:
                ks, vs = cst
                self.dma("gpsimd", ks.ap.rearrange("p (j e) -> p j e", e=DM), I["ck"][l].rearrange("(j p) e -> p j e", p=128), [], [ks.t])
                self.dma("gpsimd", vs.ap.rearrange("p (j e) -> p j e", e=DM), I["cv"][l].rearrange("(j p) e -> p j e", p=128), [], [vs.t])
                self.cp(vv[:, 0:PC, :], vs.ap.rearrange("p (j e) -> p j e", e=DM), [vs.t], [Va.t])
                n = 0
                for j in range(PC):
                    for h in range(HA):
                        ps = self.PS[6 + n % 2]
                        n += 1
                        self.tr(ps.ap[:, 0:128], ks.ap[:, j * DM + h * 128:j * DM + (h + 1) * 128], self.ident.ap,
                                [ks.t, self.ident.t], [ps.t])
                        self.cp(kv[:, h, j * 128:(j + 1) * 128], ps.ap[:, 0:128], [ps.t], [kT.t], eng=("scalar" if n % 2 else "vector"))
            self.dma("gpsimd", kv[:, :, P0:P0 + L], S["KT"][:, tok0:tok0 + L].rearrange("(h p) t -> p h t", p=128), [self.wtok], [kT.t])
            self.dma("gpsimd", vv[:, P0 // 128:NKC, :], S["V"][tok0:tok0 + L, :].rearrange("(j p) e -> p j e", p=128), [self.wtok], [Va.t])
            QB = min(512, L)
            for h in range(HA):
                for qb in range(L // QB):
                    self.p.barrier()
                    q = qp[qi % 2]
                    qi += 1
                    qv = q.ap.rearrange("p (j t) -> p j t", t=512)
                    c0 = tok0 + qb * QB
                    self.dma("gpsimd", qv[0:64, 0, 0:QB], S["Q"][h * 128:h * 128 + 64, c0:c0 + QB], [self.wtok], [q.t])
                    self.dma("gpsimd", qv[64:128, 1, 0:QB], S["Q"][h * 128 + 64:h * 128 + 128, c0:c0 + QB], [self.wtok], [q.t])
                    steps = [(kc, j) for kc in range(NKC) for j in range(2)]

                    def emitS(i):
                        kc, j = steps[i]
                        ps = psS[i % 2]
                        self.mm(ps.ap[:, 0:QB], kv[:, h, kc * 128:(kc + 1) * 128], qv[:, j, 0:QB], True, True, [kT.t, q.t], [ps.t])

                    emitS(0)
                    for i, (kc, j) in enumerate(steps):
                        if i + 1 < len(steps):
                            emitS(i + 1)
                        ps = psS[i % 2]
                        pb = pt[i % 4]
                        self.act(pb.ap[:, 0:QB], ps.ap[:, 0:QB], AF.Exp, [ps.t], [pb.t], scale=0.125)
                        self.mm(psO[j].ap[:, 0:QB], vv[:, kc, h * 128:(h + 1) * 128], pb.ap[:, 0:QB], kc == 0, kc == NKC - 1,
                                [Va.t, pb.t], [psO[j].t])
                        self.mm(psL[j].ap[:, 0:QB], self.onesb.ap, pb.ap[:, 0:QB], kc == 0, kc == NKC - 1,
                                [self.onesb.t, pb.t], [psL[j].t])
                    for j in range(2):
                        self.recip(rl[j].ap[:, 0:QB], psL[j].ap[:, 0:QB], [psL[j].t], [rl[j].t])
                    self.tt(o0.ap[:, 0:QB], psO[0].ap[:, 0:QB], rl[0].ap[:, 0:QB], ALU.mult, [psO[0].t, rl[0].t], [o0.t])
                    self.tt(o1.ap[:, 0:QB], psO[1].ap[:, 0:QB], rl[1].ap[:, 0:QB], ALU.mult, [psO[1].t, rl[1].t], [o1.t])
                    self.stt(o0.ap[:, 0:QB], o1.ap[:, 0:QB], neglam, o0.ap[:, 0:QB], ALU.mult, ALU.add, [o0.t, o1.t, self.misc_t], [o0.t])
                    self.act(sqb.ap[:, 0:QB], o0.ap[:, 0:QB], AF.Square, [o0.t], [sqb.t])
                    self.mm(psX.ap[:, 0:QB], self.ones.ap, sqb.ap[:, 0:QB], True, True, [self.ones.t, sqb.t], [psX.t])
                    self.act(sqb.ap[:, 0:QB], psX.ap[:, 0:QB], AF.Sqrt, [psX.t], [sqb.t], scale=1.0 / 128, bias=EPS)
                    self.recip(sqb.ap[:, 0:QB], sqb.ap[:, 0:QB], [sqb.t], [sqb.t])
                    y = yb[qi % 2]
                    self.stt(y.ap[:, 0:QB], o0.ap[:, 0:QB], gsub, sqb.ap[:, 0:QB], ALU.mult, ALU.mult, [o0.t, sqb.t, self.misc_t], [y.t])
                    self.dma("gpsimd", S["Y"][h * 128:(h + 1) * 128, c0:c0 + QB], y.ap[:, 0:QB], [y.t], [self.wtok])

    def phase_C(self, l):
        c, S = self.c, self.S
        KC, MC, DM = c.KC, c.MC, c.DM
        ar = self.ar
        ar.reset()
        TM = 512
        x = Buf(ar.f32(KC * TM))
        h = Buf(ar.bf16(KC * TM))
        yt = Buf(ar.bf16(4 * MC * TM))
        mg = Buf(ar.bf16(KC * TM))
        h2 = Buf(ar.bf16(KC * TM))
        g = [Buf(ar.f32(TM)) for _ in range(3)]
        tt_ = [Buf(ar.f32(TM)) for _ in range(2)]
        acc = [Buf(ar.f32(TM)) for _ in range(2)]
        sq = [Buf(ar.f32(TM)) for _ in range(2)]
        sd = Buf(ar.f32(TM))
        wbr = Buf(ar.bf16(4 * MC * 128))
        W, WB, WO = S[f"WIN{l}"], S[f"WBR{l}"], S[f"WOUT{l}"]
        psG = [self.PS[i] for i in (0, 1, 2)]
        psT = [self.PS[i] for i in (3, 4)]
        psM = [self.PS[i] for i in (5, 6)]
        gi = 0
        for (tok0, T, v, seq, first, last) in self.tiles:
            self.p.barrier()
            kt = lambda a: a.rearrange("p (k t) -> p k t", t=T)
            self.dma("gpsimd", kt(x.ap[:, 0:KC * T]), S["XT"][:, tok0:tok0 + T].rearrange("(k p) t -> p k t", p=128), [self.wtok], [x.t])
            self.dma("gpsimd", kt(h.ap[:, 0:KC * T]), S["HT"][:, tok0:tok0 + T].rearrange("(k p) t -> p k t", p=128), [self.wtok], [h.t])
            self.dma("gpsimd", kt(yt.ap[:, 0:4 * MC * T]), S["Y"][:, tok0:tok0 + T].rearrange("(k p) t -> p k t", p=128), [self.wtok], [yt.t])
            for n in range(KC):
                ub = wbr
                self.dma("sync", ub.ap, WB[n], [], [ub.t])
                a = acc[n % 2]
                for j in range(4):
                    ug = self.wload(W[10 * MC + j * KC + n])
                    pg = psG[gi % 3]
                    gb = g[gi % 3]
                    gi += 1
                    for k in range(KC):
                        self.mm(pg.ap[:, 0:T], ug.ap[:, k * 128:(k + 1) * 128], h.ap[:, k * T:(k + 1) * T], k == 0, k == KC - 1,
                                [ug.t, h.t], [pg.t])
                    self.act(gb.ap[:, 0:T], pg.ap[:, 0:T], AF.Sigmoid, [pg.t], [gb.t])
                    pt_ = psT[j % 2]
                    for m in range(MC):
                        self.mm(pt_.ap[:, 0:T], ub.ap[:, (j * MC + m) * 128:(j * MC + m + 1) * 128],
                                yt.ap[:, (j * MC + m) * T:(j * MC + m + 1) * T], m == 0, m == MC - 1, [ub.t, yt.t], [pt_.t])
                    if j == 0:
                        self.tt(a.ap[:, 0:T], pt_.ap[:, 0:T], gb.ap[:, 0:T], ALU.mult, [pt_.t, gb.t], [a.t])
                    else:
                        t = tt_[j % 2]
                        self.tt(t.ap[:, 0:T], pt_.ap[:, 0:T], gb.ap[:, 0:T], ALU.mult, [pt_.t, gb.t], [t.t])
                        if j < 3:
                            self.tt(a.ap[:, 0:T], a.ap[:, 0:T], t.ap[:, 0:T], ALU.add, [a.t, t.t], [a.t], eng="gpsimd")
                        else:
                            self.tt(mg.ap[:, n * T:(n + 1) * T], a.ap[:, 0:T], t.ap[:, 0:T], ALU.add, [a.t, t.t], [mg.t], eng="gpsimd")
            for n in range(KC):
                uo = self.wload(WO[n])
                pm_ = psM[n % 2]
                for k in range(KC):
                    self.mm(pm_.ap[:, 0:T], uo.ap[:, k * 128:(k + 1) * 128], mg.ap[:, k * T:(k + 1) * T], k == 0, k == KC - 1,
                            [uo.t, mg.t], [pm_.t])
                self.stt(x.ap[:, n * T:(n + 1) * T], pm_.ap[:, 0:T], self.modv(l, v, 2)[:, n:n + 1], x.ap[:, n * T:(n + 1) * T],
                         ALU.mult, ALU.add, [pm_.t, x.t, self.mod_t], [x.t])
            self.dma("gpsimd", S["XT"][:, tok0:tok0 + T].rearrange("(k p) t -> p k t", p=128), kt(x.ap[:, 0:KC * T]), [x.t], [self.wtok])
            self.norm_mod(x, T, None, None, None, self.PS[7], sq, sd, tt_)
            self.apply_mod(x, T, sd, self.abv(l, v, 1), self.modv(l, v, 3), lambda k: h2.ap[:, k * T:(k + 1) * T], h2.t, tt_,
                           [self.mod_t])
            self.dma("gpsimd", S["H2"][:, tok0:tok0 + T].rearrange("(k p) t -> p k t", p=128), kt(h2.ap[:, 0:KC * T]), [h2.t], [self.wtok])

    def phase_D(self, l):
        c, S = self.c, self.S
        KC, FC = c.KC, c.FC
        ar = self.ar
        ar.reset()
        TM = 512
        TH = TM + 2
        x = Buf(ar.f32(KC * TM))
        h2 = Buf(ar.bf16(KC * TH))
        gbuf = Buf(ar.bf16(FC * TM))
        ab = [Buf(ar.f32(TH)) for _ in range(2)]
        ac = [Buf(ar.f32(TM)) for _ in range(2)]
        sl = [Buf(ar.f32(TM)) for _ in range(2)]
        sq = [Buf(ar.f32(TM)) for _ in range(2)]
        tmp = sq
        sd = Buf(ar.f32(TM))
        last_layer = (l == c.DEPTH - 1)
        if last_layer:
            yf = [Buf(ar.f32(128)) for _ in range(2)]
            ost = [Buf(ar.f32(c.D)) for _ in range(1)] * 2
        WU, WD = S[f"WUP{l}"], S[f"WDN{l}"]
        psA = [self.PS[0], self.PS[1]]
        psU = [self.PS[2], self.PS[3]]
        psH = self.PS[4]
        psD = [self.PS[5], self.PS[6]]
        fw = [self.prm[f"fcw{t}_{l}"] for t in range(3)]
        fb = self.prm[f"fcb_{l}"]
        oi = 0
        for (tok0, T, v, seq, first, last) in self.tiles:
            self.p.barrier()
            TT = T + 2
            hv = h2.ap[:, 0:KC * TT].rearrange("p (k t) -> p k t", t=TT)
            kt = lambda a: a.rearrange("p (k t) -> p k t", t=T)
            self.dma("gpsimd", kt(x.ap[:, 0:KC * T]), S["XT"][:, tok0:tok0 + T].rearrange("(k p) t -> p k t", p=128), [self.wtok], [x.t])
            lo = tok0 - (0 if first else 1)
            hi = tok0 + T + (0 if last else 1)
            if first:
                self.memset(hv[:, :, 0:1], 0.0, [h2.t])
            if last:
                self.memset(hv[:, :, TT - 1:TT], 0.0, [h2.t])
            self.dma("gpsimd", hv[:, :, (1 if first else 0):(TT - 1 if last else TT)],
                     S["H2"][:, lo:hi].rearrange("(k p) t -> p k t", p=128), [self.wtok], [h2.t])
            for i in range(FC):
                u = self.wload(WU[i])
                pa, pu = psA[i % 2], psU[i % 2]
                for k in range(KC):
                    self.mm(pa.ap[:, 0:T], u.ap[:, k * 128:(k + 1) * 128], hv[:, k, 1:1 + T], k == 0, k == KC - 1, [u.t, h2.t], [pa.t])
                for k in range(KC):
                    self.mm(psH.ap[:, 2 * i:2 * i + 2], u.ap[:, k * 128:(k + 1) * 128], hv[:, k, 0:TT:TT - 1], k == 0, k == KC - 1,
                            [u.t, h2.t], [psH.t])
                for k in range(KC):
                    self.mm(pu.ap[:, 0:T], u.ap[:, (KC + k) * 128:(KC + k + 1) * 128], hv[:, k, 1:1 + T], k == 0, k == KC - 1,
                            [u.t, h2.t], [pu.t])
                a = ab[i % 2]
                self.cp(a.ap[:, 1:1 + T], pa.ap[:, 0:T], [pa.t], [a.t], eng="scalar")
                self.cp(a.ap[:, 0:TT:TT - 1], psH.ap[:, 2 * i:2 * i + 2], [psH.t], [a.t], eng="vector")
                cc = ac[i % 2]
                self.ts(cc.ap[:, 0:T], a.ap[:, 0:T], fw[0][:, i:i + 1], fb[:, i:i + 1], ALU.mult, ALU.add, [a.t, self.prm_t], [cc.t])
                self.stt(cc.ap[:, 0:T], a.ap[:, 1:1 + T], fw[1][:, i:i + 1], cc.ap[:, 0:T], ALU.mult, ALU.add, [a.t, cc.t, self.prm_t], [cc.t])
                self.stt(cc.ap[:, 0:T], a.ap[:, 2:2 + T], fw[2][:, i:i + 1], cc.ap[:, 0:T], ALU.mult, ALU.add, [a.t, cc.t, self.prm_t], [cc.t])
                s = sl[i % 2]
                self.act(s.ap[:, 0:T], cc.ap[:, 0:T], AF.Silu, [cc.t], [s.t])
                self.tt(gbuf.ap[:, i * T:(i + 1) * T], pu.ap[:, 0:T], s.ap[:, 0:T], ALU.mult, [pu.t, s.t], [gbuf.t])
            for n in range(KC):
                u = self.wload(WD[n])
                pd = psD[n % 2]
                for f in range(FC):
                    self.mm(pd.ap[:, 0:T], u.ap[:, f * 128:(f + 1) * 128], gbuf.ap[:, f * T:(f + 1) * T], f == 0, f == FC - 1,
                            [u.t, gbuf.t], [pd.t])
                self.stt(x.ap[:, n * T:(n + 1) * T], pd.ap[:, 0:T], self.modv(l, v, 5)[:, n:n + 1], x.ap[:, n * T:(n + 1) * T],
                         ALU.mult, ALU.add, [pd.t, x.t, self.mod_t], [x.t])
            if not last_layer:
                self.dma("gpsimd", S["XT"][:, tok0:tok0 + T].rearrange("(k p) t -> p k t", p=128), kt(x.ap[:, 0:KC * T]), [x.t], [self.wtok])
            else:
                self.norm_mod(x, T, None, None, None, self.PS[7], sq, sd, tmp)
                dst = self.O["ys"] if v == 0 else self.O["yp"]
                r0 = tok0 if v == 0 else tok0 - c.NS
                gF = self.prm["gfinal"]
                for tb in range(T // 128):
                    o = ost[oi % 2]
                    oi += 1
                    for k0 in range(0, KC, 4):
                        ps = psA[(k0 // 4) % 2] if (k0 // 4) % 4 < 2 else psU[(k0 // 4) % 2]
                        for kk in range(4):
                            k = k0 + kk
                            y = yf[k % 2]
                            self.stt(y.ap[:, 0:128], x.ap[:, k * T + tb * 128:k * T + (tb + 1) * 128], gF[:, k:k + 1],
                                     sd.ap[:, tb * 128:(tb + 1) * 128], ALU.mult, ALU.mult, [x.t, sd.t, self.prm_t], [y.t])
                            self.tr(ps.ap[:, kk * 128:(kk + 1) * 128], y.ap[:, 0:128], self.ident.ap, [y.t, self.ident.t], [ps.t])
                        self.cp(o.ap[:, k0 * 128:(k0 + 4) * 128], ps.ap[:, 0:512], [ps.t], [o.t], eng=("scalar" if (k0 // 4) % 2 else "vector"))
                    self.dma("gpsimd", dst[r0 + tb * 128:r0 + (tb + 1) * 128, :], o.ap, [o.t], [self.wtok])

    def build(self, stop=None):
        c = self.c
        self.wtok = None
        for nm, fn in (("consts", self.setup_consts), ("params", self.load_params), ("convert", self.convert_weights),
                       ("mod", self.compute_mod), ("misc", self.setup_misc), ("rope", self.setup_rope)):
            fn()
            self.p.barrier()
            if stop == nm:
                self.p.finish()
                return self.nc
        for l in range(c.DEPTH):
            done = False
            for ph, fn in (("A", self.phase_A), ("B", self.phase_B), ("C", self.phase_C), ("D", self.phase_D)):
                fn(l)
                self.p.barrier()
                if stop == f"{ph}{l}":
                    done = True
                    break
            if done:
                break
        self.p.finish()
        return self.nc


def make_in_maps(cfg, inputs, n_cores):
    c = cfg
    maps = []
    wn = list(WSHAPES(c).keys())
    for i in range(n_cores):
        m = {}
        m["xs"] = np.ascontiguousarray(inputs["x_sample"][i])
        m["xp"] = np.ascontiguousarray(inputs["x_prompt"][i * c.NPB:(i + 1) * c.NPB]).reshape(c.NPB * c.SP, c.D)
        m["ck"] = np.ascontiguousarray(inputs["cache_k"][i]).reshape(c.DEPTH, c.PAST, c.DM)
        m["cv"] = np.ascontiguousarray(inputs["cache_v"][i]).reshape(c.DEPTH, c.PAST, c.DM)
        m["st"] = np.ascontiguousarray(inputs["state_lru"][i]).reshape(c.DEPTH, 2 * c.DM)
        m["cvec"] = np.concatenate([np.asarray(inputs["c"][i]).reshape(-1), np.asarray(inputs["c_ctx"]).reshape(-1)])
        for n in wn:
            m[n] = np.ascontiguousarray(inputs[n])
        maps.append(m)
    return maps


def gather_outputs(cfg, results, n_cores):
    c = cfg
    B = n_cores * c.NPB
    ys = np.stack([results[i]["ys"] for i in range(n_cores)], 0)
    yp = np.concatenate([results[i]["yp"].reshape(c.NPB, c.SP, c.D) for i in range(n_cores)], 0)
    nk = np.concatenate([results[i]["nk"] for i in range(n_cores)], 0).reshape(B, c.DEPTH, c.SP, c.DM // 128, 2, 64)
    nv = np.concatenate([results[i]["nv"] for i in range(n_cores)], 0).reshape(B, c.DEPTH, c.SP, c.DM // 128, 128)
    ns = np.concatenate([results[i]["nst"] for i in range(n_cores)], 0).reshape(B, c.DEPTH, 2, c.DM)
    return (yp.astype(np.float32), ys.astype(np.float32), nk.astype(np.float32), nv.astype(np.float32), ns.astype(np.float32))


def kernel(**inputs):
    inputs = {k: np.asarray(v, dtype=np.float32) for k, v in inputs.items()}
    cfg = Cfg()
    n = 8
    nc = K(cfg).build()
    maps = make_in_maps(cfg, inputs, n)
    res = run_bass_kernel_spmd(nc, maps, core_ids=list(range(n)))
    return gather_outputs(cfg, res.results, n)
```

```python
import contextlib
import math
import numpy as np
import concourse.bass as bass
import concourse.mybir as mybir
from concourse.bass_utils import run_bass_kernel_spmd

F32 = mybir.dt.float32
BF16 = mybir.dt.bfloat16
I32 = mybir.dt.int32
AF = mybir.ActivationFunctionType
ALU = mybir.AluOpType

COMPUTE = ("tensor", "vector", "scalar", "gpsimd")
QUEUES = ("sync", "scalar", "gpsimd")
ALL = ("sync", "tensor", "vector", "scalar", "gpsimd")
EPS = 1e-6


class Tok:
    __slots__ = ("w", "r")

    def __init__(self):
        self.w = None
        self.r = {}


class Prog:
    def __init__(self, nc, n_dma_sems=6):
        self.nc = nc
        self.es = contextlib.ExitStack()
        self.lists = {e: [] for e in ALL}
        self.sems = {}
        self.cnt = {}
        for e in COMPUTE:
            self.sems["c_" + e] = self.es.enter_context(nc.semaphore("c_" + e))
            self.cnt["c_" + e] = 0
        self.dpool = {}
        self.dnext = {}
        for q in QUEUES:
            keys = []
            for i in range(n_dma_sems if q != "scalar" else 3):
                k = f"d_{q}{i}"
                self.sems[k] = self.es.enter_context(nc.semaphore(k))
                self.cnt[k] = 0
                keys.append(k)
            self.dpool[q] = keys
            self.dnext[q] = 0
        self.waited = {e: {} for e in ALL}

    def sbuf(self, name, shape, dtype):
        return self.es.enter_context(self.nc.sbuf_tensor(name, list(shape), dtype))

    def psum(self, name, shape, dtype=F32):
        return self.es.enter_context(self.nc.psum_tensor(name, list(shape), dtype))

    def _collect(self, eng, reads, writes, extra=()):
        need = {}

        def add(ev):
            if ev is None:
                return
            k, v = ev
            if need.get(k, 0) < v:
                need[k] = v

        for t in reads:
            add(t.w)
        for t in writes:
            if t.r:
                for k, v in t.r.items():
                    add((k, v))
            else:
                add(t.w)
        for ev in extra:
            add(ev)
        out = []
        wd = self.waited[eng]
        own = "c_" + eng
        for k, v in need.items():
            if k == own and v > self.cnt.get(own, 0):
                continue
            if k == own and eng == "tensor":
                continue
            if wd.get(k, 0) < v:
                wd[k] = v
                out.append((self.sems[k], v))
        return out

    def _commit(self, ev, reads, writes):
        k, v = ev
        for t in reads:
            if t.r.get(k, 0) < v:
                t.r[k] = v
        for t in writes:
            t.w = ev
            t.r = {}

    def op(self, eng, fn, reads=(), writes=(), inc=True):
        waits = self._collect(eng, reads, writes)
        key = "c_" + eng
        sem = self.sems[key]
        if inc:
            self.cnt[key] += 1
            ev = (key, self.cnt[key])
        else:
            ev = None

        def emit(e, waits=waits, fn=fn, inc=inc, sem=sem):
            for s, v in waits:
                e.wait_ge(s, v)
            ins = fn(e)
            if inc:
                ins.then_inc(sem, 1)

        self.lists[eng].append(emit)
        if inc:
            self._commit(ev, reads, writes)
        else:
            nxt = (key, self.cnt[key] + 1)
            self._commit(nxt, reads, ())
            for t in writes:
                t.w = nxt
                t.r = {}
        return ev

    def dma(self, q, fn, reads=(), writes=()):
        pool = self.dpool[q]
        k = pool[self.dnext[q] % len(pool)]
        self.dnext[q] += 1
        prev = (k, self.cnt[k]) if self.cnt[k] > 0 else None
        waits = self._collect(q, reads, writes, extra=(prev,) if prev else ())
        self.cnt[k] += 16
        ev = (k, self.cnt[k])
        sem = self.sems[k]

        def emit(e, waits=waits, fn=fn, sem=sem):
            for s, v in waits:
                e.wait_ge(s, v)
            fn(e).then_inc(sem, 16)

        self.lists[q].append(emit)
        self._commit(ev, reads, writes)
        return ev

    def barrier(self):
        allev = [(k, v) for k, v in self.cnt.items() if v > 0]
        for eng in ALL:
            waits = []
            wd = self.waited[eng]
            for k, v in allev:
                if wd.get(k, 0) < v:
                    wd[k] = v
                    waits.append((self.sems[k], v))
            if waits:
                def emit(e, waits=waits):
                    for s, v in waits:
                        e.wait_ge(s, v)
                self.lists[eng].append(emit)

    def finish(self):
        self.barrier()
        lists = self.lists
        with self.nc.Block() as block:
            @block.sync
            def _(e):
                for f in lists["sync"]:
                    f(e)

            @block.tensor
            def _(e):
                for f in lists["tensor"]:
                    f(e)

            @block.vector
            def _(e):
                for f in lists["vector"]:
                    f(e)

            @block.scalar
            def _(e):
                for f in lists["scalar"]:
                    f(e)

            @block.gpsimd
            def _(e):
                for f in lists["gpsimd"]:
                    f(e)
        self.es.close()


class Buf:
    __slots__ = ("ap", "t")

    def __init__(self, ap):
        self.ap = ap
        self.t = Tok()


class Arena:
    def __init__(self, ap):
        self.ap = ap
        self.W = ap.shape[1]
        self.off = 0

    def reset(self):
        self.off = 0

    def f32(self, n):
        n2 = (n + 1) // 2 * 2
        a = self.ap[:, self.off:self.off + n]
        self.off += n2
        assert self.off <= self.W, ("arena overflow", self.off, self.W)
        return a

    def bf16(self, n):
        w = (n + 3) // 4 * 2
        a = self.ap[:, self.off:self.off + w].bitcast(BF16)[:, 0:n]
        self.off += w
        assert self.off <= self.W, ("arena overflow", self.off, self.W)
        return a

    def i32(self, n):
        return self.f32(n).bitcast(I32)


class Cfg:
    def __init__(self, D=2048, NS=4096, SP=256, NPB=4, PAST=512, DEPTH=2, GW=64, ARENA=31500, NWB=3):
        self.D = D
        self.KC = D // 128
        self.DM = D // 4
        self.MC = self.DM // 128
        self.DFF = ((8 * D // 3 + 127) // 128) * 128
        self.FC = self.DFF // 128
        self.NS, self.SP, self.NPB, self.PAST, self.DEPTH, self.GW = NS, SP, NPB, PAST, DEPTH, GW
        self.PC = PAST // 128
        self.NTOK = NS + NPB * SP
        self.NIN = 10 * self.DM + 4 * D
        self.NINC = self.NIN // 128
        self.TS = min(512, NS)
        self.ARENA = ARENA
        self.NWB = NWB
        self.LRU_TT = 256
        self.UMAX = max(self.KC * 128 * 2, self.FC * 128, 4 * self.MC * 128)
        assert self.MC >= 1 and NS % self.TS == 0 and SP % 128 == 0 and SP <= 512 and PAST % 128 == 0


WSHAPES = lambda c: dict(
    w_mod=(c.DEPTH, c.D, 6 * c.D), b_mod=(c.DEPTH, 6 * c.D), g_norm1=(c.DEPTH, c.D), g_norm2=(c.DEPTH, c.D),
    g_final=(c.D,), w_in=(c.DEPTH, c.D, c.NIN), lam_q1=(c.DEPTH, 64), lam_k1=(c.DEPTH, 64), lam_q2=(c.DEPTH, 64),
    lam_k2=(c.DEPTH, 64), g_subln=(c.DEPTH, 128), w_dw31=(c.DEPTH, 31, c.DM), b_dw31=(c.DEPTH, c.DM),
    g_ln_conv=(c.DEPTH, c.DM), b_ln_conv=(c.DEPTH, c.DM), w_dw3=(c.DEPTH, 3, c.DM), w_conv4=(c.DEPTH, 4, c.DM),
    b_conv4=(c.DEPTH, c.DM), w_rg_a=(c.DEPTH, 2, c.DM // 64, 64, 64), b_rg_a=(c.DEPTH, 2, c.DM),
    w_rg_x=(c.DEPTH, 2, c.DM // 64, 64, 64), b_rg_x=(c.DEPTH, 2, c.DM), lru_lambda=(c.DEPTH, 2, c.DM),
    w_branch=(c.DEPTH, 4, c.DM, c.D), w_out=(c.DEPTH, c.D, c.D), w_ffn_up=(c.DEPTH, c.D, 2 * c.DFF),
    w_ffn_conv=(c.DEPTH, 3, c.DFF), b_ffn_conv=(c.DEPTH, c.DFF), w_ffn_down=(c.DEPTH, c.DFF, c.D))


def rows128(ap):
    nd = len(ap.shape)
    if nd == 1:
        return ap.rearrange("(r c) -> r c", c=128)
    if nd == 2:
        return ap.rearrange("a (r c) -> (a r) c", c=128)
    raise ValueError


class K:
    def __init__(self, cfg):
        c = self.c = cfg
        nc = self.nc = bass.Bass("TRN2", target_bir_lowering=False)
        self.p = Prog(nc)
        din = lambda n, s: nc.dram_tensor(n, list(s), F32, kind="ExternalInput").ap()
        dout = lambda n, s: nc.dram_tensor(n, list(s), F32, kind="ExternalOutput").ap()
        dscr = lambda n, s, dt: nc.dram_tensor(n, list(s), dt, kind="Internal").ap()
        self.I = dict(xs=din("xs", (c.NS, c.D)), xp=din("xp", (c.NPB * c.SP, c.D)),
                      ck=din("ck", (c.DEPTH, c.PAST, c.DM)), cv=din("cv", (c.DEPTH, c.PAST, c.DM)),
                      st=din("st", (c.DEPTH, 2 * c.DM)), cvec=din("cvec", (2 * c.D,)))
        for n, s in WSHAPES(c).items():
            self.I[n] = din(n, s)
        self.O = dict(ys=dout("ys", (c.NS, c.D)), yp=dout("yp", (c.NPB * c.SP, c.D)),
                      nk=dout("nk", (c.NPB, c.DEPTH, c.SP, c.DM)), nv=dout("nv", (c.NPB, c.DEPTH, c.SP, c.DM)),
                      nst=dout("nst", (c.NPB, c.DEPTH, 2 * c.DM)))
        S = self.S = {}
        S["XT"] = dscr("s_xt", (c.D, c.NTOK), F32)
        S["HT"] = dscr("s_ht", (c.D, c.NTOK), BF16)
        S["H2"] = dscr("s_h2", (c.D, c.NTOK), BF16)
        S["Q"] = dscr("s_q", (c.DM, c.NTOK), BF16)
        S["KT"] = dscr("s_k", (c.DM, c.NTOK), BF16)
        S["V"] = dscr("s_v", (c.NTOK, c.DM), BF16)
        for n in ("UB", "GCX", "GB", "XR", "GY"):
            S[n] = dscr("s_" + n, (c.DM, c.NTOK), F32)
        S["Y"] = dscr("s_y", (4 * c.DM, c.NTOK), BF16)
        S["RC"] = dscr("s_rc", (128, c.NS), F32)
        S["RS"] = dscr("s_rs", (128, c.NS), F32)
        for l in range(c.DEPTH):
            S[f"WIN{l}"] = dscr(f"w_in_b{l}", (c.NINC, 128, c.KC * 128), BF16)
            S[f"WBR{l}"] = dscr(f"w_br_b{l}", (c.KC, 128, 4 * c.MC * 128), BF16)
            S[f"WOUT{l}"] = dscr(f"w_out_b{l}", (c.KC, 128, c.KC * 128), BF16)
            S[f"WUP{l}"] = dscr(f"w_up_b{l}", (c.FC, 128, 2 * c.KC * 128), BF16)
            S[f"WDN{l}"] = dscr(f"w_dn_b{l}", (c.KC, 128, c.FC * 128), BF16)
        p = self.p
        self.ident = Buf(p.sbuf("ident", (128, 128), F32)[:])
        self.ones = Buf(p.sbuf("ones", (128, 128), F32)[:])
        self.onesb = Buf(p.sbuf("onesb", (128, 128), BF16)[:])
        self.pm = Buf(p.sbuf("pm", (128, 128), F32)[:])
        self.wball = p.sbuf("wball", (128, c.NWB * c.UMAX), BF16)[:]
        self.set_wbufs(c.UMAX)
        self.PS = [Buf(p.psum(f"ps{i}", (128, 512))[:]) for i in range(8)]
        self.ar = Arena(p.sbuf("arena", (128, c.ARENA), F32)[:])
        self.tiles = [(i * c.TS, c.TS, 0, 0, i == 0, i == c.NS // c.TS - 1) for i in range(c.NS // c.TS)] + \
                     [(c.NS + b * c.SP, c.SP, 1, 1 + b, True, True) for b in range(c.NPB)]
        self.seqs = [(0, c.NS, True, 0)] + [(c.NS + b * c.SP, c.SP, False, b) for b in range(c.NPB)]

    def mm(self, out, lhsT, rhs, start, stop, reads, writes, inc=None):
        self.p.op("tensor", lambda e: e.matmul(out, lhsT=lhsT, rhs=rhs, start=start, stop=stop),
                  reads, writes, inc=True)

    def tr(self, out, in_, ident, reads, writes, inc=True):
        self.p.op("tensor", lambda e: e.transpose(out=out, in_=in_, identity=ident), reads, writes, inc=inc)

    def act(self, out, in_, func, reads, writes, scale=1.0, bias=0.0):
        self.p.op("scalar", lambda e: e.activation(out=out, in_=in_, func=func, bias=bias, scale=scale), reads, writes)

    def tt(self, out, in0, in1, op, reads, writes, eng="vector"):
        self.p.op(eng, lambda e: e.tensor_tensor(out=out, in0=in0, in1=in1, op=op), reads, writes)

    def ts(self, out, in0, s1, s2, op0, op1, reads, writes, eng="vector"):
        if op1 is None:
            self.p.op(eng, lambda e: e.tensor_scalar(out=out, in0=in0, scalar1=s1, scalar2=None, op0=op0), reads, writes)
        else:
            self.p.op(eng, lambda e: e.tensor_scalar(out=out, in0=in0, scalar1=s1, scalar2=s2, op0=op0, op1=op1), reads, writes)

    def stt(self, out, in0, scalar, in1, op0, op1, reads, writes):
        self.p.op("vector", lambda e: e.scalar_tensor_tensor(out=out, in0=in0, scalar=scalar, in1=in1, op0=op0, op1=op1),
                  reads, writes)

    def cp(self, out, in_, reads, writes, eng="vector"):
        if eng == "scalar":
            self.p.op("scalar", lambda e: e.copy(out=out, in_=in_), reads, writes)
        else:
            self.p.op(eng, lambda e: e.tensor_copy(out=out, in_=in_), reads, writes)

    def recip(self, out, in_, reads, writes):
        self.p.op("vector", lambda e: e.reciprocal(out=out, in_=in_), reads, writes)

    def memset(self, ap, val, writes, eng="gpsimd"):
        self.p.op(eng, lambda e: e.memset(ap, val), (), writes)

    def dma(self, q, out, in_, reads, writes):
        reads = [t for t in reads if t is not None]
        writes = [t for t in writes if t is not None]
        self.p.dma(q, lambda e: e.dma_start(out=out, in_=in_), reads, writes)

    def set_wbufs(self, E):
        n = self.wball.shape[1] // E
        self.wb = [Buf(self.wball[:, i * E:(i + 1) * E]) for i in range(n)]
        self.wbi = 0

    def wload(self, src):
        b = self.wb[self.wbi % len(self.wb)]
        self.wbi += 1
        E = src.shape[1]
        self.dma("sync", b.ap[:, 0:E], src, [self.wtok], [b.t])
        return b

    def setup_consts(self):
        p = self.p
        self.memset(self.ident.ap, 0.0, [self.ident.t])
        p.op("gpsimd", lambda e: e.affine_select(out=self.ident.ap, in_=self.ident.ap, compare_op=ALU.not_equal, fill=1.0,
                                                 base=0, pattern=[[-1, 128]], channel_multiplier=1),
             [self.ident.t], [self.ident.t])
        self.memset(self.ones.ap, 1.0, [self.ones.t])
        self.memset(self.onesb.ap, 1.0, [self.onesb.t])

    def load_params(self):
        c, I = self.c, self.I
        ents = []

        def add(name, ap):
            r = rows128(ap)
            ents.append((name, r, r.shape[0]))

        add("gfinal", I["g_final"])
        add("cvec", I["cvec"])
        for l in range(c.DEPTH):
            add(f"g1_{l}", I["g_norm1"][l])
            add(f"g2_{l}", I["g_norm2"][l])
            add(f"bmod_{l}", I["b_mod"][l])
            add(f"dw31_{l}", I["w_dw31"][l])
            for n in ("b_dw31", "g_ln_conv", "b_ln_conv", "b_conv4", "g_subln"):
                add(f"{n}_{l}", I[n][l])
            for n in ("w_dw3", "w_conv4", "b_rg_a", "b_rg_x", "lru_lambda"):
                add(f"{n}_{l}", I[n][l])
            add(f"st_{l}", I["st"][l])
            for t in range(3):
                add(f"fcw{t}_{l}", I["w_ffn_conv"][l, t])
            add(f"fcb_{l}", I["b_ffn_conv"][l])
        tiles_ = [[]]
        used = 0
        for name, r, R in ents:
            assert R <= 128
            if used + R > 128:
                tiles_.append([])
                used = 0
            tiles_[-1].append((name, r, R, used))
            used += R
        NT = len(tiles_)
        self.PRM = self.p.sbuf("prm", (128, NT * 128), F32)[:]
        self.prm_t = Tok()
        self.prm = {}
        stg = [Buf(self.ar.f32(128)) for _ in range(2)]
        for s in stg:
            self.memset(s.ap, 0.0, [s.t])
        for ti, tl in enumerate(tiles_):
            s = stg[ti % 2]
            ps = self.PS[ti % 2]
            for name, r, R, r0 in tl:
                self.dma("gpsimd", s.ap[r0:r0 + R, :], r, [], [s.t])
                self.prm[name] = self.PRM[:, ti * 128 + r0: ti * 128 + r0 + R]
            self.tr(ps.ap[:, 0:128], s.ap, self.ident.ap, [s.t, self.ident.t], [ps.t])
            self.cp(self.PRM[:, ti * 128:(ti + 1) * 128], ps.ap[:, 0:128], [ps.t], [self.prm_t])

    def convert_weights(self):
        c, I, S = self.c, self.I, self.S
        self.ar.reset()
        NB = 4
        CB = 2048
        st32 = [Buf(self.ar.f32(CB)) for _ in range(NB)]
        st16 = [Buf(self.ar.bf16(CB)) for _ in range(NB)]
        step = [0]

        def conv(src, dst, u0, offf):
            Kr, N = src.shape
            for kc in range(Kr // 128):
                for n0 in range(0, N, CB):
                    nn = min(CB, N - n0)
                    nb = nn // 128
                    i = step[0] % NB
                    step[0] += 1
                    a, b = st32[i], st16[i]
                    self.dma("sync", a.ap[:, 0:nn], src[kc * 128:(kc + 1) * 128, n0:n0 + nn], [], [a.t])
                    self.cp(b.ap[:, 0:nn], a.ap[:, 0:nn], [a.t], [b.t], eng=("vector", "scalar", "vector", "gpsimd", "scalar")[step[0] % 5])
                    off = offf(kc)
                    j0 = u0 + n0 // 128
                    self.dma("scalar", dst[j0:j0 + nb, :, off:off + 128].rearrange("u p c -> p u c"),
                             b.ap[:, 0:nn].rearrange("p (u c) -> p u c", c=128), [b.t], [self.wtok])

        for l in range(c.DEPTH):
            conv(I["w_in"][l], S[f"WIN{l}"], 0, lambda kc: kc * 128)
            for j in range(4):
                conv(I["w_branch"][l, j], S[f"WBR{l}"], 0, lambda kc, j=j: (j * c.MC + kc) * 128)
            conv(I["w_out"][l], S[f"WOUT{l}"], 0, lambda kc: kc * 128)
            for s in range(2):
                conv(I["w_ffn_up"][l][:, s * c.DFF:(s + 1) * c.DFF], S[f"WUP{l}"], 0,
                     lambda kc, s=s: (s * c.KC + kc) * 128)
            conv(I["w_ffn_down"][l], S[f"WDN{l}"], 0, lambda kc: kc * 128)

    def compute_mod(self):
        c, I = self.c, self.I
        KC = c.KC
        self.ar.reset()
        NMC = 6 * KC
        self.MOD = self.p.sbuf("mod", (128, c.DEPTH * NMC * 2), F32)[:]
        self.AB = self.p.sbuf("ab", (128, c.DEPTH * 2 * 2 * KC), F32)[:]
        self.mod_t = Tok()
        scv = Buf(self.ar.f32(2 * KC))
        self.act(scv.ap, self.prm["cvec"], AF.Silu, [self.prm_t], [scv.t])
        CBK = 256
        wst = [Buf(self.ar.f32(KC * CBK)) for _ in range(2)]
        ps = self.PS[2]
        k = 0
        for l in range(c.DEPTH):
            for cb in range(6 * c.D // CBK):
                w = wst[k % 2]
                k += 1
                self.dma("sync", w.ap.rearrange("p (k n) -> p k n", n=CBK),
                         I["w_mod"][l][:, cb * CBK:(cb + 1) * CBK].rearrange("(k p) n -> p k n", p=128), [], [w.t])
                for nn in range(CBK // 128):
                    n = cb * (CBK // 128) + nn
                    for kc in range(KC):
                        self.mm(ps.ap[:, 2 * n:2 * n + 2], w.ap[:, kc * CBK + nn * 128: kc * CBK + nn * 128 + 128],
                                scv.ap[:, kc::KC], kc == 0, kc == KC - 1, [w.t, scv.t], [ps.t])
            for v in range(2):
                base = (l * 2 + v) * NMC
                self.tt(self.MOD[:, base:base + NMC], ps.ap[:, v:2 * NMC:2], self.prm[f"bmod_{l}"], ALU.add,
                        [ps.t, self.prm_t], [self.mod_t])
                for s, (gname, scoff) in enumerate((("g1", KC), ("g2", 4 * KC))):
                    o = ((l * 2 + v) * 2 + s) * KC
                    self.stt(self.AB[:, o:o + KC], self.MOD[:, base + scoff: base + scoff + KC], 1.0,
                             self.prm[f"{gname}_{l}"], ALU.add, ALU.mult, [self.mod_t, self.prm_t], [self.mod_t])

    def modv(self, l, v, which):
        KC = self.c.KC
        base = (l * 2 + v) * 6 * KC + which * KC
        return self.MOD[:, base:base + KC]

    def abv(self, l, v, s):
        KC = self.c.KC
        o = ((l * 2 + v) * 2 + s) * KC
        return self.AB[:, o:o + KC]

    def setup_misc(self):
        c, I = self.c, self.I
        MC = c.MC
        self.ar.reset()
        self.MISC = self.p.sbuf("misc", (128, c.DEPTH * (4 + 4 * MC)), F32)[:]
        self.misc_t = Tok()
        self.BD = self.p.sbuf("bd", (128, c.DEPTH * 4 * MC * 128), BF16)[:]
        self.bd_t = Tok()
        lamst = Buf(self.ar.f32(4 * 64))
        tmp = Buf(self.ar.f32(8))
        bst = Buf(self.ar.f32(4 * MC * 128))
        for l in range(c.DEPTH):
            mb = l * (4 + 4 * MC)
            for i, n in enumerate(("lam_q1", "lam_k1", "lam_q2", "lam_k2")):
                self.dma("gpsimd", lamst.ap[:, i * 64:(i + 1) * 64], I[n][l].partition_broadcast(128), [], [lamst.t])
            for i in range(2):
                self.tt(lamst.ap[:, i * 128:i * 128 + 64], lamst.ap[:, i * 128:i * 128 + 64],
                        lamst.ap[:, i * 128 + 64:i * 128 + 128], ALU.mult, [lamst.t], [lamst.t])
                self.p.op("vector", lambda e, i=i: e.reduce_sum(out=tmp.ap[:, i:i + 1], in_=lamst.ap[:, i * 128:i * 128 + 64],
                                                               axis=mybir.AxisListType.X), [lamst.t], [tmp.t])
            self.act(tmp.ap[:, 2:4], tmp.ap[:, 0:2], AF.Exp, [tmp.t], [tmp.t])
            lam_init = 0.8 - 0.6 * math.exp(-0.3 * l)
            self.stt(self.MISC[:, mb:mb + 1], tmp.ap[:, 3:4], -lam_init, tmp.ap[:, 2:3], ALU.add, ALU.subtract,
                     [tmp.t], [self.misc_t])
            self.ts(self.MISC[:, mb + 1:mb + 2], self.prm[f"g_subln_{l}"], 1.0 - lam_init, None, ALU.mult, None,
                    [self.prm_t], [self.misc_t])
            sp = Buf(self.ar.f32(2 * MC))
            self.act(sp.ap, self.prm[f"lru_lambda_{l}"], AF.Exp, [self.prm_t], [sp.t], scale=-1.0)
            self.act(sp.ap, sp.ap, AF.Ln, [sp.t], [sp.t], bias=1.0)
            self.ts(self.MISC[:, mb + 4:mb + 4 + 2 * MC], sp.ap, -8.0, None, ALU.mult, None, [sp.t], [self.misc_t])
            self.ts(self.MISC[:, mb + 4 + 2 * MC:mb + 4 + 4 * MC], sp.ap, -16.0, None, ALU.mult, None, [sp.t], [self.misc_t])
            self.memset(bst.ap, 0.0, [bst.t])
            for g, n in enumerate(("w_rg_a", "w_rg_x")):
                for d in range(2):
                    for m in range(MC):
                        o = ((g * 2 + d) * MC + m) * 128
                        for hb in range(2):
                            self.dma("gpsimd", bst.ap[hb * 64:(hb + 1) * 64, o + hb * 64:o + hb * 64 + 64],
                                     I[n][l, d, 2 * m + hb], [], [bst.t])
            self.cp(self.BD[:, l * 4 * MC * 128:(l + 1) * 4 * MC * 128], bst.ap, [bst.t], [self.bd_t])

    def misc(self, l, i):
        mb = l * (4 + 4 * self.c.MC)
        return self.MISC[:, mb + i:mb + i + 1]

    def setup_rope(self):
        c = self.c
        self.ar.reset()
        NS, GW = c.NS, c.GW
        pi_ = Buf(self.ar.i32(2))
        ti = Buf(self.ar.i32(8))
        tf = Buf(self.ar.f32(16))
        self.p.op("gpsimd", lambda e: e.iota(pi_.ap[:, 0:1], pattern=[[0, 1]], base=0, channel_multiplier=1), [], [pi_.t])
        sh = lambda o, s, m: self.p.op("vector", lambda e: e.tensor_scalar(out=ti.ap[:, o:o + 1], in0=pi_.ap[:, 0:1], scalar1=s,
                                                                          scalar2=m, op0=ALU.arith_shift_right,
                                                                          op1=ALU.bitwise_and), [pi_.t], [ti.t])
        sh(0, 0, 15)
        sh(1, 5, 1)
        sh(2, 4, 1)
        self.cp(tf.ap[:, 0:3], ti.ap[:, 0:3], [ti.t], [tf.t])
        self.act(tf.ap[:, 3:4], tf.ap[:, 0:1], AF.Exp, [tf.t], [tf.t], scale=-math.log(10000.0) / 16.0)
        self.tt(tf.ap[:, 5:6], tf.ap[:, 3:4], tf.ap[:, 1:2], ALU.mult, [tf.t], [tf.t])
        self.tt(tf.ap[:, 4:5], tf.ap[:, 3:4], tf.ap[:, 5:6], ALU.subtract, [tf.t], [tf.t])
        R = Buf(self.ar.f32(NS))
        Cc = Buf(self.ar.f32(NS))
        ang = Buf(self.ar.f32(NS))
        t2 = Buf(self.ar.f32(NS))
        rows = NS // GW
        self.p.op("gpsimd", lambda e: e.iota(R.ap.rearrange("p (r g) -> p r g", g=GW), pattern=[[1, rows], [0, GW]], base=0,
                                             channel_multiplier=0, allow_small_or_imprecise_dtypes=True), [], [R.t])
        self.p.op("gpsimd", lambda e: e.iota(Cc.ap.rearrange("p (r g) -> p r g", g=GW), pattern=[[0, rows], [1, GW]], base=0,
                                             channel_multiplier=0, allow_small_or_imprecise_dtypes=True), [], [Cc.t])
        self.ts(ang.ap, R.ap, tf.ap[:, 4:5], None, ALU.mult, None, [R.t, tf.t], [ang.t])
        self.stt(ang.ap, Cc.ap, tf.ap[:, 5:6], ang.ap, ALU.mult, ALU.add, [Cc.t, tf.t, ang.t], [ang.t])
        MAGIC = 12582912.0
        TWO_PI = 2.0 * math.pi
        for which, shift in ((0, math.pi / 2), (1, 0.0)):
            self.ts(t2.ap, ang.ap, shift, 1.0 / TWO_PI, ALU.add, ALU.mult, [ang.t], [t2.t])
            self.ts(t2.ap, t2.ap, MAGIC, MAGIC, ALU.add, ALU.subtract, [t2.t], [t2.t])
            self.stt(t2.ap, t2.ap, -TWO_PI, ang.ap, ALU.mult, ALU.add, [t2.t, ang.t], [t2.t])
            self.ts(t2.ap, t2.ap, shift, 3.14159, ALU.add, ALU.min, [t2.t], [t2.t])
            self.ts(t2.ap, t2.ap, -3.14159, None, ALU.max, None, [t2.t], [t2.t])
            self.act(t2.ap, t2.ap, AF.Sin, [t2.t], [t2.t])
            self.dma("gpsimd", self.S["RC" if which == 0 else "RS"], t2.ap, [t2.t], [self.wtok])
        A = Buf(self.ar.f32(128))
        B_ = Buf(self.ar.f32(128))
        mi = Buf(self.ar.i32(128))
        mf = Buf(self.ar.f32(128))
        self.memset(A.ap, 0.0, [A.t])
        self.memset(B_.ap, 0.0, [B_.t])
        self.p.op("gpsimd", lambda e: e.affine_select(out=A.ap, in_=A.ap, compare_op=ALU.not_equal, fill=-1.0, base=-16,
                                                      pattern=[[-1, 128]], channel_multiplier=1), [A.t], [A.t])
        self.p.op("gpsimd", lambda e: e.affine_select(out=B_.ap, in_=B_.ap, compare_op=ALU.not_equal, fill=1.0, base=16,
                                                      pattern=[[-1, 128]], channel_multiplier=1), [B_.t], [B_.t])
        self.p.op("gpsimd", lambda e: e.iota(mi.ap, pattern=[[1, 128]], base=0, channel_multiplier=0), [], [mi.t])
        self.p.op("vector", lambda e: e.tensor_scalar(out=mi.ap, in0=mi.ap, scalar1=4, scalar2=1, op0=ALU.arith_shift_right,
                                                      op1=ALU.bitwise_and), [mi.t], [mi.t])
        self.cp(mf.ap, mi.ap, [mi.t], [mf.t])
        self.tt(B_.ap, B_.ap, mf.ap, ALU.mult, [B_.t, mf.t], [B_.t])
        self.ts(mf.ap, mf.ap, -1.0, 1.0, ALU.mult, ALU.add, [mf.t], [mf.t])
        self.tt(A.ap, A.ap, mf.ap, ALU.mult, [A.t, mf.t], [A.t])
        self.tt(self.pm.ap, A.ap, B_.ap, ALU.add, [A.t, B_.t], [self.pm.t])

    def norm_mod(self, x, T, Acols, Bcols, out, ps, sq, sd, tmp, extra_reads=()):
        c = self.c
        KC = c.KC
        for k in range(KC):
            s = sq[k % len(sq)]
            self.act(s.ap[:, 0:T], x.ap[:, k * T:(k + 1) * T], AF.Square, [x.t], [s.t])
            self.mm(ps.ap[:, 0:T], self.ones.ap, s.ap[:, 0:T], k == 0, k == KC - 1, [s.t, self.ones.t], [ps.t])
        self.act(sd.ap[:, 0:T], ps.ap[:, 0:T], AF.Sqrt, [ps.t], [sd.t], scale=1.0 / c.D, bias=EPS)
        self.recip(sd.ap[:, 0:T], sd.ap[:, 0:T], [sd.t], [sd.t])

    def apply_mod(self, x, T, sd, Acols, Bcols, outfn, out_t, tmp, reads):
        KC = self.c.KC
        for k in range(KC):
            t = tmp[k % len(tmp)]
            self.tt(t.ap[:, 0:T], x.ap[:, k * T:(k + 1) * T], sd.ap[:, 0:T], ALU.mult, [x.t, sd.t], [t.t])
            if Bcols is None:
                self.act(outfn(k), t.ap[:, 0:T], AF.Copy, [t.t] + reads, [out_t], scale=Acols[:, k:k + 1])
            else:
                self.act(outfn(k), t.ap[:, 0:T], AF.Identity, [t.t] + reads, [out_t], scale=Acols[:, k:k + 1],
                         bias=Bcols[:, k:k + 1])

    def phase_A(self, l):
        c, S, I = self.c, self.S, self.I
        KC, MC, DM = c.KC, c.MC, c.DM
        ar = self.ar
        ar.reset()
        TM = 512
        xT = [Buf(ar.f32(KC * TM)) for _ in range(1)]
        hT = [Buf(ar.bf16(KC * TM)) for _ in range(1)]
        xin = [Buf(ar.f32(c.D)) for _ in range(2)] if l == 0 else []
        sq = [Buf(ar.f32(TM)) for _ in range(2)]
        tmp = [Buf(ar.f32(TM)) for _ in range(2)]
        sd = Buf(ar.f32(TM))
        rc = Buf(ar.f32(TM))
        rs = Buf(ar.f32(TM))
        ob = [Buf(ar.f32(TM)) for _ in range(4)]
        o16 = [Buf(ar.bf16(TM)) for _ in range(3)]
        xf = [Buf(ar.f32(TM)) for _ in range(2)]
        vst = Buf(ar.bf16(4 * DM))
        kvf = [Buf(ar.f32(4 * 128)) for _ in range(2)]
        obi = [0]
        o16i = [0]
        W = S[f"WIN{l}"]
        self.set_wbufs(KC * 128)
        psr = [self.PS[i] for i in (0, 1, 2, 3, 4)]
        pi = [0]

        def nps():
            b = psr[pi[0] % len(psr)]
            pi[0] += 1
            return b

        def fm(unit, h, T):
            ps = nps()
            for k in range(KC):
                self.mm(ps.ap[:, 0:T], unit.ap[:, k * 128:(k + 1) * 128], h.ap[:, k * T:(k + 1) * T], k == 0, k == KC - 1,
                        [unit.t, h.t], [ps.t])
            return ps

        def nob():
            b = ob[obi[0] % len(ob)]
            obi[0] += 1
            return b

        def no16():
            b = o16[o16i[0] % len(o16)]
            o16i[0] += 1
            return b

        for tix, (tok0, T, v, seq, first, last) in enumerate(self.tiles):
            if getattr(self, "a_tiles", None) is not None and tix not in self.a_tiles:
                continue
            self.p.barrier()
            x, h = xT[0], hT[0]
            TB = T // 128
            src_in = I["xs"] if v == 0 else I["xp"]
            r0 = tok0 if v == 0 else tok0 - c.NS
            if l == 0:
                for tb in range(TB):
                    xi = xin[tb % 2]
                    self.dma("gpsimd", xi.ap, src_in[r0 + tb * 128: r0 + (tb + 1) * 128, :], [], [xi.t])
                    for k0 in range(0, KC, 4):
                        ps = self.PS[5 + (k0 // 4) % 2]
                        for kk in range(4):
                            k = k0 + kk
                            self.tr(ps.ap[:, kk * 128:(kk + 1) * 128], xi.ap[:, k * 128:(k + 1) * 128], self.ident.ap,
                                    [xi.t, self.ident.t], [ps.t])
                        dst = x.ap[:, 0:KC * T].rearrange("p (k t) -> p k t", t=T)[:, k0:k0 + 4, tb * 128:(tb + 1) * 128]
                        self.cp(dst, ps.ap.rearrange("p (k t) -> p k t", t=128), [ps.t], [x.t],
                                eng=("vector" if (k0 // 4) % 2 else "scalar"))
                self.dma("gpsimd", S["XT"][:, tok0:tok0 + T].rearrange("(k p) t -> p k t", p=128),
                         x.ap[:, 0:KC * T].rearrange("p (k t) -> p k t", t=T), [x.t], [self.wtok])
            else:
                self.dma("gpsimd", x.ap[:, 0:KC * T].rearrange("p (k t) -> p k t", t=T),
                         S["XT"][:, tok0:tok0 + T].rearrange("(k p) t -> p k t", p=128), [self.wtok], [x.t])
            if getattr(self, "a_stop", 0) == 1:
                return
            self.norm_mod(x, T, None, None, None, self.PS[7], sq, sd, tmp)
            if getattr(self, "a_stop", 0) == 2:
                return
            self.apply_mod(x, T, sd, self.abv(l, v, 0), self.modv(l, v, 0), lambda k: h.ap[:, k * T:(k + 1) * T], h.t, tmp,
                           [self.mod_t])
            if getattr(self, "a_stop", 0) == 3:
                return
            self.dma("gpsimd", S["HT"][:, tok0:tok0 + T].rearrange("(k p) t -> p k t", p=128),
                     h.ap[:, 0:KC * T].rearrange("p (k t) -> p k t", t=T), [h.t], [self.wtok])
            if v == 0:
                self.dma("gpsimd", rc.ap[:, 0:T], S["RC"][:, tok0:tok0 + T], [self.wtok], [rc.t])
                self.dma("gpsimd", rs.ap[:, 0:T], S["RS"][:, tok0:tok0 + T], [self.wtok], [rs.t])
            for kind, base, dst in (("q", 0, S["Q"]), ("k", MC, S["KT"])):
                for m in range(MC):
                    u = self.wload(W[base + m])
                    ps = fm(u, h, T)
                    o = no16()
                    if v == 0:
                        f = xf[m % 2]
                        self.cp(f.ap[:, 0:T], ps.ap[:, 0:T], [ps.t], [f.t], eng="scalar")
                        ps2 = nps()
                        self.mm(ps2.ap[:, 0:T], self.pm.ap, f.ap[:, 0:T], True, True, [self.pm.t, f.t], [ps2.t])
                        t1 = tmp[m % 2]
                        self.tt(t1.ap[:, 0:T], f.ap[:, 0:T], rc.ap[:, 0:T], ALU.mult, [f.t, rc.t], [t1.t])
                        self.tt(f.ap[:, 0:T], ps2.ap[:, 0:T], rs.ap[:, 0:T], ALU.mult, [ps2.t, rs.t], [f.t])
                        self.tt(o.ap[:, 0:T], t1.ap[:, 0:T], f.ap[:, 0:T], ALU.add, [t1.t, f.t], [o.t])
                    else:
                        self.cp(o.ap[:, 0:T], ps.ap[:, 0:T], [ps.t], [o.t], eng="scalar")
                    self.dma("gpsimd", dst[m * 128:(m + 1) * 128, tok0:tok0 + T], o.ap[:, 0:T], [o.t], [self.wtok])
                    if kind == "k" and v == 1:
                        for tb in range(TB):
                            ps3 = nps()
                            for k in range(KC):
                                self.mm(ps3.ap[:, 0:128], h.ap[:, k * T + tb * 128:k * T + (tb + 1) * 128],
                                        u.ap[:, k * 128:(k + 1) * 128], k == 0, k == KC - 1, [u.t, h.t], [ps3.t])
                            kf = kvf[tb % 2]
                            self.cp(kf.ap[:, 0:128], ps3.ap[:, 0:128], [ps3.t], [kf.t])
                            self.dma("gpsimd", self.O["nk"][seq - 1, l, tb * 128:(tb + 1) * 128, m * 128:(m + 1) * 128],
                                     kf.ap[:, 0:128], [kf.t], [self.wtok])
            if getattr(self, "a_stop", 0) == 4:
                return
            for m in range(MC):
                u = self.wload(W[2 * MC + m])
                ps = nps()
                for tb in range(TB):
                    for k in range(KC):
                        self.mm(ps.ap[:, tb * 128:(tb + 1) * 128], h.ap[:, k * T + tb * 128:k * T + (tb + 1) * 128],
                                u.ap[:, k * 128:(k + 1) * 128], k == 0, k == KC - 1, [u.t, h.t], [ps.t],
                                inc=(k == KC - 1 and tb == TB - 1))
                dstv = vst.ap[:, 0:TB * DM].rearrange("p (b e) -> p b e", e=DM)[:, :, m * 128:(m + 1) * 128]
                if v == 0:
                    self.cp(dstv, ps.ap[:, 0:TB * 128].rearrange("p (b e) -> p b e", e=128), [ps.t], [vst.t])
                if v == 1:
                    kf = kvf[m % 2]
                    self.cp(kf.ap[:, 0:TB * 128], ps.ap[:, 0:TB * 128], [ps.t], [kf.t])
                    self.cp(dstv, kf.ap[:, 0:TB * 128].rearrange("p (b e) -> p b e", e=128), [kf.t], [vst.t])
                    for tb in range(TB):
                        self.dma("gpsimd", self.O["nv"][seq - 1, l, tb * 128:(tb + 1) * 128, m * 128:(m + 1) * 128],
                                 kf.ap[:, tb * 128:(tb + 1) * 128], [kf.t], [self.wtok])
            self.dma("gpsimd", S["V"][tok0:tok0 + T, :].rearrange("(b p) e -> p b e", p=128),
                     vst.ap[:, 0:TB * DM].rearrange("p (b e) -> p b e", e=DM), [vst.t], [self.wtok])
            if getattr(self, "a_stop", 0) == 5:
                return
            for m in range(MC):
                ua = self.wload(W[3 * MC + m])
                pa = fm(ua, h, T)
                ug = self.wload(W[4 * MC + m])
                pg = fm(ug, h, T)
                sg = tmp[m % 2]
                self.act(sg.ap[:, 0:T], pg.ap[:, 0:T], AF.Sigmoid, [pg.t], [sg.t])
                o = nob()
                self.tt(o.ap[:, 0:T], pa.ap[:, 0:T], sg.ap[:, 0:T], ALU.mult, [pa.t, sg.t], [o.t])
                self.dma("gpsimd", S["UB"][m * 128:(m + 1) * 128, tok0:tok0 + T], o.ap[:, 0:T], [o.t], [self.wtok])
            if getattr(self, "a_stop", 0) == 6:
                return
            for m in range(MC):
                ub_ = self.wload(W[5 * MC + m])
                pb = fm(ub_, h, T)
                o = nob()
                self.cp(o.ap[:, 0:T], pb.ap[:, 0:T], [pb.t], [o.t], eng="scalar")
                self.dma("gpsimd", S["GB"][m * 128:(m + 1) * 128, tok0:tok0 + T], o.ap[:, 0:T], [o.t], [self.wtok])
                uc = self.wload(W[6 * MC + m])
                pc = fm(uc, h, T)
                ux = self.wload(W[7 * MC + m])
                px = fm(ux, h, T)
                g = tmp[m % 2]
                self.cp(g.ap[:, 0:T], pc.ap[:, 0:T], [pc.t], [g.t], eng="scalar")
                o = nob()
                self.tt(o.ap[:, 0:T], px.ap[:, 0:T], g.ap[:, 0:T], ALU.mult, [px.t, g.t], [o.t])
                self.dma("gpsimd", S["GCX"][m * 128:(m + 1) * 128, tok0:tok0 + T], o.ap[:, 0:T], [o.t], [self.wtok])
            if getattr(self, "a_stop", 0) == 7:
                return
            for m in range(MC):
                u1 = self.wload(W[8 * MC + m])
                p1 = fm(u1, h, T)
                o = nob()
                self.cp(o.ap[:, 0:T], p1.ap[:, 0:T], [p1.t], [o.t])
                self.dma("gpsimd", S["XR"][m * 128:(m + 1) * 128, tok0:tok0 + T], o.ap[:, 0:T], [o.t], [self.wtok])
                u2 = self.wload(W[9 * MC + m])
                p2 = fm(u2, h, T)
                o = nob()
                self.act(o.ap[:, 0:T], p2.ap[:, 0:T], AF.Gelu, [p2.t], [o.t])
                self.dma("gpsimd", S["GY"][m * 128:(m + 1) * 128, tok0:tok0 + T], o.ap[:, 0:T], [o.t], [self.wtok])
            if getattr(self, "a_stop", 0) == 8:
                return

    def phase_B(self, l):
        self.B_conformer(l)
        self.p.barrier()
        self.B_sconv_lru(l)
        self.p.barrier()
        self.B_attn(l)

    def B_conformer(self, l):
        c, S = self.c, self.S
        MC, DM = c.MC, c.DM
        ar = self.ar
        ar.reset()
        LM = c.NS
        cb = Buf(ar.f32(MC * LM))
        ubp = [Buf(ar.f32(LM + 30)) for _ in range(2)]
        sq = [Buf(ar.f32(512)) for _ in range(2)]
        mean = Buf(ar.f32(512))
        msq = Buf(ar.f32(512))
        rstd = Buf(ar.f32(512))
        t1 = [Buf(ar.f32(512)) for _ in range(2)]
        yo = [Buf(ar.bf16(512)) for _ in range(2)]
        w31 = self.prm[f"dw31_{l}"]
        for si, (tok0, L, is_s, b) in enumerate(self.seqs):
            self.p.barrier()
            for m in range(MC):
                u = ubp[m % 2]
                self.memset(u.ap[:, 0:15], 0.0, [u.t])
                self.memset(u.ap[:, 15 + L:30 + L], 0.0, [u.t])
                self.dma("gpsimd", u.ap[:, 15:15 + L], S["UB"][m * 128:(m + 1) * 128, tok0:tok0 + L], [self.wtok], [u.t])
                acc = cb.ap[:, m * LM:m * LM + L]
                self.ts(acc, u.ap[:, 0:L], w31[:, m:m + 1], self.prm[f"b_dw31_{l}"][:, m:m + 1], ALU.mult, ALU.add,
                        [u.t, self.prm_t], [cb.t])
                for j in range(1, 31):
                    self.stt(acc, u.ap[:, j:j + L], w31[:, j * MC + m:j * MC + m + 1], acc, ALU.mult, ALU.add,
                             [u.t, cb.t, self.prm_t], [cb.t])
            TT = min(512, L)
            for t0 in range(0, L, TT):
                p1, p2 = self.PS[0 + (t0 // TT) % 2 * 2], self.PS[1 + (t0 // TT) % 2 * 2]
                for m in range(MC):
                    x = cb.ap[:, m * LM + t0:m * LM + t0 + TT]
                    self.mm(p1.ap[:, 0:TT], self.ones.ap, x, m == 0, m == MC - 1, [cb.t, self.ones.t], [p1.t])
                    s = sq[m % 2]
                    self.act(s.ap[:, 0:TT], x, AF.Square, [cb.t], [s.t])
                    self.mm(p2.ap[:, 0:TT], self.ones.ap, s.ap[:, 0:TT], m == 0, m == MC - 1, [s.t, self.ones.t], [p2.t])
                self.ts(mean.ap[:, 0:TT], p1.ap[:, 0:TT], 1.0 / DM, None, ALU.mult, None, [p1.t], [mean.t])
                self.tt(msq.ap[:, 0:TT], mean.ap[:, 0:TT], mean.ap[:, 0:TT], ALU.mult, [mean.t], [msq.t])
                self.stt(msq.ap[:, 0:TT], p2.ap[:, 0:TT], 1.0 / DM, msq.ap[:, 0:TT], ALU.mult, ALU.subtract, [p2.t, msq.t], [msq.t])
                self.act(rstd.ap[:, 0:TT], msq.ap[:, 0:TT], AF.Sqrt, [msq.t], [rstd.t], bias=EPS)
                self.recip(rstd.ap[:, 0:TT], rstd.ap[:, 0:TT], [rstd.t], [rstd.t])
                for m in range(MC):
                    x = cb.ap[:, m * LM + t0:m * LM + t0 + TT]
                    t = t1[m % 2]
                    self.tt(t.ap[:, 0:TT], x, mean.ap[:, 0:TT], ALU.subtract, [cb.t, mean.t], [t.t])
                    self.tt(t.ap[:, 0:TT], t.ap[:, 0:TT], rstd.ap[:, 0:TT], ALU.mult, [t.t, rstd.t], [t.t])
                    y = yo[m % 2]
                    self.act(y.ap[:, 0:TT], t.ap[:, 0:TT], AF.Silu, [t.t, self.prm_t], [y.t],
                             scale=self.prm[f"g_ln_conv_{l}"][:, m:m + 1], bias=self.prm[f"b_ln_conv_{l}"][:, m:m + 1])
                    self.dma("gpsimd", S["Y"][DM + m * 128:DM + (m + 1) * 128, tok0 + t0:tok0 + t0 + TT], y.ap[:, 0:TT],
                             [y.t], [self.wtok])

    def B_sconv_lru(self, l):
        c, S = self.c, self.S
        MC, DM = c.MC, c.DM
        ar = self.ar
        ar.reset()
        LM = c.NS
        xp = Buf(ar.f32(LM + 4))
        gy = Buf(ar.f32(LM))
        gp = [xp, xp]
        gb = [gy, gy]
        yo = [Buf(ar.bf16(LM)) for _ in range(1)] * 2
        xr = Buf(ar.f32(LM))
        xrb = Buf(ar.bf16(LM))
        hs = [Buf(ar.f32(LM)) for _ in range(2)]
        tA = [Buf(ar.f32(512)) for _ in range(8)]
        hl = Buf(ar.f32(128))
        hlt = Buf(ar.f32(128))
        w3 = self.prm[f"w_dw3_{l}"]
        w4 = self.prm[f"w_conv4_{l}"]
        bd0 = l * 4 * MC * 128
        self.memset(hl.ap, 0.0, [hl.t])
        for si, (tok0, L, is_s, b) in enumerate(self.seqs):
            self.p.barrier()
            for m in range(MC):
                self.p.barrier()
                g = gp[m % 2]
                self.memset(g.ap[:, 0:1], 0.0, [g.t])
                self.memset(g.ap[:, L + 1:L + 2], 0.0, [g.t])
                self.dma("gpsimd", g.ap[:, 1:1 + L], S["GCX"][m * 128:(m + 1) * 128, tok0:tok0 + L], [self.wtok], [g.t])
                gg = gb[m % 2]
                self.dma("gpsimd", gg.ap[:, 0:L], S["GB"][m * 128:(m + 1) * 128, tok0:tok0 + L], [self.wtok], [gg.t])
                acc = xr
                self.ts(acc.ap[:, 0:L], g.ap[:, 0:L], w3[:, m:m + 1], None, ALU.mult, None, [g.t, self.prm_t], [acc.t])
                for j in (1, 2):
                    self.stt(acc.ap[:, 0:L], g.ap[:, j:j + L], w3[:, j * MC + m:j * MC + m + 1], acc.ap[:, 0:L], ALU.mult, ALU.add,
                             [g.t, acc.t, self.prm_t], [acc.t])
                y = yo[m % 2]
                self.tt(y.ap[:, 0:L], acc.ap[:, 0:L], gg.ap[:, 0:L], ALU.mult, [acc.t, gg.t], [y.t])
                self.dma("gpsimd", S["Y"][2 * DM + m * 128:2 * DM + (m + 1) * 128, tok0:tok0 + L], y.ap[:, 0:L], [y.t], [self.wtok])
            TT = min(self.c.LRU_TT, L)
            NT = L // TT
            for m in range(MC):
                self.p.barrier()
                self.memset(xp.ap[:, 0:1], 0.0, [xp.t])
                self.memset(xp.ap[:, L + 1:L + 3], 0.0, [xp.t])
                self.dma("gpsimd", xp.ap[:, 1:1 + L], S["XR"][m * 128:(m + 1) * 128, tok0:tok0 + L], [self.wtok], [xp.t])
                self.dma("gpsimd", gy.ap[:, 0:L], S["GY"][m * 128:(m + 1) * 128, tok0:tok0 + L], [self.wtok], [gy.t])
                self.ts(xr.ap[:, 0:L], xp.ap[:, 0:L], w4[:, m:m + 1], self.prm[f"b_conv4_{l}"][:, m:m + 1], ALU.mult, ALU.add,
                        [xp.t, self.prm_t], [xr.t])
                for j in (1, 2, 3):
                    self.stt(xr.ap[:, 0:L], xp.ap[:, j:j + L], w4[:, j * MC + m:j * MC + m + 1], xr.ap[:, 0:L], ALU.mult, ALU.add,
                             [xp.t, xr.t, self.prm_t], [xr.t])
                self.cp(xrb.ap[:, 0:L], xr.ap[:, 0:L], [xr.t], [xrb.t], eng="gpsimd")
                for d in range(2):
                    h = hs[d]
                    order = range(NT) if d == 0 else range(NT - 1, -1, -1)
                    s1 = self.MISC[:, l * (4 + 4 * MC) + 4 + d * MC + m: l * (4 + 4 * MC) + 4 + d * MC + m + 1]
                    s2 = self.MISC[:, l * (4 + 4 * MC) + 4 + 2 * MC + d * MC + m: l * (4 + 4 * MC) + 4 + 2 * MC + d * MC + m + 1]
                    for ti_, tI in enumerate(order):
                        t0 = tI * TT
                        pa, px = self.PS[(ti_ % 2) * 2], self.PS[(ti_ % 2) * 2 + 1]
                        wa = self.BD[:, bd0 + ((0 * 2 + d) * MC + m) * 128: bd0 + ((0 * 2 + d) * MC + m + 1) * 128]
                        wx = self.BD[:, bd0 + ((1 * 2 + d) * MC + m) * 128: bd0 + ((1 * 2 + d) * MC + m + 1) * 128]
                        self.mm(pa.ap[:, 0:TT], wa, xrb.ap[:, t0:t0 + TT], True, True, [self.bd_t, xrb.t], [pa.t])
                        self.mm(px.ap[:, 0:TT], wx, xrb.ap[:, t0:t0 + TT], True, True, [self.bd_t, xrb.t], [px.t])
                        r, ig, a, a2, u = tA[0 + 4 * (ti_ % 2)], tA[1 + 4 * (ti_ % 2)], tA[2 + 4 * (ti_ % 2)], tA[3 + 4 * (ti_ % 2)], None
                        self.act(r.ap[:, 0:TT], pa.ap[:, 0:TT], AF.Sigmoid, [pa.t, self.prm_t], [r.t],
                                 bias=self.prm[f"b_rg_a_{l}"][:, d * MC + m:d * MC + m + 1])
                        self.act(ig.ap[:, 0:TT], px.ap[:, 0:TT], AF.Sigmoid, [px.t, self.prm_t], [ig.t],
                                 bias=self.prm[f"b_rg_x_{l}"][:, d * MC + m:d * MC + m + 1])
                        self.act(a.ap[:, 0:TT], r.ap[:, 0:TT], AF.Exp, [r.t, self.misc_t], [a.t], scale=s1)
                        self.act(a2.ap[:, 0:TT], r.ap[:, 0:TT], AF.Exp, [r.t, self.misc_t], [a2.t], scale=s2)
                        self.act(a2.ap[:, 0:TT], a2.ap[:, 0:TT], AF.Sqrt, [a2.t], [a2.t], scale=-1.0, bias=1.0)
                        self.tt(ig.ap[:, 0:TT], ig.ap[:, 0:TT], xr.ap[:, t0:t0 + TT], ALU.mult, [ig.t, xr.t], [ig.t])
                        self.tt(ig.ap[:, 0:TT], ig.ap[:, 0:TT], a2.ap[:, 0:TT], ALU.mult, [ig.t, a2.t], [ig.t])
                        if ti_ == 0:
                            init = self.prm[f"st_{l}"][:, d * MC + m:d * MC + m + 1] if is_s else 0.0
                        else:
                            pt0 = order[ti_ - 1] * TT
                            init = h.ap[:, pt0 + TT - 1:pt0 + TT] if d == 0 else h.ap[:, pt0:pt0 + 1]
                        if d == 0:
                            self.p.op("vector", lambda e, h=h, a=a, ig=ig, init=init, t0=t0: e.tensor_tensor_scan(
                                out=h.ap[:, t0:t0 + TT], data0=a.ap[:, 0:TT], data1=ig.ap[:, 0:TT], initial=init,
                                op0=ALU.mult, op1=ALU.add), [a.t, ig.t, h.t, self.prm_t], [h.t])
                        else:
                            self.p.op("vector", lambda e, h=h, a=a, ig=ig, init=init, t0=t0: e.tensor_tensor_scan(
                                out=h.ap[:, t0:t0 + TT][:, ::-1], data0=a.ap[:, 0:TT][:, ::-1], data1=ig.ap[:, 0:TT][:, ::-1],
                                initial=init, op0=ALU.mult, op1=ALU.add), [a.t, ig.t, h.t, self.prm_t], [h.t])
                    if not is_s:
                        col = (b * 2 + d) * MC + m
                        src = h.ap[:, L - 1:L] if d == 0 else h.ap[:, 0:1]
                        self.cp(hl.ap[:, col:col + 1], src, [h.t], [hl.t], eng="gpsimd")
                y = yo[m % 2]
                self.tt(hs[0].ap[:, 0:L], hs[0].ap[:, 0:L], hs[1].ap[:, 0:L], ALU.add, [hs[0].t, hs[1].t], [hs[0].t])
                self.tt(y.ap[:, 0:L], hs[0].ap[:, 0:L], gy.ap[:, 0:L], ALU.mult, [hs[0].t, gy.t], [y.t])
                self.dma("gpsimd", S["Y"][3 * DM + m * 128:3 * DM + (m + 1) * 128, tok0:tok0 + L], y.ap[:, 0:L], [y.t], [self.wtok])
        ncol = c.NPB * 2 * MC
        ps = self.PS[4]
        self.tr(ps.ap[0:ncol, 0:128], hl.ap[:, 0:ncol], self.ident.ap, [hl.t, self.ident.t], [ps.t])
        self.cp(hlt.ap[0:ncol, :], ps.ap[0:ncol, 0:128], [ps.t], [hlt.t])
        for b in range(c.NPB):
            self.dma("gpsimd", self.O["nst"][b, l, :].rearrange("(r c) -> r c", c=128),
                     hlt.ap[b * 2 * MC:(b + 1) * 2 * MC, :], [hlt.t], [self.wtok])

    def B_attn(self, l):
        c, S, I = self.c, self.S, self.I
        MC, DM, PC, PAST = c.MC, c.DM, c.PC, c.PAST
        HA = MC
        ar = self.ar
        ar.reset()
        MMAX = PAST + c.NS
        kT = Buf(ar.bf16(HA * MMAX))
        Va = Buf(ar.bf16((MMAX // 128) * DM))
        cst = [Buf(ar.f32(PC * DM)) for _ in range(2)]
        qp = [Buf(ar.bf16(2 * 512)) for _ in range(2)]
        pt = [Buf(ar.bf16(512)) for _ in range(6)]
        lacc = [Buf(ar.f32(512)) for _ in range(2)]
        o0 = Buf(ar.f32(512))
        o1 = Buf(ar.f32(512))
        rl = [Buf(ar.f32(512)) for _ in range(2)]
        sqb = Buf(ar.f32(512))
        yb = [Buf(ar.bf16(512)) for _ in range(2)]
        for q in qp:
            self.memset(q.ap, 0.0, [q.t])
        neglam = self.misc(l, 0)
        gsub = self.misc(l, 1)
        psS = [self.PS[0], self.PS[1], self.PS[7]]
        psO = [self.PS[2], self.PS[3]]
        psL = [self.PS[4], self.PS[5]]
        psX = self.PS[6]
        qi = 0
        for si, (tok0, L, is_s, b) in enumerate(self.seqs):
            self.p.barrier()
            P0 = PAST if is_s else 0
            M = P0 + L
            NKC = M // 128
            kv = kT.ap[:, 0:HA * M].rearrange("p (h m) -> p h m", m=M)
            vv = Va.ap[:, 0:NKC * DM].rearrange("p (k e) -> p k e", e=DM)
            if is_s:
                ks, vs = cst
                self.dma("gpsimd", ks.ap.rearrange("p (j e) -> p j e", e=DM), I["ck"][l].rearrange("(j p) e -> p j e", p=128), [], [ks.t])
                self.dma("gpsimd", vs.ap.rearrange("p (j e) -> p j e", e=DM), I["cv"][l].rearrange("(j p) e -> p j e", p=128), [], [vs.t])
                self.cp(vv[:, 0:PC, :], vs.ap.rearrange("p (j e) -> p j e", e=DM), [vs.t], [Va.t])
                n = 0
                for j in range(PC):
                    for h in range(HA):
                        ps = self.PS[6 + n % 2]
                        n += 1
                        self.tr(ps.ap[:, 0:128], ks.ap[:, j * DM + h * 128:j * DM + (h + 1) * 128], self.ident.ap,
                                [ks.t, self.ident.t], [ps.t])
                        self.cp(kv[:, h, j * 128:(j + 1) * 128], ps.ap[:, 0:128], [ps.t], [kT.t], eng=("scalar" if n % 2 else "vector"))
            self.dma("gpsimd", kv[:, :, P0:P0 + L], S["KT"][:, tok0:tok0 + L].rearrange("(h p) t -> p h t", p=128), [self.wtok], [kT.t])
            self.dma("gpsimd", vv[:, P0 // 128:NKC, :], S["V"][tok0:tok0 + L, :].rearrange("(j p) e -> p j e", p=128), [self.wtok], [Va.t])
            QB = min(512, L)
            for h in range(HA):
                for qb in range(L // QB):
                    self.p.barrier()
                    q = qp[qi % 2]
                    qi += 1
                    qv = q.ap.rearrange("p (j t) -> p j t", t=512)
                    c0 = tok0 + qb * QB
                    self.dma("gpsimd", qv[0:64, 0, 0:QB], S["Q"][h * 128:h * 128 + 64, c0:c0 + QB], [self.wtok], [q.t])
                    self.dma("gpsimd", qv[64:128, 1, 0:QB], S["Q"][h * 128 + 64:h * 128 + 128, c0:c0 + QB], [self.wtok], [q.t])
                    steps = [(kc, j) for kc in range(NKC) for j in range(2)]

                    def emitS(i):
                        kc, j = steps[i]
                        ps = psS[i % 3]
                        self.mm(ps.ap[:, 0:QB], kv[:, h, kc * 128:(kc + 1) * 128], qv[:, j, 0:QB], True, True, [kT.t, q.t], [ps.t])

                    emitS(0)
                    if len(steps) > 1:
                        emitS(1)
                    for i, (kc, j) in enumerate(steps):
                        if i + 2 < len(steps):
                            emitS(i + 2)
                        ps = psS[i % 3]
                        pb = pt[i % 6]
                        self.act(pb.ap[:, 0:QB], ps.ap[:, 0:QB], AF.Exp, [ps.t], [pb.t], scale=0.125)
                        self.mm(psO[j].ap[:, 0:QB], vv[:, kc, h * 128:(h + 1) * 128], pb.ap[:, 0:QB], kc == 0, kc == NKC - 1,
                                [Va.t, pb.t], [psO[j].t])
                        la = lacc[j]
                        aeng = "vector" if j == 0 else "gpsimd"
                        if kc == 0:
                            self.cp(la.ap[:, 0:QB], pb.ap[:, 0:QB], [pb.t], [la.t], eng=aeng)
                        else:
                            self.tt(la.ap[:, 0:QB], la.ap[:, 0:QB], pb.ap[:, 0:QB], ALU.add, [la.t, pb.t], [la.t], eng=aeng)
                    for j in range(2):
                        self.mm(psL[j].ap[:, 0:QB], self.ones.ap, lacc[j].ap[:, 0:QB], True, True,
                                [self.ones.t, lacc[j].t], [psL[j].t])
                    for j in range(2):
                        self.recip(rl[j].ap[:, 0:QB], psL[j].ap[:, 0:QB], [psL[j].t], [rl[j].t])
                    self.tt(o0.ap[:, 0:QB], psO[0].ap[:, 0:QB], rl[0].ap[:, 0:QB], ALU.mult, [psO[0].t, rl[0].t], [o0.t])
                    self.tt(o1.ap[:, 0:QB], psO[1].ap[:, 0:QB], rl[1].ap[:, 0:QB], ALU.mult, [psO[1].t, rl[1].t], [o1.t])
                    self.stt(o0.ap[:, 0:QB], o1.ap[:, 0:QB], neglam, o0.ap[:, 0:QB], ALU.mult, ALU.add, [o0.t, o1.t, self.misc_t], [o0.t])
                    self.act(sqb.ap[:, 0:QB], o0.ap[:, 0:QB], AF.Square, [o0.t], [sqb.t])
                    self.mm(psX.ap[:, 0:QB], self.ones.ap, sqb.ap[:, 0:QB], True, True, [self.ones.t, sqb.t], [psX.t])
                    self.act(sqb.ap[:, 0:QB], psX.ap[:, 0:QB], AF.Sqrt, [psX.t], [sqb.t], scale=1.0 / 128, bias=EPS)
                    self.recip(sqb.ap[:, 0:QB], sqb.ap[:, 0:QB], [sqb.t], [sqb.t])
                    y = yb[qi % 2]
                    self.stt(y.ap[:, 0:QB], o0.ap[:, 0:QB], gsub, sqb.ap[:, 0:QB], ALU.mult, ALU.mult, [o0.t, sqb.t, self.misc_t], [y.t])
                    self.dma("gpsimd", S["Y"][h * 128:(h + 1) * 128, c0:c0 + QB], y.ap[:, 0:QB], [y.t], [self.wtok])

    def phase_C(self, l):
        c, S = self.c, self.S
        KC, MC, DM = c.KC, c.MC, c.DM
        ar = self.ar
        ar.reset()
        TM = 512
        x = Buf(ar.f32(KC * TM))
        h = Buf(ar.bf16(KC * TM))
        yt = Buf(ar.bf16(4 * MC * TM))
        mg = Buf(ar.bf16(KC * TM))
        h2 = Buf(ar.bf16(KC * TM))
        g = [Buf(ar.f32(TM)) for _ in range(3)]
        tt_ = [Buf(ar.f32(TM)) for _ in range(2)]
        acc = [Buf(ar.f32(TM)) for _ in range(2)]
        sq = acc
        sd = Buf(ar.f32(TM))
        wbrs = [Buf(ar.bf16(4 * MC * 128)) for _ in range(2)]
        self.set_wbufs(KC * 128)
        W, WB, WO = S[f"WIN{l}"], S[f"WBR{l}"], S[f"WOUT{l}"]
        psG = [self.PS[i] for i in (0, 1, 2)]
        psT = [self.PS[i] for i in (3, 4)]
        psM = [self.PS[i] for i in (5, 6)]
        gi = 0
        for (tok0, T, v, seq, first, last) in self.tiles:
            self.p.barrier()
            kt = lambda a: a.rearrange("p (k t) -> p k t", t=T)
            self.dma("gpsimd", kt(x.ap[:, 0:KC * T]), S["XT"][:, tok0:tok0 + T].rearrange("(k p) t -> p k t", p=128), [self.wtok], [x.t])
            self.dma("gpsimd", kt(h.ap[:, 0:KC * T]), S["HT"][:, tok0:tok0 + T].rearrange("(k p) t -> p k t", p=128), [self.wtok], [h.t])
            self.dma("gpsimd", kt(yt.ap[:, 0:4 * MC * T]), S["Y"][:, tok0:tok0 + T].rearrange("(k p) t -> p k t", p=128), [self.wtok], [yt.t])
            for n in range(KC):
                ub = wbrs[n % 2]
                self.dma("sync", ub.ap, WB[n], [], [ub.t])
                a = acc[n % 2]
                for j in range(4):
                    ug = self.wload(W[10 * MC + j * KC + n])
                    pg = psG[gi % 3]
                    gb = g[gi % 3]
                    gi += 1
                    for k in range(KC):
                        self.mm(pg.ap[:, 0:T], ug.ap[:, k * 128:(k + 1) * 128], h.ap[:, k * T:(k + 1) * T], k == 0, k == KC - 1,
                                [ug.t, h.t], [pg.t])
                    self.act(gb.ap[:, 0:T], pg.ap[:, 0:T], AF.Sigmoid, [pg.t], [gb.t])
                    pt_ = psT[j % 2]
                    for m in range(MC):
                        self.mm(pt_.ap[:, 0:T], ub.ap[:, (j * MC + m) * 128:(j * MC + m + 1) * 128],
                                yt.ap[:, (j * MC + m) * T:(j * MC + m + 1) * T], m == 0, m == MC - 1, [ub.t, yt.t], [pt_.t])
                    if j == 0:
                        self.tt(a.ap[:, 0:T], pt_.ap[:, 0:T], gb.ap[:, 0:T], ALU.mult, [pt_.t, gb.t], [a.t])
                    else:
                        t = tt_[j % 2]
                        self.tt(t.ap[:, 0:T], pt_.ap[:, 0:T], gb.ap[:, 0:T], ALU.mult, [pt_.t, gb.t], [t.t])
                        if j < 3:
                            self.tt(a.ap[:, 0:T], a.ap[:, 0:T], t.ap[:, 0:T], ALU.add, [a.t, t.t], [a.t], eng="gpsimd")
                        else:
                            self.tt(mg.ap[:, n * T:(n + 1) * T], a.ap[:, 0:T], t.ap[:, 0:T], ALU.add, [a.t, t.t], [mg.t], eng="gpsimd")
            for n in range(KC):
                uo = self.wload(WO[n])
                pm_ = psM[n % 2]
                for k in range(KC):
                    self.mm(pm_.ap[:, 0:T], uo.ap[:, k * 128:(k + 1) * 128], mg.ap[:, k * T:(k + 1) * T], k == 0, k == KC - 1,
                            [uo.t, mg.t], [pm_.t])
                self.stt(x.ap[:, n * T:(n + 1) * T], pm_.ap[:, 0:T], self.modv(l, v, 2)[:, n:n + 1], x.ap[:, n * T:(n + 1) * T],
                         ALU.mult, ALU.add, [pm_.t, x.t, self.mod_t], [x.t])
            self.dma("gpsimd", S["XT"][:, tok0:tok0 + T].rearrange("(k p) t -> p k t", p=128), kt(x.ap[:, 0:KC * T]), [x.t], [self.wtok])
            self.norm_mod(x, T, None, None, None, self.PS[7], sq, sd, tt_)
            self.apply_mod(x, T, sd, self.abv(l, v, 1), self.modv(l, v, 3), lambda k: h2.ap[:, k * T:(k + 1) * T], h2.t, tt_,
                           [self.mod_t])
            self.dma("gpsimd", S["H2"][:, tok0:tok0 + T].rearrange("(k p) t -> p k t", p=128), kt(h2.ap[:, 0:KC * T]), [h2.t], [self.wtok])

    def phase_D(self, l):
        c, S = self.c, self.S
        KC, FC = c.KC, c.FC
        ar = self.ar
        ar.reset()
        TM = 512
        TH = TM + 2
        x = Buf(ar.f32(KC * TM))
        h2 = Buf(ar.bf16(KC * TH))
        gbuf = Buf(ar.bf16(FC * TM))
        ab = [Buf(ar.f32(TH)) for _ in range(2)]
        ac = [Buf(ar.f32(TM)) for _ in range(2)]
        sl = [Buf(ar.f32(TM)) for _ in range(2)]
        sq = [Buf(ar.f32(TM)) for _ in range(2)]
        tmp = sq
        sd = Buf(ar.f32(TM))
        last_layer = (l == c.DEPTH - 1)
        if last_layer:
            yf = [Buf(ar.f32(128)) for _ in range(2)]
            ost = [Buf(ar.f32(c.D)) for _ in range(1)] * 2
        WU, WD = S[f"WUP{l}"], S[f"WDN{l}"]
        self.set_wbufs(c.UMAX)
        psA = [self.PS[0], self.PS[1]]
        psU = [self.PS[2], self.PS[3]]
        psH = self.PS[4]
        psD = [self.PS[5], self.PS[6]]
        fw = [self.prm[f"fcw{t}_{l}"] for t in range(3)]
        fb = self.prm[f"fcb_{l}"]
        oi = 0
        for (tok0, T, v, seq, first, last) in self.tiles:
            self.p.barrier()
            TT = T + 2
            hv = h2.ap[:, 0:KC * TT].rearrange("p (k t) -> p k t", t=TT)
            kt = lambda a: a.rearrange("p (k t) -> p k t", t=T)
            self.dma("gpsimd", kt(x.ap[:, 0:KC * T]), S["XT"][:, tok0:tok0 + T].rearrange("(k p) t -> p k t", p=128), [self.wtok], [x.t])
            lo = tok0 - (0 if first else 1)
            hi = tok0 + T + (0 if last else 1)
            if first:
                self.memset(hv[:, :, 0:1], 0.0, [h2.t])
            if last:
                self.memset(hv[:, :, TT - 1:TT], 0.0, [h2.t])
            self.dma("gpsimd", hv[:, :, (1 if first else 0):(TT - 1 if last else TT)],
                     S["H2"][:, lo:hi].rearrange("(k p) t -> p k t", p=128), [self.wtok], [h2.t])
            for i in range(FC):
                u = self.wload(WU[i])
                pa, pu = psA[i % 2], psU[i % 2]
                for k in range(KC):
                    self.mm(pa.ap[:, 0:T], u.ap[:, k * 128:(k + 1) * 128], hv[:, k, 1:1 + T], k == 0, k == KC - 1, [u.t, h2.t], [pa.t])
                for k in range(KC):
                    self.mm(psH.ap[:, 2 * i:2 * i + 2], u.ap[:, k * 128:(k + 1) * 128], hv[:, k, 0:TT:TT - 1], k == 0, k == KC - 1,
                            [u.t, h2.t], [psH.t])
                for k in range(KC):
                    self.mm(pu.ap[:, 0:T], u.ap[:, (KC + k) * 128:(KC + k + 1) * 128], hv[:, k, 1:1 + T], k == 0, k == KC - 1,
                            [u.t, h2.t], [pu.t])
                a = ab[i % 2]
                self.cp(a.ap[:, 1:1 + T], pa.ap[:, 0:T], [pa.t], [a.t], eng="scalar")
                self.cp(a.ap[:, 0:TT:TT - 1], psH.ap[:, 2 * i:2 * i + 2], [psH.t], [a.t], eng="vector")
                cc = ac[i % 2]
                self.ts(cc.ap[:, 0:T], a.ap[:, 0:T], fw[0][:, i:i + 1], fb[:, i:i + 1], ALU.mult, ALU.add, [a.t, self.prm_t], [cc.t])
                self.stt(cc.ap[:, 0:T], a.ap[:, 1:1 + T], fw[1][:, i:i + 1], cc.ap[:, 0:T], ALU.mult, ALU.add, [a.t, cc.t, self.prm_t], [cc.t])
                self.stt(cc.ap[:, 0:T], a.ap[:, 2:2 + T], fw[2][:, i:i + 1], cc.ap[:, 0:T], ALU.mult, ALU.add, [a.t, cc.t, self.prm_t], [cc.t])
                s = sl[i % 2]
                self.act(s.ap[:, 0:T], cc.ap[:, 0:T], AF.Silu, [cc.t], [s.t])
                self.tt(gbuf.ap[:, i * T:(i + 1) * T], pu.ap[:, 0:T], s.ap[:, 0:T], ALU.mult, [pu.t, s.t], [gbuf.t])
            for n in range(KC):
                u = self.wload(WD[n])
                pd = psD[n % 2]
                for f in range(FC):
                    self.mm(pd.ap[:, 0:T], u.ap[:, f * 128:(f + 1) * 128], gbuf.ap[:, f * T:(f + 1) * T], f == 0, f == FC - 1,
                            [u.t, gbuf.t], [pd.t])
                self.stt(x.ap[:, n * T:(n + 1) * T], pd.ap[:, 0:T], self.modv(l, v, 5)[:, n:n + 1], x.ap[:, n * T:(n + 1) * T],
                         ALU.mult, ALU.add, [pd.t, x.t, self.mod_t], [x.t])
            if not last_layer:
                self.dma("gpsimd", S["XT"][:, tok0:tok0 + T].rearrange("(k p) t -> p k t", p=128), kt(x.ap[:, 0:KC * T]), [x.t], [self.wtok])
            else:
                self.norm_mod(x, T, None, None, None, self.PS[7], sq, sd, tmp)
                dst = self.O["ys"] if v == 0 else self.O["yp"]
                r0 = tok0 if v == 0 else tok0 - c.NS
                gF = self.prm["gfinal"]
                for tb in range(T // 128):
                    o = ost[oi % 2]
                    oi += 1
                    for k0 in range(0, KC, 4):
                        ps = psA[(k0 // 4) % 2] if (k0 // 4) % 4 < 2 else psU[(k0 // 4) % 2]
                        for kk in range(4):
                            k = k0 + kk
                            y = yf[k % 2]
                            self.stt(y.ap[:, 0:128], x.ap[:, k * T + tb * 128:k * T + (tb + 1) * 128], gF[:, k:k + 1],
                                     sd.ap[:, tb * 128:(tb + 1) * 128], ALU.mult, ALU.mult, [x.t, sd.t, self.prm_t], [y.t])
                            self.tr(ps.ap[:, kk * 128:(kk + 1) * 128], y.ap[:, 0:128], self.ident.ap, [y.t, self.ident.t], [ps.t])
                        self.cp(o.ap[:, k0 * 128:(k0 + 4) * 128], ps.ap[:, 0:512], [ps.t], [o.t], eng=("scalar" if (k0 // 4) % 2 else "vector"))
                    self.dma("gpsimd", dst[r0 + tb * 128:r0 + (tb + 1) * 128, :], o.ap, [o.t], [self.wtok])

    def build(self, stop=None):
        c = self.c
        self.wtok = None
        for nm, fn in (("consts", self.setup_consts), ("params", self.load_params), ("convert", self.convert_weights),
                       ("mod", self.compute_mod), ("misc", self.setup_misc), ("rope", self.setup_rope)):
            fn()
            self.p.barrier()
            if stop == nm:
                self.p.finish()
                return self.nc
        for l in range(c.DEPTH):
            done = False
            for ph, fn in (("A", self.phase_A), ("B", self.phase_B), ("C", self.phase_C), ("D", self.phase_D)):
                fn(l)
                self.p.barrier()
                if stop == f"{ph}{l}":
                    done = True
                    break
            if done:
                break
        self.p.finish()
        return self.nc


def make_in_maps(cfg, inputs, n_cores):
    c = cfg
    maps = []
    wn = list(WSHAPES(c).keys())
    for i in range(n_cores):
        m = {}
        m["xs"] = np.ascontiguousarray(inputs["x_sample"][i])
        m["xp"] = np.ascontiguousarray(inputs["x_prompt"][i * c.NPB:(i + 1) * c.NPB]).reshape(c.NPB * c.SP, c.D)
        m["ck"] = np.ascontiguousarray(inputs["cache_k"][i]).reshape(c.DEPTH, c.PAST, c.DM)
        m["cv"] = np.ascontiguousarray(inputs["cache_v"][i]).reshape(c.DEPTH, c.PAST, c.DM)
        m["st"] = np.ascontiguousarray(inputs["state_lru"][i]).reshape(c.DEPTH, 2 * c.DM)
        m["cvec"] = np.concatenate([np.asarray(inputs["c"][i]).reshape(-1), np.asarray(inputs["c_ctx"]).reshape(-1)])
        for n in wn:
            m[n] = np.ascontiguousarray(inputs[n])
        maps.append(m)
    return maps


def gather_outputs(cfg, results, n_cores):
    c = cfg
    B = n_cores * c.NPB
    ys = np.stack([results[i]["ys"] for i in range(n_cores)], 0)
    yp = np.concatenate([results[i]["yp"].reshape(c.NPB, c.SP, c.D) for i in range(n_cores)], 0)
    nk = np.concatenate([results[i]["nk"] for i in range(n_cores)], 0).reshape(B, c.DEPTH, c.SP, c.DM // 128, 2, 64)
    nv = np.concatenate([results[i]["nv"] for i in range(n_cores)], 0).reshape(B, c.DEPTH, c.SP, c.DM // 128, 128)
    ns = np.concatenate([results[i]["nst"] for i in range(n_cores)], 0).reshape(B, c.DEPTH, 2, c.DM)
    return (yp.astype(np.float32), ys.astype(np.float32), nk.astype(np.float32), nv.astype(np.float32), ns.astype(np.float32))


def kernel(**inputs):
    inputs = {k: np.asarray(v, dtype=np.float32) for k, v in inputs.items()}
    cfg = Cfg()
    n = 8
    nc = K(cfg).build()
    maps = make_in_maps(cfg, inputs, n)
    res = run_bass_kernel_spmd(nc, maps, core_ids=list(range(n)))
    return gather_outputs(cfg, res.results, n)
```

```python
import contextlib
import math
import numpy as np
import concourse.bass as bass
import concourse.mybir as mybir
from concourse.bass_utils import run_bass_kernel_spmd

F32 = mybir.dt.float32
BF16 = mybir.dt.bfloat16
I32 = mybir.dt.int32
AF = mybir.ActivationFunctionType
ALU = mybir.AluOpType

COMPUTE = ("tensor", "vector", "scalar", "gpsimd")
QUEUES = ("sync", "scalar", "gpsimd")
ALL = ("sync", "tensor", "vector", "scalar", "gpsimd")
EPS = 1e-6


class Tok:
    __slots__ = ("w", "r")

    def __init__(self):
        self.w = None
        self.r = {}


class Prog:
    def __init__(self, nc, n_dma_sems=6):
        self.nc = nc
        self.es = contextlib.ExitStack()
        self.lists = {e: [] for e in ALL}
        self.sems = {}
        self.cnt = {}
        for e in COMPUTE:
            self.sems["c_" + e] = self.es.enter_context(nc.semaphore("c_" + e))
            self.cnt["c_" + e] = 0
        self.dpool = {}
        self.dnext = {}
        for q in QUEUES:
            keys = []
            for i in range(n_dma_sems if q != "scalar" else 3):
                k = f"d_{q}{i}"
                self.sems[k] = self.es.enter_context(nc.semaphore(k))
                self.cnt[k] = 0
                keys.append(k)
            self.dpool[q] = keys
            self.dnext[q] = 0
        self.waited = {e: {} for e in ALL}

    def sbuf(self, name, shape, dtype):
        return self.es.enter_context(self.nc.sbuf_tensor(name, list(shape), dtype))

    def psum(self, name, shape, dtype=F32):
        return self.es.enter_context(self.nc.psum_tensor(name, list(shape), dtype))

    def _collect(self, eng, reads, writes, extra=()):
        need = {}

        def add(ev):
            if ev is None:
                return
            k, v = ev
            if need.get(k, 0) < v:
                need[k] = v

        for t in reads:
            add(t.w)
        for t in writes:
            if t.r:
                for k, v in t.r.items():
                    add((k, v))
            else:
                add(t.w)
        for ev in extra:
            add(ev)
        out = []
        wd = self.waited[eng]
        own = "c_" + eng
        for k, v in need.items():
            if k == own and v > self.cnt.get(own, 0):
                continue
            if k == own and eng == "tensor":
                continue
            if wd.get(k, 0) < v:
                wd[k] = v
                out.append((self.sems[k], v))
        return out

    def _commit(self, ev, reads, writes):
        k, v = ev
        for t in reads:
            if t.r.get(k, 0) < v:
                t.r[k] = v
        for t in writes:
            t.w = ev
            t.r = {}

    def op(self, eng, fn, reads=(), writes=(), inc=True):
        waits = self._collect(eng, reads, writes)
        key = "c_" + eng
        sem = self.sems[key]
        if inc:
            self.cnt[key] += 1
            ev = (key, self.cnt[key])
        else:
            ev = None

        def emit(e, waits=waits, fn=fn, inc=inc, sem=sem):
            for s, v in waits:
                e.wait_ge(s, v)
            ins = fn(e)
            if inc:
                ins.then_inc(sem, 1)

        self.lists[eng].append(emit)
        if inc:
            self._commit(ev, reads, writes)
        else:
            nxt = (key, self.cnt[key] + 1)
            self._commit(nxt, reads, ())
            for t in writes:
                t.w = nxt
                t.r = {}
        return ev

    def dma(self, q, fn, reads=(), writes=()):
        pool = self.dpool[q]
        k = pool[self.dnext[q] % len(pool)]
        self.dnext[q] += 1
        prev = (k, self.cnt[k]) if self.cnt[k] > 0 else None
        waits = self._collect(q, reads, writes, extra=(prev,) if prev else ())
        self.cnt[k] += 16
        ev = (k, self.cnt[k])
        sem = self.sems[k]

        def emit(e, waits=waits, fn=fn, sem=sem):
            for s, v in waits:
                e.wait_ge(s, v)
            fn(e).then_inc(sem, 16)

        self.lists[q].append(emit)
        self._commit(ev, reads, writes)
        return ev

    def barrier(self):
        allev = [(k, v) for k, v in self.cnt.items() if v > 0]
        for eng in ALL:
            waits = []
            wd = self.waited[eng]
            for k, v in allev:
                if wd.get(k, 0) < v:
                    wd[k] = v
                    waits.append((self.sems[k], v))
            if waits:
                def emit(e, waits=waits):
                    for s, v in waits:
                        e.wait_ge(s, v)
                self.lists[eng].append(emit)

    def finish(self):
        self.barrier()
        lists = self.lists
        with self.nc.Block() as block:
            @block.sync
            def _(e):
                for f in lists["sync"]:
                    f(e)

            @block.tensor
            def _(e):
                for f in lists["tensor"]:
                    f(e)

            @block.vector
            def _(e):
                for f in lists["vector"]:
                    f(e)

            @block.scalar
            def _(e):
                for f in lists["scalar"]:
                    f(e)

            @block.gpsimd
            def _(e):
                for f in lists["gpsimd"]:
                    f(e)
        self.es.close()


class Buf:
    __slots__ = ("ap", "t")

    def __init__(self, ap):
        self.ap = ap
        self.t = Tok()


class Arena:
    def __init__(self, ap):
        self.ap = ap
        self.W = ap.shape[1]
        self.off = 0

    def reset(self):
        self.off = 0

    def f32(self, n):
        n2 = (n + 1) // 2 * 2
        a = self.ap[:, self.off:self.off + n]
        self.off += n2
        assert self.off <= self.W, ("arena overflow", self.off, self.W)
        return a

    def bf16(self, n):
        w = (n + 3) // 4 * 2
        a = self.ap[:, self.off:self.off + w].bitcast(BF16)[:, 0:n]
        self.off += w
        assert self.off <= self.W, ("arena overflow", self.off, self.W)
        return a

    def i32(self, n):
        return self.f32(n).bitcast(I32)


class Cfg:
    def __init__(self, D=2048, NS=4096, SP=256, NPB=4, PAST=512, DEPTH=2, GW=64, ARENA=31500, NWB=3):
        self.D = D
        self.KC = D // 128
        self.DM = D // 4
        self.MC = self.DM // 128
        self.DFF = ((8 * D // 3 + 127) // 128) * 128
        self.FC = self.DFF // 128
        self.NS, self.SP, self.NPB, self.PAST, self.DEPTH, self.GW = NS, SP, NPB, PAST, DEPTH, GW
        self.PC = PAST // 128
        self.NTOK = NS + NPB * SP
        self.NIN = 10 * self.DM + 4 * D
        self.NINC = self.NIN // 128
        self.TS = min(512, NS)
        self.ARENA = ARENA
        self.NWB = NWB
        self.LRU_TT = 256
        self.UMAX = max(self.KC * 128 * 2, self.FC * 128, 4 * self.MC * 128)
        assert self.MC >= 1 and NS % self.TS == 0 and SP % 128 == 0 and SP <= 512 and PAST % 128 == 0


WSHAPES = lambda c: dict(
    w_mod=(c.DEPTH, c.D, 6 * c.D), b_mod=(c.DEPTH, 6 * c.D), g_norm1=(c.DEPTH, c.D), g_norm2=(c.DEPTH, c.D),
    g_final=(c.D,), w_in=(c.DEPTH, c.D, c.NIN), lam_q1=(c.DEPTH, 64), lam_k1=(c.DEPTH, 64), lam_q2=(c.DEPTH, 64),
    lam_k2=(c.DEPTH, 64), g_subln=(c.DEPTH, 128), w_dw31=(c.DEPTH, 31, c.DM), b_dw31=(c.DEPTH, c.DM),
    g_ln_conv=(c.DEPTH, c.DM), b_ln_conv=(c.DEPTH, c.DM), w_dw3=(c.DEPTH, 3, c.DM), w_conv4=(c.DEPTH, 4, c.DM),
    b_conv4=(c.DEPTH, c.DM), w_rg_a=(c.DEPTH, 2, c.DM // 64, 64, 64), b_rg_a=(c.DEPTH, 2, c.DM),
    w_rg_x=(c.DEPTH, 2, c.DM // 64, 64, 64), b_rg_x=(c.DEPTH, 2, c.DM), lru_lambda=(c.DEPTH, 2, c.DM),
    w_branch=(c.DEPTH, 4, c.DM, c.D), w_out=(c.DEPTH, c.D, c.D), w_ffn_up=(c.DEPTH, c.D, 2 * c.DFF),
    w_ffn_conv=(c.DEPTH, 3, c.DFF), b_ffn_conv=(c.DEPTH, c.DFF), w_ffn_down=(c.DEPTH, c.DFF, c.D))


def rows128(ap):
    nd = len(ap.shape)
    if nd == 1:
        return ap.rearrange("(r c) -> r c", c=128)
    if nd == 2:
        return ap.rearrange("a (r c) -> (a r) c", c=128)
    raise ValueError


class K:
    def __init__(self, cfg):
        c = self.c = cfg
        nc = self.nc = bass.Bass("TRN2", target_bir_lowering=False)
        self.p = Prog(nc)
        din = lambda n, s: nc.dram_tensor(n, list(s), F32, kind="ExternalInput").ap()
        dout = lambda n, s: nc.dram_tensor(n, list(s), F32, kind="ExternalOutput").ap()
        dscr = lambda n, s, dt: nc.dram_tensor(n, list(s), dt, kind="Internal").ap()
        self.I = dict(xs=din("xs", (c.NS, c.D)), xp=din("xp", (c.NPB * c.SP, c.D)),
                      ck=din("ck", (c.DEPTH, c.PAST, c.DM)), cv=din("cv", (c.DEPTH, c.PAST, c.DM)),
                      st=din("st", (c.DEPTH, 2 * c.DM)), cvec=din("cvec", (2 * c.D,)))
        for n, s in WSHAPES(c).items():
            self.I[n] = din(n, s)
        self.O = dict(ys=dout("ys", (c.NS, c.D)), yp=dout("yp", (c.NPB * c.SP, c.D)),
                      nk=dout("nk", (c.NPB, c.DEPTH, c.SP, c.DM)), nv=dout("nv", (c.NPB, c.DEPTH, c.SP, c.DM)),
                      nst=dout("nst", (c.NPB, c.DEPTH, 2 * c.DM)))
        S = self.S = {}
        S["XT"] = dscr("s_xt", (c.D, c.NTOK), F32)
        S["HT"] = dscr("s_ht", (c.D, c.NTOK), BF16)
        S["H2"] = dscr("s_h2", (c.D, c.NTOK), BF16)
        S["Q"] = dscr("s_q", (c.DM, c.NTOK), BF16)
        S["KT"] = dscr("s_k", (c.DM, c.NTOK), BF16)
        S["V"] = dscr("s_v", (c.NTOK, c.DM), BF16)
        for n in ("UB", "GCX", "GB", "XR", "GY"):
            S[n] = dscr("s_" + n, (c.DM, c.NTOK), F32)
        S["Y"] = dscr("s_y", (4 * c.DM, c.NTOK), BF16)
        S["RC"] = dscr("s_rc", (128, c.NS), F32)
        S["RS"] = dscr("s_rs", (128, c.NS), F32)
        for l in range(c.DEPTH):
            S[f"WIN{l}"] = dscr(f"w_in_b{l}", (c.NINC, 128, c.KC * 128), BF16)
            S[f"WBR{l}"] = dscr(f"w_br_b{l}", (c.KC, 128, 4 * c.MC * 128), BF16)
            S[f"WOUT{l}"] = dscr(f"w_out_b{l}", (c.KC, 128, c.KC * 128), BF16)
            S[f"WUP{l}"] = dscr(f"w_up_b{l}", (c.FC, 128, 2 * c.KC * 128), BF16)
            S[f"WDN{l}"] = dscr(f"w_dn_b{l}", (c.KC, 128, c.FC * 128), BF16)
        p = self.p
        self.ident = Buf(p.sbuf("ident", (128, 128), F32)[:])
        self.ones = Buf(p.sbuf("ones", (128, 128), F32)[:])
        self.onesb = Buf(p.sbuf("onesb", (128, 128), BF16)[:])
        self.pm = Buf(p.sbuf("pm", (128, 128), F32)[:])
        self.wball = p.sbuf("wball", (128, c.NWB * c.UMAX), BF16)[:]
        self.set_wbufs(c.UMAX)
        self.PS = [Buf(p.psum(f"ps{i}", (128, 512))[:]) for i in range(8)]
        self.ar = Arena(p.sbuf("arena", (128, c.ARENA), F32)[:])
        self.tiles = [(i * c.TS, c.TS, 0, 0, i == 0, i == c.NS // c.TS - 1) for i in range(c.NS // c.TS)] + \
                     [(c.NS + b * c.SP, c.SP, 1, 1 + b, True, True) for b in range(c.NPB)]
        self.seqs = [(0, c.NS, True, 0)] + [(c.NS + b * c.SP, c.SP, False, b) for b in range(c.NPB)]

    def mm(self, out, lhsT, rhs, start, stop, reads, writes, inc=None):
        self.p.op("tensor", lambda e: e.matmul(out, lhsT=lhsT, rhs=rhs, start=start, stop=stop),
                  reads, writes, inc=True)

    def tr(self, out, in_, ident, reads, writes, inc=True):
        self.p.op("tensor", lambda e: e.transpose(out=out, in_=in_, identity=ident), reads, writes, inc=inc)

    def act(self, out, in_, func, reads, writes, scale=1.0, bias=0.0):
        self.p.op("scalar", lambda e: e.activation(out=out, in_=in_, func=func, bias=bias, scale=scale), reads, writes)

    def tt(self, out, in0, in1, op, reads, writes, eng="vector"):
        self.p.op(eng, lambda e: e.tensor_tensor(out=out, in0=in0, in1=in1, op=op), reads, writes)

    def ts(self, out, in0, s1, s2, op0, op1, reads, writes, eng="vector"):
        if op1 is None:
            self.p.op(eng, lambda e: e.tensor_scalar(out=out, in0=in0, scalar1=s1, scalar2=None, op0=op0), reads, writes)
        else:
            self.p.op(eng, lambda e: e.tensor_scalar(out=out, in0=in0, scalar1=s1, scalar2=s2, op0=op0, op1=op1), reads, writes)

    def stt(self, out, in0, scalar, in1, op0, op1, reads, writes):
        self.p.op("vector", lambda e: e.scalar_tensor_tensor(out=out, in0=in0, scalar=scalar, in1=in1, op0=op0, op1=op1),
                  reads, writes)

    def cp(self, out, in_, reads, writes, eng="vector"):
        if eng == "scalar":
            self.p.op("scalar", lambda e: e.copy(out=out, in_=in_), reads, writes)
        else:
            self.p.op(eng, lambda e: e.tensor_copy(out=out, in_=in_), reads, writes)

    def recip(self, out, in_, reads, writes):
        self.p.op("vector", lambda e: e.reciprocal(out=out, in_=in_), reads, writes)

    def memset(self, ap, val, writes, eng="gpsimd"):
        self.p.op(eng, lambda e: e.memset(ap, val), (), writes)

    def dma(self, q, out, in_, reads, writes):
        reads = [t for t in reads if t is not None]
        writes = [t for t in writes if t is not None]
        self.p.dma(q, lambda e: e.dma_start(out=out, in_=in_), reads, writes)

    def set_wbufs(self, E):
        n = self.wball.shape[1] // E
        self.wb = [Buf(self.wball[:, i * E:(i + 1) * E]) for i in range(n)]
        self.wbi = 0

    def wload(self, src):
        b = self.wb[self.wbi % len(self.wb)]
        self.wbi += 1
        E = src.shape[1]
        self.dma("sync", b.ap[:, 0:E], src, [self.wtok], [b.t])
        return b

    def setup_consts(self):
        p = self.p
        self.memset(self.ident.ap, 0.0, [self.ident.t])
        p.op("gpsimd", lambda e: e.affine_select(out=self.ident.ap, in_=self.ident.ap, compare_op=ALU.not_equal, fill=1.0,
                                                 base=0, pattern=[[-1, 128]], channel_multiplier=1),
             [self.ident.t], [self.ident.t])
        self.memset(self.ones.ap, 1.0, [self.ones.t])
        self.memset(self.onesb.ap, 1.0, [self.onesb.t])

    def load_params(self):
        c, I = self.c, self.I
        ents = []

        def add(name, ap):
            r = rows128(ap)
            ents.append((name, r, r.shape[0]))

        add("gfinal", I["g_final"])
        add("cvec", I["cvec"])
        for l in range(c.DEPTH):
            add(f"g1_{l}", I["g_norm1"][l])
            add(f"g2_{l}", I["g_norm2"][l])
            add(f"bmod_{l}", I["b_mod"][l])
            add(f"dw31_{l}", I["w_dw31"][l])
            for n in ("b_dw31", "g_ln_conv", "b_ln_conv", "b_conv4", "g_subln"):
                add(f"{n}_{l}", I[n][l])
            for n in ("w_dw3", "w_conv4", "b_rg_a", "b_rg_x", "lru_lambda"):
                add(f"{n}_{l}", I[n][l])
            add(f"st_{l}", I["st"][l])
            for t in range(3):
                add(f"fcw{t}_{l}", I["w_ffn_conv"][l, t])
            add(f"fcb_{l}", I["b_ffn_conv"][l])
        tiles_ = [[]]
        used = 0
        for name, r, R in ents:
            assert R <= 128
            if used + R > 128:
                tiles_.append([])
                used = 0
            tiles_[-1].append((name, r, R, used))
            used += R
        NT = len(tiles_)
        self.PRM = self.p.sbuf("prm", (128, NT * 128), F32)[:]
        self.prm_t = Tok()
        self.prm = {}
        stg = [Buf(self.ar.f32(128)) for _ in range(2)]
        for s in stg:
            self.memset(s.ap, 0.0, [s.t])
        for ti, tl in enumerate(tiles_):
            s = stg[ti % 2]
            ps = self.PS[ti % 2]
            for name, r, R, r0 in tl:
                self.dma("gpsimd", s.ap[r0:r0 + R, :], r, [], [s.t])
                self.prm[name] = self.PRM[:, ti * 128 + r0: ti * 128 + r0 + R]
            self.tr(ps.ap[:, 0:128], s.ap, self.ident.ap, [s.t, self.ident.t], [ps.t])
            self.cp(self.PRM[:, ti * 128:(ti + 1) * 128], ps.ap[:, 0:128], [ps.t], [self.prm_t])

    def convert_weights(self):
        c, I, S = self.c, self.I, self.S
        self.ar.reset()
        NB = 4
        CB = 2048
        st32 = [Buf(self.ar.f32(CB)) for _ in range(NB)]
        st16 = [Buf(self.ar.bf16(CB)) for _ in range(NB)]
        step = [0]

        def conv(src, dst, u0, offf):
            Kr, N = src.shape
            for kc in range(Kr // 128):
                for n0 in range(0, N, CB):
                    nn = min(CB, N - n0)
                    nb = nn // 128
                    i = step[0] % NB
                    step[0] += 1
                    a, b = st32[i], st16[i]
                    self.dma("sync", a.ap[:, 0:nn], src[kc * 128:(kc + 1) * 128, n0:n0 + nn], [], [a.t])
                    self.cp(b.ap[:, 0:nn], a.ap[:, 0:nn], [a.t], [b.t], eng=("vector", "scalar", "vector", "gpsimd", "scalar")[step[0] % 5])
                    off = offf(kc)
                    j0 = u0 + n0 // 128
                    self.dma("scalar", dst[j0:j0 + nb, :, off:off + 128].rearrange("u p c -> p u c"),
                             b.ap[:, 0:nn].rearrange("p (u c) -> p u c", c=128), [b.t], [self.wtok])

        for l in range(c.DEPTH):
            conv(I["w_in"][l], S[f"WIN{l}"], 0, lambda kc: kc * 128)
            for j in range(4):
                conv(I["w_branch"][l, j], S[f"WBR{l}"], 0, lambda kc, j=j: (j * c.MC + kc) * 128)
            conv(I["w_out"][l], S[f"WOUT{l}"], 0, lambda kc: kc * 128)
            for s in range(2):
                conv(I["w_ffn_up"][l][:, s * c.DFF:(s + 1) * c.DFF], S[f"WUP{l}"], 0,
                     lambda kc, s=s: (s * c.KC + kc) * 128)
            conv(I["w_ffn_down"][l], S[f"WDN{l}"], 0, lambda kc: kc * 128)

    def compute_mod(self):
        c, I = self.c, self.I
        KC = c.KC
        self.ar.reset()
        NMC = 6 * KC
        self.MOD = self.p.sbuf("mod", (128, c.DEPTH * NMC * 2), F32)[:]
        self.AB = self.p.sbuf("ab", (128, c.DEPTH * 2 * 2 * KC), F32)[:]
        self.mod_t = Tok()
        scv = Buf(self.ar.f32(2 * KC))
        self.act(scv.ap, self.prm["cvec"], AF.Silu, [self.prm_t], [scv.t])
        CBK = 256
        wst = [Buf(self.ar.f32(KC * CBK)) for _ in range(2)]
        ps = self.PS[2]
        k = 0
        for l in range(c.DEPTH):
            for cb in range(6 * c.D // CBK):
                w = wst[k % 2]
                k += 1
                self.dma("sync", w.ap.rearrange("p (k n) -> p k n", n=CBK),
                         I["w_mod"][l][:, cb * CBK:(cb + 1) * CBK].rearrange("(k p) n -> p k n", p=128), [], [w.t])
                for nn in range(CBK // 128):
                    n = cb * (CBK // 128) + nn
                    for kc in range(KC):
                        self.mm(ps.ap[:, 2 * n:2 * n + 2], w.ap[:, kc * CBK + nn * 128: kc * CBK + nn * 128 + 128],
                                scv.ap[:, kc::KC], kc == 0, kc == KC - 1, [w.t, scv.t], [ps.t])
            for v in range(2):
                base = (l * 2 + v) * NMC
                self.tt(self.MOD[:, base:base + NMC], ps.ap[:, v:2 * NMC:2], self.prm[f"bmod_{l}"], ALU.add,
                        [ps.t, self.prm_t], [self.mod_t])
                for s, (gname, scoff) in enumerate((("g1", KC), ("g2", 4 * KC))):
                    o = ((l * 2 + v) * 2 + s) * KC
                    self.stt(self.AB[:, o:o + KC], self.MOD[:, base + scoff: base + scoff + KC], 1.0,
                             self.prm[f"{gname}_{l}"], ALU.add, ALU.mult, [self.mod_t, self.prm_t], [self.mod_t])

    def modv(self, l, v, which):
        KC = self.c.KC
        base = (l * 2 + v) * 6 * KC + which * KC
        return self.MOD[:, base:base + KC]

    def abv(self, l, v, s):
        KC = self.c.KC
        o = ((l * 2 + v) * 2 + s) * KC
        return self.AB[:, o:o + KC]

    def setup_misc(self):
        c, I = self.c, self.I
        MC = c.MC
        self.ar.reset()
        self.MISC = self.p.sbuf("misc", (128, c.DEPTH * (4 + 4 * MC)), F32)[:]
        self.misc_t = Tok()
        self.BD = self.p.sbuf("bd", (128, c.DEPTH * 4 * MC * 128), BF16)[:]
        self.bd_t = Tok()
        lamst = Buf(self.ar.f32(4 * 64))
        tmp = Buf(self.ar.f32(8))
        bst = Buf(self.ar.f32(4 * MC * 128))
        for l in range(c.DEPTH):
            mb = l * (4 + 4 * MC)
            for i, n in enumerate(("lam_q1", "lam_k1", "lam_q2", "lam_k2")):
                self.dma("gpsimd", lamst.ap[:, i * 64:(i + 1) * 64], I[n][l].partition_broadcast(128), [], [lamst.t])
            for i in range(2):
                self.tt(lamst.ap[:, i * 128:i * 128 + 64], lamst.ap[:, i * 128:i * 128 + 64],
                        lamst.ap[:, i * 128 + 64:i * 128 + 128], ALU.mult, [lamst.t], [lamst.t])
                self.p.op("vector", lambda e, i=i: e.reduce_sum(out=tmp.ap[:, i:i + 1], in_=lamst.ap[:, i * 128:i * 128 + 64],
                                                               axis=mybir.AxisListType.X), [lamst.t], [tmp.t])
            self.act(tmp.ap[:, 2:4], tmp.ap[:, 0:2], AF.Exp, [tmp.t], [tmp.t])
            lam_init = 0.8 - 0.6 * math.exp(-0.3 * l)
            self.stt(self.MISC[:, mb:mb + 1], tmp.ap[:, 3:4], -lam_init, tmp.ap[:, 2:3], ALU.add, ALU.subtract,
                     [tmp.t], [self.misc_t])
            self.ts(self.MISC[:, mb + 1:mb + 2], self.prm[f"g_subln_{l}"], 1.0 - lam_init, None, ALU.mult, None,
                    [self.prm_t], [self.misc_t])
            sp = Buf(self.ar.f32(2 * MC))
            self.act(sp.ap, self.prm[f"lru_lambda_{l}"], AF.Exp, [self.prm_t], [sp.t], scale=-1.0)
            self.act(sp.ap, sp.ap, AF.Ln, [sp.t], [sp.t], bias=1.0)
            self.ts(self.MISC[:, mb + 4:mb + 4 + 2 * MC], sp.ap, -8.0, None, ALU.mult, None, [sp.t], [self.misc_t])
            self.ts(self.MISC[:, mb + 4 + 2 * MC:mb + 4 + 4 * MC], sp.ap, -16.0, None, ALU.mult, None, [sp.t], [self.misc_t])
            self.memset(bst.ap, 0.0, [bst.t])
            for g, n in enumerate(("w_rg_a", "w_rg_x")):
                for d in range(2):
                    for m in range(MC):
                        o = ((g * 2 + d) * MC + m) * 128
                        for hb in range(2):
                            self.dma("gpsimd", bst.ap[hb * 64:(hb + 1) * 64, o + hb * 64:o + hb * 64 + 64],
                                     I[n][l, d, 2 * m + hb], [], [bst.t])
            self.cp(self.BD[:, l * 4 * MC * 128:(l + 1) * 4 * MC * 128], bst.ap, [bst.t], [self.bd_t])

    def misc(self, l, i):
        mb = l * (4 + 4 * self.c.MC)
        return self.MISC[:, mb + i:mb + i + 1]

    def setup_rope(self):
        c = self.c
        self.ar.reset()
        NS, GW = c.NS, c.GW
        pi_ = Buf(self.ar.i32(2))
        ti = Buf(self.ar.i32(8))
        tf = Buf(self.ar.f32(16))
        self.p.op("gpsimd", lambda e: e.iota(pi_.ap[:, 0:1], pattern=[[0, 1]], base=0, channel_multiplier=1), [], [pi_.t])
        sh = lambda o, s, m: self.p.op("vector", lambda e: e.tensor_scalar(out=ti.ap[:, o:o + 1], in0=pi_.ap[:, 0:1], scalar1=s,
                                                                          scalar2=m, op0=ALU.arith_shift_right,
                                                                          op1=ALU.bitwise_and), [pi_.t], [ti.t])
        sh(0, 0, 15)
        sh(1, 5, 1)
        sh(2, 4, 1)
        self.cp(tf.ap[:, 0:3], ti.ap[:, 0:3], [ti.t], [tf.t])
        self.act(tf.ap[:, 3:4], tf.ap[:, 0:1], AF.Exp, [tf.t], [tf.t], scale=-math.log(10000.0) / 16.0)
        self.tt(tf.ap[:, 5:6], tf.ap[:, 3:4], tf.ap[:, 1:2], ALU.mult, [tf.t], [tf.t])
        self.tt(tf.ap[:, 4:5], tf.ap[:, 3:4], tf.ap[:, 5:6], ALU.subtract, [tf.t], [tf.t])
        R = Buf(self.ar.f32(NS))
        Cc = Buf(self.ar.f32(NS))
        ang = Buf(self.ar.f32(NS))
        t2 = Buf(self.ar.f32(NS))
        rows = NS // GW
        self.p.op("gpsimd", lambda e: e.iota(R.ap.rearrange("p (r g) -> p r g", g=GW), pattern=[[1, rows], [0, GW]], base=0,
                                             channel_multiplier=0, allow_small_or_imprecise_dtypes=True), [], [R.t])
        self.p.op("gpsimd", lambda e: e.iota(Cc.ap.rearrange("p (r g) -> p r g", g=GW), pattern=[[0, rows], [1, GW]], base=0,
                                             channel_multiplier=0, allow_small_or_imprecise_dtypes=True), [], [Cc.t])
        self.ts(ang.ap, R.ap, tf.ap[:, 4:5], None, ALU.mult, None, [R.t, tf.t], [ang.t])
        self.stt(ang.ap, Cc.ap, tf.ap[:, 5:6], ang.ap, ALU.mult, ALU.add, [Cc.t, tf.t, ang.t], [ang.t])
        MAGIC = 12582912.0
        TWO_PI = 2.0 * math.pi
        for which, shift in ((0, math.pi / 2), (1, 0.0)):
            self.ts(t2.ap, ang.ap, shift, 1.0 / TWO_PI, ALU.add, ALU.mult, [ang.t], [t2.t])
            self.ts(t2.ap, t2.ap, MAGIC, MAGIC, ALU.add, ALU.subtract, [t2.t], [t2.t])
            self.stt(t2.ap, t2.ap, -TWO_PI, ang.ap, ALU.mult, ALU.add, [t2.t, ang.t], [t2.t])
            self.ts(t2.ap, t2.ap, shift, 3.14159, ALU.add, ALU.min, [t2.t], [t2.t])
            self.ts(t2.ap, t2.ap, -3.14159, None, ALU.max, None, [t2.t], [t2.t])
            self.act(t2.ap, t2.ap, AF.Sin, [t2.t], [t2.t])
            self.dma("gpsimd", self.S["RC" if which == 0 else "RS"], t2.ap, [t2.t], [self.wtok])
        A = Buf(self.ar.f32(128))
        B_ = Buf(self.ar.f32(128))
        mi = Buf(self.ar.i32(128))
        mf = Buf(self.ar.f32(128))
        self.memset(A.ap, 0.0, [A.t])
        self.memset(B_.ap, 0.0, [B_.t])
        self.p.op("gpsimd", lambda e: e.affine_select(out=A.ap, in_=A.ap, compare_op=ALU.not_equal, fill=-1.0, base=-16,
                                                      pattern=[[-1, 128]], channel_multiplier=1), [A.t], [A.t])
        self.p.op("gpsimd", lambda e: e.affine_select(out=B_.ap, in_=B_.ap, compare_op=ALU.not_equal, fill=1.0, base=16,
                                                      pattern=[[-1, 128]], channel_multiplier=1), [B_.t], [B_.t])
        self.p.op("gpsimd", lambda e: e.iota(mi.ap, pattern=[[1, 128]], base=0, channel_multiplier=0), [], [mi.t])
        self.p.op("vector", lambda e: e.tensor_scalar(out=mi.ap, in0=mi.ap, scalar1=4, scalar2=1, op0=ALU.arith_shift_right,
                                                      op1=ALU.bitwise_and), [mi.t], [mi.t])
        self.cp(mf.ap, mi.ap, [mi.t], [mf.t])
        self.tt(B_.ap, B_.ap, mf.ap, ALU.mult, [B_.t, mf.t], [B_.t])
        self.ts(mf.ap, mf.ap, -1.0, 1.0, ALU.mult, ALU.add, [mf.t], [mf.t])
        self.tt(A.ap, A.ap, mf.ap, ALU.mult, [A.t, mf.t], [A.t])
        self.tt(self.pm.ap, A.ap, B_.ap, ALU.add, [A.t, B_.t], [self.pm.t])

    def norm_mod(self, x, T, Acols, Bcols, out, ps, sq, sd, tmp, extra_reads=()):
        c = self.c
        KC = c.KC
        for k in range(KC):
            s = sq[k % len(sq)]
            self.act(s.ap[:, 0:T], x.ap[:, k * T:(k + 1) * T], AF.Square, [x.t], [s.t])
            self.mm(ps.ap[:, 0:T], self.ones.ap, s.ap[:, 0:T], k == 0, k == KC - 1, [s.t, self.ones.t], [ps.t])
        self.act(sd.ap[:, 0:T], ps.ap[:, 0:T], AF.Sqrt, [ps.t], [sd.t], scale=1.0 / c.D, bias=EPS)
        self.recip(sd.ap[:, 0:T], sd.ap[:, 0:T], [sd.t], [sd.t])

    def apply_mod(self, x, T, sd, Acols, Bcols, outfn, out_t, tmp, reads):
        KC = self.c.KC
        for k in range(KC):
            t = tmp[k % len(tmp)]
            self.tt(t.ap[:, 0:T], x.ap[:, k * T:(k + 1) * T], sd.ap[:, 0:T], ALU.mult, [x.t, sd.t], [t.t])
            if Bcols is None:
                self.act(outfn(k), t.ap[:, 0:T], AF.Copy, [t.t] + reads, [out_t], scale=Acols[:, k:k + 1])
            else:
                self.act(outfn(k), t.ap[:, 0:T], AF.Identity, [t.t] + reads, [out_t], scale=Acols[:, k:k + 1],
                         bias=Bcols[:, k:k + 1])

    def phase_A(self, l):
        c, S, I = self.c, self.S, self.I
        KC, MC, DM = c.KC, c.MC, c.DM
        ar = self.ar
        ar.reset()
        TM = 512
        xT = [Buf(ar.f32(KC * TM)) for _ in range(1)]
        hT = [Buf(ar.bf16(KC * TM)) for _ in range(1)]
        xin = [Buf(ar.f32(c.D)) for _ in range(2)] if l == 0 else []
        sq = [Buf(ar.f32(TM)) for _ in range(2)]
        tmp = [Buf(ar.f32(TM)) for _ in range(2)]
        sd = Buf(ar.f32(TM))
        rc = Buf(ar.f32(TM))
        rs = Buf(ar.f32(TM))
        ob = [Buf(ar.f32(TM)) for _ in range(4)]
        o16 = [Buf(ar.bf16(TM)) for _ in range(3)]
        xf = [Buf(ar.f32(TM)) for _ in range(2)]
        vst = Buf(ar.bf16(4 * DM))
        kvf = [Buf(ar.f32(4 * 128)) for _ in range(2)]
        obi = [0]
        o16i = [0]
        W = S[f"WIN{l}"]
        self.set_wbufs(KC * 128)
        psr = [self.PS[i] for i in (0, 1, 2, 3, 4)]
        pi = [0]

        def nps():
            b = psr[pi[0] % len(psr)]
            pi[0] += 1
            return b

        def fm(unit, h, T):
            ps = nps()
            for k in range(KC):
                self.mm(ps.ap[:, 0:T], unit.ap[:, k * 128:(k + 1) * 128], h.ap[:, k * T:(k + 1) * T], k == 0, k == KC - 1,
                        [unit.t, h.t], [ps.t])
            return ps

        def nob():
            b = ob[obi[0] % len(ob)]
            obi[0] += 1
            return b

        def no16():
            b = o16[o16i[0] % len(o16)]
            o16i[0] += 1
            return b

        for tix, (tok0, T, v, seq, first, last) in enumerate(self.tiles):
            if getattr(self, "a_tiles", None) is not None and tix not in self.a_tiles:
                continue
            self.p.barrier()
            x, h = xT[0], hT[0]
            TB = T // 128
            src_in = I["xs"] if v == 0 else I["xp"]
            r0 = tok0 if v == 0 else tok0 - c.NS
            if l == 0:
                for tb in range(TB):
                    xi = xin[tb % 2]
                    self.dma("gpsimd", xi.ap, src_in[r0 + tb * 128: r0 + (tb + 1) * 128, :], [], [xi.t])
                    for k0 in range(0, KC, 4):
                        ps = self.PS[5 + (k0 // 4) % 2]
                        for kk in range(4):
                            k = k0 + kk
                            self.tr(ps.ap[:, kk * 128:(kk + 1) * 128], xi.ap[:, k * 128:(k + 1) * 128], self.ident.ap,
                                    [xi.t, self.ident.t], [ps.t])
                        dst = x.ap[:, 0:KC * T].rearrange("p (k t) -> p k t", t=T)[:, k0:k0 + 4, tb * 128:(tb + 1) * 128]
                        self.cp(dst, ps.ap.rearrange("p (k t) -> p k t", t=128), [ps.t], [x.t],
                                eng=("vector" if (k0 // 4) % 2 else "scalar"))
                self.dma("gpsimd", S["XT"][:, tok0:tok0 + T].rearrange("(k p) t -> p k t", p=128),
                         x.ap[:, 0:KC * T].rearrange("p (k t) -> p k t", t=T), [x.t], [self.wtok])
            else:
                self.dma("gpsimd", x.ap[:, 0:KC * T].rearrange("p (k t) -> p k t", t=T),
                         S["XT"][:, tok0:tok0 + T].rearrange("(k p) t -> p k t", p=128), [self.wtok], [x.t])
            if getattr(self, "a_stop", 0) == 1:
                return
            self.norm_mod(x, T, None, None, None, self.PS[7], sq, sd, tmp)
            if getattr(self, "a_stop", 0) == 2:
                return
            self.apply_mod(x, T, sd, self.abv(l, v, 0), self.modv(l, v, 0), lambda k: h.ap[:, k * T:(k + 1) * T], h.t, tmp,
                           [self.mod_t])
            if getattr(self, "a_stop", 0) == 3:
                return
            self.dma("gpsimd", S["HT"][:, tok0:tok0 + T].rearrange("(k p) t -> p k t", p=128),
                     h.ap[:, 0:KC * T].rearrange("p (k t) -> p k t", t=T), [h.t], [self.wtok])
            if v == 0:
                self.dma("gpsimd", rc.ap[:, 0:T], S["RC"][:, tok0:tok0 + T], [self.wtok], [rc.t])
                self.dma("gpsimd", rs.ap[:, 0:T], S["RS"][:, tok0:tok0 + T], [self.wtok], [rs.t])
            for kind, base, dst in (("q", 0, S["Q"]), ("k", MC, S["KT"])):
                for m in range(MC):
                    u = self.wload(W[base + m])
                    ps = fm(u, h, T)
                    o = no16()
                    if v == 0:
                        f = xf[m % 2]
                        self.cp(f.ap[:, 0:T], ps.ap[:, 0:T], [ps.t], [f.t], eng="scalar")
                        ps2 = nps()
                        self.mm(ps2.ap[:, 0:T], self.pm.ap, f.ap[:, 0:T], True, True, [self.pm.t, f.t], [ps2.t])
                        t1 = tmp[m % 2]
                        self.tt(t1.ap[:, 0:T], f.ap[:, 0:T], rc.ap[:, 0:T], ALU.mult, [f.t, rc.t], [t1.t])
                        self.tt(f.ap[:, 0:T], ps2.ap[:, 0:T], rs.ap[:, 0:T], ALU.mult, [ps2.t, rs.t], [f.t])
                        self.tt(o.ap[:, 0:T], t1.ap[:, 0:T], f.ap[:, 0:T], ALU.add, [t1.t, f.t], [o.t])
                    else:
                        self.cp(o.ap[:, 0:T], ps.ap[:, 0:T], [ps.t], [o.t], eng="scalar")
                    self.dma("gpsimd", dst[m * 128:(m + 1) * 128, tok0:tok0 + T], o.ap[:, 0:T], [o.t], [self.wtok])
                    if kind == "k" and v == 1:
                        for tb in range(TB):
                            ps3 = nps()
                            for k in range(KC):
                                self.mm(ps3.ap[:, 0:128], h.ap[:, k * T + tb * 128:k * T + (tb + 1) * 128],
                                        u.ap[:, k * 128:(k + 1) * 128], k == 0, k == KC - 1, [u.t, h.t], [ps3.t])
                            kf = kvf[tb % 2]
                            self.cp(kf.ap[:, 0:128], ps3.ap[:, 0:128], [ps3.t], [kf.t])
                            self.dma("gpsimd", self.O["nk"][seq - 1, l, tb * 128:(tb + 1) * 128, m * 128:(m + 1) * 128],
                                     kf.ap[:, 0:128], [kf.t], [self.wtok])
            if getattr(self, "a_stop", 0) == 4:
                return
            for m in range(MC):
                u = self.wload(W[2 * MC + m])
                ps = nps()
                for tb in range(TB):
                    for k in range(KC):
                        self.mm(ps.ap[:, tb * 128:(tb + 1) * 128], h.ap[:, k * T + tb * 128:k * T + (tb + 1) * 128],
                                u.ap[:, k * 128:(k + 1) * 128], k == 0, k == KC - 1, [u.t, h.t], [ps.t],
                                inc=(k == KC - 1 and tb == TB - 1))
                dstv = vst.ap[:, 0:TB * DM].rearrange("p (b e) -> p b e", e=DM)[:, :, m * 128:(m + 1) * 128]
                if v == 0:
                    self.cp(dstv, ps.ap[:, 0:TB * 128].rearrange("p (b e) -> p b e", e=128), [ps.t], [vst.t])
                if v == 1:
                    kf = kvf[m % 2]
                    self.cp(kf.ap[:, 0:TB * 128], ps.ap[:, 0:TB * 128], [ps.t], [kf.t])
                    self.cp(dstv, kf.ap[:, 0:TB * 128].rearrange("p (b e) -> p b e", e=128), [kf.t], [vst.t])
                    for tb in range(TB):
                        self.dma("gpsimd", self.O["nv"][seq - 1, l, tb * 128:(tb + 1) * 128, m * 128:(m + 1) * 128],
                                 kf.ap[:, tb * 128:(tb + 1) * 128], [kf.t], [self.wtok])
            self.dma("gpsimd", S["V"][tok0:tok0 + T, :].rearrange("(b p) e -> p b e", p=128),
                     vst.ap[:, 0:TB * DM].rearrange("p (b e) -> p b e", e=DM), [vst.t], [self.wtok])
            if getattr(self, "a_stop", 0) == 5:
                return
            for m in range(MC):
                ua = self.wload(W[3 * MC + m])
                pa = fm(ua, h, T)
                ug = self.wload(W[4 * MC + m])
                pg = fm(ug, h, T)
                sg = tmp[m % 2]
                self.act(sg.ap[:, 0:T], pg.ap[:, 0:T], AF.Sigmoid, [pg.t], [sg.t])
                o = nob()
                self.tt(o.ap[:, 0:T], pa.ap[:, 0:T], sg.ap[:, 0:T], ALU.mult, [pa.t, sg.t], [o.t])
                self.dma("gpsimd", S["UB"][m * 128:(m + 1) * 128, tok0:tok0 + T], o.ap[:, 0:T], [o.t], [self.wtok])
            if getattr(self, "a_stop", 0) == 6:
                return
            for m in range(MC):
                ub_ = self.wload(W[5 * MC + m])
                pb = fm(ub_, h, T)
                o = nob()
                self.cp(o.ap[:, 0:T], pb.ap[:, 0:T], [pb.t], [o.t], eng="scalar")
                self.dma("gpsimd", S["GB"][m * 128:(m + 1) * 128, tok0:tok0 + T], o.ap[:, 0:T], [o.t], [self.wtok])
                uc = self.wload(W[6 * MC + m])
                pc = fm(uc, h, T)
                ux = self.wload(W[7 * MC + m])
                px = fm(ux, h, T)
                g = tmp[m % 2]
                self.cp(g.ap[:, 0:T], pc.ap[:, 0:T], [pc.t], [g.t], eng="scalar")
                o = nob()
                self.tt(o.ap[:, 0:T], px.ap[:, 0:T], g.ap[:, 0:T], ALU.mult, [px.t, g.t], [o.t])
                self.dma("gpsimd", S["GCX"][m * 128:(m + 1) * 128, tok0:tok0 + T], o.ap[:, 0:T], [o.t], [self.wtok])
            if getattr(self, "a_stop", 0) == 7:
                return
            for m in range(MC):
                u1 = self.wload(W[8 * MC + m])
                p1 = fm(u1, h, T)
                o = nob()
                self.cp(o.ap[:, 0:T], p1.ap[:, 0:T], [p1.t], [o.t])
                self.dma("gpsimd", S["XR"][m * 128:(m + 1) * 128, tok0:tok0 + T], o.ap[:, 0:T], [o.t], [self.wtok])
                u2 = self.wload(W[9 * MC + m])
                p2 = fm(u2, h, T)
                o = nob()
                self.act(o.ap[:, 0:T], p2.ap[:, 0:T], AF.Gelu, [p2.t], [o.t])
                self.dma("gpsimd", S["GY"][m * 128:(m + 1) * 128, tok0:tok0 + T], o.ap[:, 0:T], [o.t], [self.wtok])
            if getattr(self, "a_stop", 0) == 8:
                return

    def phase_B(self, l):
        self.B_conformer(l)
        self.p.barrier()
        self.B_sconv_lru(l)
        self.p.barrier()
        self.B_attn(l)

    def B_conformer(self, l):
        c, S = self.c, self.S
        MC, DM = c.MC, c.DM
        ar = self.ar
        ar.reset()
        LM = c.NS
        cb = Buf(ar.f32(MC * LM))
        ubp = [Buf(ar.f32(LM + 30)) for _ in range(2)]
        sq = [Buf(ar.f32(512)) for _ in range(2)]
        mean = Buf(ar.f32(512))
        msq = Buf(ar.f32(512))
        rstd = Buf(ar.f32(512))
        t1 = [Buf(ar.f32(512)) for _ in range(2)]
        yo = [Buf(ar.bf16(512)) for _ in range(2)]
        w31 = self.prm[f"dw31_{l}"]
        for si, (tok0, L, is_s, b) in enumerate(self.seqs):
            self.p.barrier()
            for m in range(MC):
                u = ubp[m % 2]
                self.memset(u.ap[:, 0:15], 0.0, [u.t])
                self.memset(u.ap[:, 15 + L:30 + L], 0.0, [u.t])
                self.dma("gpsimd", u.ap[:, 15:15 + L], S["UB"][m * 128:(m + 1) * 128, tok0:tok0 + L], [self.wtok], [u.t])
                acc = cb.ap[:, m * LM:m * LM + L]
                self.ts(acc, u.ap[:, 0:L], w31[:, m:m + 1], self.prm[f"b_dw31_{l}"][:, m:m + 1], ALU.mult, ALU.add,
                        [u.t, self.prm_t], [cb.t])
                for j in range(1, 31):
                    self.stt(acc, u.ap[:, j:j + L], w31[:, j * MC + m:j * MC + m + 1], acc, ALU.mult, ALU.add,
                             [u.t, cb.t, self.prm_t], [cb.t])
            TT = min(512, L)
            for t0 in range(0, L, TT):
                p1, p2 = self.PS[0 + (t0 // TT) % 2 * 2], self.PS[1 + (t0 // TT) % 2 * 2]
                for m in range(MC):
                    x = cb.ap[:, m * LM + t0:m * LM + t0 + TT]
                    self.mm(p1.ap[:, 0:TT], self.ones.ap, x, m == 0, m == MC - 1, [cb.t, self.ones.t], [p1.t])
                    s = sq[m % 2]
                    self.act(s.ap[:, 0:TT], x, AF.Square, [cb.t], [s.t])
                    self.mm(p2.ap[:, 0:TT], self.ones.ap, s.ap[:, 0:TT], m == 0, m == MC - 1, [s.t, self.ones.t], [p2.t])
                self.ts(mean.ap[:, 0:TT], p1.ap[:, 0:TT], 1.0 / DM, None, ALU.mult, None, [p1.t], [mean.t])
                self.tt(msq.ap[:, 0:TT], mean.ap[:, 0:TT], mean.ap[:, 0:TT], ALU.mult, [mean.t], [msq.t])
                self.stt(msq.ap[:, 0:TT], p2.ap[:, 0:TT], 1.0 / DM, msq.ap[:, 0:TT], ALU.mult, ALU.subtract, [p2.t, msq.t], [msq.t])
                self.act(rstd.ap[:, 0:TT], msq.ap[:, 0:TT], AF.Sqrt, [msq.t], [rstd.t], bias=EPS)
                self.recip(rstd.ap[:, 0:TT], rstd.ap[:, 0:TT], [rstd.t], [rstd.t])
                for m in range(MC):
                    x = cb.ap[:, m * LM + t0:m * LM + t0 + TT]
                    t = t1[m % 2]
                    self.tt(t.ap[:, 0:TT], x, mean.ap[:, 0:TT], ALU.subtract, [cb.t, mean.t], [t.t])
                    self.tt(t.ap[:, 0:TT], t.ap[:, 0:TT], rstd.ap[:, 0:TT], ALU.mult, [t.t, rstd.t], [t.t])
                    y = yo[m % 2]
                    self.act(y.ap[:, 0:TT], t.ap[:, 0:TT], AF.Silu, [t.t, self.prm_t], [y.t],
                             scale=self.prm[f"g_ln_conv_{l}"][:, m:m + 1], bias=self.prm[f"b_ln_conv_{l}"][:, m:m + 1])
                    self.dma("gpsimd", S["Y"][DM + m * 128:DM + (m + 1) * 128, tok0 + t0:tok0 + t0 + TT], y.ap[:, 0:TT],
                             [y.t], [self.wtok])

    def B_sconv_lru(self, l):
        c, S = self.c, self.S
        MC, DM = c.MC, c.DM
        ar = self.ar
        ar.reset()
        LM = c.NS
        xp = Buf(ar.f32(LM + 4))
        gy = Buf(ar.f32(LM))
        gp = [xp, xp]
        gb = [gy, gy]
        yo = [Buf(ar.bf16(LM)) for _ in range(1)] * 2
        xr = Buf(ar.f32(LM))
        xrb = Buf(ar.bf16(LM))
        hs = [Buf(ar.f32(LM)) for _ in range(2)]
        tA = [Buf(ar.f32(512)) for _ in range(8)]
        hl = Buf(ar.f32(128))
        hlt = Buf(ar.f32(128))
        w3 = self.prm[f"w_dw3_{l}"]
        w4 = self.prm[f"w_conv4_{l}"]
        bd0 = l * 4 * MC * 128
        self.memset(hl.ap, 0.0, [hl.t])
        for si, (tok0, L, is_s, b) in enumerate(self.seqs):
            self.p.barrier()
            for m in range(MC):
                self.p.barrier()
                g = gp[m % 2]
                self.memset(g.ap[:, 0:1], 0.0, [g.t])
                self.memset(g.ap[:, L + 1:L + 2], 0.0, [g.t])
                self.dma("gpsimd", g.ap[:, 1:1 + L], S["GCX"][m * 128:(m + 1) * 128, tok0:tok0 + L], [self.wtok], [g.t])
                gg = gb[m % 2]
                self.dma("gpsimd", gg.ap[:, 0:L], S["GB"][m * 128:(m + 1) * 128, tok0:tok0 + L], [self.wtok], [gg.t])
                acc = xr
                self.ts(acc.ap[:, 0:L], g.ap[:, 0:L], w3[:, m:m + 1], None, ALU.mult, None, [g.t, self.prm_t], [acc.t])
                for j in (1, 2):
                    self.stt(acc.ap[:, 0:L], g.ap[:, j:j + L], w3[:, j * MC + m:j * MC + m + 1], acc.ap[:, 0:L], ALU.mult, ALU.add,
                             [g.t, acc.t, self.prm_t], [acc.t])
                y = yo[m % 2]
                self.tt(y.ap[:, 0:L], acc.ap[:, 0:L], gg.ap[:, 0:L], ALU.mult, [acc.t, gg.t], [y.t])
                self.dma("gpsimd", S["Y"][2 * DM + m * 128:2 * DM + (m + 1) * 128, tok0:tok0 + L], y.ap[:, 0:L], [y.t], [self.wtok])
            TT = min(self.c.LRU_TT, L)
            NT = L // TT
            for m in range(MC):
                self.p.barrier()
                self.memset(xp.ap[:, 0:1], 0.0, [xp.t])
                self.memset(xp.ap[:, L + 1:L + 3], 0.0, [xp.t])
                self.dma("gpsimd", xp.ap[:, 1:1 + L], S["XR"][m * 128:(m + 1) * 128, tok0:tok0 + L], [self.wtok], [xp.t])
                self.dma("gpsimd", gy.ap[:, 0:L], S["GY"][m * 128:(m + 1) * 128, tok0:tok0 + L], [self.wtok], [gy.t])
                self.ts(xr.ap[:, 0:L], xp.ap[:, 0:L], w4[:, m:m + 1], self.prm[f"b_conv4_{l}"][:, m:m + 1], ALU.mult, ALU.add,
                        [xp.t, self.prm_t], [xr.t])
                for j in (1, 2, 3):
                    self.stt(xr.ap[:, 0:L], xp.ap[:, j:j + L], w4[:, j * MC + m:j * MC + m + 1], xr.ap[:, 0:L], ALU.mult, ALU.add,
                             [xp.t, xr.t, self.prm_t], [xr.t])
                self.cp(xrb.ap[:, 0:L], xr.ap[:, 0:L], [xr.t], [xrb.t], eng="gpsimd")
                for d in range(2):
                    h = hs[d]
                    order = range(NT) if d == 0 else range(NT - 1, -1, -1)
                    s1 = self.MISC[:, l * (4 + 4 * MC) + 4 + d * MC + m: l * (4 + 4 * MC) + 4 + d * MC + m + 1]
                    s2 = self.MISC[:, l * (4 + 4 * MC) + 4 + 2 * MC + d * MC + m: l * (4 + 4 * MC) + 4 + 2 * MC + d * MC + m + 1]
                    for ti_, tI in enumerate(order):
                        t0 = tI * TT
                        pa, px = self.PS[(ti_ % 2) * 2], self.PS[(ti_ % 2) * 2 + 1]
                        wa = self.BD[:, bd0 + ((0 * 2 + d) * MC + m) * 128: bd0 + ((0 * 2 + d) * MC + m + 1) * 128]
                        wx = self.BD[:, bd0 + ((1 * 2 + d) * MC + m) * 128: bd0 + ((1 * 2 + d) * MC + m + 1) * 128]
                        self.mm(pa.ap[:, 0:TT], wa, xrb.ap[:, t0:t0 + TT], True, True, [self.bd_t, xrb.t], [pa.t])
                        self.mm(px.ap[:, 0:TT], wx, xrb.ap[:, t0:t0 + TT], True, True, [self.bd_t, xrb.t], [px.t])
                        r, ig, a, a2, u = tA[0 + 4 * (ti_ % 2)], tA[1 + 4 * (ti_ % 2)], tA[2 + 4 * (ti_ % 2)], tA[3 + 4 * (ti_ % 2)], None
                        self.act(r.ap[:, 0:TT], pa.ap[:, 0:TT], AF.Sigmoid, [pa.t, self.prm_t], [r.t],
                                 bias=self.prm[f"b_rg_a_{l}"][:, d * MC + m:d * MC + m + 1])
                        self.act(ig.ap[:, 0:TT], px.ap[:, 0:TT], AF.Sigmoid, [px.t, self.prm_t], [ig.t],
                                 bias=self.prm[f"b_rg_x_{l}"][:, d * MC + m:d * MC + m + 1])
                        self.act(a.ap[:, 0:TT], r.ap[:, 0:TT], AF.Exp, [r.t, self.misc_t], [a.t], scale=s1)
                        self.act(a2.ap[:, 0:TT], r.ap[:, 0:TT], AF.Exp, [r.t, self.misc_t], [a2.t], scale=s2)
                        self.act(a2.ap[:, 0:TT], a2.ap[:, 0:TT], AF.Sqrt, [a2.t], [a2.t], scale=-1.0, bias=1.0)
                        self.tt(ig.ap[:, 0:TT], ig.ap[:, 0:TT], xr.ap[:, t0:t0 + TT], ALU.mult, [ig.t, xr.t], [ig.t])
                        self.tt(ig.ap[:, 0:TT], ig.ap[:, 0:TT], a2.ap[:, 0:TT], ALU.mult, [ig.t, a2.t], [ig.t])
                        if ti_ == 0:
                            init = self.prm[f"st_{l}"][:, d * MC + m:d * MC + m + 1] if is_s else 0.0
                        else:
                            pt0 = order[ti_ - 1] * TT
                            init = h.ap[:, pt0 + TT - 1:pt0 + TT] if d == 0 else h.ap[:, pt0:pt0 + 1]
                        if d == 0:
                            self.p.op("vector", lambda e, h=h, a=a, ig=ig, init=init, t0=t0: e.tensor_tensor_scan(
                                out=h.ap[:, t0:t0 + TT], data0=a.ap[:, 0:TT], data1=ig.ap[:, 0:TT], initial=init,
                                op0=ALU.mult, op1=ALU.add), [a.t, ig.t, h.t, self.prm_t], [h.t])
                        else:
                            self.p.op("vector", lambda e, h=h, a=a, ig=ig, init=init, t0=t0: e.tensor_tensor_scan(
                                out=h.ap[:, t0:t0 + TT][:, ::-1], data0=a.ap[:, 0:TT][:, ::-1], data1=ig.ap[:, 0:TT][:, ::-1],
                                initial=init, op0=ALU.mult, op1=ALU.add), [a.t, ig.t, h.t, self.prm_t], [h.t])
                    if not is_s:
                        col = (b * 2 + d) * MC + m
                        src = h.ap[:, L - 1:L] if d == 0 else h.ap[:, 0:1]
                        self.cp(hl.ap[:, col:col + 1], src, [h.t], [hl.t], eng="gpsimd")
                y = yo[m % 2]
                self.tt(hs[0].ap[:, 0:L], hs[0].ap[:, 0:L], hs[1].ap[:, 0:L], ALU.add, [hs[0].t, hs[1].t], [hs[0].t])
                self.tt(y.ap[:, 0:L], hs[0].ap[:, 0:L], gy.ap[:, 0:L], ALU.mult, [hs[0].t, gy.t], [y.t])
                self.dma("gpsimd", S["Y"][3 * DM + m * 128:3 * DM + (m + 1) * 128, tok0:tok0 + L], y.ap[:, 0:L], [y.t], [self.wtok])
        ncol = c.NPB * 2 * MC
        ps = self.PS[4]
        self.tr(ps.ap[0:ncol, 0:128], hl.ap[:, 0:ncol], self.ident.ap, [hl.t, self.ident.t], [ps.t])
        self.cp(hlt.ap[0:ncol, :], ps.ap[0:ncol, 0:128], [ps.t], [hlt.t])
        for b in range(c.NPB):
            self.dma("gpsimd", self.O["nst"][b, l, :].rearrange("(r c) -> r c", c=128),
                     hlt.ap[b * 2 * MC:(b + 1) * 2 * MC, :], [hlt.t], [self.wtok])

    def B_attn(self, l):
        c, S, I = self.c, self.S, self.I
        MC, DM, PC, PAST = c.MC, c.DM, c.PC, c.PAST
        HA = MC
        ar = self.ar
        ar.reset()
        MMAX = PAST + c.NS
        kT = Buf(ar.bf16(HA * MMAX))
        Va = Buf(ar.bf16((MMAX // 128) * DM))
        cst = [Buf(ar.f32(PC * DM)) for _ in range(2)]
        qp = [Buf(ar.bf16(2 * 512)) for _ in range(2)]
        pt = [Buf(ar.bf16(512)) for _ in range(6)]
        lacc = [Buf(ar.f32(512)) for _ in range(2)]
        o0 = Buf(ar.f32(512))
        o1 = Buf(ar.f32(512))
        rl = [Buf(ar.f32(512)) for _ in range(2)]
        sqb = Buf(ar.f32(512))
        yb = [Buf(ar.bf16(512)) for _ in range(2)]
        for q in qp:
            self.memset(q.ap, 0.0, [q.t])
        neglam = self.misc(l, 0)
        gsub = self.misc(l, 1)
        psS = [self.PS[0], self.PS[1], self.PS[7]]
        psO = [self.PS[2], self.PS[3]]
        psL = [self.PS[4], self.PS[5]]
        psX = self.PS[6]
        qi = 0
        for si, (tok0, L, is_s, b) in enumerate(self.seqs):
            self.p.barrier()
            P0 = PAST if is_s else 0
            M = P0 + L
            NKC = M // 128
            kv = kT.ap[:, 0:HA * M].rearrange("p (h m) -> p h m", m=M)
            vv = Va.ap[:, 0:NKC * DM].rearrange("p (k e) -> p k e", e=DM)
            if is_s:
                ks, vs = cst
                self.dma("gpsimd", ks.ap.rearrange("p (j e) -> p j e", e=DM), I["ck"][l].rearrange("(j p) e -> p j e", p=128), [], [ks.t])
                self.dma("gpsimd", vs.ap.rearrange("p (j e) -> p j e", e=DM), I["cv"][l].rearrange("(j p) e -> p j e", p=128), [], [vs.t])
                self.cp(vv[:, 0:PC, :], vs.ap.rearrange("p (j e) -> p j e", e=DM), [vs.t], [Va.t])
                n = 0
                for j in range(PC):
                    for h in range(HA):
                        ps = self.PS[6 + n % 2]
                        n += 1
                        self.tr(ps.ap[:, 0:128], ks.ap[:, j * DM + h * 128:j * DM + (h + 1) * 128], self.ident.ap,
                                [ks.t, self.ident.t], [ps.t])
                        self.cp(kv[:, h, j * 128:(j + 1) * 128], ps.ap[:, 0:128], [ps.t], [kT.t], eng=("scalar" if n % 2 else "vector"))
            self.dma("gpsimd", kv[:, :, P0:P0 + L], S["KT"][:, tok0:tok0 + L].rearrange("(h p) t -> p h t", p=128), [self.wtok], [kT.t])
            self.dma("gpsimd", vv[:, P0 // 128:NKC, :], S["V"][tok0:tok0 + L, :].rearrange("(j p) e -> p j e", p=128), [self.wtok], [Va.t])
            QB = min(512, L)
            for h in range(HA):
                for qb in range(L // QB):
                    self.p.barrier()
                    q = qp[qi % 2]
                    qi += 1
                    qv = q.ap.rearrange("p (j t) -> p j t", t=512)
                    c0 = tok0 + qb * QB
                    self.dma("gpsimd", qv[0:64, 0, 0:QB], S["Q"][h * 128:h * 128 + 64, c0:c0 + QB], [self.wtok], [q.t])
                    self.dma("gpsimd", qv[64:128, 1, 0:QB], S["Q"][h * 128 + 64:h * 128 + 128, c0:c0 + QB], [self.wtok], [q.t])
                    steps = [(kc, j) for kc in range(NKC) for j in range(2)]

                    def emitS(i):
                        kc, j = steps[i]
                        ps = psS[i % 3]
                        self.mm(ps.ap[:, 0:QB], kv[:, h, kc * 128:(kc + 1) * 128], qv[:, j, 0:QB], True, True, [kT.t, q.t], [ps.t])

                    emitS(0)
                    if len(steps) > 1:
                        emitS(1)
                    for i, (kc, j) in enumerate(steps):
                        if i + 2 < len(steps):
                            emitS(i + 2)
                        ps = psS[i % 3]
                        pb = pt[i % 6]
                        self.act(pb.ap[:, 0:QB], ps.ap[:, 0:QB], AF.Exp, [ps.t], [pb.t], scale=0.125)
                        self.mm(psO[j].ap[:, 0:QB], vv[:, kc, h * 128:(h + 1) * 128], pb.ap[:, 0:QB], kc == 0, kc == NKC - 1,
                                [Va.t, pb.t], [psO[j].t])
                        self.mm(psL[j].ap[:, 0:QB], self.onesb.ap, pb.ap[:, 0:QB], kc == 0, kc == NKC - 1,
                                [self.onesb.t, pb.t], [psL[j].t])
                    for j in range(2):
                        self.recip(rl[j].ap[:, 0:QB], psL[j].ap[:, 0:QB], [psL[j].t], [rl[j].t])
                    self.tt(o0.ap[:, 0:QB], psO[0].ap[:, 0:QB], rl[0].ap[:, 0:QB], ALU.mult, [psO[0].t, rl[0].t], [o0.t])
                    self.tt(o1.ap[:, 0:QB], psO[1].ap[:, 0:QB], rl[1].ap[:, 0:QB], ALU.mult, [psO[1].t, rl[1].t], [o1.t])
                    self.stt(o0.ap[:, 0:QB], o1.ap[:, 0:QB], neglam, o0.ap[:, 0:QB], ALU.mult, ALU.add, [o0.t, o1.t, self.misc_t], [o0.t])
                    self.act(sqb.ap[:, 0:QB], o0.ap[:, 0:QB], AF.Square, [o0.t], [sqb.t])
                    self.mm(psX.ap[:, 0:QB], self.ones.ap, sqb.ap[:, 0:QB], True, True, [self.ones.t, sqb.t], [psX.t])
                    self.act(sqb.ap[:, 0:QB], psX.ap[:, 0:QB], AF.Sqrt, [psX.t], [sqb.t], scale=1.0 / 128, bias=EPS)
                    self.recip(sqb.ap[:, 0:QB], sqb.ap[:, 0:QB], [sqb.t], [sqb.t])
                    y = yb[qi % 2]
                    self.stt(y.ap[:, 0:QB], o0.ap[:, 0:QB], gsub, sqb.ap[:, 0:QB], ALU.mult, ALU.mult, [o0.t, sqb.t, self.misc_t], [y.t])
                    self.dma("gpsimd", S["Y"][h * 128:(h + 1) * 128, c0:c0 + QB], y.ap[:, 0:QB], [y.t], [self.wtok])

    def phase_C(self, l):
        c, S = self.c, self.S
        KC, MC, DM = c.KC, c.MC, c.DM
        ar = self.ar
        ar.reset()
        TM = 512
        x = Buf(ar.f32(KC * TM))
        h = Buf(ar.bf16(KC * TM))
        yt = Buf(ar.bf16(4 * MC * TM))
        mg = Buf(ar.bf16(KC * TM))
        h2 = Buf(ar.bf16(KC * TM))
        g = [Buf(ar.f32(TM)) for _ in range(3)]
        tt_ = [Buf(ar.f32(TM)) for _ in range(2)]
        acc = [Buf(ar.f32(TM)) for _ in range(2)]
        sq = acc
        sd = Buf(ar.f32(TM))
        wbrs = [Buf(ar.bf16(4 * MC * 128)) for _ in range(2)]
        self.set_wbufs(KC * 128)
        W, WB, WO = S[f"WIN{l}"], S[f"WBR{l}"], S[f"WOUT{l}"]
        psG = [self.PS[i] for i in (0, 1, 2)]
        psT = [self.PS[i] for i in (3, 4)]
        psM = [self.PS[i] for i in (5, 6)]
        gi = 0
        for (tok0, T, v, seq, first, last) in self.tiles:
            self.p.barrier()
            kt = lambda a: a.rearrange("p (k t) -> p k t", t=T)
            self.dma("gpsimd", kt(x.ap[:, 0:KC * T]), S["XT"][:, tok0:tok0 + T].rearrange("(k p) t -> p k t", p=128), [self.wtok], [x.t])
            self.dma("gpsimd", kt(h.ap[:, 0:KC * T]), S["HT"][:, tok0:tok0 + T].rearrange("(k p) t -> p k t", p=128), [self.wtok], [h.t])
            self.dma("gpsimd", kt(yt.ap[:, 0:4 * MC * T]), S["Y"][:, tok0:tok0 + T].rearrange("(k p) t -> p k t", p=128), [self.wtok], [yt.t])
            for n in range(KC):
                ub = wbrs[n % 2]
                self.dma("sync", ub.ap, WB[n], [], [ub.t])
                a = acc[n % 2]
                for j in range(4):
                    ug = self.wload(W[10 * MC + j * KC + n])
                    pg = psG[gi % 3]
                    gb = g[gi % 3]
                    gi += 1
                    for k in range(KC):
                        self.mm(pg.ap[:, 0:T], ug.ap[:, k * 128:(k + 1) * 128], h.ap[:, k * T:(k + 1) * T], k == 0, k == KC - 1,
                                [ug.t, h.t], [pg.t])
                    self.act(gb.ap[:, 0:T], pg.ap[:, 0:T], AF.Sigmoid, [pg.t], [gb.t])
                    pt_ = psT[j % 2]
                    for m in range(MC):
                        self.mm(pt_.ap[:, 0:T], ub.ap[:, (j * MC + m) * 128:(j * MC + m + 1) * 128],
                                yt.ap[:, (j * MC + m) * T:(j * MC + m + 1) * T], m == 0, m == MC - 1, [ub.t, yt.t], [pt_.t])
                    if j == 0:
                        self.tt(a.ap[:, 0:T], pt_.ap[:, 0:T], gb.ap[:, 0:T], ALU.mult, [pt_.t, gb.t], [a.t])
                    else:
                        t = tt_[j % 2]
                        self.tt(t.ap[:, 0:T], pt_.ap[:, 0:T], gb.ap[:, 0:T], ALU.mult, [pt_.t, gb.t], [t.t])
                        if j < 3:
                            self.tt(a.ap[:, 0:T], a.ap[:, 0:T], t.ap[:, 0:T], ALU.add, [a.t, t.t], [a.t], eng="gpsimd")
                        else:
                            self.tt(mg.ap[:, n * T:(n + 1) * T], a.ap[:, 0:T], t.ap[:, 0:T], ALU.add, [a.t, t.t], [mg.t], eng="gpsimd")
            for n in range(KC):
                uo = self.wload(WO[n])
                pm_ = psM[n % 2]
                for k in range(KC):
                    self.mm(pm_.ap[:, 0:T], uo.ap[:, k * 128:(k + 1) * 128], mg.ap[:, k * T:(k + 1) * T], k == 0, k == KC - 1,
                            [uo.t, mg.t], [pm_.t])
                self.stt(x.ap[:, n * T:(n + 1) * T], pm_.ap[:, 0:T], self.modv(l, v, 2)[:, n:n + 1], x.ap[:, n * T:(n + 1) * T],
                         ALU.mult, ALU.add, [pm_.t, x.t, self.mod_t], [x.t])
            self.dma("gpsimd", S["XT"][:, tok0:tok0 + T].rearrange("(k p) t -> p k t", p=128), kt(x.ap[:, 0:KC * T]), [x.t], [self.wtok])
            self.norm_mod(x, T, None, None, None, self.PS[7], sq, sd, tt_)
            self.apply_mod(x, T, sd, self.abv(l, v, 1), self.modv(l, v, 3), lambda k: h2.ap[:, k * T:(k + 1) * T], h2.t, tt_,
                           [self.mod_t])
            self.dma("gpsimd", S["H2"][:, tok0:tok0 + T].rearrange("(k p) t -> p k t", p=128), kt(h2.ap[:, 0:KC * T]), [h2.t], [self.wtok])

    def phase_D(self, l):
        c, S = self.c, self.S
        KC, FC = c.KC, c.FC
        ar = self.ar
        ar.reset()
        TM = 512
        TH = TM + 2
        x = Buf(ar.f32(KC * TM))
        h2 = Buf(ar.bf16(KC * TH))
        gbuf = Buf(ar.bf16(FC * TM))
        ab = [Buf(ar.f32(TH)) for _ in range(2)]
        ac = [Buf(ar.f32(TM)) for _ in range(2)]
        sl = [Buf(ar.f32(TM)) for _ in range(2)]
        sq = [Buf(ar.f32(TM)) for _ in range(2)]
        tmp = sq
        sd = Buf(ar.f32(TM))
        last_layer = (l == c.DEPTH - 1)
        if last_layer:
            yf = [Buf(ar.f32(128)) for _ in range(2)]
            ost = [Buf(ar.f32(c.D)) for _ in range(1)] * 2
        WU, WD = S[f"WUP{l}"], S[f"WDN{l}"]
        self.set_wbufs(c.UMAX)
        psA = [self.PS[0], self.PS[1]]
        psU = [self.PS[2], self.PS[3]]
        psH = self.PS[4]
        psD = [self.PS[5], self.PS[6]]
        fw = [self.prm[f"fcw{t}_{l}"] for t in range(3)]
        fb = self.prm[f"fcb_{l}"]
        oi = 0
        for (tok0, T, v, seq, first, last) in self.tiles:
            self.p.barrier()
            TT = T + 2
            hv = h2.ap[:, 0:KC * TT].rearrange("p (k t) -> p k t", t=TT)
            kt = lambda a: a.rearrange("p (k t) -> p k t", t=T)
            self.dma("gpsimd", kt(x.ap[:, 0:KC * T]), S["XT"][:, tok0:tok0 + T].rearrange("(k p) t -> p k t", p=128), [self.wtok], [x.t])
            lo = tok0 - (0 if first else 1)
            hi = tok0 + T + (0 if last else 1)
            if first:
                self.memset(hv[:, :, 0:1], 0.0, [h2.t])
            if last:
                self.memset(hv[:, :, TT - 1:TT], 0.0, [h2.t])
            self.dma("gpsimd", hv[:, :, (1 if first else 0):(TT - 1 if last else TT)],
                     S["H2"][:, lo:hi].rearrange("(k p) t -> p k t", p=128), [self.wtok], [h2.t])
            for i in range(FC):
                u = self.wload(WU[i])
                pa, pu = psA[i % 2], psU[i % 2]
                for k in range(KC):
                    self.mm(pa.ap[:, 0:T], u.ap[:, k * 128:(k + 1) * 128], hv[:, k, 1:1 + T], k == 0, k == KC - 1, [u.t, h2.t], [pa.t])
                for k in range(KC):
                    self.mm(psH.ap[:, 2 * i:2 * i + 2], u.ap[:, k * 128:(k + 1) * 128], hv[:, k, 0:TT:TT - 1], k == 0, k == KC - 1,
                            [u.t, h2.t], [psH.t])
                for k in range(KC):
                    self.mm(pu.ap[:, 0:T], u.ap[:, (KC + k) * 128:(KC + k + 1) * 128], hv[:, k, 1:1 + T], k == 0, k == KC - 1,
                            [u.t, h2.t], [pu.t])
                a = ab[i % 2]
                self.cp(a.ap[:, 1:1 + T], pa.ap[:, 0:T], [pa.t], [a.t], eng="scalar")
                self.cp(a.ap[:, 0:TT:TT - 1], psH.ap[:, 2 * i:2 * i + 2], [psH.t], [a.t], eng="vector")
                cc = ac[i % 2]
                self.ts(cc.ap[:, 0:T], a.ap[:, 0:T], fw[0][:, i:i + 1], fb[:, i:i + 1], ALU.mult, ALU.add, [a.t, self.prm_t], [cc.t])
                self.stt(cc.ap[:, 0:T], a.ap[:, 1:1 + T], fw[1][:, i:i + 1], cc.ap[:, 0:T], ALU.mult, ALU.add, [a.t, cc.t, self.prm_t], [cc.t])
                self.stt(cc.ap[:, 0:T], a.ap[:, 2:2 + T], fw[2][:, i:i + 1], cc.ap[:, 0:T], ALU.mult, ALU.add, [a.t, cc.t, self.prm_t], [cc.t])
                s = sl[i % 2]
                self.act(s.ap[:, 0:T], cc.ap[:, 0:T], AF.Silu, [cc.t], [s.t])
                self.tt(gbuf.ap[:, i * T:(i + 1) * T], pu.ap[:, 0:T], s.ap[:, 0:T], ALU.mult, [pu.t, s.t], [gbuf.t])
            for n in range(KC):
                u = self.wload(WD[n])
                pd = psD[n % 2]
                for f in range(FC):
                    self.mm(pd.ap[:, 0:T], u.ap[:, f * 128:(f + 1) * 128], gbuf.ap[:, f * T:(f + 1) * T], f == 0, f == FC - 1,
                            [u.t, gbuf.t], [pd.t])
                self.stt(x.ap[:, n * T:(n + 1) * T], pd.ap[:, 0:T], self.modv(l, v, 5)[:, n:n + 1], x.ap[:, n * T:(n + 1) * T],
                         ALU.mult, ALU.add, [pd.t, x.t, self.mod_t], [x.t])
            if not last_layer:
                self.dma("gpsimd", S["XT"][:, tok0:tok0 + T].rearrange("(k p) t -> p k t", p=128), kt(x.ap[:, 0:KC * T]), [x.t], [self.wtok])
            else:
                self.norm_mod(x, T, None, None, None, self.PS[7], sq, sd, tmp)
                dst = self.O["ys"] if v == 0 else self.O["yp"]
                r0 = tok0 if v == 0 else tok0 - c.NS
                gF = self.prm["gfinal"]
                for tb in range(T // 128):
                    o = ost[oi % 2]
                    oi += 1
                    for k0 in range(0, KC, 4):
                        ps = psA[(k0 // 4) % 2] if (k0 // 4) % 4 < 2 else psU[(k0 // 4) % 2]
                        for kk in range(4):
                            k = k0 + kk
                            y = yf[k % 2]
                            self.stt(y.ap[:, 0:128], x.ap[:, k * T + tb * 128:k * T + (tb + 1) * 128], gF[:, k:k + 1],
                                     sd.ap[:, tb * 128:(tb + 1) * 128], ALU.mult, ALU.mult, [x.t, sd.t, self.prm_t], [y.t])
                            self.tr(ps.ap[:, kk * 128:(kk + 1) * 128], y.ap[:, 0:128], self.ident.ap, [y.t, self.ident.t], [ps.t])
                        self.cp(o.ap[:, k0 * 128:(k0 + 4) * 128], ps.ap[:, 0:512], [ps.t], [o.t], eng=("scalar" if (k0 // 4) % 2 else "vector"))
                    self.dma("gpsimd", dst[r0 + tb * 128:r0 + (tb + 1) * 128, :], o.ap, [o.t], [self.wtok])

    def build(self, stop=None):
        c = self.c
        self.wtok = None
        for nm, fn in (("consts", self.setup_consts), ("params", self.load_params), ("convert", self.convert_weights),
                       ("mod", self.compute_mod), ("misc", self.setup_misc), ("rope", self.setup_rope)):
            fn()
            self.p.barrier()
            if stop == nm:
                self.p.finish()
                return self.nc
        for l in range(c.DEPTH):
            done = False
            for ph, fn in (("A", self.phase_A), ("B", self.phase_B), ("C", self.phase_C), ("D", self.phase_D)):
                fn(l)
                self.p.barrier()
                if stop == f"{ph}{l}":
                    done = True
                    break
            if done:
                break
        self.p.finish()
        return self.nc


def make_in_maps(cfg, inputs, n_cores):
    c = cfg
    maps = []
    wn = list(WSHAPES(c).keys())
    for i in range(n_cores):
        m = {}
        m["xs"] = np.ascontiguousarray(inputs["x_sample"][i])
        m["xp"] = np.ascontiguousarray(inputs["x_prompt"][i * c.NPB:(i + 1) * c.NPB]).reshape(c.NPB * c.SP, c.D)
        m["ck"] = np.ascontiguousarray(inputs["cache_k"][i]).reshape(c.DEPTH, c.PAST, c.DM)
        m["cv"] = np.ascontiguousarray(inputs["cache_v"][i]).reshape(c.DEPTH, c.PAST, c.DM)
        m["st"] = np.ascontiguousarray(inputs["state_lru"][i]).reshape(c.DEPTH, 2 * c.DM)
        m["cvec"] = np.concatenate([np.asarray(inputs["c"][i]).reshape(-1), np.asarray(inputs["c_ctx"]).reshape(-1)])
        for n in wn:
            m[n] = np.ascontiguousarray(inputs[n])
        maps.append(m)
    return maps


def gather_outputs(cfg, results, n_cores):
    c = cfg
    B = n_cores * c.NPB
    ys = np.stack([results[i]["ys"] for i in range(n_cores)], 0)
    yp = np.concatenate([results[i]["yp"].reshape(c.NPB, c.SP, c.D) for i in range(n_cores)], 0)
    nk = np.concatenate([results[i]["nk"] for i in range(n_cores)], 0).reshape(B, c.DEPTH, c.SP, c.DM // 128, 2, 64)
    nv = np.concatenate([results[i]["nv"] for i in range(n_cores)], 0).reshape(B, c.DEPTH, c.SP, c.DM // 128, 128)
    ns = np.concatenate([results[i]["nst"] for i in range(n_cores)], 0).reshape(B, c.DEPTH, 2, c.DM)
    return (yp.astype(np.float32), ys.astype(np.float32), nk.astype(np.float32), nv.astype(np.float32), ns.astype(np.float32))


def kernel(**inputs):
    inputs = {k: np.asarray(v, dtype=np.float32) for k, v in inputs.items()}
    cfg = Cfg()
    n = 8
    nc = K(cfg).build()
    maps = make_in_maps(cfg, inputs, n)
    res = run_bass_kernel_spmd(nc, maps, core_ids=list(range(n)))
    return gather_outputs(cfg, res.results, n)
```

```python
import contextlib
import math
import numpy as np
import concourse.bass as bass
import concourse.mybir as mybir
from concourse.bass_utils import run_bass_kernel_spmd

F32 = mybir.dt.float32
BF16 = mybir.dt.bfloat16
I32 = mybir.dt.int32
AF = mybir.ActivationFunctionType
ALU = mybir.AluOpType

COMPUTE = ("tensor", "vector", "scalar", "gpsimd")
QUEUES = ("sync", "scalar", "gpsimd")
ALL = ("sync", "tensor", "vector", "scalar", "gpsimd")
EPS = 1e-6


class Tok:
    __slots__ = ("w", "r")

    def __init__(self):
        self.w = None
        self.r = {}


class Prog:
    def __init__(self, nc, n_dma_sems=6):
        self.nc = nc
        self.es = contextlib.ExitStack()
        self.lists = {e: [] for e in ALL}
        self.sems = {}
        self.cnt = {}
        for e in COMPUTE:
            self.sems["c_" + e] = self.es.enter_context(nc.semaphore("c_" + e))
            self.cnt["c_" + e] = 0
        self.dpool = {}
        self.dnext = {}
        for q in QUEUES:
            keys = []
            for i in range(n_dma_sems if q != "scalar" else 3):
                k = f"d_{q}{i}"
                self.sems[k] = self.es.enter_context(nc.semaphore(k))
                self.cnt[k] = 0
                keys.append(k)
            self.dpool[q] = keys
            self.dnext[q] = 0
        self.waited = {e: {} for e in ALL}

    def sbuf(self, name, shape, dtype):
        return self.es.enter_context(self.nc.sbuf_tensor(name, list(shape), dtype))

    def psum(self, name, shape, dtype=F32):
        return self.es.enter_context(self.nc.psum_tensor(name, list(shape), dtype))

    def _collect(self, eng, reads, writes, extra=()):
        need = {}

        def add(ev):
            if ev is None:
                return
            k, v = ev
            if need.get(k, 0) < v:
                need[k] = v

        for t in reads:
            add(t.w)
        for t in writes:
            if t.r:
                for k, v in t.r.items():
                    add((k, v))
            else:
                add(t.w)
        for ev in extra:
            add(ev)
        out = []
        wd = self.waited[eng]
        own = "c_" + eng
        for k, v in need.items():
            if k == own and v > self.cnt.get(own, 0):
                continue
            if k == own and eng == "tensor":
                continue
            if wd.get(k, 0) < v:
                wd[k] = v
                out.append((self.sems[k], v))
        return out

    def _commit(self, ev, reads, writes):
        k, v = ev
        for t in reads:
            if t.r.get(k, 0) < v:
                t.r[k] = v
        for t in writes:
            t.w = ev
            t.r = {}

    def op(self, eng, fn, reads=(), writes=(), inc=True):
        waits = self._collect(eng, reads, writes)
        key = "c_" + eng
        sem = self.sems[key]
        if inc:
            self.cnt[key] += 1
            ev = (key, self.cnt[key])
        else:
            ev = None

        def emit(e, waits=waits, fn=fn, inc=inc, sem=sem):
            for s, v in waits:
                e.wait_ge(s, v)
            ins = fn(e)
            if inc:
                ins.then_inc(sem, 1)

        self.lists[eng].append(emit)
        if inc:
            self._commit(ev, reads, writes)
        else:
            nxt = (key, self.cnt[key] + 1)
            self._commit(nxt, reads, ())
            for t in writes:
                t.w = nxt
                t.r = {}
        return ev

    def dma(self, q, fn, reads=(), writes=()):
        pool = self.dpool[q]
        k = pool[self.dnext[q] % len(pool)]
        self.dnext[q] += 1
        prev = (k, self.cnt[k]) if self.cnt[k] > 0 else None
        waits = self._collect(q, reads, writes, extra=(prev,) if prev else ())
        self.cnt[k] += 16
        ev = (k, self.cnt[k])
        sem = self.sems[k]

        def emit(e, waits=waits, fn=fn, sem=sem):
            for s, v in waits:
                e.wait_ge(s, v)
            fn(e).then_inc(sem, 16)

        self.lists[q].append(emit)
        self._commit(ev, reads, writes)
        return ev

    def barrier(self):
        allev = [(k, v) for k, v in self.cnt.items() if v > 0]
        for eng in ALL:
            waits = []
            wd = self.waited[eng]
            for k, v in allev:
                if wd.get(k, 0) < v:
                    wd[k] = v
                    waits.append((self.sems[k], v))
            if waits:
                def emit(e, waits=waits):
                    for s, v in waits:
                        e.wait_ge(s, v)
                self.lists[eng].append(emit)

    def finish(self):
        self.barrier()
        lists = self.lists
        with self.nc.Block() as block:
            @block.sync
            def _(e):
                for f in lists["sync"]:
                    f(e)

            @block.tensor
            def _(e):
                for f in lists["tensor"]:
                    f(e)

            @block.vector
            def _(e):
                for f in lists["vector"]:
                    f(e)

            @block.scalar
            def _(e):
                for f in lists["scalar"]:
                    f(e)

            @block.gpsimd
            def _(e):
                for f in lists["gpsimd"]:
                    f(e)
        self.es.close()


class Buf:
    __slots__ = ("ap", "t")

    def __init__(self, ap):
        self.ap = ap
        self.t = Tok()


class Arena:
    def __init__(self, ap):
        self.ap = ap
        self.W = ap.shape[1]
        self.off = 0

    def reset(self):
        self.off = 0

    def f32(self, n):
        n2 = (n + 1) // 2 * 2
        a = self.ap[:, self.off:self.off + n]
        self.off += n2
        assert self.off <= self.W, ("arena overflow", self.off, self.W)
        return a

    def bf16(self, n):
        w = (n + 3) // 4 * 2
        a = self.ap[:, self.off:self.off + w].bitcast(BF16)[:, 0:n]
        self.off += w
        assert self.off <= self.W, ("arena overflow", self.off, self.W)
        return a

    def i32(self, n):
        return self.f32(n).bitcast(I32)


class Cfg:
    def __init__(self, D=2048, NS=4096, SP=256, NPB=4, PAST=512, DEPTH=2, GW=64, ARENA=31500, NWB=3):
        self.D = D
        self.KC = D // 128
        self.DM = D // 4
        self.MC = self.DM // 128
        self.DFF = ((8 * D // 3 + 127) // 128) * 128
        self.FC = self.DFF // 128
        self.NS, self.SP, self.NPB, self.PAST, self.DEPTH, self.GW = NS, SP, NPB, PAST, DEPTH, GW
        self.PC = PAST // 128
        self.NTOK = NS + NPB * SP
        self.NIN = 10 * self.DM + 4 * D
        self.NINC = self.NIN // 128
        self.TS = min(512, NS)
        self.ARENA = ARENA
        self.NWB = NWB
        self.LRU_TT = 256
        self.UMAX = max(self.KC * 128 * 2, self.FC * 128, 4 * self.MC * 128)
        assert self.MC >= 1 and NS % self.TS == 0 and SP % 128 == 0 and SP <= 512 and PAST % 128 == 0


WSHAPES = lambda c: dict(
    w_mod=(c.DEPTH, c.D, 6 * c.D), b_mod=(c.DEPTH, 6 * c.D), g_norm1=(c.DEPTH, c.D), g_norm2=(c.DEPTH, c.D),
    g_final=(c.D,), w_in=(c.DEPTH, c.D, c.NIN), lam_q1=(c.DEPTH, 64), lam_k1=(c.DEPTH, 64), lam_q2=(c.DEPTH, 64),
    lam_k2=(c.DEPTH, 64), g_subln=(c.DEPTH, 128), w_dw31=(c.DEPTH, 31, c.DM), b_dw31=(c.DEPTH, c.DM),
    g_ln_conv=(c.DEPTH, c.DM), b_ln_conv=(c.DEPTH, c.DM), w_dw3=(c.DEPTH, 3, c.DM), w_conv4=(c.DEPTH, 4, c.DM),
    b_conv4=(c.DEPTH, c.DM), w_rg_a=(c.DEPTH, 2, c.DM // 64, 64, 64), b_rg_a=(c.DEPTH, 2, c.DM),
    w_rg_x=(c.DEPTH, 2, c.DM // 64, 64, 64), b_rg_x=(c.DEPTH, 2, c.DM), lru_lambda=(c.DEPTH, 2, c.DM),
    w_branch=(c.DEPTH, 4, c.DM, c.D), w_out=(c.DEPTH, c.D, c.D), w_ffn_up=(c.DEPTH, c.D, 2 * c.DFF),
    w_ffn_conv=(c.DEPTH, 3, c.DFF), b_ffn_conv=(c.DEPTH, c.DFF), w_ffn_down=(c.DEPTH, c.DFF, c.D))


def rows128(ap):
    nd = len(ap.shape)
    if nd == 1:
        return ap.rearrange("(r c) -> r c", c=128)
    if nd == 2:
        return ap.rearrange("a (r c) -> (a r) c", c=128)
    raise ValueError


class K:
    def __init__(self, cfg):
        c = self.c = cfg
        nc = self.nc = bass.Bass("TRN2", target_bir_lowering=False)
        self.p = Prog(nc)
        din = lambda n, s: nc.dram_tensor(n, list(s), F32, kind="ExternalInput").ap()
        dout = lambda n, s: nc.dram_tensor(n, list(s), F32, kind="ExternalOutput").ap()
        dscr = lambda n, s, dt: nc.dram_tensor(n, list(s), dt, kind="Internal").ap()
        self.I = dict(xs=din("xs", (c.NS, c.D)), xp=din("xp", (c.NPB * c.SP, c.D)),
                      ck=din("ck", (c.DEPTH, c.PAST, c.DM)), cv=din("cv", (c.DEPTH, c.PAST, c.DM)),
                      st=din("st", (c.DEPTH, 2 * c.DM)), cvec=din("cvec", (2 * c.D,)))
        for n, s in WSHAPES(c).items():
            self.I[n] = din(n, s)
        self.O = dict(ys=dout("ys", (c.NS, c.D)), yp=dout("yp", (c.NPB * c.SP, c.D)),
                      nk=dout("nk", (c.NPB, c.DEPTH, c.SP, c.DM)), nv=dout("nv", (c.NPB, c.DEPTH, c.SP, c.DM)),
                      nst=dout("nst", (c.NPB, c.DEPTH, 2 * c.DM)))
        S = self.S = {}
        S["XT"] = dscr("s_xt", (c.D, c.NTOK), F32)
        S["HT"] = dscr("s_ht", (c.D, c.NTOK), BF16)
        S["H2"] = dscr("s_h2", (c.D, c.NTOK), BF16)
        S["Q"] = dscr("s_q", (c.DM, c.NTOK), BF16)
        S["KT"] = dscr("s_k", (c.DM, c.NTOK), BF16)
        S["V"] = dscr("s_v", (c.NTOK, c.DM), BF16)
        for n in ("UB", "GCX", "GB", "XR", "GY"):
            S[n] = dscr("s_" + n, (c.DM, c.NTOK), F32)
        S["Y"] = dscr("s_y", (4 * c.DM, c.NTOK), BF16)
        S["RC"] = dscr("s_rc", (128, c.NS), F32)
        S["RS"] = dscr("s_rs", (128, c.NS), F32)
        for l in range(c.DEPTH):
            S[f"WIN{l}"] = dscr(f"w_in_b{l}", (c.NINC, 128, c.KC * 128), BF16)
            S[f"WBR{l}"] = dscr(f"w_br_b{l}", (c.KC, 128, 4 * c.MC * 128), BF16)
            S[f"WOUT{l}"] = dscr(f"w_out_b{l}", (c.KC, 128, c.KC * 128), BF16)
            S[f"WUP{l}"] = dscr(f"w_up_b{l}", (c.FC, 128, 2 * c.KC * 128), BF16)
            S[f"WDN{l}"] = dscr(f"w_dn_b{l}", (c.KC, 128, c.FC * 128), BF16)
        p = self.p
        self.ident = Buf(p.sbuf("ident", (128, 128), F32)[:])
        self.ones = Buf(p.sbuf("ones", (128, 128), F32)[:])
        self.onesb = Buf(p.sbuf("onesb", (128, 128), BF16)[:])
        self.pm = Buf(p.sbuf("pm", (128, 128), F32)[:])
        self.wball = p.sbuf("wball", (128, c.NWB * c.UMAX), BF16)[:]
        self.set_wbufs(c.UMAX)
        self.PS = [Buf(p.psum(f"ps{i}", (128, 512))[:]) for i in range(8)]
        self.ar = Arena(p.sbuf("arena", (128, c.ARENA), F32)[:])
        self.tiles = [(i * c.TS, c.TS, 0, 0, i == 0, i == c.NS // c.TS - 1) for i in range(c.NS // c.TS)] + \
                     [(c.NS + b * c.SP, c.SP, 1, 1 + b, True, True) for b in range(c.NPB)]
        self.seqs = [(0, c.NS, True, 0)] + [(c.NS + b * c.SP, c.SP, False, b) for b in range(c.NPB)]

    def mm(self, out, lhsT, rhs, start, stop, reads, writes, inc=None):
        self.p.op("tensor", lambda e: e.matmul(out, lhsT=lhsT, rhs=rhs, start=start, stop=stop),
                  reads, writes, inc=True)

    def tr(self, out, in_, ident, reads, writes, inc=True):
        self.p.op("tensor", lambda e: e.transpose(out=out, in_=in_, identity=ident), reads, writes, inc=inc)

    def act(self, out, in_, func, reads, writes, scale=1.0, bias=0.0):
        self.p.op("scalar", lambda e: e.activation(out=out, in_=in_, func=func, bias=bias, scale=scale), reads, writes)

    def tt(self, out, in0, in1, op, reads, writes, eng="vector"):
        self.p.op(eng, lambda e: e.tensor_tensor(out=out, in0=in0, in1=in1, op=op), reads, writes)

    def ts(self, out, in0, s1, s2, op0, op1, reads, writes, eng="vector"):
        if op1 is None:
            self.p.op(eng, lambda e: e.tensor_scalar(out=out, in0=in0, scalar1=s1, scalar2=None, op0=op0), reads, writes)
        else:
            self.p.op(eng, lambda e: e.tensor_scalar(out=out, in0=in0, scalar1=s1, scalar2=s2, op0=op0, op1=op1), reads, writes)

    def stt(self, out, in0, scalar, in1, op0, op1, reads, writes):
        self.p.op("vector", lambda e: e.scalar_tensor_tensor(out=out, in0=in0, scalar=scalar, in1=in1, op0=op0, op1=op1),
                  reads, writes)

    def cp(self, out, in_, reads, writes, eng="vector"):
        if eng == "scalar":
            self.p.op("scalar", lambda e: e.copy(out=out, in_=in_), reads, writes)
        else:
            self.p.op(eng, lambda e: e.tensor_copy(out=out, in_=in_), reads, writes)

    def recip(self, out, in_, reads, writes):
        self.p.op("vector", lambda e: e.reciprocal(out=out, in_=in_), reads, writes)

    def memset(self, ap, val, writes, eng="gpsimd"):
        self.p.op(eng, lambda e: e.memset(ap, val), (), writes)

    def dma(self, q, out, in_, reads, writes):
        reads = [t for t in reads if t is not None]
        writes = [t for t in writes if t is not None]
        self.p.dma(q, lambda e: e.dma_start(out=out, in_=in_), reads, writes)

    def set_wbufs(self, E):
        n = self.wball.shape[1] // E
        self.wb = [Buf(self.wball[:, i * E:(i + 1) * E]) for i in range(n)]
        self.wbi = 0

    def wload(self, src):
        b = self.wb[self.wbi % len(self.wb)]
        self.wbi += 1
        E = src.shape[1]
        self.dma("sync", b.ap[:, 0:E], src, [self.wtok], [b.t])
        return b

    def setup_consts(self):
        p = self.p
        self.memset(self.ident.ap, 0.0, [self.ident.t])
        p.op("gpsimd", lambda e: e.affine_select(out=self.ident.ap, in_=self.ident.ap, compare_op=ALU.not_equal, fill=1.0,
                                                 base=0, pattern=[[-1, 128]], channel_multiplier=1),
             [self.ident.t], [self.ident.t])
        self.memset(self.ones.ap, 1.0, [self.ones.t])
        self.memset(self.onesb.ap, 1.0, [self.onesb.t])

    def load_params(self):
        c, I = self.c, self.I
        ents = []

        def add(name, ap):
            r = rows128(ap)
            ents.append((name, r, r.shape[0]))

        add("gfinal", I["g_final"])
        add("cvec", I["cvec"])
        for l in range(c.DEPTH):
            add(f"g1_{l}", I["g_norm1"][l])
            add(f"g2_{l}", I["g_norm2"][l])
            add(f"bmod_{l}", I["b_mod"][l])
            add(f"dw31_{l}", I["w_dw31"][l])
            for n in ("b_dw31", "g_ln_conv", "b_ln_conv", "b_conv4", "g_subln"):
                add(f"{n}_{l}", I[n][l])
            for n in ("w_dw3", "w_conv4", "b_rg_a", "b_rg_x", "lru_lambda"):
                add(f"{n}_{l}", I[n][l])
            add(f"st_{l}", I["st"][l])
            for t in range(3):
                add(f"fcw{t}_{l}", I["w_ffn_conv"][l, t])
            add(f"fcb_{l}", I["b_ffn_conv"][l])
        tiles_ = [[]]
        used = 0
        for name, r, R in ents:
            assert R <= 128
            if used + R > 128:
                tiles_.append([])
                used = 0
            tiles_[-1].append((name, r, R, used))
            used += R
        NT = len(tiles_)
        self.PRM = self.p.sbuf("prm", (128, NT * 128), F32)[:]
        self.prm_t = Tok()
        self.prm = {}
        stg = [Buf(self.ar.f32(128)) for _ in range(2)]
        for s in stg:
            self.memset(s.ap, 0.0, [s.t])
        for ti, tl in enumerate(tiles_):
            s = stg[ti % 2]
            ps = self.PS[ti % 2]
            for name, r, R, r0 in tl:
                self.dma("gpsimd", s.ap[r0:r0 + R, :], r, [], [s.t])
                self.prm[name] = self.PRM[:, ti * 128 + r0: ti * 128 + r0 + R]
            self.tr(ps.ap[:, 0:128], s.ap, self.ident.ap, [s.t, self.ident.t], [ps.t])
            self.cp(self.PRM[:, ti * 128:(ti + 1) * 128], ps.ap[:, 0:128], [ps.t], [self.prm_t])

    def conv_gen(self, layers, st32, st16, engs, CB=2048):
        c, I, S = self.c, self.I, self.S
        NB = len(st32)
        step = [0]

        def conv(src, dst, u0, offf):
            Kr, N = src.shape
            for kc in range(Kr // 128):
                for n0 in range(0, N, CB):
                    nn = min(CB, N - n0)
                    nb = nn // 128
                    i = step[0] % NB
                    step[0] += 1
                    a, b = st32[i], st16[i]
                    self.dma("sync", a.ap[:, 0:nn], src[kc * 128:(kc + 1) * 128, n0:n0 + nn], [], [a.t])
                    self.cp(b.ap[:, 0:nn], a.ap[:, 0:nn], [a.t], [b.t], eng=engs[step[0] % len(engs)])
                    off = offf(kc)
                    j0 = u0 + n0 // 128
                    self.dma("scalar", dst[j0:j0 + nb, :, off:off + 128].rearrange("u p c -> p u c"),
                             b.ap[:, 0:nn].rearrange("p (u c) -> p u c", c=128), [b.t], [self.wtok])
                    yield 1

        for l in layers:
            yield from conv(I["w_in"][l], S[f"WIN{l}"], 0, lambda kc: kc * 128)
            for j in range(4):
                yield from conv(I["w_branch"][l, j], S[f"WBR{l}"], 0, lambda kc, j=j: (j * c.MC + kc) * 128)
            yield from conv(I["w_out"][l], S[f"WOUT{l}"], 0, lambda kc: kc * 128)
            for s_ in range(2):
                yield from conv(I["w_ffn_up"][l][:, s_ * c.DFF:(s_ + 1) * c.DFF], S[f"WUP{l}"], 0,
                                lambda kc, s_=s_: (s_ * c.KC + kc) * 128)
            yield from conv(I["w_ffn_down"][l], S[f"WDN{l}"], 0, lambda kc: kc * 128)

    def convert_weights(self):
        c = self.c
        self.ar.reset()
        NB = 4
        st32 = [Buf(self.ar.f32(2048)) for _ in range(NB)]
        st16 = [Buf(self.ar.bf16(2048)) for _ in range(NB)]
        first = [0] if c.DEPTH > 1 else list(range(c.DEPTH))
        for _ in self.conv_gen(first, st32, st16, ("vector", "scalar", "vector", "gpsimd", "scalar")):
            pass
        self.late_gen = None
        if c.DEPTH > 1:
            wf = self.wball.bitcast(F32) if False else None
            n16 = self.wball.shape[1]
            CBL = min(2048, (n16 // 6) // 128 * 128)
            nst = n16 // (3 * CBL)
            assert nst >= 2 and CBL >= 128
            l32, l16 = [], []
            for i in range(nst):
                base = i * 3 * CBL
                l32.append(Buf(self.wball[:, base:base + 2 * CBL].bitcast(F32)))
                l16.append(Buf(self.wball[:, base + 2 * CBL:base + 3 * CBL]))
            self.late_gen = self.conv_gen(list(range(1, c.DEPTH)), l32, l16, ("gpsimd", "scalar"), CB=CBL)

    def late_step(self, n=1):
        g = getattr(self, "late_gen", None)
        if g is None:
            return
        for _ in range(n):
            if next(g, None) is None:
                self.late_gen = None
                return

    def late_drain(self):
        g = getattr(self, "late_gen", None)
        if g is not None:
            for _ in g:
                pass
            self.late_gen = None

    def compute_mod(self):
        c, I = self.c, self.I
        KC = c.KC
        self.ar.reset()
        NMC = 6 * KC
        self.MOD = self.p.sbuf("mod", (128, c.DEPTH * NMC * 2), F32)[:]
        self.AB = self.p.sbuf("ab", (128, c.DEPTH * 2 * 2 * KC), F32)[:]
        self.mod_t = Tok()
        scv = Buf(self.ar.f32(2 * KC))
        self.act(scv.ap, self.prm["cvec"], AF.Silu, [self.prm_t], [scv.t])
        CBK = 256
        wst = [Buf(self.ar.f32(KC * CBK)) for _ in range(2)]
        ps = self.PS[2]
        k = 0
        for l in range(c.DEPTH):
            for cb in range(6 * c.D // CBK):
                w = wst[k % 2]
                k += 1
                self.dma("sync", w.ap.rearrange("p (k n) -> p k n", n=CBK),
                         I["w_mod"][l][:, cb * CBK:(cb + 1) * CBK].rearrange("(k p) n -> p k n", p=128), [], [w.t])
                for nn in range(CBK // 128):
                    n = cb * (CBK // 128) + nn
                    for kc in range(KC):
                        self.mm(ps.ap[:, 2 * n:2 * n + 2], w.ap[:, kc * CBK + nn * 128: kc * CBK + nn * 128 + 128],
                                scv.ap[:, kc::KC], kc == 0, kc == KC - 1, [w.t, scv.t], [ps.t])
            for v in range(2):
                base = (l * 2 + v) * NMC
                self.tt(self.MOD[:, base:base + NMC], ps.ap[:, v:2 * NMC:2], self.prm[f"bmod_{l}"], ALU.add,
                        [ps.t, self.prm_t], [self.mod_t])
                for s, (gname, scoff) in enumerate((("g1", KC), ("g2", 4 * KC))):
                    o = ((l * 2 + v) * 2 + s) * KC
                    self.stt(self.AB[:, o:o + KC], self.MOD[:, base + scoff: base + scoff + KC], 1.0,
                             self.prm[f"{gname}_{l}"], ALU.add, ALU.mult, [self.mod_t, self.prm_t], [self.mod_t])

    def modv(self, l, v, which):
        KC = self.c.KC
        base = (l * 2 + v) * 6 * KC + which * KC
        return self.MOD[:, base:base + KC]

    def abv(self, l, v, s):
        KC = self.c.KC
        o = ((l * 2 + v) * 2 + s) * KC
        return self.AB[:, o:o + KC]

    def setup_misc(self):
        c, I = self.c, self.I
        MC = c.MC
        self.ar.reset()
        self.MISC = self.p.sbuf("misc", (128, c.DEPTH * (4 + 4 * MC)), F32)[:]
        self.misc_t = Tok()
        self.BD = self.p.sbuf("bd", (128, c.DEPTH * 4 * MC * 128), BF16)[:]
        self.bd_t = Tok()
        lamst = Buf(self.ar.f32(4 * 64))
        tmp = Buf(self.ar.f32(8))
        bst = Buf(self.ar.f32(4 * MC * 128))
        for l in range(c.DEPTH):
            mb = l * (4 + 4 * MC)
            for i, n in enumerate(("lam_q1", "lam_k1", "lam_q2", "lam_k2")):
                self.dma("gpsimd", lamst.ap[:, i * 64:(i + 1) * 64], I[n][l].partition_broadcast(128), [], [lamst.t])
            for i in range(2):
                self.tt(lamst.ap[:, i * 128:i * 128 + 64], lamst.ap[:, i * 128:i * 128 + 64],
                        lamst.ap[:, i * 128 + 64:i * 128 + 128], ALU.mult, [lamst.t], [lamst.t])
                self.p.op("vector", lambda e, i=i: e.reduce_sum(out=tmp.ap[:, i:i + 1], in_=lamst.ap[:, i * 128:i * 128 + 64],
                                                               axis=mybir.AxisListType.X), [lamst.t], [tmp.t])
            self.act(tmp.ap[:, 2:4], tmp.ap[:, 0:2], AF.Exp, [tmp.t], [tmp.t])
            lam_init = 0.8 - 0.6 * math.exp(-0.3 * l)
            self.stt(self.MISC[:, mb:mb + 1], tmp.ap[:, 3:4], -lam_init, tmp.ap[:, 2:3], ALU.add, ALU.subtract,
                     [tmp.t], [self.misc_t])
            self.ts(self.MISC[:, mb + 1:mb + 2], self.prm[f"g_subln_{l}"], 1.0 - lam_init, None, ALU.mult, None,
                    [self.prm_t], [self.misc_t])
            sp = Buf(self.ar.f32(2 * MC))
            self.act(sp.ap, self.prm[f"lru_lambda_{l}"], AF.Exp, [self.prm_t], [sp.t], scale=-1.0)
            self.act(sp.ap, sp.ap, AF.Ln, [sp.t], [sp.t], bias=1.0)
            self.ts(self.MISC[:, mb + 4:mb + 4 + 2 * MC], sp.ap, -8.0, None, ALU.mult, None, [sp.t], [self.misc_t])
            self.ts(self.MISC[:, mb + 4 + 2 * MC:mb + 4 + 4 * MC], sp.ap, -16.0, None, ALU.mult, None, [sp.t], [self.misc_t])
            self.memset(bst.ap, 0.0, [bst.t])
            for g, n in enumerate(("w_rg_a", "w_rg_x")):
                for d in range(2):
                    for m in range(MC):
                        o = ((g * 2 + d) * MC + m) * 128
                        for hb in range(2):
                            self.dma("gpsimd", bst.ap[hb * 64:(hb + 1) * 64, o + hb * 64:o + hb * 64 + 64],
                                     I[n][l, d, 2 * m + hb], [], [bst.t])
            self.cp(self.BD[:, l * 4 * MC * 128:(l + 1) * 4 * MC * 128], bst.ap, [bst.t], [self.bd_t])

    def misc(self, l, i):
        mb = l * (4 + 4 * self.c.MC)
        return self.MISC[:, mb + i:mb + i + 1]

    def setup_rope(self):
        c = self.c
        self.ar.reset()
        NS, GW = c.NS, c.GW
        pi_ = Buf(self.ar.i32(2))
        ti = Buf(self.ar.i32(8))
        tf = Buf(self.ar.f32(16))
        self.p.op("gpsimd", lambda e: e.iota(pi_.ap[:, 0:1], pattern=[[0, 1]], base=0, channel_multiplier=1), [], [pi_.t])
        sh = lambda o, s, m: self.p.op("vector", lambda e: e.tensor_scalar(out=ti.ap[:, o:o + 1], in0=pi_.ap[:, 0:1], scalar1=s,
                                                                          scalar2=m, op0=ALU.arith_shift_right,
                                                                          op1=ALU.bitwise_and), [pi_.t], [ti.t])
        sh(0, 0, 15)
        sh(1, 5, 1)
        sh(2, 4, 1)
        self.cp(tf.ap[:, 0:3], ti.ap[:, 0:3], [ti.t], [tf.t])
        self.act(tf.ap[:, 3:4], tf.ap[:, 0:1], AF.Exp, [tf.t], [tf.t], scale=-math.log(10000.0) / 16.0)
        self.tt(tf.ap[:, 5:6], tf.ap[:, 3:4], tf.ap[:, 1:2], ALU.mult, [tf.t], [tf.t])
        self.tt(tf.ap[:, 4:5], tf.ap[:, 3:4], tf.ap[:, 5:6], ALU.subtract, [tf.t], [tf.t])
        R = Buf(self.ar.f32(NS))
        Cc = Buf(self.ar.f32(NS))
        ang = Buf(self.ar.f32(NS))
        t2 = Buf(self.ar.f32(NS))
        rows = NS // GW
        self.p.op("gpsimd", lambda e: e.iota(R.ap.rearrange("p (r g) -> p r g", g=GW), pattern=[[1, rows], [0, GW]], base=0,
                                             channel_multiplier=0, allow_small_or_imprecise_dtypes=True), [], [R.t])
        self.p.op("gpsimd", lambda e: e.iota(Cc.ap.rearrange("p (r g) -> p r g", g=GW), pattern=[[0, rows], [1, GW]], base=0,
                                             channel_multiplier=0, allow_small_or_imprecise_dtypes=True), [], [Cc.t])
        self.ts(ang.ap, R.ap, tf.ap[:, 4:5], None, ALU.mult, None, [R.t, tf.t], [ang.t])
        self.stt(ang.ap, Cc.ap, tf.ap[:, 5:6], ang.ap, ALU.mult, ALU.add, [Cc.t, tf.t, ang.t], [ang.t])
        MAGIC = 12582912.0
        TWO_PI = 2.0 * math.pi
        for which, shift in ((0, math.pi / 2), (1, 0.0)):
            self.ts(t2.ap, ang.ap, shift, 1.0 / TWO_PI, ALU.add, ALU.mult, [ang.t], [t2.t])
            self.ts(t2.ap, t2.ap, MAGIC, MAGIC, ALU.add, ALU.subtract, [t2.t], [t2.t])
            self.stt(t2.ap, t2.ap, -TWO_PI, ang.ap, ALU.mult, ALU.add, [t2.t, ang.t], [t2.t])
            self.ts(t2.ap, t2.ap, shift, 3.14159, ALU.add, ALU.min, [t2.t], [t2.t])
            self.ts(t2.ap, t2.ap, -3.14159, None, ALU.max, None, [t2.t], [t2.t])
            self.act(t2.ap, t2.ap, AF.Sin, [t2.t], [t2.t])
            self.dma("gpsimd", self.S["RC" if which == 0 else "RS"], t2.ap, [t2.t], [self.wtok])
        A = Buf(self.ar.f32(128))
        B_ = Buf(self.ar.f32(128))
        mi = Buf(self.ar.i32(128))
        mf = Buf(self.ar.f32(128))
        self.memset(A.ap, 0.0, [A.t])
        self.memset(B_.ap, 0.0, [B_.t])
        self.p.op("gpsimd", lambda e: e.affine_select(out=A.ap, in_=A.ap, compare_op=ALU.not_equal, fill=-1.0, base=-16,
                                                      pattern=[[-1, 128]], channel_multiplier=1), [A.t], [A.t])
        self.p.op("gpsimd", lambda e: e.affine_select(out=B_.ap, in_=B_.ap, compare_op=ALU.not_equal, fill=1.0, base=16,
                                                      pattern=[[-1, 128]], channel_multiplier=1), [B_.t], [B_.t])
        self.p.op("gpsimd", lambda e: e.iota(mi.ap, pattern=[[1, 128]], base=0, channel_multiplier=0), [], [mi.t])
        self.p.op("vector", lambda e: e.tensor_scalar(out=mi.ap, in0=mi.ap, scalar1=4, scalar2=1, op0=ALU.arith_shift_right,
                                                      op1=ALU.bitwise_and), [mi.t], [mi.t])
        self.cp(mf.ap, mi.ap, [mi.t], [mf.t])
        self.tt(B_.ap, B_.ap, mf.ap, ALU.mult, [B_.t, mf.t], [B_.t])
        self.ts(mf.ap, mf.ap, -1.0, 1.0, ALU.mult, ALU.add, [mf.t], [mf.t])
        self.tt(A.ap, A.ap, mf.ap, ALU.mult, [A.t, mf.t], [A.t])
        self.tt(self.pm.ap, A.ap, B_.ap, ALU.add, [A.t, B_.t], [self.pm.t])

    def norm_mod(self, x, T, Acols, Bcols, out, ps, sq, sd, tmp, extra_reads=()):
        c = self.c
        KC = c.KC
        for k in range(KC):
            s = sq[k % len(sq)]
            self.act(s.ap[:, 0:T], x.ap[:, k * T:(k + 1) * T], AF.Square, [x.t], [s.t])
            self.mm(ps.ap[:, 0:T], self.ones.ap, s.ap[:, 0:T], k == 0, k == KC - 1, [s.t, self.ones.t], [ps.t])
        self.act(sd.ap[:, 0:T], ps.ap[:, 0:T], AF.Sqrt, [ps.t], [sd.t], scale=1.0 / c.D, bias=EPS)
        self.recip(sd.ap[:, 0:T], sd.ap[:, 0:T], [sd.t], [sd.t])

    def apply_mod(self, x, T, sd, Acols, Bcols, outfn, out_t, tmp, reads):
        KC = self.c.KC
        for k in range(KC):
            t = tmp[k % len(tmp)]
            self.tt(t.ap[:, 0:T], x.ap[:, k * T:(k + 1) * T], sd.ap[:, 0:T], ALU.mult, [x.t, sd.t], [t.t])
            if Bcols is None:
                self.act(outfn(k), t.ap[:, 0:T], AF.Copy, [t.t] + reads, [out_t], scale=Acols[:, k:k + 1])
            else:
                self.act(outfn(k), t.ap[:, 0:T], AF.Identity, [t.t] + reads, [out_t], scale=Acols[:, k:k + 1],
                         bias=Bcols[:, k:k + 1])

    def phase_A(self, l):
        c, S, I = self.c, self.S, self.I
        KC, MC, DM = c.KC, c.MC, c.DM
        ar = self.ar
        ar.reset()
        TM = 512
        xT = [Buf(ar.f32(KC * TM)) for _ in range(1)]
        hT = [Buf(ar.bf16(KC * TM)) for _ in range(1)]
        xin = [Buf(ar.f32(c.D)) for _ in range(2)] if l == 0 else []
        sq = [Buf(ar.f32(TM)) for _ in range(2)]
        tmp = [Buf(ar.f32(TM)) for _ in range(2)]
        sd = Buf(ar.f32(TM))
        rc = Buf(ar.f32(TM))
        rs = Buf(ar.f32(TM))
        ob = [Buf(ar.f32(TM)) for _ in range(4)]
        o16 = [Buf(ar.bf16(TM)) for _ in range(3)]
        xf = [Buf(ar.f32(TM)) for _ in range(2)]
        vst = Buf(ar.bf16(4 * DM))
        kvf = [Buf(ar.f32(4 * 128)) for _ in range(2)]
        obi = [0]
        o16i = [0]
        W = S[f"WIN{l}"]
        self.set_wbufs(KC * 128)
        psr = [self.PS[i] for i in (0, 1, 2, 3, 4)]
        pi = [0]

        def nps():
            b = psr[pi[0] % len(psr)]
            pi[0] += 1
            return b

        def fm(unit, h, T):
            ps = nps()
            for k in range(KC):
                self.mm(ps.ap[:, 0:T], unit.ap[:, k * 128:(k + 1) * 128], h.ap[:, k * T:(k + 1) * T], k == 0, k == KC - 1,
                        [unit.t, h.t], [ps.t])
            return ps

        def nob():
            b = ob[obi[0] % len(ob)]
            obi[0] += 1
            return b

        def no16():
            b = o16[o16i[0] % len(o16)]
            o16i[0] += 1
            return b

        for tix, (tok0, T, v, seq, first, last) in enumerate(self.tiles):
            if getattr(self, "a_tiles", None) is not None and tix not in self.a_tiles:
                continue
            self.p.barrier()
            x, h = xT[0], hT[0]
            TB = T // 128
            src_in = I["xs"] if v == 0 else I["xp"]
            r0 = tok0 if v == 0 else tok0 - c.NS
            if l == 0:
                for tb in range(TB):
                    xi = xin[tb % 2]
                    self.dma("gpsimd", xi.ap, src_in[r0 + tb * 128: r0 + (tb + 1) * 128, :], [], [xi.t])
                    for k0 in range(0, KC, 4):
                        ps = self.PS[5 + (k0 // 4) % 2]
                        for kk in range(4):
                            k = k0 + kk
                            self.tr(ps.ap[:, kk * 128:(kk + 1) * 128], xi.ap[:, k * 128:(k + 1) * 128], self.ident.ap,
                                    [xi.t, self.ident.t], [ps.t])
                        dst = x.ap[:, 0:KC * T].rearrange("p (k t) -> p k t", t=T)[:, k0:k0 + 4, tb * 128:(tb + 1) * 128]
                        self.cp(dst, ps.ap.rearrange("p (k t) -> p k t", t=128), [ps.t], [x.t],
                                eng=("vector" if (k0 // 4) % 2 else "scalar"))
                self.dma("gpsimd", S["XT"][:, tok0:tok0 + T].rearrange("(k p) t -> p k t", p=128),
                         x.ap[:, 0:KC * T].rearrange("p (k t) -> p k t", t=T), [x.t], [self.wtok])
            else:
                self.dma("gpsimd", x.ap[:, 0:KC * T].rearrange("p (k t) -> p k t", t=T),
                         S["XT"][:, tok0:tok0 + T].rearrange("(k p) t -> p k t", p=128), [self.wtok], [x.t])
            if getattr(self, "a_stop", 0) == 1:
                return
            self.norm_mod(x, T, None, None, None, self.PS[7], sq, sd, tmp)
            if getattr(self, "a_stop", 0) == 2:
                return
            self.apply_mod(x, T, sd, self.abv(l, v, 0), self.modv(l, v, 0), lambda k: h.ap[:, k * T:(k + 1) * T], h.t, tmp,
                           [self.mod_t])
            if getattr(self, "a_stop", 0) == 3:
                return
            self.dma("gpsimd", S["HT"][:, tok0:tok0 + T].rearrange("(k p) t -> p k t", p=128),
                     h.ap[:, 0:KC * T].rearrange("p (k t) -> p k t", t=T), [h.t], [self.wtok])
            if v == 0:
                self.dma("gpsimd", rc.ap[:, 0:T], S["RC"][:, tok0:tok0 + T], [self.wtok], [rc.t])
                self.dma("gpsimd", rs.ap[:, 0:T], S["RS"][:, tok0:tok0 + T], [self.wtok], [rs.t])
            for kind, base, dst in (("q", 0, S["Q"]), ("k", MC, S["KT"])):
                for m in range(MC):
                    u = self.wload(W[base + m])
                    ps = fm(u, h, T)
                    o = no16()
                    if v == 0:
                        f = xf[m % 2]
                        self.cp(f.ap[:, 0:T], ps.ap[:, 0:T], [ps.t], [f.t], eng="scalar")
                        ps2 = nps()
                        self.mm(ps2.ap[:, 0:T], self.pm.ap, f.ap[:, 0:T], True, True, [self.pm.t, f.t], [ps2.t])
                        t1 = tmp[m % 2]
                        self.tt(t1.ap[:, 0:T], f.ap[:, 0:T], rc.ap[:, 0:T], ALU.mult, [f.t, rc.t], [t1.t])
                        self.tt(f.ap[:, 0:T], ps2.ap[:, 0:T], rs.ap[:, 0:T], ALU.mult, [ps2.t, rs.t], [f.t])
                        self.tt(o.ap[:, 0:T], t1.ap[:, 0:T], f.ap[:, 0:T], ALU.add, [t1.t, f.t], [o.t])
                    else:
                        self.cp(o.ap[:, 0:T], ps.ap[:, 0:T], [ps.t], [o.t], eng="scalar")
                    self.dma("gpsimd", dst[m * 128:(m + 1) * 128, tok0:tok0 + T], o.ap[:, 0:T], [o.t], [self.wtok])
                    if kind == "k" and v == 1:
                        for tb in range(TB):
                            ps3 = nps()
                            for k in range(KC):
                                self.mm(ps3.ap[:, 0:128], h.ap[:, k * T + tb * 128:k * T + (tb + 1) * 128],
                                        u.ap[:, k * 128:(k + 1) * 128], k == 0, k == KC - 1, [u.t, h.t], [ps3.t])
                            kf = kvf[tb % 2]
                            self.cp(kf.ap[:, 0:128], ps3.ap[:, 0:128], [ps3.t], [kf.t])
                            self.dma("gpsimd", self.O["nk"][seq - 1, l, tb * 128:(tb + 1) * 128, m * 128:(m + 1) * 128],
                                     kf.ap[:, 0:128], [kf.t], [self.wtok])
            if getattr(self, "a_stop", 0) == 4:
                return
            for m in range(MC):
                u = self.wload(W[2 * MC + m])
                ps = nps()
                for tb in range(TB):
                    for k in range(KC):
                        self.mm(ps.ap[:, tb * 128:(tb + 1) * 128], h.ap[:, k * T + tb * 128:k * T + (tb + 1) * 128],
                                u.ap[:, k * 128:(k + 1) * 128], k == 0, k == KC - 1, [u.t, h.t], [ps.t],
                                inc=(k == KC - 1 and tb == TB - 1))
                dstv = vst.ap[:, 0:TB * DM].rearrange("p (b e) -> p b e", e=DM)[:, :, m * 128:(m + 1) * 128]
                if v == 0:
                    self.cp(dstv, ps.ap[:, 0:TB * 128].rearrange("p (b e) -> p b e", e=128), [ps.t], [vst.t])
                if v == 1:
                    kf = kvf[m % 2]
                    self.cp(kf.ap[:, 0:TB * 128], ps.ap[:, 0:TB * 128], [ps.t], [kf.t])
                    self.cp(dstv, kf.ap[:, 0:TB * 128].rearrange("p (b e) -> p b e", e=128), [kf.t], [vst.t])
                    for tb in range(TB):
                        self.dma("gpsimd", self.O["nv"][seq - 1, l, tb * 128:(tb + 1) * 128, m * 128:(m + 1) * 128],
                                 kf.ap[:, tb * 128:(tb + 1) * 128], [kf.t], [self.wtok])
            self.dma("gpsimd", S["V"][tok0:tok0 + T, :].rearrange("(b p) e -> p b e", p=128),
                     vst.ap[:, 0:TB * DM].rearrange("p (b e) -> p b e", e=DM), [vst.t], [self.wtok])
            if getattr(self, "a_stop", 0) == 5:
                return
            for m in range(MC):
                ua = self.wload(W[3 * MC + m])
                pa = fm(ua, h, T)
                ug = self.wload(W[4 * MC + m])
                pg = fm(ug, h, T)
                sg = tmp[m % 2]
                self.act(sg.ap[:, 0:T], pg.ap[:, 0:T], AF.Sigmoid, [pg.t], [sg.t])
                o = nob()
                self.tt(o.ap[:, 0:T], pa.ap[:, 0:T], sg.ap[:, 0:T], ALU.mult, [pa.t, sg.t], [o.t])
                self.dma("gpsimd", S["UB"][m * 128:(m + 1) * 128, tok0:tok0 + T], o.ap[:, 0:T], [o.t], [self.wtok])
            if getattr(self, "a_stop", 0) == 6:
                return
            for m in range(MC):
                ub_ = self.wload(W[5 * MC + m])
                pb = fm(ub_, h, T)
                o = nob()
                self.cp(o.ap[:, 0:T], pb.ap[:, 0:T], [pb.t], [o.t], eng="scalar")
                self.dma("gpsimd", S["GB"][m * 128:(m + 1) * 128, tok0:tok0 + T], o.ap[:, 0:T], [o.t], [self.wtok])
                uc = self.wload(W[6 * MC + m])
                pc = fm(uc, h, T)
                ux = self.wload(W[7 * MC + m])
                px = fm(ux, h, T)
                g = tmp[m % 2]
                self.cp(g.ap[:, 0:T], pc.ap[:, 0:T], [pc.t], [g.t], eng="scalar")
                o = nob()
                self.tt(o.ap[:, 0:T], px.ap[:, 0:T], g.ap[:, 0:T], ALU.mult, [px.t, g.t], [o.t])
                self.dma("gpsimd", S["GCX"][m * 128:(m + 1) * 128, tok0:tok0 + T], o.ap[:, 0:T], [o.t], [self.wtok])
            if getattr(self, "a_stop", 0) == 7:
                return
            for m in range(MC):
                u1 = self.wload(W[8 * MC + m])
                p1 = fm(u1, h, T)
                o = nob()
                self.cp(o.ap[:, 0:T], p1.ap[:, 0:T], [p1.t], [o.t])
                self.dma("gpsimd", S["XR"][m * 128:(m + 1) * 128, tok0:tok0 + T], o.ap[:, 0:T], [o.t], [self.wtok])
                u2 = self.wload(W[9 * MC + m])
                p2 = fm(u2, h, T)
                o = nob()
                self.act(o.ap[:, 0:T], p2.ap[:, 0:T], AF.Gelu, [p2.t], [o.t])
                self.dma("gpsimd", S["GY"][m * 128:(m + 1) * 128, tok0:tok0 + T], o.ap[:, 0:T], [o.t], [self.wtok])
            if getattr(self, "a_stop", 0) == 8:
                return

    def phase_B(self, l):
        self.B_conformer(l)
        if l == 0:
            self.late_drain()
        self.p.barrier()
        self.B_sconv_lru(l)
        self.p.barrier()
        self.B_attn(l)

    def B_conformer(self, l):
        c, S = self.c, self.S
        MC, DM = c.MC, c.DM
        ar = self.ar
        ar.reset()
        LM = c.NS
        cb = Buf(ar.f32(MC * LM))
        ubp = [Buf(ar.f32(LM + 30)) for _ in range(2)]
        sq = [Buf(ar.f32(512)) for _ in range(2)]
        mean = Buf(ar.f32(512))
        msq = Buf(ar.f32(512))
        rstd = Buf(ar.f32(512))
        t1 = [Buf(ar.f32(512)) for _ in range(2)]
        yo = [Buf(ar.bf16(512)) for _ in range(2)]
        w31 = self.prm[f"dw31_{l}"]
        for si, (tok0, L, is_s, b) in enumerate(self.seqs):
            self.p.barrier()
            for m in range(MC):
                u = ubp[m % 2]
                self.memset(u.ap[:, 0:15], 0.0, [u.t])
                self.memset(u.ap[:, 15 + L:30 + L], 0.0, [u.t])
                self.dma("gpsimd", u.ap[:, 15:15 + L], S["UB"][m * 128:(m + 1) * 128, tok0:tok0 + L], [self.wtok], [u.t])
                acc = cb.ap[:, m * LM:m * LM + L]
                self.ts(acc, u.ap[:, 0:L], w31[:, m:m + 1], self.prm[f"b_dw31_{l}"][:, m:m + 1], ALU.mult, ALU.add,
                        [u.t, self.prm_t], [cb.t])
                for j in range(1, 31):
                    self.stt(acc, u.ap[:, j:j + L], w31[:, j * MC + m:j * MC + m + 1], acc, ALU.mult, ALU.add,
                             [u.t, cb.t, self.prm_t], [cb.t])
                    if l == 0 and L >= 1024:
                        self.late_step(1)
                if l == 0 and L < 1024:
                    self.late_step(8)
            TT = min(512, L)
            for t0 in range(0, L, TT):
                p1, p2 = self.PS[0 + (t0 // TT) % 2 * 2], self.PS[1 + (t0 // TT) % 2 * 2]
                for m in range(MC):
                    x = cb.ap[:, m * LM + t0:m * LM + t0 + TT]
                    self.mm(p1.ap[:, 0:TT], self.ones.ap, x, m == 0, m == MC - 1, [cb.t, self.ones.t], [p1.t])
                    s = sq[m % 2]
                    self.act(s.ap[:, 0:TT], x, AF.Square, [cb.t], [s.t])
                    self.mm(p2.ap[:, 0:TT], self.ones.ap, s.ap[:, 0:TT], m == 0, m == MC - 1, [s.t, self.ones.t], [p2.t])
                self.ts(mean.ap[:, 0:TT], p1.ap[:, 0:TT], 1.0 / DM, None, ALU.mult, None, [p1.t], [mean.t])
                self.tt(msq.ap[:, 0:TT], mean.ap[:, 0:TT], mean.ap[:, 0:TT], ALU.mult, [mean.t], [msq.t])
                self.stt(msq.ap[:, 0:TT], p2.ap[:, 0:TT], 1.0 / DM, msq.ap[:, 0:TT], ALU.mult, ALU.subtract, [p2.t, msq.t], [msq.t])
                self.act(rstd.ap[:, 0:TT], msq.ap[:, 0:TT], AF.Sqrt, [msq.t], [rstd.t], bias=EPS)
                self.recip(rstd.ap[:, 0:TT], rstd.ap[:, 0:TT], [rstd.t], [rstd.t])
                for m in range(MC):
                    x = cb.ap[:, m * LM + t0:m * LM + t0 + TT]
                    t = t1[m % 2]
                    self.tt(t.ap[:, 0:TT], x, mean.ap[:, 0:TT], ALU.subtract, [cb.t, mean.t], [t.t])
                    self.tt(t.ap[:, 0:TT], t.ap[:, 0:TT], rstd.ap[:, 0:TT], ALU.mult, [t.t, rstd.t], [t.t])
                    y = yo[m % 2]
                    self.act(y.ap[:, 0:TT], t.ap[:, 0:TT], AF.Silu, [t.t, self.prm_t], [y.t],
                             scale=self.prm[f"g_ln_conv_{l}"][:, m:m + 1], bias=self.prm[f"b_ln_conv_{l}"][:, m:m + 1])
                    self.dma("gpsimd", S["Y"][DM + m * 128:DM + (m + 1) * 128, tok0 + t0:tok0 + t0 + TT], y.ap[:, 0:TT],
                             [y.t], [self.wtok])

    def B_sconv_lru(self, l):
        c, S = self.c, self.S
        MC, DM = c.MC, c.DM
        ar = self.ar
        ar.reset()
        LM = c.NS
        xp = Buf(ar.f32(LM + 4))
        gy = Buf(ar.f32(LM))
        gp = [xp, xp]
        gb = [gy, gy]
        yo = [Buf(ar.bf16(LM)) for _ in range(1)] * 2
        xr = Buf(ar.f32(LM))
        xrb = Buf(ar.bf16(LM))
        hs = [Buf(ar.f32(LM)) for _ in range(2)]
        tA = [Buf(ar.f32(512)) for _ in range(8)]
        hl = Buf(ar.f32(128))
        hlt = Buf(ar.f32(128))
        w3 = self.prm[f"w_dw3_{l}"]
        w4 = self.prm[f"w_conv4_{l}"]
        bd0 = l * 4 * MC * 128
        self.memset(hl.ap, 0.0, [hl.t])
        for si, (tok0, L, is_s, b) in enumerate(self.seqs):
            self.p.barrier()
            for m in range(MC):
                self.p.barrier()
                g = gp[m % 2]
                self.memset(g.ap[:, 0:1], 0.0, [g.t])
                self.memset(g.ap[:, L + 1:L + 2], 0.0, [g.t])
                self.dma("gpsimd", g.ap[:, 1:1 + L], S["GCX"][m * 128:(m + 1) * 128, tok0:tok0 + L], [self.wtok], [g.t])
                gg = gb[m % 2]
                self.dma("gpsimd", gg.ap[:, 0:L], S["GB"][m * 128:(m + 1) * 128, tok0:tok0 + L], [self.wtok], [gg.t])
                acc = xr
                self.ts(acc.ap[:, 0:L], g.ap[:, 0:L], w3[:, m:m + 1], None, ALU.mult, None, [g.t, self.prm_t], [acc.t])
                for j in (1, 2):
                    self.stt(acc.ap[:, 0:L], g.ap[:, j:j + L], w3[:, j * MC + m:j * MC + m + 1], acc.ap[:, 0:L], ALU.mult, ALU.add,
                             [g.t, acc.t, self.prm_t], [acc.t])
                y = yo[m % 2]
                self.tt(y.ap[:, 0:L], acc.ap[:, 0:L], gg.ap[:, 0:L], ALU.mult, [acc.t, gg.t], [y.t])
                self.dma("gpsimd", S["Y"][2 * DM + m * 128:2 * DM + (m + 1) * 128, tok0:tok0 + L], y.ap[:, 0:L], [y.t], [self.wtok])
            TT = min(self.c.LRU_TT, L)
            NT = L // TT
            for m in range(MC):
                self.p.barrier()
                self.memset(xp.ap[:, 0:1], 0.0, [xp.t])
                self.memset(xp.ap[:, L + 1:L + 3], 0.0, [xp.t])
                self.dma("gpsimd", xp.ap[:, 1:1 + L], S["XR"][m * 128:(m + 1) * 128, tok0:tok0 + L], [self.wtok], [xp.t])
                self.dma("gpsimd", gy.ap[:, 0:L], S["GY"][m * 128:(m + 1) * 128, tok0:tok0 + L], [self.wtok], [gy.t])
                self.ts(xr.ap[:, 0:L], xp.ap[:, 0:L], w4[:, m:m + 1], self.prm[f"b_conv4_{l}"][:, m:m + 1], ALU.mult, ALU.add,
                        [xp.t, self.prm_t], [xr.t])
                for j in (1, 2, 3):
                    self.stt(xr.ap[:, 0:L], xp.ap[:, j:j + L], w4[:, j * MC + m:j * MC + m + 1], xr.ap[:, 0:L], ALU.mult, ALU.add,
                             [xp.t, xr.t, self.prm_t], [xr.t])
                self.cp(xrb.ap[:, 0:L], xr.ap[:, 0:L], [xr.t], [xrb.t], eng="gpsimd")
                for d in range(2):
                    h = hs[d]
                    order = range(NT) if d == 0 else range(NT - 1, -1, -1)
                    s1 = self.MISC[:, l * (4 + 4 * MC) + 4 + d * MC + m: l * (4 + 4 * MC) + 4 + d * MC + m + 1]
                    s2 = self.MISC[:, l * (4 + 4 * MC) + 4 + 2 * MC + d * MC + m: l * (4 + 4 * MC) + 4 + 2 * MC + d * MC + m + 1]
                    for ti_, tI in enumerate(order):
                        t0 = tI * TT
                        pa, px = self.PS[(ti_ % 2) * 2], self.PS[(ti_ % 2) * 2 + 1]
                        wa = self.BD[:, bd0 + ((0 * 2 + d) * MC + m) * 128: bd0 + ((0 * 2 + d) * MC + m + 1) * 128]
                        wx = self.BD[:, bd0 + ((1 * 2 + d) * MC + m) * 128: bd0 + ((1 * 2 + d) * MC + m + 1) * 128]
                        self.mm(pa.ap[:, 0:TT], wa, xrb.ap[:, t0:t0 + TT], True, True, [self.bd_t, xrb.t], [pa.t])
                        self.mm(px.ap[:, 0:TT], wx, xrb.ap[:, t0:t0 + TT], True, True, [self.bd_t, xrb.t], [px.t])
                        r, ig, a, a2, u = tA[0 + 4 * (ti_ % 2)], tA[1 + 4 * (ti_ % 2)], tA[2 + 4 * (ti_ % 2)], tA[3 + 4 * (ti_ % 2)], None
                        self.act(r.ap[:, 0:TT], pa.ap[:, 0:TT], AF.Sigmoid, [pa.t, self.prm_t], [r.t],
                                 bias=self.prm[f"b_rg_a_{l}"][:, d * MC + m:d * MC + m + 1])
                        self.act(ig.ap[:, 0:TT], px.ap[:, 0:TT], AF.Sigmoid, [px.t, self.prm_t], [ig.t],
                                 bias=self.prm[f"b_rg_x_{l}"][:, d * MC + m:d * MC + m + 1])
                        self.act(a.ap[:, 0:TT], r.ap[:, 0:TT], AF.Exp, [r.t, self.misc_t], [a.t], scale=s1)
                        self.tt(a2.ap[:, 0:TT], a.ap[:, 0:TT], a.ap[:, 0:TT], ALU.mult, [a.t], [a2.t], eng="gpsimd")
                        self.act(a2.ap[:, 0:TT], a2.ap[:, 0:TT], AF.Sqrt, [a2.t], [a2.t], scale=-1.0, bias=1.0)
                        self.tt(ig.ap[:, 0:TT], ig.ap[:, 0:TT], xr.ap[:, t0:t0 + TT], ALU.mult, [ig.t, xr.t], [ig.t])
                        self.tt(ig.ap[:, 0:TT], ig.ap[:, 0:TT], a2.ap[:, 0:TT], ALU.mult, [ig.t, a2.t], [ig.t])
                        if ti_ == 0:
                            init = self.prm[f"st_{l}"][:, d * MC + m:d * MC + m + 1] if is_s else 0.0
                        else:
                            pt0 = order[ti_ - 1] * TT
                            init = h.ap[:, pt0 + TT - 1:pt0 + TT] if d == 0 else h.ap[:, pt0:pt0 + 1]
                        if d == 0:
                            self.p.op("vector", lambda e, h=h, a=a, ig=ig, init=init, t0=t0: e.tensor_tensor_scan(
                                out=h.ap[:, t0:t0 + TT], data0=a.ap[:, 0:TT], data1=ig.ap[:, 0:TT], initial=init,
                                op0=ALU.mult, op1=ALU.add), [a.t, ig.t, h.t, self.prm_t], [h.t])
                        else:
                            self.p.op("vector", lambda e, h=h, a=a, ig=ig, init=init, t0=t0: e.tensor_tensor_scan(
                                out=h.ap[:, t0:t0 + TT][:, ::-1], data0=a.ap[:, 0:TT][:, ::-1], data1=ig.ap[:, 0:TT][:, ::-1],
                                initial=init, op0=ALU.mult, op1=ALU.add), [a.t, ig.t, h.t, self.prm_t], [h.t])
                    if not is_s:
                        col = (b * 2 + d) * MC + m
                        src = h.ap[:, L - 1:L] if d == 0 else h.ap[:, 0:1]
                        self.cp(hl.ap[:, col:col + 1], src, [h.t], [hl.t], eng="gpsimd")
                y = yo[m % 2]
                self.tt(hs[0].ap[:, 0:L], hs[0].ap[:, 0:L], hs[1].ap[:, 0:L], ALU.add, [hs[0].t, hs[1].t], [hs[0].t])
                self.tt(y.ap[:, 0:L], hs[0].ap[:, 0:L], gy.ap[:, 0:L], ALU.mult, [hs[0].t, gy.t], [y.t])
                self.dma("gpsimd", S["Y"][3 * DM + m * 128:3 * DM + (m + 1) * 128, tok0:tok0 + L], y.ap[:, 0:L], [y.t], [self.wtok])
        ncol = c.NPB * 2 * MC
        ps = self.PS[4]
        self.tr(ps.ap[0:ncol, 0:128], hl.ap[:, 0:ncol], self.ident.ap, [hl.t, self.ident.t], [ps.t])
        self.cp(hlt.ap[0:ncol, :], ps.ap[0:ncol, 0:128], [ps.t], [hlt.t])
        for b in range(c.NPB):
            self.dma("gpsimd", self.O["nst"][b, l, :].rearrange("(r c) -> r c", c=128),
                     hlt.ap[b * 2 * MC:(b + 1) * 2 * MC, :], [hlt.t], [self.wtok])

    def B_attn(self, l):
        c, S, I = self.c, self.S, self.I
        MC, DM, PC, PAST = c.MC, c.DM, c.PC, c.PAST
        HA = MC
        ar = self.ar
        ar.reset()
        MMAX = PAST + c.NS
        kT = Buf(ar.bf16(HA * MMAX))
        Va = Buf(ar.bf16((MMAX // 128) * DM))
        cst = [Buf(ar.f32(PC * DM)) for _ in range(2)]
        qp = [Buf(ar.bf16(2 * 512)) for _ in range(2)]
        pt = [Buf(ar.bf16(512)) for _ in range(6)]
        lacc = [Buf(ar.f32(512)) for _ in range(2)]
        o0 = Buf(ar.f32(512))
        o1 = Buf(ar.f32(512))
        rl = [Buf(ar.f32(512)) for _ in range(2)]
        sqb = Buf(ar.f32(512))
        yb = [Buf(ar.bf16(512)) for _ in range(2)]
        for q in qp:
            self.memset(q.ap, 0.0, [q.t])
        neglam = self.misc(l, 0)
        gsub = self.misc(l, 1)
        psS = [self.PS[0], self.PS[1], self.PS[7]]
        psO = [self.PS[2], self.PS[3]]
        psL = [self.PS[4], self.PS[5]]
        psX = self.PS[6]
        qi = 0
        for si, (tok0, L, is_s, b) in enumerate(self.seqs):
            self.p.barrier()
            P0 = PAST if is_s else 0
            M = P0 + L
            NKC = M // 128
            kv = kT.ap[:, 0:HA * M].rearrange("p (h m) -> p h m", m=M)
            vv = Va.ap[:, 0:NKC * DM].rearrange("p (k e) -> p k e", e=DM)
            if is_s:
                ks, vs = cst
                self.dma("gpsimd", ks.ap.rearrange("p (j e) -> p j e", e=DM), I["ck"][l].rearrange("(j p) e -> p j e", p=128), [], [ks.t])
                self.dma("gpsimd", vs.ap.rearrange("p (j e) -> p j e", e=DM), I["cv"][l].rearrange("(j p) e -> p j e", p=128), [], [vs.t])
                self.cp(vv[:, 0:PC, :], vs.ap.rearrange("p (j e) -> p j e", e=DM), [vs.t], [Va.t])
                n = 0
                for j in range(PC):
                    for h in range(HA):
                        ps = self.PS[6 + n % 2]
                        n += 1
                        self.tr(ps.ap[:, 0:128], ks.ap[:, j * DM + h * 128:j * DM + (h + 1) * 128], self.ident.ap,
                                [ks.t, self.ident.t], [ps.t])
                        self.cp(kv[:, h, j * 128:(j + 1) * 128], ps.ap[:, 0:128], [ps.t], [kT.t], eng=("scalar" if n % 2 else "vector"))
            self.dma("gpsimd", kv[:, :, P0:P0 + L], S["KT"][:, tok0:tok0 + L].rearrange("(h p) t -> p h t", p=128), [self.wtok], [kT.t])
            self.dma("gpsimd", vv[:, P0 // 128:NKC, :], S["V"][tok0:tok0 + L, :].rearrange("(j p) e -> p j e", p=128), [self.wtok], [Va.t])
            QB = min(512, L)
            for h in range(HA):
                for qb in range(L // QB):
                    self.p.barrier()
                    q = qp[qi % 2]
                    qi += 1
                    qv = q.ap.rearrange("p (j t) -> p j t", t=512)
                    c0 = tok0 + qb * QB
                    self.dma("gpsimd", qv[0:64, 0, 0:QB], S["Q"][h * 128:h * 128 + 64, c0:c0 + QB], [self.wtok], [q.t])
                    self.dma("gpsimd", qv[64:128, 1, 0:QB], S["Q"][h * 128 + 64:h * 128 + 128, c0:c0 + QB], [self.wtok], [q.t])
                    steps = [(kc, j) for kc in range(NKC) for j in range(2)]

                    def emitS(i):
                        kc, j = steps[i]
                        ps = psS[i % 3]
                        self.mm(ps.ap[:, 0:QB], kv[:, h, kc * 128:(kc + 1) * 128], qv[:, j, 0:QB], True, True, [kT.t, q.t], [ps.t])

                    emitS(0)
                    if len(steps) > 1:
                        emitS(1)
                    for i, (kc, j) in enumerate(steps):
                        if i + 2 < len(steps):
                            emitS(i + 2)
                        ps = psS[i % 3]
                        pb = pt[i % 6]
                        self.act(pb.ap[:, 0:QB], ps.ap[:, 0:QB], AF.Exp, [ps.t], [pb.t], scale=0.125)
                        self.mm(psO[j].ap[:, 0:QB], vv[:, kc, h * 128:(h + 1) * 128], pb.ap[:, 0:QB], kc == 0, kc == NKC - 1,
                                [Va.t, pb.t], [psO[j].t])
                        self.mm(psL[j].ap[:, 0:QB], self.onesb.ap, pb.ap[:, 0:QB], kc == 0, kc == NKC - 1,
                                [self.onesb.t, pb.t], [psL[j].t])
                    for j in range(2):
                        self.recip(rl[j].ap[:, 0:QB], psL[j].ap[:, 0:QB], [psL[j].t], [rl[j].t])
                    self.tt(o0.ap[:, 0:QB], psO[0].ap[:, 0:QB], rl[0].ap[:, 0:QB], ALU.mult, [psO[0].t, rl[0].t], [o0.t])
                    self.tt(o1.ap[:, 0:QB], psO[1].ap[:, 0:QB], rl[1].ap[:, 0:QB], ALU.mult, [psO[1].t, rl[1].t], [o1.t])
                    self.stt(o0.ap[:, 0:QB], o1.ap[:, 0:QB], neglam, o0.ap[:, 0:QB], ALU.mult, ALU.add, [o0.t, o1.t, self.misc_t], [o0.t])
                    self.act(sqb.ap[:, 0:QB], o0.ap[:, 0:QB], AF.Square, [o0.t], [sqb.t])
                    self.mm(psX.ap[:, 0:QB], self.ones.ap, sqb.ap[:, 0:QB], True, True, [self.ones.t, sqb.t], [psX.t])
                    self.act(sqb.ap[:, 0:QB], psX.ap[:, 0:QB], AF.Sqrt, [psX.t], [sqb.t], scale=1.0 / 128, bias=EPS)
                    self.recip(sqb.ap[:, 0:QB], sqb.ap[:, 0:QB], [sqb.t], [sqb.t])
                    y = yb[qi % 2]
                    self.stt(y.ap[:, 0:QB], o0.ap[:, 0:QB], gsub, sqb.ap[:, 0:QB], ALU.mult, ALU.mult, [o0.t, sqb.t, self.misc_t], [y.t])
                    self.dma("gpsimd", S["Y"][h * 128:(h + 1) * 128, c0:c0 + QB], y.ap[:, 0:QB], [y.t], [self.wtok])

    def phase_C(self, l):
        c, S = self.c, self.S
        KC, MC, DM = c.KC, c.MC, c.DM
        ar = self.ar
        ar.reset()
        TM = 512
        x = Buf(ar.f32(KC * TM))
        h = Buf(ar.bf16(KC * TM))
        yt = Buf(ar.bf16(4 * MC * TM))
        mg = Buf(ar.bf16(KC * TM))
        h2 = Buf(ar.bf16(KC * TM))
        g = [Buf(ar.f32(TM)) for _ in range(3)]
        tt_ = [Buf(ar.f32(TM)) for _ in range(2)]
        acc = [Buf(ar.f32(TM)) for _ in range(2)]
        sq = acc
        sd = Buf(ar.f32(TM))
        wbrs = [Buf(ar.bf16(4 * MC * 128)) for _ in range(2)]
        self.set_wbufs(KC * 128)
        W, WB, WO = S[f"WIN{l}"], S[f"WBR{l}"], S[f"WOUT{l}"]
        psG = [self.PS[i] for i in (0, 1, 2)]
        psT = [self.PS[i] for i in (3, 4)]
        psM = [self.PS[i] for i in (5, 6)]
        gi = 0
        for (tok0, T, v, seq, first, last) in self.tiles:
            self.p.barrier()
            kt = lambda a: a.rearrange("p (k t) -> p k t", t=T)
            self.dma("gpsimd", kt(x.ap[:, 0:KC * T]), S["XT"][:, tok0:tok0 + T].rearrange("(k p) t -> p k t", p=128), [self.wtok], [x.t])
            self.dma("gpsimd", kt(h.ap[:, 0:KC * T]), S["HT"][:, tok0:tok0 + T].rearrange("(k p) t -> p k t", p=128), [self.wtok], [h.t])
            self.dma("gpsimd", kt(yt.ap[:, 0:4 * MC * T]), S["Y"][:, tok0:tok0 + T].rearrange("(k p) t -> p k t", p=128), [self.wtok], [yt.t])
            for n in range(KC):
                ub = wbrs[n % 2]
                self.dma("sync", ub.ap, WB[n], [], [ub.t])
                a = acc[n % 2]
                for j in range(4):
                    ug = self.wload(W[10 * MC + j * KC + n])
                    pg = psG[gi % 3]
                    gb = g[gi % 3]
                    gi += 1
                    for k in range(KC):
                        self.mm(pg.ap[:, 0:T], ug.ap[:, k * 128:(k + 1) * 128], h.ap[:, k * T:(k + 1) * T], k == 0, k == KC - 1,
                                [ug.t, h.t], [pg.t])
                    self.act(gb.ap[:, 0:T], pg.ap[:, 0:T], AF.Sigmoid, [pg.t], [gb.t])
                    pt_ = psT[j % 2]
                    for m in range(MC):
                        self.mm(pt_.ap[:, 0:T], ub.ap[:, (j * MC + m) * 128:(j * MC + m + 1) * 128],
                                yt.ap[:, (j * MC + m) * T:(j * MC + m + 1) * T], m == 0, m == MC - 1, [ub.t, yt.t], [pt_.t])
                    if j == 0:
                        self.tt(a.ap[:, 0:T], pt_.ap[:, 0:T], gb.ap[:, 0:T], ALU.mult, [pt_.t, gb.t], [a.t])
                    else:
                        t = tt_[j % 2]
                        self.tt(t.ap[:, 0:T], pt_.ap[:, 0:T], gb.ap[:, 0:T], ALU.mult, [pt_.t, gb.t], [t.t])
                        if j < 3:
                            self.tt(a.ap[:, 0:T], a.ap[:, 0:T], t.ap[:, 0:T], ALU.add, [a.t, t.t], [a.t], eng="gpsimd")
                        else:
                            self.tt(mg.ap[:, n * T:(n + 1) * T], a.ap[:, 0:T], t.ap[:, 0:T], ALU.add, [a.t, t.t], [mg.t], eng="gpsimd")
            for n in range(KC):
                uo = self.wload(WO[n])
                pm_ = psM[n % 2]
                for k in range(KC):
                    self.mm(pm_.ap[:, 0:T], uo.ap[:, k * 128:(k + 1) * 128], mg.ap[:, k * T:(k + 1) * T], k == 0, k == KC - 1,
                            [uo.t, mg.t], [pm_.t])
                self.stt(x.ap[:, n * T:(n + 1) * T], pm_.ap[:, 0:T], self.modv(l, v, 2)[:, n:n + 1], x.ap[:, n * T:(n + 1) * T],
                         ALU.mult, ALU.add, [pm_.t, x.t, self.mod_t], [x.t])
            self.dma("gpsimd", S["XT"][:, tok0:tok0 + T].rearrange("(k p) t -> p k t", p=128), kt(x.ap[:, 0:KC * T]), [x.t], [self.wtok])
            self.norm_mod(x, T, None, None, None, self.PS[7], sq, sd, tt_)
            self.apply_mod(x, T, sd, self.abv(l, v, 1), self.modv(l, v, 3), lambda k: h2.ap[:, k * T:(k + 1) * T], h2.t, tt_,
                           [self.mod_t])
            self.dma("gpsimd", S["H2"][:, tok0:tok0 + T].rearrange("(k p) t -> p k t", p=128), kt(h2.ap[:, 0:KC * T]), [h2.t], [self.wtok])

    def phase_D(self, l):
        c, S = self.c, self.S
        KC, FC = c.KC, c.FC
        ar = self.ar
        ar.reset()
        TM = 512
        TH = TM + 2
        x = Buf(ar.f32(KC * TM))
        h2 = Buf(ar.bf16(KC * TH))
        gbuf = Buf(ar.bf16(FC * TM))
        ab = [Buf(ar.f32(TH)) for _ in range(2)]
        ac = [Buf(ar.f32(TM)) for _ in range(2)]
        sl = [Buf(ar.f32(TM)) for _ in range(2)]
        sq = [Buf(ar.f32(TM)) for _ in range(2)]
        tmp = sq
        sd = Buf(ar.f32(TM))
        last_layer = (l == c.DEPTH - 1)
        if last_layer:
            yf = [Buf(ar.f32(128)) for _ in range(2)]
            ost = [Buf(ar.f32(c.D)) for _ in range(1)] * 2
        WU, WD = S[f"WUP{l}"], S[f"WDN{l}"]
        self.set_wbufs(c.UMAX)
        psA = [self.PS[0], self.PS[1]]
        psU = [self.PS[2], self.PS[3]]
        psH = self.PS[4]
        psD = [self.PS[5], self.PS[6]]
        fw = [self.prm[f"fcw{t}_{l}"] for t in range(3)]
        fb = self.prm[f"fcb_{l}"]
        oi = 0
        for (tok0, T, v, seq, first, last) in self.tiles:
            self.p.barrier()
            TT = T + 2
            hv = h2.ap[:, 0:KC * TT].rearrange("p (k t) -> p k t", t=TT)
            kt = lambda a: a.rearrange("p (k t) -> p k t", t=T)
            self.dma("gpsimd", kt(x.ap[:, 0:KC * T]), S["XT"][:, tok0:tok0 + T].rearrange("(k p) t -> p k t", p=128), [self.wtok], [x.t])
            lo = tok0 - (0 if first else 1)
            hi = tok0 + T + (0 if last else 1)
            if first:
                self.memset(hv[:, :, 0:1], 0.0, [h2.t])
            if last:
                self.memset(hv[:, :, TT - 1:TT], 0.0, [h2.t])
            self.dma("gpsimd", hv[:, :, (1 if first else 0):(TT - 1 if last else TT)],
                     S["H2"][:, lo:hi].rearrange("(k p) t -> p k t", p=128), [self.wtok], [h2.t])
            for i in range(FC):
                u = self.wload(WU[i])
                pa, pu = psA[i % 2], psU[i % 2]
                for k in range(KC):
                    self.mm(pa.ap[:, 0:T], u.ap[:, k * 128:(k + 1) * 128], hv[:, k, 1:1 + T], k == 0, k == KC - 1, [u.t, h2.t], [pa.t])
                for k in range(KC):
                    self.mm(psH.ap[:, 2 * i:2 * i + 2], u.ap[:, k * 128:(k + 1) * 128], hv[:, k, 0:TT:TT - 1], k == 0, k == KC - 1,
                            [u.t, h2.t], [psH.t])
                for k in range(KC):
                    self.mm(pu.ap[:, 0:T], u.ap[:, (KC + k) * 128:(KC + k + 1) * 128], hv[:, k, 1:1 + T], k == 0, k == KC - 1,
                            [u.t, h2.t], [pu.t])
                a = ab[i % 2]
                self.cp(a.ap[:, 1:1 + T], pa.ap[:, 0:T], [pa.t], [a.t], eng="scalar")
                self.cp(a.ap[:, 0:TT:TT - 1], psH.ap[:, 2 * i:2 * i + 2], [psH.t], [a.t], eng="vector")
                cc = ac[i % 2]
                self.ts(cc.ap[:, 0:T], a.ap[:, 0:T], fw[0][:, i:i + 1], fb[:, i:i + 1], ALU.mult, ALU.add, [a.t, self.prm_t], [cc.t])
                self.stt(cc.ap[:, 0:T], a.ap[:, 1:1 + T], fw[1][:, i:i + 1], cc.ap[:, 0:T], ALU.mult, ALU.add, [a.t, cc.t, self.prm_t], [cc.t])
                self.stt(cc.ap[:, 0:T], a.ap[:, 2:2 + T], fw[2][:, i:i + 1], cc.ap[:, 0:T], ALU.mult, ALU.add, [a.t, cc.t, self.prm_t], [cc.t])
                s = sl[i % 2]
                self.act(s.ap[:, 0:T], cc.ap[:, 0:T], AF.Silu, [cc.t], [s.t])
                self.tt(gbuf.ap[:, i * T:(i + 1) * T], pu.ap[:, 0:T], s.ap[:, 0:T], ALU.mult, [pu.t, s.t], [gbuf.t])
            for n in range(KC):
                u = self.wload(WD[n])
                pd = psD[n % 2]
                for f in range(FC):
                    self.mm(pd.ap[:, 0:T], u.ap[:, f * 128:(f + 1) * 128], gbuf.ap[:, f * T:(f + 1) * T], f == 0, f == FC - 1,
                            [u.t, gbuf.t], [pd.t])
                self.stt(x.ap[:, n * T:(n + 1) * T], pd.ap[:, 0:T], self.modv(l, v, 5)[:, n:n + 1], x.ap[:, n * T:(n + 1) * T],
                         ALU.mult, ALU.add, [pd.t, x.t, self.mod_t], [x.t])
            if not last_layer:
                self.dma("gpsimd", S["XT"][:, tok0:tok0 + T].rearrange("(k p) t -> p k t", p=128), kt(x.ap[:, 0:KC * T]), [x.t], [self.wtok])
            else:
                self.norm_mod(x, T, None, None, None, self.PS[7], sq, sd, tmp)
                dst = self.O["ys"] if v == 0 else self.O["yp"]
                r0 = tok0 if v == 0 else tok0 - c.NS
                gF = self.prm["gfinal"]
                for tb in range(T // 128):
                    o = ost[oi % 2]
                    oi += 1
                    for k0 in range(0, KC, 4):
                        ps = psA[(k0 // 4) % 2] if (k0 // 4) % 4 < 2 else psU[(k0 // 4) % 2]
                        for kk in range(4):
                            k = k0 + kk
                            y = yf[k % 2]
                            self.stt(y.ap[:, 0:128], x.ap[:, k * T + tb * 128:k * T + (tb + 1) * 128], gF[:, k:k + 1],
                                     sd.ap[:, tb * 128:(tb + 1) * 128], ALU.mult, ALU.mult, [x.t, sd.t, self.prm_t], [y.t])
                            self.tr(ps.ap[:, kk * 128:(kk + 1) * 128], y.ap[:, 0:128], self.ident.ap, [y.t, self.ident.t], [ps.t])
                        self.cp(o.ap[:, k0 * 128:(k0 + 4) * 128], ps.ap[:, 0:512], [ps.t], [o.t], eng=("scalar" if (k0 // 4) % 2 else "vector"))
                    self.dma("gpsimd", dst[r0 + tb * 128:r0 + (tb + 1) * 128, :], o.ap, [o.t], [self.wtok])

    def build(self, stop=None):
        c = self.c
        self.wtok = None
        for nm, fn in (("consts", self.setup_consts), ("params", self.load_params), ("convert", self.convert_weights),
                       ("mod", self.compute_mod), ("misc", self.setup_misc), ("rope", self.setup_rope)):
            fn()
            self.p.barrier()
            if stop == nm:
                self.p.finish()
                return self.nc
        for l in range(c.DEPTH):
            done = False
            for ph, fn in (("A", self.phase_A), ("B", self.phase_B), ("C", self.phase_C), ("D", self.phase_D)):
                fn(l)
                self.p.barrier()
                if stop == f"{ph}{l}":
                    done = True
                    break
            if done:
                break
        self.p.finish()
        return self.nc


def make_in_maps(cfg, inputs, n_cores):
    c = cfg
    maps = []
    wn = list(WSHAPES(c).keys())
    for i in range(n_cores):
        m = {}
        m["xs"] = np.ascontiguousarray(inputs["x_sample"][i])
        m["xp"] = np.ascontiguousarray(inputs["x_prompt"][i * c.NPB:(i + 1) * c.NPB]).reshape(c.NPB * c.SP, c.D)
        m["ck"] = np.ascontiguousarray(inputs["cache_k"][i]).reshape(c.DEPTH, c.PAST, c.DM)
        m["cv"] = np.ascontiguousarray(inputs["cache_v"][i]).reshape(c.DEPTH, c.PAST, c.DM)
        m["st"] = np.ascontiguousarray(inputs["state_lru"][i]).reshape(c.DEPTH, 2 * c.DM)
        m["cvec"] = np.concatenate([np.asarray(inputs["c"][i]).reshape(-1), np.asarray(inputs["c_ctx"]).reshape(-1)])
        for n in wn:
            m[n] = np.ascontiguousarray(inputs[n])
        maps.append(m)
    return maps


def gather_outputs(cfg, results, n_cores):
    c = cfg
    B = n_cores * c.NPB
    ys = np.stack([results[i]["ys"] for i in range(n_cores)], 0)
    yp = np.concatenate([results[i]["yp"].reshape(c.NPB, c.SP, c.D) for i in range(n_cores)], 0)
    nk = np.concatenate([results[i]["nk"] for i in range(n_cores)], 0).reshape(B, c.DEPTH, c.SP, c.DM // 128, 2, 64)
    nv = np.concatenate([results[i]["nv"] for i in range(n_cores)], 0).reshape(B, c.DEPTH, c.SP, c.DM // 128, 128)
    ns = np.concatenate([results[i]["nst"] for i in range(n_cores)], 0).reshape(B, c.DEPTH, 2, c.DM)
    return (yp.astype(np.float32), ys.astype(np.float32), nk.astype(np.float32), nv.astype(np.float32), ns.astype(np.float32))


def kernel(**inputs):
    inputs = {k: np.asarray(v, dtype=np.float32) for k, v in inputs.items()}
    cfg = Cfg()
    n = 8
    nc = K(cfg).build()
    maps = make_in_maps(cfg, inputs, n)
    res = run_bass_kernel_spmd(nc, maps, core_ids=list(range(n)))
    return gather_outputs(cfg, res.results, n)
```
